# Optimizing a Trainium2 kernel written in Bass

```python
import jax
import jax.numpy as jnp
from jax import lax
import numpy as np

D_MODEL = 1024
BATCH = 8
SEQ = 2048
DEPTH = 1

N_MEM = 256
GRID_W = 64
EPS = 1e-6

ATT_HEADS = 8
ATT_KV_HEADS = 2
ATT_HEAD_DIM = 64
ATT_Q_DIM = ATT_HEADS * ATT_HEAD_DIM
ATT_KV_DIM = ATT_KV_HEADS * ATT_HEAD_DIM
ATT_BLOCK = 128
ROPE_THETA = 10000.0
ROPE_PAIRS_PER_AXIS = ATT_HEAD_DIM // 4

HG_HEADS = 4
HG_HEAD_K = 128
HG_HEAD_V = 128
HG_KEY_DIM = HG_HEADS * HG_HEAD_K
HG_VAL_DIM = HG_HEADS * HG_HEAD_V
HG_CHUNK = 32

MIX_WIDTH = ATT_Q_DIM + HG_VAL_DIM
IN_SPLITS = (ATT_Q_DIM, ATT_KV_DIM, ATT_KV_DIM, HG_KEY_DIM, HG_KEY_DIM, HG_KEY_DIM, HG_VAL_DIM, HG_VAL_DIM)
N_IN = ATT_Q_DIM + 2 * ATT_KV_DIM + 3 * HG_KEY_DIM + 2 * HG_VAL_DIM

X_HEADS = 4
X_HEAD_DIM = D_MODEL // X_HEADS

D_FF = 2816
CONV_W = 3

kernel_name = "hymba_axial_gqa_hgrn2_sandwich_convffn"


def rmsnorm(x, g):
    xf = x.astype(jnp.float32)
    y = xf * lax.rsqrt(jnp.mean(xf * xf, axis=-1, keepdims=True) + EPS)
    return (y * g.astype(jnp.float32)).astype(x.dtype)


def split_columns(p, sizes):
    idx = np.cumsum(np.array(sizes))[:-1].tolist()
    return jnp.split(p, idx, axis=-1)


def axial_rope_tables(n):
    rows = n // GRID_W
    r, c = jnp.meshgrid(jnp.arange(rows), jnp.arange(GRID_W), indexing="ij")
    inv = jnp.power(ROPE_THETA, -jnp.arange(ROPE_PAIRS_PER_AXIS, dtype=jnp.float32) / ROPE_PAIRS_PER_AXIS)
    ang = jnp.concatenate([r.reshape(-1, 1).astype(jnp.float32) * inv,
                           c.reshape(-1, 1).astype(jnp.float32) * inv], axis=-1)
    return jnp.cos(ang), jnp.sin(ang)


def apply_rope(x, cos, sin):
    b, n, h, d = x.shape
    xp = x.reshape(b, n, h, d // 2, 2)
    x0, x1 = xp[..., 0], xp[..., 1]
    c = cos[None, :, None, :].astype(x.dtype)
    s = sin[None, :, None, :].astype(x.dtype)
    return jnp.stack([x0 * c - x1 * s, x0 * s + x1 * c], axis=-1).reshape(b, n, h, d)


def axial_gqa_attention(q, k, v):
    b, n = q.shape[0], q.shape[1]
    grp = ATT_HEADS // ATT_KV_HEADS
    nb = n // ATT_BLOCK
    qb = jnp.moveaxis(q.reshape(b, nb, ATT_BLOCK, ATT_KV_HEADS, grp, ATT_HEAD_DIM), 1, 0)
    scale = ATT_HEAD_DIM ** -0.5

    def one_block(qblk):
        s = jnp.einsum("bqhgd,bkhd->bhgqk", qblk, k).astype(jnp.float32) * scale
        p = jax.nn.softmax(s, axis=-1).astype(v.dtype)
        return jnp.einsum("bhgqk,bkhd->bqhgd", p, v)

    o = lax.map(one_block, qb)
    return jnp.moveaxis(o, 0, 1).reshape(b, n, ATT_Q_DIM)


def bidirectional_gated_scan(q, logf_fwd, logf_bwd, v):
    b, n, h, kd = q.shape
    vd = v.shape[-1]
    nc = n // HG_CHUNK

    def flip(a):
        return a[:, ::-1]

    qs = jnp.stack([q, flip(q)])
    lf = jnp.stack([logf_fwd, flip(logf_bwd)])
    vs = jnp.stack([v, flip(v)])
    ks = -jnp.expm1(lf)

    def to_chunks(a):
        a = a.reshape(2, b, nc, HG_CHUNK, h, a.shape[-1])
        return jnp.transpose(a, (2, 0, 1, 4, 3, 5))

    tri = jnp.tril(jnp.ones((HG_CHUNK, HG_CHUNK), dtype=bool))[:, :, None]

    def step(S, xs):
        qx, kx, vx, lx = xs
        bcum = jnp.cumsum(lx, axis=-2)
        diff = bcum[..., :, None, :] - bcum[..., None, :, :]
        dec = jnp.exp(jnp.where(tri, diff, -jnp.inf))
        a = jnp.einsum("...tk,...sk,...tsk->...ts", qx, kx, dec)
        o_intra = jnp.einsum("...ts,...sv->...tv", a, vx)
        o_inter = jnp.einsum("...tk,...kv->...tv", qx * jnp.exp(bcum), S)
        b_last = bcum[..., -1:, :]
        k_dec = kx * jnp.exp(b_last - bcum)
        S_new = jnp.exp(b_last[..., 0, :])[..., :, None] * S + jnp.einsum("...sk,...sv->...kv", k_dec, vx)
        return S_new, o_intra + o_inter

    S0 = jnp.zeros((2, b, h, kd, vd), jnp.float32)
    _, out = lax.scan(step, S0, (to_chunks(qs), to_chunks(ks), to_chunks(vs), to_chunks(lf)))
    out = jnp.transpose(out, (1, 2, 0, 4, 3, 5)).reshape(2, b, n, h, vd)
    return out[0] + flip(out[1])


def hgrn2_group(q, zf_fwd, zf_bwd, i_in, g, lb, out_g):
    b, n = q.shape[0], q.shape[1]

    def heads(a, d):
        return a.reshape(b, n, HG_HEADS, d)

    def log_forget(z, lb_dir):
        f = lb_dir + (1.0 - lb_dir) * jax.nn.sigmoid(z.astype(jnp.float32))
        return heads(jnp.log(f), HG_HEAD_K)

    qf = heads(jax.nn.silu(q.astype(jnp.float32)), HG_HEAD_K)
    o = bidirectional_gated_scan(qf, log_forget(zf_fwd, lb[0]), log_forget(zf_bwd, lb[1]),
                                 heads(i_in.astype(jnp.float32), HG_HEAD_V))
    o = rmsnorm(o, out_g).reshape(b, n, HG_VAL_DIM)
    return (o * jax.nn.silu(g.astype(jnp.float32))).astype(g.dtype)


def memory_cross_attention(h, m, wq, wkv, wo):
    b, n = h.shape[0], h.shape[1]
    nm = m.shape[1]
    q = (h @ wq).reshape(b, n, X_HEADS, X_HEAD_DIM)
    kv = (m @ wkv).reshape(b, nm, 2, X_HEADS, X_HEAD_DIM)
    k, v = kv[:, :, 0], kv[:, :, 1]
    s = jnp.einsum("bqhd,bkhd->bhqk", q, k).astype(jnp.float32) * (X_HEAD_DIM ** -0.5)
    p = jax.nn.softmax(s, axis=-1).astype(v.dtype)
    o = jnp.einsum("bhqk,bkhd->bqhd", p, v).reshape(b, n, X_HEADS * X_HEAD_DIM)
    return o @ wo


def conv_ffn(h, w_up, conv_w, conv_b, w_down):
    n = h.shape[1]
    u = h @ w_up
    half = CONV_W // 2
    up = jnp.pad(u, ((0, 0), (half, half), (0, 0)))
    acc = conv_b
    for j in range(CONV_W):
        acc = acc + up[:, j:j + n] * conv_w[j]
    gate, val = jnp.split(acc, 2, axis=-1)
    return (jax.nn.silu(gate) * val) @ w_down


def setup_inputs(seed: int = 0) -> dict:
    key = jax.random.key(seed)
    ks = jax.random.split(key, 22)
    f32 = jnp.float32
    L = DEPTH

    def nrm(k, shape, scale):
        return jax.random.normal(k, shape, f32) * scale

    def gain(k, shape):
        return 1.0 + nrm(k, shape, 0.05)

    return {
        "x": nrm(ks[0], (BATCH, SEQ, D_MODEL), 1.0),
        "mem": nrm(ks[1], (BATCH, N_MEM, D_MODEL), 1.0),
        "pre_mix_g": gain(ks[2], (L, D_MODEL)),
        "w_in": nrm(ks[3], (L, D_MODEL, N_IN), D_MODEL ** -0.5),
        "q_norm_g": gain(ks[4], (L, ATT_HEAD_DIM)),
        "k_norm_g": gain(ks[5], (L, ATT_HEAD_DIM)),
        "hg_lb": nrm(ks[6], (2, L + 1, HG_KEY_DIM), 0.5),
        "hg_out_norm_g": gain(ks[7], (L, HG_HEAD_V)),
        "w_out": nrm(ks[8], (L, MIX_WIDTH, D_MODEL), MIX_WIDTH ** -0.5),
        "post_mix_g": gain(ks[9], (L, D_MODEL)),
        "pre_x_g": gain(ks[10], (L, D_MODEL)),
        "mem_norm_g": gain(ks[11], (L, D_MODEL)),
        "w_xq": nrm(ks[12], (L, D_MODEL, D_MODEL), D_MODEL ** -0.5),
        "w_xkv": nrm(ks[13], (L, D_MODEL, 2 * D_MODEL), D_MODEL ** -0.5),
        "w_xo": nrm(ks[14], (L, D_MODEL, D_MODEL), D_MODEL ** -0.5),
        "post_x_g": gain(ks[15], (L, D_MODEL)),
        "pre_ffn_g": gain(ks[16], (L, D_MODEL)),
        "w_up": nrm(ks[17], (L, D_MODEL, 2 * D_FF), D_MODEL ** -0.5),
        "conv_w": nrm(ks[18], (L, CONV_W, 2 * D_FF), CONV_W ** -0.5),
        "conv_b": nrm(ks[19], (L, 2 * D_FF), 0.02),
        "w_down": nrm(ks[20], (L, D_FF, D_MODEL), D_FF ** -0.5),
        "post_ffn_g": gain(ks[21], (L, D_MODEL)),
    }


def reference(x, mem, pre_mix_g, w_in, q_norm_g, k_norm_g, hg_lb, hg_out_norm_g, w_out, post_mix_g,
              pre_x_g, mem_norm_g, w_xq, w_xkv, w_xo, post_x_g, pre_ffn_g, w_up, conv_w, conv_b,
              w_down, post_ffn_g):
    b, n = x.shape[0], x.shape[1]
    cos, sin = axial_rope_tables(n)
    lb_all = jnp.cumsum(jax.nn.softmax(hg_lb.astype(jnp.float32), axis=1), axis=1)
    for l in range(DEPTH):
        h = rmsnorm(x, pre_mix_g[l])
        aq, ak, av, hq, hf_fwd, hf_bwd, hi, hg = split_columns(h @ w_in[l], IN_SPLITS)
        aq = rmsnorm(aq.reshape(b, n, ATT_HEADS, ATT_HEAD_DIM), q_norm_g[l])
        ak = rmsnorm(ak.reshape(b, n, ATT_KV_HEADS, ATT_HEAD_DIM), k_norm_g[l])
        av = av.reshape(b, n, ATT_KV_HEADS, ATT_HEAD_DIM)
        att = axial_gqa_attention(apply_rope(aq, cos, sin), apply_rope(ak, cos, sin), av)
        rec = hgrn2_group(hq, hf_fwd, hf_bwd, hi, hg, lb_all[:, l], hg_out_norm_g[l])
        mixed = jnp.concatenate([att, rec], axis=-1) @ w_out[l]
        x = x + rmsnorm(mixed, post_mix_g[l])
        h = rmsnorm(x, pre_x_g[l])
        m = rmsnorm(mem, mem_norm_g[l])
        x = x + rmsnorm(memory_cross_attention(h, m, w_xq[l], w_xkv[l], w_xo[l]), post_x_g[l])
        h = rmsnorm(x, pre_ffn_g[l])
        x = x + rmsnorm(conv_ffn(h, w_up[l], conv_w[l], conv_b[l], w_down[l]), post_ffn_g[l])
    return x
```

```python
import numpy as np
import ml_dtypes
import concourse.bass as bass
import concourse.mybir as mybir
from concourse.bass_utils import run_bass_kernel_spmd

F32 = mybir.dt.float32
BF16 = mybir.dt.bfloat16
AF = mybir.ActivationFunctionType
ALU = mybir.AluOpType
AX = mybir.AxisListType

N_TOK = 2048
D = 1024
NB = 16
EPS = 1e-6
D_FF = 2816
N_IN = 3328


class Res:
    __slots__ = ("name", "w", "r", "big")

    def __init__(self, name):
        self.name = name
        self.w = None
        self.r = {}
        self.big = False


class EngQ:
    def __init__(self, name):
        self.name = name
        self.ops = []
        self.epoch = 0
        self.cnt = 0
        self.total = 0
        self.waited = {}
        self.pending = False


class Prog:
    ENGS = ("pe", "act", "dve", "pool", "sp")
    LIMIT = 900
    NEPOCH = {"pe": 14, "act": 10, "dve": 14, "pool": 4, "sp": 1}

    def __init__(self, nc, n_dma_sems=32):
        self.nc = nc
        self.q = {e: EngQ(e) for e in self.ENGS}
        self.sems = {}
        self.n_dma_sems = n_dma_sems
        self.dma_cnt = [0] * n_dma_sems
        self.dma_pool = {"pool": list(range(0, 16)), "sp": list(range(16, 26)), "act": list(range(26, 32))}
        self.dma_rr = {"pool": 0, "sp": 0, "act": 0}
        self._ctx = []
        self.res = {}

    def R(self, *key):
        r = self.res.get(key)
        if r is None:
            r = Res(key)
            self.res[key] = r
        return r

    def open(self):
        nc = self.nc
        for e in self.ENGS:
            for ep in range(self.NEPOCH[e]):
                c = nc.semaphore("s_%s%d" % (e, ep))
                self.sems[(e, ep)] = c.__enter__()
                self._ctx.append(c)
        for i in range(self.n_dma_sems):
            c = nc.semaphore("s_dma%d" % i)
            self.sems[("dma", i)] = c.__enter__()
            self._ctx.append(c)

    def close(self):
        for c in reversed(self._ctx):
            c.__exit__(None, None, None)

    def _need(self, eng, tok, same_ok):
        if tok is None:
            return
        key, val = tok
        q = self.q[eng]
        if key[0] == "dma":
            if q.waited.get(key, 0) >= val:
                return
            q.waited[key] = val
        else:
            src, ep = key
            if src == eng and not same_ok:
                return
            if q.waited.get(src, (-1, 0)) >= (ep, val):
                return
            q.waited[src] = (ep, val)
        sem = self.sems[key]
        q.ops.append(lambda h, sem=sem, val=val: h.wait_ge(sem, val))

    def _deps(self, eng, reads, writes):
        for r in reads:
            self._need(eng, r.w, same_ok=(eng != "pe" and not r.big))
        so = (eng != "pe")
        for w in writes:
            self._need(eng, w.w, same_ok=so)
            for tok in w.r.values():
                self._need(eng, tok, same_ok=so)

    def op(self, eng, fn, reads=(), writes=(), inc=True, big=False):
        q = self.q[eng]
        if q.cnt >= self.LIMIT and not q.pending:
            q.epoch += 1
            q.cnt = 0
            assert q.epoch < self.NEPOCH[eng], "out of epochs for " + eng
        self._deps(eng, reads, writes)
        key = (eng, q.epoch)
        val = q.cnt + 1
        tok = (key, val)
        if inc:
            q.cnt = val
            q.total += 1
            sem = self.sems[key]
            q.ops.append(lambda h, fn=fn, sem=sem: fn(h).then_inc(sem, 1))
            q.pending = False
        else:
            q.ops.append(lambda h, fn=fn: fn(h))
            q.pending = True
        for r in reads:
            r.r[eng] = tok
        for w in writes:
            w.w = tok
            w.r = {}
            w.big = big
        return tok

    def dma(self, eng, out, in_, reads=(), writes=(), also_wait=(), **kw):
        pl = self.dma_pool[eng]
        i = pl[self.dma_rr[eng] % len(pl)]
        self.dma_rr[eng] += 1
        key = ("dma", i)
        if self.dma_cnt[i] > 0:
            self._need(eng, (key, 16 * self.dma_cnt[i]), same_ok=True)
        self._deps(eng, reads, list(writes) + list(also_wait))
        self.dma_cnt[i] += 1
        assert self.dma_cnt[i] < 60
        val = 16 * self.dma_cnt[i]
        sem = self.sems[key]
        q = self.q[eng]
        q.ops.append(lambda h, sem=sem, out=out, in_=in_, kw=kw:
                     h.dma_start(out=out, in_=in_, **kw).then_inc(sem, 16))
        tok = (key, val)
        for r in reads:
            r.r[key] = tok
        for w in writes:
            w.w = tok
            w.r = {}
            w.big = False
        return tok

    def wait_all(self, eng, skip=()):
        for r in self.res.values():
            if r.name[0] in skip:
                continue
            self._need(eng, r.w, same_ok=True)
            for tok in list(r.r.values()):
                self._need(eng, tok, same_ok=True)

    def emit(self):
        nc = self.nc
        for e in self.ENGS:
            assert not self.q[e].pending, "engine %s ends with non-inc'd instr" % e
        with nc.Block() as block:
            @block.tensor
            def _(h):
                for f in self.q["pe"].ops:
                    f(h)

            @block.scalar
            def _(h):
                for f in self.q["act"].ops:
                    f(h)

            @block.vector
            def _(h):
                for f in self.q["dve"].ops:
                    f(h)

            @block.gpsimd
            def _(h):
                for f in self.q["pool"].ops:
                    f(h)

            @block.sync
            def _(h):
                for f in self.q["sp"].ops:
                    f(h)


class StopBuild(Exception):
    pass


def build(stop=99, dbg=()):
    st = {}
    try:
        return _build(stop, dbg, st)
    except StopBuild:
        return finish(st["nc"], st["P"])


def _build(stop, dbg, st):
    nc = bass.Bass("TRN2", target_bir_lowering=False)
    P = Prog(nc)
    P.open()
    R = P.R
    st["nc"] = nc
    st["P"] = P

    def chk(level):
        if stop <= level:
            raise StopBuild()

    def din(name, shape, dt=F32):
        return nc.dram_tensor(name, list(shape), dt, kind="ExternalInput").ap()

    x_d = din("x", [N_TOK, D])
    mem_d = din("mem", [256, D])
    w_in_d = din("w_in", [D, N_IN])
    w_out_d = din("w_out", [D, D])
    w_xq_d = din("w_xq", [D, D])
    w_xkv_d = din("w_xkv", [D, 2 * D])
    w_xo_d = din("w_xo", [D, D])
    w_up_d = din("w_up", [D, 2 * D_FF])
    w_down_d = din("w_down", [D_FF, D])
    gains_d = din("gains", [7, D])
    qkg_d = din("qkg", [1, 128])
    hgo_d = din("hgo", [1, 128])
    lbT_d = din("lbT", [128, 16])
    convT_d = din("convT", [128, 44 * 4])
    ident_d = din("ident", [128, 128], BF16)
    rope_d = din("rope", [128, 2 * 16 * 32])
    mask_d = din("mask", [128, 2 * 128], BF16)
    out_d = nc.dram_tensor("out", [N_TOK, D], F32, kind="ExternalOutput").ap()
    dbg_d = {}
    for name, shape in dbg:
        dbg_d[name] = nc.dram_tensor("dbg_" + name, list(shape), F32, kind="ExternalOutput").ap()

    def sb(name, shape, dt):
        return nc.alloc_sbuf_tensor("sb_" + name, list(shape), dt)

    hT = sb("hT", [128, 8, N_TOK], BF16)
    mixT = sb("mixT", [128, 8, N_TOK], BF16)
    arena1 = sb("arena1", [128, 16384], F32)
    arena2 = sb("arena2", [128, 15872], F32)
    ident = sb("ident", [128, 128], BF16)
    ones_bf = sb("ones_bf", [128, 128], BF16)
    gpre = sb("gpre", [128, D], F32)
    gpost = sb("gpost", [128, D], F32)
    stat = sb("stat", [128, 64], F32)
    stat2 = sb("stat2", [128, 64], F32)
    junk = sb("junk", [128, D], BF16)
    hb = sb("hb", [128, 2, D], BF16)
    lbT = sb("lbT", [128, 16], F32)
    lbv = sb("lbv", [128, 40], F32)
    convT = sb("convT", [128, 44 * 4], F32)
    epsc = sb("epsc", [128, 1], F32)
    psf = nc.alloc_psum_tensor("psf", [128, 3, 2, 512], F32)
    psb = nc.alloc_psum_tensor("psb", [128, 2, 1024], BF16)

    X = arena1[:].rearrange("p (b d) -> p b d", d=D)

    def a1_bf(off_bytes, shape):
        n = int(np.prod(shape))
        v = arena1[:, off_bytes // 4: off_bytes // 4 + n // 2].bitcast(BF16)
        return v, off_bytes + n * 2

    def a2_bf(off_bytes, n):
        v = arena2[:, off_bytes // 4: off_bytes // 4 + n // 2].bitcast(BF16)
        return v, off_bytes + n * 2

    def a2_f(off_bytes, n):
        v = arena2[:, off_bytes // 4: off_bytes // 4 + n]
        return v, off_bytes + n * 4

    def a1_f(off_bytes, n):
        v = arena1[:, off_bytes // 4: off_bytes // 4 + n]
        return v, off_bytes + n * 4

    def MM(out, lhsT, rhs, start, stop, reads, writes, inc=True, tp=None):
        if tp is None:
            return P.op("pe", lambda h: h.matmul(out, lhsT=lhsT, rhs=rhs, start=start, stop=stop),
                        reads=reads, writes=writes, inc=inc)
        return P.op("pe", lambda h: h.matmul(out, lhsT=lhsT, rhs=rhs, start=start, stop=stop, tile_position=tp),
                    reads=reads, writes=writes, inc=inc)

    def TR(out, in_, reads, writes, inc=True):
        return P.op("pe", lambda h: h.transpose(out, in_, ident[:]), reads=reads, writes=writes, inc=inc)

    BIG_N = 1024

    def isbig(ap):
        n = 1
        for d_ in ap.shape[1:]:
            n *= d_
        return n >= BIG_N

    def ACT(out, in_, func, reads, writes, scale=None, bias=None, accum=None):
        kw = {}
        if scale is not None:
            kw["scale"] = scale
        if bias is not None:
            kw["bias"] = bias
        if accum is not None:
            kw["accum_out"] = accum
        return P.op("act", lambda h: h.activation(out=out, in_=in_, func=func, **kw), reads=reads, writes=writes,
                    big=(accum is None and isbig(out)))

    def TT(eng, out, in0, in1, op, reads, writes):
        return P.op(eng, lambda h: h.tensor_tensor(out=out, in0=in0, in1=in1, op=op), reads=reads, writes=writes, big=isbig(out))

    def TS(eng, out, in0, s1, s2, op0, op1, reads, writes):
        if s2 is None:
            return P.op(eng, lambda h: h.tensor_scalar(out=out, in0=in0, scalar1=s1, scalar2=None, op0=op0),
                        reads=reads, writes=writes, big=isbig(out))
        return P.op(eng, lambda h: h.tensor_scalar(out=out, in0=in0, scalar1=s1, scalar2=s2, op0=op0, op1=op1),
                    reads=reads, writes=writes, big=isbig(out))

    def STT(eng, out, in0, scalar, in1, op0, op1, reads, writes):
        return P.op(eng, lambda h: h.scalar_tensor_tensor(out=out, in0=in0, scalar=scalar, in1=in1, op0=op0, op1=op1),
                    reads=reads, writes=writes, big=isbig(out))

    def CP(eng, out, in_, reads, writes):
        if eng == "act":
            return ACT(out, in_, AF.Copy, reads, writes)
        return P.op(eng, lambda h: h.tensor_copy(out=out, in_=in_), reads=reads, writes=writes, big=isbig(out))

    def RECIP(out, in_, reads, writes):
        return P.op("dve", lambda h: h.reciprocal(out=out, in_=in_), reads=reads, writes=writes)

    def MEMSET(eng, ap, val, writes):
        return P.op(eng, lambda h: h.memset(ap, val), writes=writes)

    def barrier(skip=()):
        for e in Prog.ENGS:
            P.wait_all(e, skip)

    def load_w(dst, src_rows_cols, reads_w, also_wait=()):
        src = src_rows_cols.rearrange("(c p) n -> p c n", p=128)
        nch = src.shape[1]
        for c in range(nch):
            P.dma("pool", dst[:, c, :], src[:, c, :], writes=reads_w, also_wait=also_wait)

    def gain_load(dst, row, res):
        P.dma("sp", dst[:], gains_d[row:row + 1, :].partition_broadcast(128), writes=[res])

    def dump(name, ap_sb, reads, view=None):
        if name in dbg_d:
            P.dma("sp", dbg_d[name] if view is None else view(dbg_d[name]), ap_sb, reads=reads)

    P.dma("sp", ident[:], ident_d, writes=[R("ident")])
    P.dma("sp", lbT[:], lbT_d, writes=[R("lbT")])
    P.dma("sp", convT[:], convT_d, writes=[R("convT")])
    MEMSET("pool", ones_bf[:], 1.0, [R("ones")])
    MEMSET("pool", epsc[:], EPS, [R("epsc")])
    MEMSET("pool", stat[:], 0.0, [R("stat")])
    MEMSET("pool", stat2[:], 0.0, [R("stat2")])

    def prenorm_T(grow, nblk=NB, src=None, dstT=None, tag="h", srcR=None):
        src = X if src is None else src
        dstT = hT if dstT is None else dstT
        srcR = (lambda b: R("X", b)) if srcR is None else srcR
        gain_load(gpre, grow, R("gpre"))
        for b in range(nblk):
            ACT(junk[:], src[:, b, :], AF.Square, [srcR(b)], [R("junk"), R("stat")], accum=stat[:, b:b + 1])
        TS("dve", stat2[:, 0:nblk], stat[:, 0:nblk], 1.0 / D, EPS, ALU.mult, ALU.add, [R("stat")], [R("stat2")])
        ACT(stat2[:, 0:nblk], stat2[:, 0:nblk], AF.Ln, [R("stat2")], [R("stat2")])
        ACT(stat2[:, 0:nblk], stat2[:, 0:nblk], AF.Exp, [R("stat2")], [R("stat2")], scale=-0.5)
        for b in range(nblk):
            k = b % 2
            STT("dve", hb[:, k, :], src[:, b, :], stat2[:, b:b + 1], gpre[:], ALU.mult, ALU.mult,
                [srcR(b), R("stat2"), R("gpre")], [R("hb", k)])
            for c in range(8):
                TR(psb[:, k, c * 128:(c + 1) * 128], hb[:, k, c * 128:(c + 1) * 128],
                   [R("hb", k), R("ident")], [R("psb", k)], inc=(c == 7))
            CP("act", dstT[:, :, b * 128:(b + 1) * 128], psb[:, k, :].rearrange("p (c t) -> p c t", t=128),
               [], [R("psb", k), R(tag, b)])

    o2 = 0
    Watt, o2 = a2_bf(o2, 8 * 768); Watt = Watt.rearrange("p (c n) -> p c n", n=768)
    rope_raw, o2 = a2_f(o2, 1024)
    qkgB, o2 = a2_f(o2, 128)
    o2_attw = o2
    load_w(Watt, w_in_d[:, 0:768], [R("Watt")])
    P.dma("sp", rope_raw, rope_d, writes=[R("rope_raw")])
    P.dma("sp", qkgB, qkg_d.partition_broadcast(128), writes=[R("qkgB")])
    for b in range(NB):
        P.dma("sp" if b % 2 else "act", X[:, b, :], x_d[b * 128:(b + 1) * 128, :], writes=[R("X", b)])
    prenorm_T(0)
    if "hT" in dbg_d:
        tmpf = arena2[:, 13000:13000 + 2048]
        for c in range(8):
            CP("dve", tmpf, hT[:, c, :], [R("h", b) for b in range(NB)], [R("dbgtmp")])
            P.dma("sp", dbg_d["hT"][c * 128:(c + 1) * 128, :], tmpf, reads=[R("dbgtmp")])
    if stop <= 1:
        return finish(nc, P)
    barrier()
    if stop <= 1.1:
        return finish(nc, P)


    o1 = 0
    ropeT = []
    for i in range(8):
        v, o1 = a1_f(o1, 512)
        ropeT.append(v.rearrange("p (b i) -> p b i", i=32))
    QT, o1 = a1_bf(o1, [4 * N_TOK]); QT = QT.rearrange("p (j t) -> p j t", t=N_TOK)
    KTd, o1 = a1_bf(o1, [2 * N_TOK]); KTd = KTd.rearrange("p (g t) -> p g t", t=N_TOK)
    VA, o1 = a1_bf(o1, [NB * 2 * 192]); VA = VA.rearrange("p (b g d) -> p b g d", g=2, d=192)
    o2 = o2_attw
    qkv = []; sqb = []; tq = []; qn = []; qr = []
    for i in range(2):
        v, o2 = a2_f(o2, 768); qkv.append(v)
        v, o2 = a2_f(o2, 640); sqb.append(v)
        v, o2 = a2_f(o2, 4 * 320); tq.append(v.rearrange("p (a n) -> p a n", n=320))
        v, o2 = a2_f(o2, 640); qn.append(v)
        v, o2 = a2_bf(o2, 768); qr.append(v)
    PT = []
    for i in range(3):
        v, o2 = a2_bf(o2, 1024); PT.append(v.rearrange("p (u n) -> p u n", n=512))
    rc = []
    for i in range(2):
        v, o2 = a2_f(o2, 512); rc.append(v)
    assert o1 <= 65536 and o2 <= 63488, (o1, o2)

    if stop <= 1.16:
        return finish(nc, P)
    MEMSET("pool", VA, 1.0, [R("VA", b) for b in range(NB)])
    if stop <= 1.17:
        return finish(nc, P)
    cosv = rope_raw[:, 0:512].rearrange("p (b i) -> p b i", i=32)
    sinv = rope_raw[:, 512:1024].rearrange("p (b i) -> p b i", i=32)
    for qk in range(2):
        gv = qkgB[:, qk * 64:(qk + 1) * 64].rearrange("p (i two) -> p i two", two=2)
        ge = gv[:, :, 0].unsqueeze(1).broadcast_to([128, NB, 32])
        go = gv[:, :, 1].unsqueeze(1).broadcast_to([128, NB, 32])
        for ti, (tab, gg) in enumerate(((cosv, ge), (sinv, go), (sinv, ge), (cosv, go))):
            TT("dve", ropeT[qk * 4 + ti], tab, gg, ALU.mult, [R("rope_raw"), R("qkgB")], [R("ropeT")])

    def att_proj(b):
        k = b % 2
        tokb = slice(b * 128, (b + 1) * 128)
        for c in range(8):
            MM(psf[:, k, 0, :], hT[:, c, tokb], Watt[:, c, 0:512], c == 0, c == 7,
               [R("h", b), R("Watt")], [R("psf", k, 0)], inc=False)
        for c in range(8):
            MM(psf[:, k, 1, 0:256], hT[:, c, tokb], Watt[:, c, 512:768], c == 0, c == 7,
               [R("h", b), R("Watt")], [R("psf", k, 1)], inc=(c == 7))
        ACT(qkv[k][:, 0:512], psf[:, k, 0, :], AF.Copy, [], [R("psf", k, 0), R("qkv", k)])
        ACT(qkv[k][:, 512:768], psf[:, k, 1, 0:256], AF.Copy, [], [R("psf", k, 1), R("qkv", k)])
        ACT(sqb[k], qkv[k][:, 0:640], AF.Square, [R("qkv", k)], [R("sqb", k)])
        ss = stat[:, 16 + 10 * k: 26 + 10 * k]
        rs = stat2[:, 16 + 10 * k: 26 + 10 * k]
        P.op("dve", lambda h: h.tensor_reduce(out=ss, in_=sqb[k].rearrange("p (h d) -> p h d", d=64),
                                              axis=AX.X, op=ALU.add),
             reads=[R("sqb", k)], writes=[R("ss", k)])
        TS("dve", rs, ss, 1.0 / 64, EPS, ALU.mult, ALU.add, [R("ss", k)], [R("rs", k)])
        ACT(rs, rs, AF.Ln, [R("rs", k)], [R("rs", k)])
        ACT(rs, rs, AF.Exp, [R("rs", k)], [R("rs", k)], scale=-0.5)
        for qk, eng, nh, c0 in ((0, "dve", 8, 0), (1, "dve", 2, 512)):
            src = qkv[k][:, c0:c0 + nh * 64].rearrange("p (h i two) -> p h i two", i=32, two=2)
            xe = src[:, :, :, 0]
            xo = src[:, :, :, 1]
            tabs = [ropeT[qk * 4 + ti][:, b, :].unsqueeze(1).broadcast_to([128, nh, 32]) for ti in range(4)]
            tt = [tq[k][:, a, 0:nh * 32].rearrange("p (h i) -> p h i", i=32) for a in range(4)]
            rd = [R("qkv", k), R("ropeT")]
            wr = [R("tq", k, qk)]
            TT(eng, tt[0], xe, tabs[0], ALU.mult, rd, wr)
            TT(eng, tt[1], xo, tabs[1], ALU.mult, rd, wr)
            TT(eng, tt[2], xe, tabs[2], ALU.mult, rd, wr)
            TT(eng, tt[3], xo, tabs[3], ALU.mult, rd, wr)
            dst = qn[k][:, qk * 512: qk * 512 + nh * 64].rearrange("p (h i two) -> p h i two", i=32, two=2)
            TT(eng, dst[:, :, :, 0], tt[0], tt[1], ALU.subtract, wr, [R("qn", k, qk)])
            TT(eng, dst[:, :, :, 1], tt[2], tt[3], ALU.add, wr, [R("qn", k, qk)])
            rsb = rs[:, qk * 8: qk * 8 + nh]
            if qk == 0:
                TT(eng, qr[k][:, 0:512].rearrange("p (h d) -> p h d", d=64),
                   qn[k][:, 0:512].rearrange("p (h d) -> p h d", d=64),
                   rsb.unsqueeze(2).broadcast_to([128, 8, 64]), ALU.mult,
                   [R("qn", k, 0), R("rs", k)], [R("qr", k, 0)])
            else:
                for dup in range(2):
                    TT(eng, qr[k][:, 512:768].rearrange("p (g u d) -> p g u d", u=2, d=64)[:, :, dup, :],
                       qn[k][:, 512:640].rearrange("p (g d) -> p g d", d=64),
                       rsb.unsqueeze(2).broadcast_to([128, 2, 64]), ALU.mult,
                       [R("qn", k, 1), R("rs", k)], [R("qr", k, 1)])
        CP("dve", VA[:, b, :, 64:128], qkv[k][:, 640:768].rearrange("p (g d) -> p g d", d=64),
           [R("qkv", k)], [R("VA", b)])
    def att_proj2(b):
        k = b % 2
        tokb = slice(b * 128, (b + 1) * 128)
        for j in range(6):
            TR(psb[:, k, j * 128:(j + 1) * 128], qr[k][:, j * 128:(j + 1) * 128],
               [R("qr", k, 0), R("qr", k, 1), R("ident")], [R("psb", k)], inc=(j == 5))
        CP("act", QT[:, :, tokb], psb[:, k, 0:512].rearrange("p (j t) -> p j t", t=128), [], [R("psb", k), R("QT", b)])
        CP("dve", KTd[:, :, tokb], psb[:, k, 512:768].rearrange("p (g t) -> p g t", t=128), [], [R("psb", k), R("KT", b)])

    if stop <= 1.2:
        return finish(nc, P)
    for b in range(NB + 1):
        if b < NB:
            att_proj(b)
        if b >= 1:
            att_proj2(b - 1)
    if "QT" in dbg_d:
        tmpf = arena2[:, 14000:14000 + 1024]
        for c in range(4):
            for hf in range(2):
                CP("dve", tmpf, QT[:, c, hf * 1024:(hf + 1) * 1024], [R("QT", b) for b in range(NB)], [R("dbgtmp")])
                P.dma("sp", dbg_d["QT"][c * 128:(c + 1) * 128, hf * 1024:(hf + 1) * 1024], tmpf, reads=[R("dbgtmp")])
    if stop <= 1.5:
        return finish(nc, P)

    Wh, _ = a2_bf(0, 8 * 640); Wh = Wh.rearrange("p (c n) -> p c n", n=640)
    for gi, c0 in enumerate((768, 1280, 1792, 2304, 2816)):
        src = w_in_d[:, c0: c0 + 128].rearrange("(c p) n -> p c n", p=128)
        P.dma("pool", Wh[:, :, gi * 128:(gi + 1) * 128], src, writes=[R("Wh")], also_wait=[R("Watt")])
    PREFETCHED_WH0 = True
    pt_i = [0]
    psbf = [psb[:, 0, :].bitcast(F32), psb[:, 1, :].bitcast(F32)]
    unit_i = [0]

    def att_pair_tile(m, qt):
        g = m // 2
        j = m
        qs = slice(qt * 512, (qt + 1) * 512)
        ui = unit_i[0] % 2
        unit_i[0] += 1
        if ui == 0:
            accs = [(psf[:, 2, 0, :], R("psf", 2, 0)), (psf[:, 2, 1, :], R("psf", 2, 1))]
        else:
            accs = [(psbf[0], R("psb", 0)), (psbf[1], R("psb", 1))]
        qreads = [R("QT", qt * 4 + i) for i in range(4)]

        def st_mm(kb):
            k = kb % 2
            for par in range(2):
                rows = slice(par * 64, par * 64 + 64)
                MM(psf[:, k, par, :], KTd[rows, g, kb * 128:(kb + 1) * 128], QT[rows, j, qs], True, True,
                   qreads + [R("KT", kb)], [R("psf", k, par)], inc=(par == 1))

        st_mm(0)
        for kb in range(16):
            if kb + 1 < 16:
                st_mm(kb + 1)
            pi = pt_i[0] % 3
            pt_i[0] += 1
            k = kb % 2
            ACT(PT[pi], psf[:, k, :, :], AF.Exp, [], [R("psf", k, 0), R("psf", k, 1), R("PT", pi)], scale=0.125)
            for par in range(2):
                vsl = slice(64, 192) if par == 0 else slice(0, 128)
                MM(accs[par][0], VA[:, kb, g, vsl], PT[pi][:, par, :], kb == 0, kb == 15,
                   [R("VA", kb), R("PT", pi)], [accs[par][1]], inc=(par == 1))
        rcb = rc[ui]
        for par in range(2):
            rows = slice(par * 64, par * 64 + 64)
            orow = slice((1 - par) * 64, (1 - par) * 64 + 64)
            acc, accR = accs[par]
            P.op("dve", lambda hh, acc=acc, rows=rows, orow=orow: hh.reciprocal(out=rcb[rows, :], in_=acc[orow, :]),
                 reads=[], writes=[accR, R("rc", ui, par)])
            TT("dve", mixT[rows, j, qs], acc[rows, :], rcb[rows, :], ALU.mult, [R("rc", ui, par)], [accR, R("mixT", j, qt)])

    for m in range(4):
        for qt in range(4):
            att_pair_tile(m, qt)

    if "att" in dbg_d:
        tmpf = arena2[:, 0:2048]
        for c in range(4):
            CP("dve", tmpf, mixT[:, c, :], [R("mixT", c, qt) for qt in range(4)], [R("dbgtmp")])
            P.dma("sp", dbg_d["att"][c * 128:(c + 1) * 128, :], tmpf, reads=[R("dbgtmp")])
    if stop <= 2:
        return finish(nc, P)
    barrier(skip=("Wh",))

    bank_i = [0]

    def bank():
        i = bank_i[0] % 6
        bank_i[0] += 1
        return psf[:, i // 2, i % 2, :], R("psf", i // 2, i % 2)

    o1 = 0
    fb = {}
    for nm in ("qs", "SG", "E", "LF", "B", "KK", "rmask", "gate"):
        fb[nm], o1 = a1_f(o1, 2048)
    Tbig = arena1[:, 2 * 2048: 4 * 2048].rearrange("p (c v) -> p c v", v=128)
    o2 = 0
    Wh, o2 = a2_bf(o2, 8 * 640); Wh = Wh.rearrange("p (c n) -> p c n", n=640)
    Of, o2 = a2_f(o2, 2048); Of3 = Of.rearrange("p (b v) -> p b v", v=128)
    qtl, o2 = a2_bf(o2, 2048)
    ktl, o2 = a2_bf(o2, 2048)
    ktok, o2 = a2_bf(o2, 2048); ktok = ktok.rearrange("p (b v) -> p b v", v=128)
    vtok, o2 = a2_bf(o2, 2048); vtok = vtok.rearrange("p (b v) -> p b v", v=128)
    Sbf, o2 = a2_bf(o2, 32 * 128); Sbf = Sbf.rearrange("p (c v) -> p c v", v=128)
    AT, o2 = a2_bf(o2, 2048); AT = AT.rearrange("p (b v) -> p b v", v=128)
    recb, o2 = a2_bf(o2, 2048); recb = recb.rearrange("p (b v) -> p b v", v=128)
    hgoB, o2 = a2_f(o2, 128)
    maskS, o2 = a2_bf(o2, 256); maskS = maskS.rearrange("p (d t) -> p d t", t=128)
    gtmp, o2 = a2_f(o2, 256)
    assert o1 <= 65536 and o2 <= 63488, (o1, o2)
    gate3 = fb["gate"].rearrange("p (b v) -> p b v", v=128)

    P.dma("sp", hgoB, hgo_d.partition_broadcast(128), writes=[R("hgoB")])
    P.dma("sp", maskS, mask_d.rearrange("p (d t) -> p d t", t=128), writes=[R("maskS")])
    MEMSET("pool", fb["rmask"], 1.0, [R("rmask")])
    MEMSET("pool", fb["rmask"].rearrange("p (c t) -> p c t", t=64)[:, :, 0:1], 0.0, [R("rmask")])
    lv = lbT[:].rearrange("p (d l h) -> p d l h", l=2, h=4)
    TT("dve", lbv[:, 0:8].rearrange("p (d h) -> p d h", h=4), lv[:, :, 0, :], lv[:, :, 1, :], ALU.subtract,
       [R("lbT")], [R("lbv")])
    ACT(lbv[:, 0:8], lbv[:, 0:8], AF.Exp, [R("lbv")], [R("lbv")], scale=-1.0)
    TS("dve", lbv[:, 0:8], lbv[:, 0:8], 1.0, None, ALU.add, None, [R("lbv")], [R("lbv")])
    RECIP(lbv[:, 0:8], lbv[:, 0:8], [R("lbv")], [R("lbv")])
    TS("dve", lbv[:, 8:16], lbv[:, 0:8], -1.0, 1.0, ALU.mult, ALU.add, [R("lbv")], [R("lbv")])
    TS("dve", lbv[:, 16:24], lbv[:, 0:8], -0.5, 0.5, ALU.mult, ALU.add, [R("lbv")], [R("lbv")])
    TS("dve", lbv[:, 24:32], lbv[:, 0:8], 0.5, 0.5, ALU.mult, ALU.add, [R("lbv")], [R("lbv")])
    TS("dve", lbv[:, 32:40], lbv[:, 0:8], 0.5, -0.5, ALU.mult, ALU.add, [R("lbv")], [R("lbv")])

    ALLH = [R("h", b) for b in range(NB)]
    chk(2.1)

    def load_Wh(hh, extra=()):
        for gi, c0 in enumerate((768, 1280, 1792, 2304, 2816)):
            src = w_in_d[:, c0 + hh * 128: c0 + (hh + 1) * 128].rearrange("(c p) n -> p c n", p=128)
            P.dma("pool", Wh[:, :, gi * 128:(gi + 1) * 128], src, writes=[R("Wh")] + list(extra))

    def hgrn_head(hh):
        for bp in range(8):
            bk, bkR = bank()
            for u in range(2):
                b = 2 * bp + u
                for c in range(8):
                    MM(bk[:, u * 256:(u + 1) * 256], hT[:, c, b * 128:(b + 1) * 128], Wh[:, c, 384:640], c == 0, c == 7,
                       [R("h", b), R("Wh")], [bkR], inc=(u == 1 and c == 7))
            bv = bk.rearrange("p (u n) -> p u n", n=256)
            ACT(vtok[:, 2 * bp:2 * bp + 2, :], bv[:, :, 0:128], AF.Copy, [], [bkR, R("vtok")])
            g3 = gtmp.rearrange("p (u n) -> p u n", n=128)
            ACT(gate3[:, 2 * bp:2 * bp + 2, :], bv[:, :, 128:256], AF.Silu, [], [bkR, R("gate")])

        chk(2.2)

        def fm_proj(gi, dst, dstname):
            kept = []
            for qt in range(4):
                bk, bkR = bank()
                for c in range(8):
                    MM(bk, Wh[:, c, gi * 128:(gi + 1) * 128], hT[:, c, qt * 512:(qt + 1) * 512], c == 0, c == 7,
                       ALLH[qt * 4:qt * 4 + 4] + [R("Wh")], [bkR], inc=(c == 7))
                ACT(dst[:, qt * 512:(qt + 1) * 512], bk, AF.Tanh, [], [bkR, R(dstname, qt // 2)], scale=0.5)
                kept.append((bk, bkR))
            return kept

        for qt in range(4):
            bk, bkR = bank()
            for c in range(8):
                MM(bk, Wh[:, c, 0:128], hT[:, c, qt * 512:(qt + 1) * 512], c == 0, c == 7,
                   ALLH[qt * 4:qt * 4 + 4] + [R("Wh")], [bkR], inc=(c == 7))
            sl = slice(qt * 512, (qt + 1) * 512)
            ACT(fb["qs"][:, sl], bk, AF.Silu, [], [bkR, R("qs", qt // 2)])

        chk(2.3)
        for dr in range(2):
            li = dr * 4 + hh
            ha_ap = lbv[:, 16 + li: 17 + li]
            hb_ap = lbv[:, 24 + li: 25 + li]
            nha_ap = lbv[:, 32 + li: 33 + li]
            fm_proj(1 + dr, fb["E"], "E")
            if dr == 1:
                if hh < 3:
                    load_Wh(hh + 1)
                else:
                    load_w(hT, w_xkv_d, [R("Wkv")], also_wait=ALLH)
            E, SG, LF, B, KK = fb["E"], fb["SG"], fb["LF"], fb["B"], fb["KK"]
            EB, ENB = SG, LF
            H2 = (slice(0, 1024), slice(1024, 2048))
            for hf in range(2):
                sl = H2[hf]
                ACT(LF[:, sl], E[:, sl], AF.Ln, [R("E", hf), R("lbv")], [R("LF", hf)], scale=ha_ap, bias=hb_ap)
            for hf in range(2):
                sl = H2[hf]
                TS("dve", KK[:, sl], E[:, sl], nha_ap, ha_ap, ALU.mult, ALU.add, [R("E", hf), R("lbv")], [R("KK", hf)])
                P.op("dve", lambda h, sl=sl: h.tensor_tensor_scan(out=B[:, sl], data0=fb["rmask"][:, sl], data1=LF[:, sl],
                                                                  initial=0.0, op0=ALU.mult, op1=ALU.add),
                     reads=[R("rmask"), R("LF", hf)], writes=[R("B", hf)])
                if dr == 1:
                    TT("dve", LF[:, sl], LF[:, sl], B[:, sl], ALU.subtract, [R("LF", hf), R("B", hf)], [R("LF", hf)])
                    B3 = B[:, sl].rearrange("p (c t) -> p c t", t=64)
                    TT("dve", E[:, sl].rearrange("p (c t) -> p c t", t=64), LF[:, sl].rearrange("p (c t) -> p c t", t=64),
                       B3[:, :, 63:64].broadcast_to([128, 16, 64]), ALU.add, [R("LF", hf), R("B", hf)], [R("E", hf)])
            Bs, Bn = (B, "B") if dr == 0 else (E, "E")
            for hf in range(2):
                sl = H2[hf]
                ACT(EB[:, sl], Bs[:, sl], AF.Exp, [R(Bn, hf)], [R("SG", hf)])
                ACT(ENB[:, sl], Bs[:, sl], AF.Exp, [R(Bn, hf)], [R("LF", hf)], scale=-1.0)
            for hf in range(2):
                sl = H2[hf]
                TT("dve", qtl[:, sl], fb["qs"][:, sl], EB[:, sl], ALU.mult, [R("qs", hf), R("SG", hf)], [R("qtl", hf)])
                TT("dve", ktl[:, sl], KK[:, sl], ENB[:, sl], ALU.mult, [R("KK", hf), R("LF", hf)], [R("ktl", hf)])
            chk(2.4)
            for half in range(2):
                for i in range(8):
                    b = half * 8 + i
                    TR(psb[:, half, i * 128:(i + 1) * 128], ktl[:, b * 128:(b + 1) * 128], [R("ktl", half), R("ident")],
                       [R("psb", half)], inc=(i == 7))
                CP("act" if half else "dve", ktok[:, half * 8:(half + 1) * 8, :],
                   psb[:, half, :].rearrange("p (b v) -> p b v", v=128), [], [R("psb", half), R("ktok")])
            for bg in range(4):
                bk, bkR = bank()
                for u in range(4):
                    b = 4 * bg + u
                    MM(bk[:, u * 128:(u + 1) * 128], ktl[:, b * 128:(b + 1) * 128], qtl[:, b * 128:(b + 1) * 128], True, True,
                       [R("ktl", bg // 2), R("qtl", bg // 2)], [bkR], inc=(u == 3))
                TT("dve", AT[:, 4 * bg:4 * bg + 4, :], bk.rearrange("p (u t) -> p u t", t=128),
                   maskS[:, dr, :].unsqueeze(1).broadcast_to([128, 4, 128]), ALU.mult, [R("maskS")], [bkR, R("AT")])
            chk(2.5)
            MEMSET("pool", Sbf[:, 0 if dr == 0 else 31, :], 0.0, [R("Sbf")])
            DEAD = [R(nm, hf) for nm in ("E", "LF") for hf in range(2)]
            SGR = [R("SG", 0), R("SG", 1)]
            n = 0
            cprev = None
            for bgi in range(4):
                bg = bgi if dr == 0 else 3 - bgi
                ub = [bank() for _ in range(2)]
                for u in range(4):
                    b = 4 * bg + u
                    for j in range(2):
                        MM(ub[j][0][:, u * 128:(u + 1) * 128], ktok[64 * j:64 * j + 64, b, :], vtok[64 * j:64 * j + 64, b, :],
                           True, True, [R("ktok"), R("vtok")], [ub[j][1]], inc=(u == 3 and j == 1), tp=(64 * j, 0))
                order = [(u, j) for u in range(4) for j in range(2)]
                if dr == 1:
                    order = order[::-1]
                for (u, j) in order:
                    c = 2 * (4 * bg + u) + j
                    Uc = ub[j][0][:, u * 128:(u + 1) * 128]
                    if n == 0:
                        CP("dve", Tbig[:, c, :], Uc, [], [ub[j][1], R("Tb", c)] + DEAD)
                    else:
                        dprev = EB[:, 64 * cprev + 63: 64 * cprev + 64] if dr == 0 else EB[:, 64 * cprev: 64 * cprev + 1]
                        STT("dve", Tbig[:, c, :], Tbig[:, cprev, :], dprev, Uc, ALU.mult, ALU.add,
                            [R("Tb", cprev)] + SGR, [ub[j][1], R("Tb", c)])
                    cprev = c
                    n += 1
            EB3 = EB.rearrange("p (c t) -> p c t", t=64)
            if dr == 0:
                TT("dve", Sbf[:, 1:32, :], Tbig[:, 0:31, :], EB3[:, 0:31, 63:64].broadcast_to([128, 31, 128]), ALU.mult,
                   [R("Tb", cprev)] + SGR + DEAD, [R("Sbf")])
            else:
                TT("dve", Sbf[:, 0:31, :], Tbig[:, 1:32, :], EB3[:, 1:32, 0:1].broadcast_to([128, 31, 128]), ALU.mult,
                   [R("Tb", cprev)] + SGR + DEAD, [R("Sbf")])
            chk(2.6)
            for bg in range(4):
                bk, bkR = bank()
                for u in range(4):
                    b = 4 * bg + u
                    us = slice(u * 128, (u + 1) * 128)
                    MM(bk[:, us], AT[:, b, :], vtok[:, b, :], True, False, [R("AT"), R("vtok")], [bkR], inc=False)
                    for j in range(2):
                        c = 2 * b + j
                        MM(bk[64 * j:64 * j + 64, us], qtl[:, b * 128 + 64 * j: b * 128 + 64 * j + 64], Sbf[:, c, :],
                           False, j == 1, [R("qtl", bg // 2), R("Sbf")], [bkR], inc=(u == 3 and j == 1), tp=(0, 64 * j))
                o3 = Of3[:, 4 * bg:4 * bg + 4, :]
                if dr == 0:
                    ACT(o3, bk.rearrange("p (u v) -> p u v", v=128), AF.Copy, [], [bkR, R("Of")])
                else:
                    TT("dve", o3, bk.rearrange("p (u v) -> p u v", v=128), o3, ALU.add, [], [bkR, R("Of")])
        chk(2.7)
        ACT(fb["E"], Of, AF.Square, [R("Of")], [R("E", 0), R("E", 1)])
        ss = stat[:, 40:56]
        rs = stat2[:, 40:56]
        P.op("dve", lambda h: h.tensor_reduce(out=ss, in_=fb["E"].rearrange("p (b v) -> p b v", v=128), axis=AX.X, op=ALU.add),
             reads=[R("E", 0), R("E", 1)], writes=[R("ssh")])
        TS("dve", rs, ss, 1.0 / 128, EPS, ALU.mult, ALU.add, [R("ssh")], [R("rsh")])
        ACT(rs, rs, AF.Ln, [R("rsh")], [R("rsh")])
        ACT(rs, rs, AF.Exp, [R("rsh")], [R("rsh")], scale=-0.5)
        SG3 = fb["SG"].rearrange("p (b v) -> p b v", v=128)
        TT("dve", SG3, Of3, rs.unsqueeze(2).broadcast_to([128, NB, 128]), ALU.mult, [R("Of"), R("rsh")], [R("SG", 0), R("SG", 1)])
        TT("dve", SG3, SG3, hgoB.unsqueeze(1).broadcast_to([128, NB, 128]), ALU.mult, [R("SG", 0), R("SG", 1), R("hgoB")],
           [R("SG", 0), R("SG", 1)])
        TT("dve", recb, SG3, gate3, ALU.mult, [R("SG", 0), R("SG", 1), R("gate")], [R("recb")])
        for half in range(2):
            for i in range(8):
                b = half * 8 + i
                TR(psb[:, half, i * 128:(i + 1) * 128], recb[:, b, :], [R("recb"), R("ident")], [R("psb", half)], inc=(i == 7))
            CP("act" if half else "dve", mixT[:, 4 + hh, half * 1024:(half + 1) * 1024], psb[:, half, :], [],
               [R("psb", half), R("mixT", 4 + hh, 2 * half), R("mixT", 4 + hh, 2 * half + 1)])

    if not PREFETCHED_WH0:
        load_Wh(0)
    for hh in range(4):
        hgrn_head(hh)
        chk(2.8 + 0.01 * hh)

    if "rec" in dbg_d:
        tmpf = fb["B"]
        for c in range(4):
            CP("dve", tmpf, mixT[:, 4 + c, :], [R("mixT", 4 + c, qt) for qt in range(4)], [R("dbgtmp")])
            P.dma("sp", dbg_d["rec"][c * 128:(c + 1) * 128, :], tmpf, reads=[R("dbgtmp")])
    if stop <= 3:
        return finish(nc, P)
    barrier(skip=("Wkv",))

    junk3 = junk[:].rearrange("p (u n) -> p u n", n=512)
    gpost3 = gpost[:].rearrange("p (u n) -> p u n", n=512)

    def proj_norm_residual(b, lhs_aps, lhs_reads, w_fn, w_reads, resid_ap, resid_reads, out_ap, out_writes):
        k = b % 2
        n = len(lhs_aps)
        for half in range(2):
            for ci in range(n):
                MM(psf[:, k, half, :], lhs_aps[ci], w_fn(ci, half), ci == 0, ci == n - 1, lhs_reads + w_reads[ci],
                   [R("psf", k, half)], inc=(half == 1 and ci == n - 1))
        ss = stat[:, 16 + b:17 + b]
        rs = stat2[:, 16 + b:17 + b]
        PB = [R("psf", k, 0), R("psf", k, 1)]
        ACT(junk3, psf[:, k, :, :], AF.Square, [], PB + [R("junk"), R("pss", b)], accum=ss)
        ACT(rs, ss, AF.Ln, [R("pss", b), R("epsc")], [R("prs", b)], scale=1.0 / D, bias=epsc[:])
        ACT(rs, rs, AF.Exp, [R("prs", b)], [R("prs", b)], scale=-0.5)
        STT("dve", psf[:, k, :, :], psf[:, k, :, :], rs, gpost3, ALU.mult, ALU.mult, [R("prs", b), R("gpost")], PB)
        TT("dve", out_ap.rearrange("p (u n) -> p u n", n=512), resid_ap.rearrange("p (u n) -> p u n", n=512), psf[:, k, :, :],
           ALU.add, resid_reads, PB + out_writes)

    def prenorm_block(b, src_ap, src_reads):
        k = b % 2
        ss = stat[:, 32 + b:33 + b]
        rs = stat2[:, 32 + b:33 + b]
        ACT(junk[:], src_ap, AF.Square, src_reads, [R("junk"), R("fss", b)], accum=ss)
        ACT(rs, ss, AF.Ln, [R("fss", b), R("epsc")], [R("frs", b)], scale=1.0 / D, bias=epsc[:])
        ACT(rs, rs, AF.Exp, [R("frs", b)], [R("frs", b)], scale=-0.5)
        STT("dve", hb[:, k, :], src_ap, rs, gpre[:], ALU.mult, ALU.mult, src_reads + [R("frs", b), R("gpre")], [R("hb", k)])

    def prenorm_block_tr(b):
        k = b % 2
        for c in range(8):
            TR(psb[:, k, c * 128:(c + 1) * 128], hb[:, k, c * 128:(c + 1) * 128], [R("hb", k), R("ident")], [R("psb", k)],
               inc=(c == 7))
        CP("act", hT[:, :, b * 128:(b + 1) * 128], psb[:, k, :].rearrange("p (c t) -> p c t", t=128), [], [R("psb", k), R("h", b)])

    o2 = 0
    Wo, o2 = a2_bf(o2, 8 * 1024); Wo = Wo.rearrange("p (c n) -> p c n", n=1024)
    load_w(Wo, w_out_d, [R("Wo")])
    gain_load(gpost, 1, R("gpost"))
    Wkv = hT
    Wq, _ = a2_bf(16384, 8 * 1024); Wq = Wq.rearrange("p (c n) -> p c n", n=1024)
    Wo2, _ = a2_bf(32768, 8 * 1024); Wo2 = Wo2.rearrange("p (c n) -> p c n", n=1024)
    load_w(Wq, w_xq_d, [R("Wq")])
    load_w(Wo2, w_xo_d, [R("Wo2")])
    for b in range(14):
        P.dma("sp" if b % 2 else "act", X[:, b, :], x_d[b * 128:(b + 1) * 128, :], writes=[R("X", b)])
    o2 = 49152
    memT, o2 = a2_bf(o2, 8 * 256); memT = memT.rearrange("p (c m) -> p c m", m=256)
    KxT, o2 = a2_bf(o2, 8 * 256); KxT = KxT.rearrange("p (j m) -> p j m", m=256)
    Vx, o2 = a2_bf(o2, 2 * 1024); Vx = Vx.rearrange("p (m n) -> p m n", n=1024)
    assert o2 <= 63488, o2
    for mb in range(2):
        P.dma("sp", X[:, 14 + mb, :], mem_d[mb * 128:(mb + 1) * 128, :], writes=[R("X", 14 + mb)])
    prenorm_T(3, nblk=2, src=X[:, 14:16, :], dstT=memT, tag="memT", srcR=lambda b: R("X", 14 + b))
    MT = [R("memT", 0), R("memT", 1)]
    for j in range(8):
        bk, bkR = bank()
        for c in range(8):
            MM(bk[:, 0:256], Wkv[:, c, j * 128:(j + 1) * 128], memT[:, c, :], c == 0, c == 7, MT + [R("Wkv")], [bkR], inc=(c == 7))
        CP("act" if j % 2 else "dve", KxT[:, j, :], bk[:, 0:256], [], [bkR, R("KxT")])
    for mb in range(2):
        for half in range(2):
            bk, bkR = bank()
            for c in range(8):
                MM(bk, memT[:, c, mb * 128:(mb + 1) * 128], Wkv[:, c, 1024 + half * 512: 1024 + (half + 1) * 512], c == 0, c == 7,
                   MT + [R("Wkv")], [bkR], inc=(c == 7))
            CP("act" if half else "dve", Vx[:, mb, half * 512:(half + 1) * 512], bk, [], [bkR, R("Vx")])
    for b in range(14, 16):
        P.dma("sp" if b % 2 else "act", X[:, b, :], x_d[b * 128:(b + 1) * 128, :], writes=[R("X", b)])
    gain_load(gpre, 2, R("gpre"))
    P._deps("act", [], [R("Wkv")])
    for b in range(NB + 2):
        if b < NB:
            tokb = slice(b * 128, (b + 1) * 128)
            proj_norm_residual(b, [mixT[:, c, tokb] for c in range(8)], [R("mixT", c, b // 4) for c in range(8)],
                               lambda ci, half: Wo[:, ci, half * 512:(half + 1) * 512], [[R("Wo")]] * 8,
                               X[:, b, :], [R("X", b)], X[:, b, :], [R("X", b)])
        if 1 <= b <= NB:
            prenorm_block(b - 1, X[:, b - 1, :], [R("X", b - 1)])
        if b >= 2:
            prenorm_block_tr(b - 2)
    if "x1" in dbg_d:
        for b in range(NB):
            P.dma("sp", dbg_d["x1"][b * 128:(b + 1) * 128, :], X[:, b, :], reads=[R("X", b)])
    if stop <= 4:
        return finish(nc, P)
    barrier(skip=("Wkv", "Wq", "Wo2"))

    o2 = 0
    Qx, o2 = a2_bf(o2, 8 * 512); Qx = Qx.rearrange("p (j n) -> p j n", n=512)
    PTx = []
    for i in range(2):
        v, o2 = a2_bf(o2, 1024); PTx.append(v.rearrange("p (u n) -> p u n", n=512))
    rcx2 = []
    for i in range(2):
        v, o2 = a2_f(o2, 512); rcx2.append(v)
    assert o2 <= 16384, o2
    gain_load(gpost, 4, R("gpost"))
    for qt in range(4):
        qs_ = slice(qt * 512, (qt + 1) * 512)
        for j in range(8):
            bk, bkR = bank()
            for c in range(8):
                MM(bk, Wq[:, c, j * 128:(j + 1) * 128], hT[:, c, qs_], c == 0, c == 7, ALLH[qt * 4:qt * 4 + 4] + [R("Wq")],
                   [bkR], inc=(c == 7))
            CP("act" if j % 2 else "dve", Qx[:, j, :], bk, [], [bkR, R("Qx", j)])
        for hx in range(4):
            k = hx % 2
            for mb in range(2):
                for half in range(2):
                    MM(psf[:, k, mb, :], KxT[:, 2 * hx + half, mb * 128:(mb + 1) * 128], Qx[:, 2 * hx + half, :], half == 0, half == 1,
                       [R("KxT"), R("Qx", 2 * hx + half)], [R("psf", k, mb)], inc=(mb == 1 and half == 1))
            ACT(PTx[k], psf[:, k, :, :], AF.Exp, [], [R("psf", k, 0), R("psf", k, 1), R("PTx", k)], scale=1.0 / 16)
            dbk, dbkR = psbf[k], R("psb", k)
            for mb in range(2):
                MM(dbk, ones_bf[:], PTx[k][:, mb, :], mb == 0, mb == 1, [R("ones"), R("PTx", k)], [dbkR], inc=(mb == 1))
            ACT(rcx2[k], dbk, AF.Ln, [], [dbkR, R("rcx", k)])
            ACT(rcx2[k], rcx2[k], AF.Exp, [R("rcx", k)], [R("rcx", k)], scale=-1.0)
            for dh in range(2):
                bk, bkR = psf[:, 2, dh, :], R("psf", 2, dh)
                for mb in range(2):
                    MM(bk, Vx[:, mb, hx * 256 + dh * 128: hx * 256 + (dh + 1) * 128], PTx[k][:, mb, :], mb == 0, mb == 1,
                       [R("Vx"), R("PTx", k)], [bkR], inc=(mb == 1))
                TT("dve", mixT[:, 2 * hx + dh, qs_], bk, rcx2[k], ALU.mult, [R("rcx", k)], [bkR, R("mixT", 2 * hx + dh, qt)])
    gain_load(gpre, 5, R("gpre"))
    for b in range(NB + 2):
        if b < NB:
            tokb = slice(b * 128, (b + 1) * 128)
            proj_norm_residual(b, [mixT[:, c, tokb] for c in range(8)], [R("mixT", c, b // 4) for c in range(8)],
                               lambda ci, half: Wo2[:, ci, half * 512:(half + 1) * 512], [[R("Wo2")]] * 8,
                               X[:, b, :], [R("X", b)], X[:, b, :], [R("X", b)])
        if 1 <= b <= NB:
            prenorm_block(b - 1, X[:, b - 1, :], [R("X", b - 1)])
        if b >= 2:
            prenorm_block_tr(b - 2)
    if "x2" in dbg_d:
        for b in range(NB):
            P.dma("sp", dbg_d["x2"][b * 128:(b + 1) * 128, :], X[:, b, :], reads=[R("X", b)])
    if stop <= 5:
        return finish(nc, P)
    barrier()

    gain_load(gpost, 6, R("gpost"))
    for b in range(NB):
        P.dma("sp" if b % 2 else "act", out_d[b * 128:(b + 1) * 128, :], X[:, b, :], reads=[R("X", b)], writes=[R("outd", b)])
    barrier()
    aT16, _ = a1_bf(0, [16 * N_TOK]); aT16 = aT16.rearrange("p (j t) -> p j t", t=N_TOK)
    aT = [aT16[:, j, :] for j in range(16)] + [mixT[:, j, :] for j in range(6)]
    Wup = [mixT[:, 6 + i, :].rearrange("p (c n) -> p c n", n=256) for i in range(2)]
    o2 = 0
    ugb = []; uvb = []; agb = []; avb = []; ebb = []
    for i in range(2):
        v, o2 = a2_f(o2, 2050); ugb.append(v)
        v, o2 = a2_f(o2, 2050); uvb.append(v)
        v, o2 = a2_f(o2, 1024); agb.append(v)
        v, o2 = a2_f(o2, 1024); avb.append(v)
        v, o2 = a2_f(o2, 1024); ebb.append(v)
    assert o2 <= 63488, o2
    for i in range(2):
        for ub_, nm in ((ugb[i], "ug"), (uvb[i], "uv")):
            MEMSET("pool", ub_[:, 0:1], 0.0, [R(nm, i)])
            MEMSET("pool", ub_[:, 2049:2050], 0.0, [R(nm, i)])

    def ffn_mm(j):
        s_ = j % 2
        wu = Wup[s_]
        P.dma("pool", wu[:, :, 0:128], w_up_d[:, j * 128:(j + 1) * 128].rearrange("(c p) n -> p c n", p=128),
              writes=[R("Wup", s_)])
        P.dma("pool", wu[:, :, 128:256],
              w_up_d[:, D_FF + j * 128: D_FF + (j + 1) * 128].rearrange("(c p) n -> p c n", p=128), writes=[R("Wup", s_)])
        for qt in range(4):
            qs_ = slice(qt * 512, (qt + 1) * 512)
            bg_, bgR = bank()
            bv_, bvR = bank()
            for c in range(8):
                MM(bg_, wu[:, c, 0:128], hT[:, c, qs_], c == 0, c == 7, ALLH[qt * 4:qt * 4 + 4] + [R("Wup", s_)], [bgR], inc=False)
            for c in range(8):
                MM(bv_, wu[:, c, 128:256], hT[:, c, qs_], c == 0, c == 7, ALLH[qt * 4:qt * 4 + 4] + [R("Wup", s_)], [bvR],
                   inc=(c == 7))
            ACT(ugb[s_][:, 1 + qt * 512: 1 + (qt + 1) * 512], bg_, AF.Copy, [], [bgR, R("ug", s_)])
            ACT(uvb[s_][:, 1 + qt * 512: 1 + (qt + 1) * 512], bv_, AF.Copy, [], [bvR, R("uv", s_)])

    def ffn_ew(j):
        s_ = j % 2
        ug, uv, ag, av, eb_ = ugb[s_], uvb[s_], agb[s_], avb[s_], ebb[s_]
        for hf in range(2):
            base = 1 + hf * 1024
            ts_ = slice(hf * 1024, (hf + 1) * 1024)
            cw = lambda cj, i: convT[:, cj * 4 + i: cj * 4 + i + 1]
            CT = [R("convT")]
            for (u_, uR, cj, dst, dR) in ((ug, R("ug", s_), j, ag, R("ag", s_)), (uv, R("uv", s_), 22 + j, av, R("av", s_))):
                ACT(dst, u_[:, base:base + 1024], AF.Identity, [uR] + CT, [dR], scale=cw(cj, 1), bias=cw(cj, 3))
                STT("dve", dst, u_[:, base - 1:base + 1023], cw(cj, 0), dst, ALU.mult, ALU.add, [uR] + CT, [dR])
                STT("dve", dst, u_[:, base + 1:base + 1025], cw(cj, 2), dst, ALU.mult, ALU.add, [uR] + CT, [dR])
            ACT(eb_, ag, AF.Silu, [R("ag", s_)], [R("eb", s_)])
            TT("dve", aT[j][:, ts_], eb_, av, ALU.mult, [R("eb", s_), R("av", s_)], [R("aT", j)])

    for j in range(23):
        if j < 22:
            ffn_mm(j)
        if j >= 1:
            ffn_ew(j - 1)
    barrier()
    Wd, o2 = a2_bf(0, 22 * 1024); Wd = Wd.rearrange("p (j n) -> p j n", n=1024)
    wstg = []
    for i in range(3):
        v, o2 = a2_f(o2, 1024); wstg.append(v)
    assert o2 <= 63488, o2
    nst = 0
    for j in range(22):
        if j % 2 == 0:
            P.dma("pool", Wd[:, j, :], w_down_d[j * 128:(j + 1) * 128, :], writes=[R("Wd", j)])
        else:
            si = nst % 3
            nst += 1
            P.dma("sp", wstg[si], w_down_d[j * 128:(j + 1) * 128, :], writes=[R("wstg", si)])
            CP("act", Wd[:, j, :], wstg[si], [R("wstg", si)], [R("Wd", j)])
    hTf = hT[:].rearrange("p c t -> p (c t)").bitcast(F32)
    xb_ = [hTf[:, 0:1024], hTf[:, 1024:2048]]
    ob_ = [hTf[:, 2048:3072], hTf[:, 3072:4096]]
    for b in range(NB):
        tokb = slice(b * 128, (b + 1) * 128)
        k = b % 2
        P.dma("sp", xb_[k], out_d[b * 128:(b + 1) * 128, :], reads=[R("outd", b)] + ALLH, writes=[R("xb", k)])
        proj_norm_residual(b, [aT[j][:, tokb] for j in range(22)], [R("aT", j) for j in range(22)],
                           lambda ci, half: Wd[:, ci, half * 512:(half + 1) * 512], [[R("Wd", j)] for j in range(22)],
                           xb_[k], [R("xb", k)], ob_[k], [R("ob", k)])
        P.dma("sp", out_d[b * 128:(b + 1) * 128, :], ob_[k], reads=[R("ob", k)], writes=[R("outd", b)])

    return finish(nc, P)


def finish(nc, P):
    print("COUNTS", {e: P.q[e].total for e in P.ENGS}, "dma", sum(P.dma_cnt), max(P.dma_cnt), flush=True)
    P.wait_all("sp")
    P.emit()
    P.close()
    return nc


def rope_tables():
    rows = N_TOK // 64
    r, c = np.meshgrid(np.arange(rows), np.arange(64), indexing="ij")
    inv = np.power(np.float32(10000.0), -np.arange(16, dtype=np.float32) / np.float32(16)).astype(np.float32)
    ang = np.concatenate([r.reshape(-1, 1).astype(np.float32) * inv, c.reshape(-1, 1).astype(np.float32) * inv], -1)
    cos = np.cos(ang).astype(np.float32)
    sin = np.sin(ang).astype(np.float32)
    f = lambda a: a.reshape(16, 128, 32).transpose(1, 0, 2).reshape(128, 512)
    return np.ascontiguousarray(np.concatenate([f(cos), f(sin)], 1))


def masks():
    s = np.arange(128)[:, None]
    t = np.arange(128)[None, :]
    same = (s // 64) == (t // 64)
    fwd = (same & (s <= t)).astype(np.float32)
    bwd = (same & (s >= t)).astype(np.float32)
    return np.concatenate([fwd, bwd], 1).astype(ml_dtypes.bfloat16)


def prep_inputs(inp):
    f32 = lambda a: np.ascontiguousarray(np.asarray(a, dtype=np.float32))
    shared = {
        "w_in": f32(inp["w_in"][0]), "w_out": f32(inp["w_out"][0]), "w_xq": f32(inp["w_xq"][0]),
        "w_xkv": f32(inp["w_xkv"][0]), "w_xo": f32(inp["w_xo"][0]), "w_up": f32(inp["w_up"][0]),
        "w_down": f32(inp["w_down"][0]),
        "gains": f32(np.stack([inp["pre_mix_g"][0], inp["post_mix_g"][0], inp["pre_x_g"][0], inp["mem_norm_g"][0],
                               inp["post_x_g"][0], inp["pre_ffn_g"][0], inp["post_ffn_g"][0]])),
        "qkg": f32(np.concatenate([inp["q_norm_g"][0], inp["k_norm_g"][0]])[None, :]),
        "hgo": f32(np.asarray(inp["hg_out_norm_g"][0])[None, :]),
        "lbT": f32(np.asarray(inp["hg_lb"]).reshape(2, 2, 4, 128).transpose(3, 0, 1, 2).reshape(128, 16)),
        "convT": f32(np.concatenate([np.asarray(inp["conv_w"][0]), np.asarray(inp["conv_b"][0])[None, :]], 0)
                     .reshape(4, 44, 128).transpose(2, 1, 0).reshape(128, 176)),
        "ident": np.eye(128, dtype=np.float32).astype(ml_dtypes.bfloat16),
        "rope": rope_tables(),
        "mask": masks(),
    }
    x = np.asarray(inp["x"], dtype=np.float32)
    mem = np.asarray(inp["mem"], dtype=np.float32)
    maps = []
    for i in range(8):
        d = dict(shared)
        d["x"] = np.ascontiguousarray(x[i])
        d["mem"] = np.ascontiguousarray(mem[i])
        maps.append(d)
    return maps


_NC_CACHE = {}


def kernel(**inputs):
    if "nc" not in _NC_CACHE:
        _NC_CACHE["nc"] = build()
    nc = _NC_CACHE["nc"]
    maps = prep_inputs(inputs)
    res = run_bass_kernel_spmd(nc, maps, core_ids=list(range(8)))
    return np.stack([np.asarray(r["out"], dtype=np.float32) for r in res.results], 0)
```

```python
import numpy as np
import ml_dtypes
import concourse.bass as bass
import concourse.mybir as mybir
from concourse.bass_utils import run_bass_kernel_spmd

F32 = mybir.dt.float32
BF16 = mybir.dt.bfloat16
AF = mybir.ActivationFunctionType
ALU = mybir.AluOpType
AX = mybir.AxisListType

N_TOK = 2048
D = 1024
NB = 16
EPS = 1e-6
D_FF = 2816
N_IN = 3328


class Res:
    __slots__ = ("name", "w", "r", "big")

    def __init__(self, name):
        self.name = name
        self.w = None
        self.r = {}
        self.big = False


class EngQ:
    def __init__(self, name):
        self.name = name
        self.ops = []
        self.epoch = 0
        self.cnt = 0
        self.total = 0
        self.waited = {}
        self.pending = False


class Prog:
    ENGS = ("pe", "act", "dve", "pool", "sp")
    LIMIT = 900
    NEPOCH = {"pe": 14, "act": 10, "dve": 14, "pool": 4, "sp": 1}

    def __init__(self, nc, n_dma_sems=32):
        self.nc = nc
        self.q = {e: EngQ(e) for e in self.ENGS}
        self.sems = {}
        self.n_dma_sems = n_dma_sems
        self.dma_cnt = [0] * n_dma_sems
        self.dma_pool = {"pool": list(range(0, 16)), "sp": list(range(16, 26)), "act": list(range(26, 32))}
        self.dma_rr = {"pool": 0, "sp": 0, "act": 0}
        self._ctx = []
        self.res = {}

    def R(self, *key):
        r = self.res.get(key)
        if r is None:
            r = Res(key)
            self.res[key] = r
        return r

    def open(self):
        nc = self.nc
        for e in self.ENGS:
            for ep in range(self.NEPOCH[e]):
                c = nc.semaphore("s_%s%d" % (e, ep))
                self.sems[(e, ep)] = c.__enter__()
                self._ctx.append(c)
        for i in range(self.n_dma_sems):
            c = nc.semaphore("s_dma%d" % i)
            self.sems[("dma", i)] = c.__enter__()
            self._ctx.append(c)

    def close(self):
        for c in reversed(self._ctx):
            c.__exit__(None, None, None)

    def _need(self, eng, tok, same_ok):
        if tok is None:
            return
        key, val = tok
        q = self.q[eng]
        if key[0] == "dma":
            if q.waited.get(key, 0) >= val:
                return
            q.waited[key] = val
        else:
            src, ep = key
            if src == eng and not same_ok:
                return
            if q.waited.get(src, (-1, 0)) >= (ep, val):
                return
            q.waited[src] = (ep, val)
        sem = self.sems[key]
        q.ops.append(lambda h, sem=sem, val=val: h.wait_ge(sem, val))

    def _deps(self, eng, reads, writes):
        for r in reads:
            self._need(eng, r.w, same_ok=(eng != "pe" and not r.big))
        so = (eng != "pe")
        for w in writes:
            self._need(eng, w.w, same_ok=so)
            for tok in w.r.values():
                self._need(eng, tok, same_ok=so)

    def op(self, eng, fn, reads=(), writes=(), inc=True, big=False):
        q = self.q[eng]
        if q.cnt >= self.LIMIT and not q.pending:
            q.epoch += 1
            q.cnt = 0
            assert q.epoch < self.NEPOCH[eng], "out of epochs for " + eng
        self._deps(eng, reads, writes)
        key = (eng, q.epoch)
        val = q.cnt + 1
        tok = (key, val)
        if inc:
            q.cnt = val
            q.total += 1
            sem = self.sems[key]
            q.ops.append(lambda h, fn=fn, sem=sem: fn(h).then_inc(sem, 1))
            q.pending = False
        else:
            q.ops.append(lambda h, fn=fn: fn(h))
            q.pending = True
        for r in reads:
            r.r[eng] = tok
        for w in writes:
            w.w = tok
            w.r = {}
            w.big = big
        return tok

    def dma(self, eng, out, in_, reads=(), writes=(), also_wait=(), **kw):
        pl = self.dma_pool[eng]
        i = pl[self.dma_rr[eng] % len(pl)]
        self.dma_rr[eng] += 1
        key = ("dma", i)
        if self.dma_cnt[i] > 0:
            self._need(eng, (key, 16 * self.dma_cnt[i]), same_ok=True)
        self._deps(eng, reads, list(writes) + list(also_wait))
        self.dma_cnt[i] += 1
        assert self.dma_cnt[i] < 60
        val = 16 * self.dma_cnt[i]
        sem = self.sems[key]
        q = self.q[eng]
        q.ops.append(lambda h, sem=sem, out=out, in_=in_, kw=kw:
                     h.dma_start(out=out, in_=in_, **kw).then_inc(sem, 16))
        tok = (key, val)
        for r in reads:
            r.r[key] = tok
        for w in writes:
            w.w = tok
            w.r = {}
            w.big = False
        return tok

    def wait_all(self, eng, skip=()):
        for r in self.res.values():
            if r.name[0] in skip:
                continue
            self._need(eng, r.w, same_ok=True)
            for tok in list(r.r.values()):
                self._need(eng, tok, same_ok=True)

    def emit(self):
        nc = self.nc
        for e in self.ENGS:
            assert not self.q[e].pending, "engine %s ends with non-inc'd instr" % e
        with nc.Block() as block:
            @block.tensor
            def _(h):
                for f in self.q["pe"].ops:
                    f(h)

            @block.scalar
            def _(h):
                for f in self.q["act"].ops:
                    f(h)

            @block.vector
            def _(h):
                for f in self.q["dve"].ops:
                    f(h)

            @block.gpsimd
            def _(h):
                for f in self.q["pool"].ops:
                    f(h)

            @block.sync
            def _(h):
                for f in self.q["sp"].ops:
                    f(h)


class StopBuild(Exception):
    pass


def build(stop=99, dbg=()):
    st = {}
    try:
        return _build(stop, dbg, st)
    except StopBuild:
        return finish(st["nc"], st["P"])


def _build(stop, dbg, st):
    nc = bass.Bass("TRN2", target_bir_lowering=False)
    P = Prog(nc)
    P.open()
    R = P.R
    st["nc"] = nc
    st["P"] = P

    def chk(level):
        if stop <= level:
            raise StopBuild()

    def din(name, shape, dt=F32):
        return nc.dram_tensor(name, list(shape), dt, kind="ExternalInput").ap()

    x_d = din("x", [N_TOK, D])
    mem_d = din("mem", [256, D])
    w_in_d = din("w_in", [D, N_IN])
    w_out_d = din("w_out", [D, D])
    w_xq_d = din("w_xq", [D, D])
    w_xkv_d = din("w_xkv", [D, 2 * D])
    w_xo_d = din("w_xo", [D, D])
    w_up_d = din("w_up", [D, 2 * D_FF])
    w_down_d = din("w_down", [D_FF, D])
    gains_d = din("gains", [7, D])
    qkg_d = din("qkg", [1, 128])
    hgo_d = din("hgo", [1, 128])
    lbT_d = din("lbT", [128, 16])
    convT_d = din("convT", [128, 44 * 4])
    ident_d = din("ident", [128, 128], BF16)
    rope_d = din("rope", [128, 2 * 16 * 32])
    mask_d = din("mask", [128, 2 * 128], BF16)
    out_d = nc.dram_tensor("out", [N_TOK, D], F32, kind="ExternalOutput").ap()
    dbg_d = {}
    for name, shape in dbg:
        dbg_d[name] = nc.dram_tensor("dbg_" + name, list(shape), F32, kind="ExternalOutput").ap()

    def sb(name, shape, dt):
        return nc.alloc_sbuf_tensor("sb_" + name, list(shape), dt)

    hT = sb("hT", [128, 8, N_TOK], BF16)
    mixT = sb("mixT", [128, 8, N_TOK], BF16)
    arena1 = sb("arena1", [128, 16384], F32)
    arena2 = sb("arena2", [128, 15872], F32)
    ident = sb("ident", [128, 128], BF16)
    ones_bf = sb("ones_bf", [128, 128], BF16)
    gpre = sb("gpre", [128, D], F32)
    gpost = sb("gpost", [128, D], F32)
    stat = sb("stat", [128, 64], F32)
    stat2 = sb("stat2", [128, 64], F32)
    junk = sb("junk", [128, D], BF16)
    hb = sb("hb", [128, 2, D], BF16)
    lbT = sb("lbT", [128, 16], F32)
    lbv = sb("lbv", [128, 40], F32)
    convT = sb("convT", [128, 44 * 4], F32)
    epsc = sb("epsc", [128, 1], F32)
    psf = nc.alloc_psum_tensor("psf", [128, 3, 2, 512], F32)
    psb = nc.alloc_psum_tensor("psb", [128, 2, 1024], BF16)

    X = arena1[:].rearrange("p (b d) -> p b d", d=D)

    def a1_bf(off_bytes, shape):
        n = int(np.prod(shape))
        v = arena1[:, off_bytes // 4: off_bytes // 4 + n // 2].bitcast(BF16)
        return v, off_bytes + n * 2

    def a2_bf(off_bytes, n):
        v = arena2[:, off_bytes // 4: off_bytes // 4 + n // 2].bitcast(BF16)
        return v, off_bytes + n * 2

    def a2_f(off_bytes, n):
        v = arena2[:, off_bytes // 4: off_bytes // 4 + n]
        return v, off_bytes + n * 4

    def a1_f(off_bytes, n):
        v = arena1[:, off_bytes // 4: off_bytes // 4 + n]
        return v, off_bytes + n * 4

    def MM(out, lhsT, rhs, start, stop, reads, writes, inc=True, tp=None):
        if tp is None:
            return P.op("pe", lambda h: h.matmul(out, lhsT=lhsT, rhs=rhs, start=start, stop=stop),
                        reads=reads, writes=writes, inc=inc)
        return P.op("pe", lambda h: h.matmul(out, lhsT=lhsT, rhs=rhs, start=start, stop=stop, tile_position=tp),
                    reads=reads, writes=writes, inc=inc)

    def TR(out, in_, reads, writes, inc=True):
        return P.op("pe", lambda h: h.transpose(out, in_, ident[:]), reads=reads, writes=writes, inc=inc)

    BIG_N = 1024

    def isbig(ap):
        n = 1
        for d_ in ap.shape[1:]:
            n *= d_
        return n >= BIG_N

    def ACT(out, in_, func, reads, writes, scale=None, bias=None, accum=None):
        kw = {}
        if scale is not None:
            kw["scale"] = scale
        if bias is not None:
            kw["bias"] = bias
        if accum is not None:
            kw["accum_out"] = accum
        return P.op("act", lambda h: h.activation(out=out, in_=in_, func=func, **kw), reads=reads, writes=writes,
                    big=(accum is None and isbig(out)))

    def TT(eng, out, in0, in1, op, reads, writes):
        return P.op(eng, lambda h: h.tensor_tensor(out=out, in0=in0, in1=in1, op=op), reads=reads, writes=writes, big=isbig(out))

    def TS(eng, out, in0, s1, s2, op0, op1, reads, writes):
        if s2 is None:
            return P.op(eng, lambda h: h.tensor_scalar(out=out, in0=in0, scalar1=s1, scalar2=None, op0=op0),
                        reads=reads, writes=writes, big=isbig(out))
        return P.op(eng, lambda h: h.tensor_scalar(out=out, in0=in0, scalar1=s1, scalar2=s2, op0=op0, op1=op1),
                    reads=reads, writes=writes, big=isbig(out))

    def STT(eng, out, in0, scalar, in1, op0, op1, reads, writes):
        return P.op(eng, lambda h: h.scalar_tensor_tensor(out=out, in0=in0, scalar=scalar, in1=in1, op0=op0, op1=op1),
                    reads=reads, writes=writes, big=isbig(out))

    def CP(eng, out, in_, reads, writes):
        if eng == "act":
            return ACT(out, in_, AF.Copy, reads, writes)
        return P.op(eng, lambda h: h.tensor_copy(out=out, in_=in_), reads=reads, writes=writes, big=isbig(out))

    def RECIP(out, in_, reads, writes):
        return P.op("dve", lambda h: h.reciprocal(out=out, in_=in_), reads=reads, writes=writes)

    def MEMSET(eng, ap, val, writes):
        return P.op(eng, lambda h: h.memset(ap, val), writes=writes)

    def barrier(skip=()):
        for e in Prog.ENGS:
            P.wait_all(e, skip)

    def load_w(dst, src_rows_cols, reads_w, also_wait=()):
        src = src_rows_cols.rearrange("(c p) n -> p c n", p=128)
        nch = src.shape[1]
        for c in range(nch):
            P.dma("pool", dst[:, c, :], src[:, c, :], writes=reads_w, also_wait=also_wait)

    def gain_load(dst, row, res):
        P.dma("sp", dst[:], gains_d[row:row + 1, :].partition_broadcast(128), writes=[res])

    def dump(name, ap_sb, reads, view=None):
        if name in dbg_d:
            P.dma("sp", dbg_d[name] if view is None else view(dbg_d[name]), ap_sb, reads=reads)

    P.dma("sp", ident[:], ident_d, writes=[R("ident")])
    P.dma("sp", lbT[:], lbT_d, writes=[R("lbT")])
    P.dma("sp", convT[:], convT_d, writes=[R("convT")])
    MEMSET("pool", ones_bf[:], 1.0, [R("ones")])
    MEMSET("pool", epsc[:], EPS, [R("epsc")])
    MEMSET("pool", stat[:], 0.0, [R("stat")])
    MEMSET("pool", stat2[:], 0.0, [R("stat2")])

    def prenorm_T(grow, nblk=NB, src=None, dstT=None, tag="h", srcR=None):
        src = X if src is None else src
        dstT = hT if dstT is None else dstT
        srcR = (lambda b: R("X", b)) if srcR is None else srcR
        gain_load(gpre, grow, R("gpre"))
        for b in range(nblk):
            ACT(junk[:], src[:, b, :], AF.Square, [srcR(b)], [R("junk"), R("stat")], accum=stat[:, b:b + 1])
        TS("dve", stat2[:, 0:nblk], stat[:, 0:nblk], 1.0 / D, EPS, ALU.mult, ALU.add, [R("stat")], [R("stat2")])
        ACT(stat2[:, 0:nblk], stat2[:, 0:nblk], AF.Ln, [R("stat2")], [R("stat2")])
        ACT(stat2[:, 0:nblk], stat2[:, 0:nblk], AF.Exp, [R("stat2")], [R("stat2")], scale=-0.5)
        for b in range(nblk):
            k = b % 2
            STT("dve", hb[:, k, :], src[:, b, :], stat2[:, b:b + 1], gpre[:], ALU.mult, ALU.mult,
                [srcR(b), R("stat2"), R("gpre")], [R("hb", k)])
            for c in range(8):
                TR(psb[:, k, c * 128:(c + 1) * 128], hb[:, k, c * 128:(c + 1) * 128],
                   [R("hb", k), R("ident")], [R("psb", k)], inc=(c == 7))
            CP("act", dstT[:, :, b * 128:(b + 1) * 128], psb[:, k, :].rearrange("p (c t) -> p c t", t=128),
               [], [R("psb", k), R(tag, b)])

    o2 = 0
    Watt, o2 = a2_bf(o2, 8 * 768); Watt = Watt.rearrange("p (c n) -> p c n", n=768)
    rope_raw, o2 = a2_f(o2, 1024)
    qkgB, o2 = a2_f(o2, 128)
    o2_attw = o2
    load_w(Watt, w_in_d[:, 0:768], [R("Watt")])
    P.dma("sp", rope_raw, rope_d, writes=[R("rope_raw")])
    P.dma("sp", qkgB, qkg_d.partition_broadcast(128), writes=[R("qkgB")])
    for b in range(NB):
        P.dma("sp" if b % 2 else "act", X[:, b, :], x_d[b * 128:(b + 1) * 128, :], writes=[R("X", b)])
    prenorm_T(0)
    if "hT" in dbg_d:
        tmpf = arena2[:, 13000:13000 + 2048]
        for c in range(8):
            CP("dve", tmpf, hT[:, c, :], [R("h", b) for b in range(NB)], [R("dbgtmp")])
            P.dma("sp", dbg_d["hT"][c * 128:(c + 1) * 128, :], tmpf, reads=[R("dbgtmp")])
    if stop <= 1:
        return finish(nc, P)
    barrier()
    if stop <= 1.1:
        return finish(nc, P)


    o1 = 0
    ropeT = []
    for i in range(8):
        v, o1 = a1_f(o1, 512)
        ropeT.append(v.rearrange("p (b i) -> p b i", i=32))
    QT, o1 = a1_bf(o1, [4 * N_TOK]); QT = QT.rearrange("p (j t) -> p j t", t=N_TOK)
    KTd, o1 = a1_bf(o1, [2 * N_TOK]); KTd = KTd.rearrange("p (g t) -> p g t", t=N_TOK)
    VA, o1 = a1_bf(o1, [NB * 2 * 192]); VA = VA.rearrange("p (b g d) -> p b g d", g=2, d=192)
    o2 = o2_attw
    qkv = []; sqb = []; tq = []; qn = []; qr = []
    for i in range(2):
        v, o2 = a2_f(o2, 768); qkv.append(v)
        v, o2 = a2_f(o2, 640); sqb.append(v)
        v, o2 = a2_f(o2, 4 * 320); tq.append(v.rearrange("p (a n) -> p a n", n=320))
        v, o2 = a2_f(o2, 640); qn.append(v)
        v, o2 = a2_bf(o2, 768); qr.append(v)
    PT = []
    for i in range(3):
        v, o2 = a2_bf(o2, 1024); PT.append(v.rearrange("p (u n) -> p u n", n=512))
    rc = []
    for i in range(2):
        v, o2 = a2_f(o2, 512); rc.append(v)
    assert o1 <= 65536 and o2 <= 63488, (o1, o2)

    if stop <= 1.16:
        return finish(nc, P)
    MEMSET("pool", VA, 1.0, [R("VA", b) for b in range(NB)])
    if stop <= 1.17:
        return finish(nc, P)
    cosv = rope_raw[:, 0:512].rearrange("p (b i) -> p b i", i=32)
    sinv = rope_raw[:, 512:1024].rearrange("p (b i) -> p b i", i=32)
    for qk in range(2):
        gv = qkgB[:, qk * 64:(qk + 1) * 64].rearrange("p (i two) -> p i two", two=2)
        ge = gv[:, :, 0].unsqueeze(1).broadcast_to([128, NB, 32])
        go = gv[:, :, 1].unsqueeze(1).broadcast_to([128, NB, 32])
        for ti, (tab, gg) in enumerate(((cosv, ge), (sinv, go), (sinv, ge), (cosv, go))):
            TT("dve", ropeT[qk * 4 + ti], tab, gg, ALU.mult, [R("rope_raw"), R("qkgB")], [R("ropeT")])

    def att_proj(b):
        k = b % 2
        tokb = slice(b * 128, (b + 1) * 128)
        for c in range(8):
            MM(psf[:, k, 0, :], hT[:, c, tokb], Watt[:, c, 0:512], c == 0, c == 7,
               [R("h", b), R("Watt")], [R("psf", k, 0)], inc=False)
        for c in range(8):
            MM(psf[:, k, 1, 0:256], hT[:, c, tokb], Watt[:, c, 512:768], c == 0, c == 7,
               [R("h", b), R("Watt")], [R("psf", k, 1)], inc=(c == 7))
        ACT(qkv[k][:, 0:512], psf[:, k, 0, :], AF.Copy, [], [R("psf", k, 0), R("qkv", k)])
        ACT(qkv[k][:, 512:768], psf[:, k, 1, 0:256], AF.Copy, [], [R("psf", k, 1), R("qkv", k)])
        ACT(sqb[k], qkv[k][:, 0:640], AF.Square, [R("qkv", k)], [R("sqb", k)])
        ss = stat[:, 16 + 10 * k: 26 + 10 * k]
        rs = stat2[:, 16 + 10 * k: 26 + 10 * k]
        P.op("dve", lambda h: h.tensor_reduce(out=ss, in_=sqb[k].rearrange("p (h d) -> p h d", d=64),
                                              axis=AX.X, op=ALU.add),
             reads=[R("sqb", k)], writes=[R("ss", k)])
        TS("dve", rs, ss, 1.0 / 64, EPS, ALU.mult, ALU.add, [R("ss", k)], [R("rs", k)])
        ACT(rs, rs, AF.Ln, [R("rs", k)], [R("rs", k)])
        ACT(rs, rs, AF.Exp, [R("rs", k)], [R("rs", k)], scale=-0.5)
        for qk, eng, nh, c0 in ((0, "dve", 8, 0), (1, "dve", 2, 512)):
            src = qkv[k][:, c0:c0 + nh * 64].rearrange("p (h i two) -> p h i two", i=32, two=2)
            xe = src[:, :, :, 0]
            xo = src[:, :, :, 1]
            tabs = [ropeT[qk * 4 + ti][:, b, :].unsqueeze(1).broadcast_to([128, nh, 32]) for ti in range(4)]
            tt = [tq[k][:, a, 0:nh * 32].rearrange("p (h i) -> p h i", i=32) for a in range(4)]
            rd = [R("qkv", k), R("ropeT")]
            wr = [R("tq", k, qk)]
            TT(eng, tt[0], xe, tabs[0], ALU.mult, rd, wr)
            TT(eng, tt[1], xo, tabs[1], ALU.mult, rd, wr)
            TT(eng, tt[2], xe, tabs[2], ALU.mult, rd, wr)
            TT(eng, tt[3], xo, tabs[3], ALU.mult, rd, wr)
            dst = qn[k][:, qk * 512: qk * 512 + nh * 64].rearrange("p (h i two) -> p h i two", i=32, two=2)
            TT(eng, dst[:, :, :, 0], tt[0], tt[1], ALU.subtract, wr, [R("qn", k, qk)])
            TT(eng, dst[:, :, :, 1], tt[2], tt[3], ALU.add, wr, [R("qn", k, qk)])
            rsb = rs[:, qk * 8: qk * 8 + nh]
            if qk == 0:
                TT(eng, qr[k][:, 0:512].rearrange("p (h d) -> p h d", d=64),
                   qn[k][:, 0:512].rearrange("p (h d) -> p h d", d=64),
                   rsb.unsqueeze(2).broadcast_to([128, 8, 64]), ALU.mult,
                   [R("qn", k, 0), R("rs", k)], [R("qr", k, 0)])
            else:
                for dup in range(2):
                    TT(eng, qr[k][:, 512:768].rearrange("p (g u d) -> p g u d", u=2, d=64)[:, :, dup, :],
                       qn[k][:, 512:640].rearrange("p (g d) -> p g d", d=64),
                       rsb.unsqueeze(2).broadcast_to([128, 2, 64]), ALU.mult,
                       [R("qn", k, 1), R("rs", k)], [R("qr", k, 1)])
        CP("dve", VA[:, b, :, 64:128], qkv[k][:, 640:768].rearrange("p (g d) -> p g d", d=64),
           [R("qkv", k)], [R("VA", b)])
    def att_proj2(b):
        k = b % 2
        tokb = slice(b * 128, (b + 1) * 128)
        for j in range(6):
            TR(psb[:, k, j * 128:(j + 1) * 128], qr[k][:, j * 128:(j + 1) * 128],
               [R("qr", k, 0), R("qr", k, 1), R("ident")], [R("psb", k)], inc=(j == 5))
        CP("act", QT[:, :, tokb], psb[:, k, 0:512].rearrange("p (j t) -> p j t", t=128), [], [R("psb", k), R("QT", b)])
        CP("dve", KTd[:, :, tokb], psb[:, k, 512:768].rearrange("p (g t) -> p g t", t=128), [], [R("psb", k), R("KT", b)])

    if stop <= 1.2:
        return finish(nc, P)
    for b in range(NB + 1):
        if b < NB:
            att_proj(b)
        if b >= 1:
            att_proj2(b - 1)
    if "QT" in dbg_d:
        tmpf = arena2[:, 14000:14000 + 1024]
        for c in range(4):
            for hf in range(2):
                CP("dve", tmpf, QT[:, c, hf * 1024:(hf + 1) * 1024], [R("QT", b) for b in range(NB)], [R("dbgtmp")])
                P.dma("sp", dbg_d["QT"][c * 128:(c + 1) * 128, hf * 1024:(hf + 1) * 1024], tmpf, reads=[R("dbgtmp")])
    if stop <= 1.5:
        return finish(nc, P)

    Wh, _ = a2_bf(0, 8 * 640); Wh = Wh.rearrange("p (c n) -> p c n", n=640)
    for gi, c0 in enumerate((768, 1280, 1792, 2304, 2816)):
        src = w_in_d[:, c0: c0 + 128].rearrange("(c p) n -> p c n", p=128)
        P.dma("pool", Wh[:, :, gi * 128:(gi + 1) * 128], src, writes=[R("Wh")], also_wait=[R("Watt")])
    PREFETCHED_WH0 = True
    pt_i = [0]
    psbf = [psb[:, 0, :].bitcast(F32), psb[:, 1, :].bitcast(F32)]
    unit_i = [0]

    def st_mm_unit(m, qt, kb):
        g = m // 2
        qs = slice(qt * 512, (qt + 1) * 512)
        qreads = [R("QT", qt * 4 + i) for i in range(4)]
        k = kb % 2
        for par in range(2):
            rows = slice(par * 64, par * 64 + 64)
            MM(psf[:, k, par, :], KTd[rows, g, kb * 128:(kb + 1) * 128], QT[rows, m, qs], True, True,
               qreads + [R("KT", kb)], [R("psf", k, par)], inc=(par == 1))

    UNITS = [(m_, qt_) for m_ in range(4) for qt_ in range(4)]

    def att_pair_tile(m, qt):
        g = m // 2
        j = m
        qs = slice(qt * 512, (qt + 1) * 512)
        uidx = UNITS.index((m, qt))
        ui = unit_i[0] % 2
        unit_i[0] += 1
        if ui == 0:
            accs = [(psf[:, 2, 0, :], R("psf", 2, 0)), (psf[:, 2, 1, :], R("psf", 2, 1))]
        else:
            accs = [(psbf[0], R("psb", 0)), (psbf[1], R("psb", 1))]
        qreads = [R("QT", qt * 4 + i) for i in range(4)]

        def st_mm(kb):
            k = kb % 2
            for par in range(2):
                rows = slice(par * 64, par * 64 + 64)
                MM(psf[:, k, par, :], KTd[rows, g, kb * 128:(kb + 1) * 128], QT[rows, j, qs], True, True,
                   qreads + [R("KT", kb)], [R("psf", k, par)], inc=(par == 1))

        if uidx == 0:
            st_mm(0)
        for kb in range(16):
            if kb + 1 < 16:
                st_mm(kb + 1)
            elif uidx + 1 < len(UNITS):
                st_mm_unit(UNITS[uidx + 1][0], UNITS[uidx + 1][1], 0)
            pi = pt_i[0] % 3
            pt_i[0] += 1
            k = kb % 2
            ACT(PT[pi], psf[:, k, :, :], AF.Exp, [], [R("psf", k, 0), R("psf", k, 1), R("PT", pi)], scale=0.125)
            for par in range(2):
                vsl = slice(64, 192) if par == 0 else slice(0, 128)
                MM(accs[par][0], VA[:, kb, g, vsl], PT[pi][:, par, :], kb == 0, kb == 15,
                   [R("VA", kb), R("PT", pi)], [accs[par][1]], inc=(par == 1))
        rcb = rc[ui]
        for par in range(2):
            rows = slice(par * 64, par * 64 + 64)
            orow = slice((1 - par) * 64, (1 - par) * 64 + 64)
            acc, accR = accs[par]
            P.op("dve", lambda hh, acc=acc, rows=rows, orow=orow: hh.reciprocal(out=rcb[rows, :], in_=acc[orow, :]),
                 reads=[], writes=[accR, R("rc", ui, par)])
            TT("dve", mixT[rows, j, qs], acc[rows, :], rcb[rows, :], ALU.mult, [R("rc", ui, par)], [accR, R("mixT", j, qt)])

    for m in range(4):
        for qt in range(4):
            att_pair_tile(m, qt)

    if "att" in dbg_d:
        tmpf = arena2[:, 0:2048]
        for c in range(4):
            CP("dve", tmpf, mixT[:, c, :], [R("mixT", c, qt) for qt in range(4)], [R("dbgtmp")])
            P.dma("sp", dbg_d["att"][c * 128:(c + 1) * 128, :], tmpf, reads=[R("dbgtmp")])
    if stop <= 2:
        return finish(nc, P)
    barrier(skip=("Wh",))

    bank_i = [0]

    def bank():
        i = bank_i[0] % 6
        bank_i[0] += 1
        return psf[:, i // 2, i % 2, :], R("psf", i // 2, i % 2)

    o1 = 0
    fb = {}
    for nm in ("qs", "SG", "E", "LF", "B", "KK", "rmask", "gate"):
        fb[nm], o1 = a1_f(o1, 2048)
    Tbig = arena1[:, 2 * 2048: 4 * 2048].rearrange("p (c v) -> p c v", v=128)
    o2 = 0
    Wh, o2 = a2_bf(o2, 8 * 640); Wh = Wh.rearrange("p (c n) -> p c n", n=640)
    Of, o2 = a2_f(o2, 2048); Of3 = Of.rearrange("p (b v) -> p b v", v=128)
    qtl, o2 = a2_bf(o2, 2048)
    ktl, o2 = a2_bf(o2, 2048)
    ktok, o2 = a2_bf(o2, 2048); ktok = ktok.rearrange("p (b v) -> p b v", v=128)
    vtok, o2 = a2_bf(o2, 2048); vtok = vtok.rearrange("p (b v) -> p b v", v=128)
    Sbf, o2 = a2_bf(o2, 32 * 128); Sbf = Sbf.rearrange("p (c v) -> p c v", v=128)
    AT, o2 = a2_bf(o2, 2048); AT = AT.rearrange("p (b v) -> p b v", v=128)
    recb, o2 = a2_bf(o2, 2048); recb = recb.rearrange("p (b v) -> p b v", v=128)
    hgoB, o2 = a2_f(o2, 128)
    maskS, o2 = a2_bf(o2, 256); maskS = maskS.rearrange("p (d t) -> p d t", t=128)
    gtmp, o2 = a2_f(o2, 256)
    assert o1 <= 65536 and o2 <= 63488, (o1, o2)
    gate3 = fb["gate"].rearrange("p (b v) -> p b v", v=128)

    P.dma("sp", hgoB, hgo_d.partition_broadcast(128), writes=[R("hgoB")])
    P.dma("sp", maskS, mask_d.rearrange("p (d t) -> p d t", t=128), writes=[R("maskS")])
    MEMSET("pool", fb["rmask"], 1.0, [R("rmask")])
    MEMSET("pool", fb["rmask"].rearrange("p (c t) -> p c t", t=64)[:, :, 0:1], 0.0, [R("rmask")])
    lv = lbT[:].rearrange("p (d l h) -> p d l h", l=2, h=4)
    TT("dve", lbv[:, 0:8].rearrange("p (d h) -> p d h", h=4), lv[:, :, 0, :], lv[:, :, 1, :], ALU.subtract,
       [R("lbT")], [R("lbv")])
    ACT(lbv[:, 0:8], lbv[:, 0:8], AF.Exp, [R("lbv")], [R("lbv")], scale=-1.0)
    TS("dve", lbv[:, 0:8], lbv[:, 0:8], 1.0, None, ALU.add, None, [R("lbv")], [R("lbv")])
    RECIP(lbv[:, 0:8], lbv[:, 0:8], [R("lbv")], [R("lbv")])
    TS("dve", lbv[:, 8:16], lbv[:, 0:8], -1.0, 1.0, ALU.mult, ALU.add, [R("lbv")], [R("lbv")])
    TS("dve", lbv[:, 16:24], lbv[:, 0:8], -0.5, 0.5, ALU.mult, ALU.add, [R("lbv")], [R("lbv")])
    TS("dve", lbv[:, 24:32], lbv[:, 0:8], 0.5, 0.5, ALU.mult, ALU.add, [R("lbv")], [R("lbv")])
    TS("dve", lbv[:, 32:40], lbv[:, 0:8], 0.5, -0.5, ALU.mult, ALU.add, [R("lbv")], [R("lbv")])

    ALLH = [R("h", b) for b in range(NB)]
    chk(2.1)

    def load_Wh(hh, extra=()):
        for gi, c0 in enumerate((768, 1280, 1792, 2304, 2816)):
            src = w_in_d[:, c0 + hh * 128: c0 + (hh + 1) * 128].rearrange("(c p) n -> p c n", p=128)
            P.dma("pool", Wh[:, :, gi * 128:(gi + 1) * 128], src, writes=[R("Wh")] + list(extra))

    def hgrn_head(hh):
        for bp in range(8):
            bk, bkR = bank()
            for u in range(2):
                b = 2 * bp + u
                for c in range(8):
                    MM(bk[:, u * 256:(u + 1) * 256], hT[:, c, b * 128:(b + 1) * 128], Wh[:, c, 384:640], c == 0, c == 7,
                       [R("h", b), R("Wh")], [bkR], inc=(u == 1 and c == 7))
            bv = bk.rearrange("p (u n) -> p u n", n=256)
            ACT(vtok[:, 2 * bp:2 * bp + 2, :], bv[:, :, 0:128], AF.Copy, [], [bkR, R("vtok")])
            g3 = gtmp.rearrange("p (u n) -> p u n", n=128)
            ACT(gate3[:, 2 * bp:2 * bp + 2, :], bv[:, :, 128:256], AF.Silu, [], [bkR, R("gate")])

        chk(2.2)

        def fm_proj(gi, dst, dstname):
            kept = []
            for qt in range(4):
                bk, bkR = bank()
                for c in range(8):
                    MM(bk, Wh[:, c, gi * 128:(gi + 1) * 128], hT[:, c, qt * 512:(qt + 1) * 512], c == 0, c == 7,
                       ALLH[qt * 4:qt * 4 + 4] + [R("Wh")], [bkR], inc=(c == 7))
                ACT(dst[:, qt * 512:(qt + 1) * 512], bk, AF.Tanh, [], [bkR, R(dstname, qt // 2)], scale=0.5)
                kept.append((bk, bkR))
            return kept

        for qt in range(4):
            bk, bkR = bank()
            for c in range(8):
                MM(bk, Wh[:, c, 0:128], hT[:, c, qt * 512:(qt + 1) * 512], c == 0, c == 7,
                   ALLH[qt * 4:qt * 4 + 4] + [R("Wh")], [bkR], inc=(c == 7))
            sl = slice(qt * 512, (qt + 1) * 512)
            ACT(fb["qs"][:, sl], bk, AF.Silu, [], [bkR, R("qs", qt // 2)])

        chk(2.3)
        for dr in range(2):
            li = dr * 4 + hh
            ha_ap = lbv[:, 16 + li: 17 + li]
            hb_ap = lbv[:, 24 + li: 25 + li]
            nha_ap = lbv[:, 32 + li: 33 + li]
            fm_proj(1 + dr, fb["E"], "E")
            if dr == 1:
                if hh < 3:
                    load_Wh(hh + 1)
                else:
                    load_w(hT, w_xkv_d, [R("Wkv")], also_wait=ALLH)
            E, SG, LF, B, KK = fb["E"], fb["SG"], fb["LF"], fb["B"], fb["KK"]
            EB, ENB = SG, LF
            H2 = (slice(0, 1024), slice(1024, 2048))
            for hf in range(2):
                sl = H2[hf]
                ACT(LF[:, sl], E[:, sl], AF.Ln, [R("E", hf), R("lbv")], [R("LF", hf)], scale=ha_ap, bias=hb_ap)
            for hf in range(2):
                sl = H2[hf]
                TS("dve", KK[:, sl], E[:, sl], nha_ap, ha_ap, ALU.mult, ALU.add, [R("E", hf), R("lbv")], [R("KK", hf)])
                P.op("dve", lambda h, sl=sl: h.tensor_tensor_scan(out=B[:, sl], data0=fb["rmask"][:, sl], data1=LF[:, sl],
                                                                  initial=0.0, op0=ALU.mult, op1=ALU.add),
                     reads=[R("rmask"), R("LF", hf)], writes=[R("B", hf)])
                if dr == 1:
                    TT("dve", LF[:, sl], LF[:, sl], B[:, sl], ALU.subtract, [R("LF", hf), R("B", hf)], [R("LF", hf)])
                    B3 = B[:, sl].rearrange("p (c t) -> p c t", t=64)
                    TT("dve", E[:, sl].rearrange("p (c t) -> p c t", t=64), LF[:, sl].rearrange("p (c t) -> p c t", t=64),
                       B3[:, :, 63:64].broadcast_to([128, 16, 64]), ALU.add, [R("LF", hf), R("B", hf)], [R("E", hf)])
            Bs, Bn = (B, "B") if dr == 0 else (E, "E")
            for hf in range(2):
                sl = H2[hf]
                ACT(EB[:, sl], Bs[:, sl], AF.Exp, [R(Bn, hf)], [R("SG", hf)])
                ACT(ENB[:, sl], Bs[:, sl], AF.Exp, [R(Bn, hf)], [R("LF", hf)], scale=-1.0)
            for hf in range(2):
                sl = H2[hf]
                TT("dve", qtl[:, sl], fb["qs"][:, sl], EB[:, sl], ALU.mult, [R("qs", hf), R("SG", hf)], [R("qtl", hf)])
                TT("dve", ktl[:, sl], KK[:, sl], ENB[:, sl], ALU.mult, [R("KK", hf), R("LF", hf)], [R("ktl", hf)])
            chk(2.4)
            for half in range(2):
                for i in range(8):
                    b = half * 8 + i
                    TR(psb[:, half, i * 128:(i + 1) * 128], ktl[:, b * 128:(b + 1) * 128], [R("ktl", half), R("ident")],
                       [R("psb", half)], inc=(i == 7))
                CP("act" if half else "dve", ktok[:, half * 8:(half + 1) * 8, :],
                   psb[:, half, :].rearrange("p (b v) -> p b v", v=128), [], [R("psb", half), R("ktok")])
            for bg in range(4):
                bk, bkR = bank()
                for u in range(4):
                    b = 4 * bg + u
                    MM(bk[:, u * 128:(u + 1) * 128], ktl[:, b * 128:(b + 1) * 128], qtl[:, b * 128:(b + 1) * 128], True, True,
                       [R("ktl", bg // 2), R("qtl", bg // 2)], [bkR], inc=(u == 3))
                TT("dve", AT[:, 4 * bg:4 * bg + 4, :], bk.rearrange("p (u t) -> p u t", t=128),
                   maskS[:, dr, :].unsqueeze(1).broadcast_to([128, 4, 128]), ALU.mult, [R("maskS")], [bkR, R("AT")])
            chk(2.5)
            MEMSET("pool", Sbf[:, 0 if dr == 0 else 31, :], 0.0, [R("Sbf")])
            DEAD = [R(nm, hf) for nm in ("E", "LF") for hf in range(2)]
            SGR = [R("SG", 0), R("SG", 1)]
            n = 0
            cprev = None
            for bgi in range(4):
                bg = bgi if dr == 0 else 3 - bgi
                ub = [bank() for _ in range(2)]
                for u in range(4):
                    b = 4 * bg + u
                    for j in range(2):
                        MM(ub[j][0][:, u * 128:(u + 1) * 128], ktok[64 * j:64 * j + 64, b, :], vtok[64 * j:64 * j + 64, b, :],
                           True, True, [R("ktok"), R("vtok")], [ub[j][1]], inc=(u == 3 and j == 1), tp=(64 * j, 0))
                order = [(u, j) for u in range(4) for j in range(2)]
                if dr == 1:
                    order = order[::-1]
                for (u, j) in order:
                    c = 2 * (4 * bg + u) + j
                    Uc = ub[j][0][:, u * 128:(u + 1) * 128]
                    if n == 0:
                        CP("dve", Tbig[:, c, :], Uc, [], [ub[j][1], R("Tb", c)] + DEAD)
                    else:
                        dprev = EB[:, 64 * cprev + 63: 64 * cprev + 64] if dr == 0 else EB[:, 64 * cprev: 64 * cprev + 1]
                        STT("dve", Tbig[:, c, :], Tbig[:, cprev, :], dprev, Uc, ALU.mult, ALU.add,
                            [R("Tb", cprev)] + SGR, [ub[j][1], R("Tb", c)])
                    cprev = c
                    n += 1
            EB3 = EB.rearrange("p (c t) -> p c t", t=64)
            if dr == 0:
                TT("dve", Sbf[:, 1:32, :], Tbig[:, 0:31, :], EB3[:, 0:31, 63:64].broadcast_to([128, 31, 128]), ALU.mult,
                   [R("Tb", cprev)] + SGR + DEAD, [R("Sbf")])
            else:
                TT("dve", Sbf[:, 0:31, :], Tbig[:, 1:32, :], EB3[:, 1:32, 0:1].broadcast_to([128, 31, 128]), ALU.mult,
                   [R("Tb", cprev)] + SGR + DEAD, [R("Sbf")])
            chk(2.6)
            for bg in range(4):
                bk, bkR = bank()
                for u in range(4):
                    b = 4 * bg + u
                    us = slice(u * 128, (u + 1) * 128)
                    MM(bk[:, us], AT[:, b, :], vtok[:, b, :], True, False, [R("AT"), R("vtok")], [bkR], inc=False)
                    for j in range(2):
                        c = 2 * b + j
                        MM(bk[64 * j:64 * j + 64, us], qtl[:, b * 128 + 64 * j: b * 128 + 64 * j + 64], Sbf[:, c, :],
                           False, j == 1, [R("qtl", bg // 2), R("Sbf")], [bkR], inc=(u == 3 and j == 1), tp=(0, 64 * j))
                o3 = Of3[:, 4 * bg:4 * bg + 4, :]
                if dr == 0:
                    ACT(o3, bk.rearrange("p (u v) -> p u v", v=128), AF.Copy, [], [bkR, R("Of")])
                else:
                    TT("dve", o3, bk.rearrange("p (u v) -> p u v", v=128), o3, ALU.add, [], [bkR, R("Of")])
        chk(2.7)
        ACT(fb["E"], Of, AF.Square, [R("Of")], [R("E", 0), R("E", 1)])
        ss = stat[:, 40:56]
        rs = stat2[:, 40:56]
        P.op("dve", lambda h: h.tensor_reduce(out=ss, in_=fb["E"].rearrange("p (b v) -> p b v", v=128), axis=AX.X, op=ALU.add),
             reads=[R("E", 0), R("E", 1)], writes=[R("ssh")])
        TS("dve", rs, ss, 1.0 / 128, EPS, ALU.mult, ALU.add, [R("ssh")], [R("rsh")])
        ACT(rs, rs, AF.Ln, [R("rsh")], [R("rsh")])
        ACT(rs, rs, AF.Exp, [R("rsh")], [R("rsh")], scale=-0.5)
        SG3 = fb["SG"].rearrange("p (b v) -> p b v", v=128)
        TT("dve", SG3, Of3, rs.unsqueeze(2).broadcast_to([128, NB, 128]), ALU.mult, [R("Of"), R("rsh")], [R("SG", 0), R("SG", 1)])
        TT("dve", SG3, SG3, hgoB.unsqueeze(1).broadcast_to([128, NB, 128]), ALU.mult, [R("SG", 0), R("SG", 1), R("hgoB")],
           [R("SG", 0), R("SG", 1)])
        TT("dve", recb, SG3, gate3, ALU.mult, [R("SG", 0), R("SG", 1), R("gate")], [R("recb")])
        for half in range(2):
            for i in range(8):
                b = half * 8 + i
                TR(psb[:, half, i * 128:(i + 1) * 128], recb[:, b, :], [R("recb"), R("ident")], [R("psb", half)], inc=(i == 7))
            CP("act" if half else "dve", mixT[:, 4 + hh, half * 1024:(half + 1) * 1024], psb[:, half, :], [],
               [R("psb", half), R("mixT", 4 + hh, 2 * half), R("mixT", 4 + hh, 2 * half + 1)])

    if not PREFETCHED_WH0:
        load_Wh(0)
    for hh in range(4):
        hgrn_head(hh)
        chk(2.8 + 0.01 * hh)

    if "rec" in dbg_d:
        tmpf = fb["B"]
        for c in range(4):
            CP("dve", tmpf, mixT[:, 4 + c, :], [R("mixT", 4 + c, qt) for qt in range(4)], [R("dbgtmp")])
            P.dma("sp", dbg_d["rec"][c * 128:(c + 1) * 128, :], tmpf, reads=[R("dbgtmp")])
    if stop <= 3:
        return finish(nc, P)
    barrier(skip=("Wkv",))

    junk3 = junk[:].rearrange("p (u n) -> p u n", n=512)
    gpost3 = gpost[:].rearrange("p (u n) -> p u n", n=512)

    def proj_norm_residual(b, lhs_aps, lhs_reads, w_fn, w_reads, resid_ap, resid_reads, out_ap, out_writes):
        k = b % 2
        n = len(lhs_aps)
        for half in range(2):
            for ci in range(n):
                MM(psf[:, k, half, :], lhs_aps[ci], w_fn(ci, half), ci == 0, ci == n - 1, lhs_reads + w_reads[ci],
                   [R("psf", k, half)], inc=(half == 1 and ci == n - 1))
        ss = stat[:, 16 + b:17 + b]
        rs = stat2[:, 16 + b:17 + b]
        PB = [R("psf", k, 0), R("psf", k, 1)]
        ACT(junk3, psf[:, k, :, :], AF.Square, [], PB + [R("junk"), R("pss", b)], accum=ss)
        ACT(rs, ss, AF.Ln, [R("pss", b), R("epsc")], [R("prs", b)], scale=1.0 / D, bias=epsc[:])
        ACT(rs, rs, AF.Exp, [R("prs", b)], [R("prs", b)], scale=-0.5)
        STT("dve", psf[:, k, :, :], psf[:, k, :, :], rs, gpost3, ALU.mult, ALU.mult, [R("prs", b), R("gpost")], PB)
        TT("dve", out_ap.rearrange("p (u n) -> p u n", n=512), resid_ap.rearrange("p (u n) -> p u n", n=512), psf[:, k, :, :],
           ALU.add, resid_reads, PB + out_writes)

    def prenorm_block(b, src_ap, src_reads):
        k = b % 2
        ss = stat[:, 32 + b:33 + b]
        rs = stat2[:, 32 + b:33 + b]
        ACT(junk[:], src_ap, AF.Square, src_reads, [R("junk"), R("fss", b)], accum=ss)
        ACT(rs, ss, AF.Ln, [R("fss", b), R("epsc")], [R("frs", b)], scale=1.0 / D, bias=epsc[:])
        ACT(rs, rs, AF.Exp, [R("frs", b)], [R("frs", b)], scale=-0.5)
        STT("dve", hb[:, k, :], src_ap, rs, gpre[:], ALU.mult, ALU.mult, src_reads + [R("frs", b), R("gpre")], [R("hb", k)])

    def prenorm_block_tr(b):
        k = b % 2
        for c in range(8):
            TR(psb[:, k, c * 128:(c + 1) * 128], hb[:, k, c * 128:(c + 1) * 128], [R("hb", k), R("ident")], [R("psb", k)],
               inc=(c == 7))
        CP("act", hT[:, :, b * 128:(b + 1) * 128], psb[:, k, :].rearrange("p (c t) -> p c t", t=128), [], [R("psb", k), R("h", b)])

    o2 = 0
    Wo, o2 = a2_bf(o2, 8 * 1024); Wo = Wo.rearrange("p (c n) -> p c n", n=1024)
    load_w(Wo, w_out_d, [R("Wo")])
    gain_load(gpost, 1, R("gpost"))
    Wkv = hT
    Wq, _ = a2_bf(16384, 8 * 1024); Wq = Wq.rearrange("p (c n) -> p c n", n=1024)
    Wo2, _ = a2_bf(32768, 8 * 1024); Wo2 = Wo2.rearrange("p (c n) -> p c n", n=1024)
    load_w(Wq, w_xq_d, [R("Wq")])
    load_w(Wo2, w_xo_d, [R("Wo2")])
    for b in range(14):
        P.dma("sp" if b % 2 else "act", X[:, b, :], x_d[b * 128:(b + 1) * 128, :], writes=[R("X", b)])
    o2 = 49152
    memT, o2 = a2_bf(o2, 8 * 256); memT = memT.rearrange("p (c m) -> p c m", m=256)
    KxT, o2 = a2_bf(o2, 8 * 256); KxT = KxT.rearrange("p (j m) -> p j m", m=256)
    Vx, o2 = a2_bf(o2, 2 * 1024); Vx = Vx.rearrange("p (m n) -> p m n", n=1024)
    assert o2 <= 63488, o2
    for mb in range(2):
        P.dma("sp", X[:, 14 + mb, :], mem_d[mb * 128:(mb + 1) * 128, :], writes=[R("X", 14 + mb)])
    prenorm_T(3, nblk=2, src=X[:, 14:16, :], dstT=memT, tag="memT", srcR=lambda b: R("X", 14 + b))
    MT = [R("memT", 0), R("memT", 1)]
    for j in range(8):
        bk, bkR = bank()
        for c in range(8):
            MM(bk[:, 0:256], Wkv[:, c, j * 128:(j + 1) * 128], memT[:, c, :], c == 0, c == 7, MT + [R("Wkv")], [bkR], inc=(c == 7))
        CP("act" if j % 2 else "dve", KxT[:, j, :], bk[:, 0:256], [], [bkR, R("KxT")])
    for mb in range(2):
        for half in range(2):
            bk, bkR = bank()
            for c in range(8):
                MM(bk, memT[:, c, mb * 128:(mb + 1) * 128], Wkv[:, c, 1024 + half * 512: 1024 + (half + 1) * 512], c == 0, c == 7,
                   MT + [R("Wkv")], [bkR], inc=(c == 7))
            CP("act" if half else "dve", Vx[:, mb, half * 512:(half + 1) * 512], bk, [], [bkR, R("Vx")])
    for b in range(14, 16):
        P.dma("sp" if b % 2 else "act", X[:, b, :], x_d[b * 128:(b + 1) * 128, :], writes=[R("X", b)])
    gain_load(gpre, 2, R("gpre"))
    P._deps("act", [], [R("Wkv")])
    for b in range(NB + 2):
        if b < NB:
            tokb = slice(b * 128, (b + 1) * 128)
            proj_norm_residual(b, [mixT[:, c, tokb] for c in range(8)], [R("mixT", c, b // 4) for c in range(8)],
                               lambda ci, half: Wo[:, ci, half * 512:(half + 1) * 512], [[R("Wo")]] * 8,
                               X[:, b, :], [R("X", b)], X[:, b, :], [R("X", b)])
        if 1 <= b <= NB:
            prenorm_block(b - 1, X[:, b - 1, :], [R("X", b - 1)])
        if b >= 2:
            prenorm_block_tr(b - 2)
    if "x1" in dbg_d:
        for b in range(NB):
            P.dma("sp", dbg_d["x1"][b * 128:(b + 1) * 128, :], X[:, b, :], reads=[R("X", b)])
    if stop <= 4:
        return finish(nc, P)
    barrier(skip=("Wkv", "Wq", "Wo2"))

    o2 = 0
    Qx, o2 = a2_bf(o2, 8 * 512); Qx = Qx.rearrange("p (j n) -> p j n", n=512)
    PTx = []
    for i in range(2):
        v, o2 = a2_bf(o2, 1024); PTx.append(v.rearrange("p (u n) -> p u n", n=512))
    rcx2 = []
    for i in range(2):
        v, o2 = a2_f(o2, 512); rcx2.append(v)
    assert o2 <= 16384, o2
    gain_load(gpost, 4, R("gpost"))
    for qt in range(4):
        qs_ = slice(qt * 512, (qt + 1) * 512)
        for j in range(8):
            bk, bkR = bank()
            for c in range(8):
                MM(bk, Wq[:, c, j * 128:(j + 1) * 128], hT[:, c, qs_], c == 0, c == 7, ALLH[qt * 4:qt * 4 + 4] + [R("Wq")],
                   [bkR], inc=(c == 7))
            CP("act" if j % 2 else "dve", Qx[:, j, :], bk, [], [bkR, R("Qx", j)])
        for hx in range(4):
            k = hx % 2
            for mb in range(2):
                for half in range(2):
                    MM(psf[:, k, mb, :], KxT[:, 2 * hx + half, mb * 128:(mb + 1) * 128], Qx[:, 2 * hx + half, :], half == 0, half == 1,
                       [R("KxT"), R("Qx", 2 * hx + half)], [R("psf", k, mb)], inc=(mb == 1 and half == 1))
            ACT(PTx[k], psf[:, k, :, :], AF.Exp, [], [R("psf", k, 0), R("psf", k, 1), R("PTx", k)], scale=1.0 / 16)
            dbk, dbkR = psbf[k], R("psb", k)
            for mb in range(2):
                MM(dbk, ones_bf[:], PTx[k][:, mb, :], mb == 0, mb == 1, [R("ones"), R("PTx", k)], [dbkR], inc=(mb == 1))
            ACT(rcx2[k], dbk, AF.Ln, [], [dbkR, R("rcx", k)])
            ACT(rcx2[k], rcx2[k], AF.Exp, [R("rcx", k)], [R("rcx", k)], scale=-1.0)
            for dh in range(2):
                bk, bkR = psf[:, 2, dh, :], R("psf", 2, dh)
                for mb in range(2):
                    MM(bk, Vx[:, mb, hx * 256 + dh * 128: hx * 256 + (dh + 1) * 128], PTx[k][:, mb, :], mb == 0, mb == 1,
                       [R("Vx"), R("PTx", k)], [bkR], inc=(mb == 1))
                TT("dve", mixT[:, 2 * hx + dh, qs_], bk, rcx2[k], ALU.mult, [R("rcx", k)], [bkR, R("mixT", 2 * hx + dh, qt)])
    gain_load(gpre, 5, R("gpre"))
    for b in range(NB + 2):
        if b < NB:
            tokb = slice(b * 128, (b + 1) * 128)
            proj_norm_residual(b, [mixT[:, c, tokb] for c in range(8)], [R("mixT", c, b // 4) for c in range(8)],
                               lambda ci, half: Wo2[:, ci, half * 512:(half + 1) * 512], [[R("Wo2")]] * 8,
                               X[:, b, :], [R("X", b)], X[:, b, :], [R("X", b)])
        if 1 <= b <= NB:
            prenorm_block(b - 1, X[:, b - 1, :], [R("X", b - 1)])
        if b >= 2:
            prenorm_block_tr(b - 2)
    if "x2" in dbg_d:
        for b in range(NB):
            P.dma("sp", dbg_d["x2"][b * 128:(b + 1) * 128, :], X[:, b, :], reads=[R("X", b)])
    if stop <= 5:
        return finish(nc, P)
    barrier()

    gain_load(gpost, 6, R("gpost"))
    for b in range(NB):
        P.dma("sp" if b % 2 else "act", out_d[b * 128:(b + 1) * 128, :], X[:, b, :], reads=[R("X", b)], writes=[R("outd", b)])
    barrier()
    aT16, _ = a1_bf(0, [16 * N_TOK]); aT16 = aT16.rearrange("p (j t) -> p j t", t=N_TOK)
    aT = [aT16[:, j, :] for j in range(16)] + [mixT[:, j, :] for j in range(6)]
    Wup = [mixT[:, 6 + i, :].rearrange("p (c n) -> p c n", n=256) for i in range(2)]
    o2 = 0
    ugb = []; uvb = []; agb = []; avb = []; ebb = []
    for i in range(2):
        v, o2 = a2_f(o2, 2050); ugb.append(v)
        v, o2 = a2_f(o2, 2050); uvb.append(v)
        v, o2 = a2_f(o2, 1024); agb.append(v)
        v, o2 = a2_f(o2, 1024); avb.append(v)
        v, o2 = a2_f(o2, 1024); ebb.append(v)
    assert o2 <= 63488, o2
    for i in range(2):
        for ub_, nm in ((ugb[i], "ug"), (uvb[i], "uv")):
            MEMSET("pool", ub_[:, 0:1], 0.0, [R(nm, i)])
            MEMSET("pool", ub_[:, 2049:2050], 0.0, [R(nm, i)])

    def ffn_mm(j):
        s_ = j % 2
        wu = Wup[s_]
        P.dma("pool", wu[:, :, 0:128], w_up_d[:, j * 128:(j + 1) * 128].rearrange("(c p) n -> p c n", p=128),
              writes=[R("Wup", s_)])
        P.dma("pool", wu[:, :, 128:256],
              w_up_d[:, D_FF + j * 128: D_FF + (j + 1) * 128].rearrange("(c p) n -> p c n", p=128), writes=[R("Wup", s_)])
        for qt in range(4):
            qs_ = slice(qt * 512, (qt + 1) * 512)
            bg_, bgR = bank()
            bv_, bvR = bank()
            for c in range(8):
                MM(bg_, wu[:, c, 0:128], hT[:, c, qs_], c == 0, c == 7, ALLH[qt * 4:qt * 4 + 4] + [R("Wup", s_)], [bgR], inc=False)
            for c in range(8):
                MM(bv_, wu[:, c, 128:256], hT[:, c, qs_], c == 0, c == 7, ALLH[qt * 4:qt * 4 + 4] + [R("Wup", s_)], [bvR],
                   inc=(c == 7))
            ACT(ugb[s_][:, 1 + qt * 512: 1 + (qt + 1) * 512], bg_, AF.Copy, [], [bgR, R("ug", s_)])
            ACT(uvb[s_][:, 1 + qt * 512: 1 + (qt + 1) * 512], bv_, AF.Copy, [], [bvR, R("uv", s_)])

    def ffn_ew(j):
        s_ = j % 2
        ug, uv, ag, av, eb_ = ugb[s_], uvb[s_], agb[s_], avb[s_], ebb[s_]
        for hf in range(2):
            base = 1 + hf * 1024
            ts_ = slice(hf * 1024, (hf + 1) * 1024)
            cw = lambda cj, i: convT[:, cj * 4 + i: cj * 4 + i + 1]
            CT = [R("convT")]
            for (u_, uR, cj, dst, dR) in ((ug, R("ug", s_), j, ag, R("ag", s_)), (uv, R("uv", s_), 22 + j, av, R("av", s_))):
                ACT(dst, u_[:, base:base + 1024], AF.Identity, [uR] + CT, [dR], scale=cw(cj, 1), bias=cw(cj, 3))
                STT("dve", dst, u_[:, base - 1:base + 1023], cw(cj, 0), dst, ALU.mult, ALU.add, [uR] + CT, [dR])
                STT("dve", dst, u_[:, base + 1:base + 1025], cw(cj, 2), dst, ALU.mult, ALU.add, [uR] + CT, [dR])
            ACT(eb_, ag, AF.Silu, [R("ag", s_)], [R("eb", s_)])
            TT("dve", aT[j][:, ts_], eb_, av, ALU.mult, [R("eb", s_), R("av", s_)], [R("aT", j)])

    for j in range(23):
        if j < 22:
            ffn_mm(j)
        if j >= 1:
            ffn_ew(j - 1)
    barrier()
    Wd, o2 = a2_bf(0, 22 * 1024); Wd = Wd.rearrange("p (j n) -> p j n", n=1024)
    wstg = []
    for i in range(3):
        v, o2 = a2_f(o2, 1024); wstg.append(v)
    assert o2 <= 63488, o2
    nst = 0
    for j in range(22):
        if j % 2 == 0:
            P.dma("pool", Wd[:, j, :], w_down_d[j * 128:(j + 1) * 128, :], writes=[R("Wd", j)])
        else:
            si = nst % 3
            nst += 1
            P.dma("sp", wstg[si], w_down_d[j * 128:(j + 1) * 128, :], writes=[R("wstg", si)])
            CP("act", Wd[:, j, :], wstg[si], [R("wstg", si)], [R("Wd", j)])
    hTf = hT[:].rearrange("p c t -> p (c t)").bitcast(F32)
    xb_ = [hTf[:, 0:1024], hTf[:, 1024:2048]]
    ob_ = [hTf[:, 2048:3072], hTf[:, 3072:4096]]
    for b in range(NB):
        tokb = slice(b * 128, (b + 1) * 128)
        k = b % 2
        P.dma("sp", xb_[k], out_d[b * 128:(b + 1) * 128, :], reads=[R("outd", b)] + ALLH, writes=[R("xb", k)])
        proj_norm_residual(b, [aT[j][:, tokb] for j in range(22)], [R("aT", j) for j in range(22)],
                           lambda ci, half: Wd[:, ci, half * 512:(half + 1) * 512], [[R("Wd", j)] for j in range(22)],
                           xb_[k], [R("xb", k)], ob_[k], [R("ob", k)])
        P.dma("sp", out_d[b * 128:(b + 1) * 128, :], ob_[k], reads=[R("ob", k)], writes=[R("outd", b)])

    return finish(nc, P)


def finish(nc, P):
    print("COUNTS", {e: P.q[e].total for e in P.ENGS}, "dma", sum(P.dma_cnt), max(P.dma_cnt), flush=True)
    P.wait_all("sp")
    P.emit()
    P.close()
    return nc


def rope_tables():
    rows = N_TOK // 64
    r, c = np.meshgrid(np.arange(rows), np.arange(64), indexing="ij")
    inv = np.power(np.float32(10000.0), -np.arange(16, dtype=np.float32) / np.float32(16)).astype(np.float32)
    ang = np.concatenate([r.reshape(-1, 1).astype(np.float32) * inv, c.reshape(-1, 1).astype(np.float32) * inv], -1)
    cos = np.cos(ang).astype(np.float32)
    sin = np.sin(ang).astype(np.float32)
    f = lambda a: a.reshape(16, 128, 32).transpose(1, 0, 2).reshape(128, 512)
    return np.ascontiguousarray(np.concatenate([f(cos), f(sin)], 1))


def masks():
    s = np.arange(128)[:, None]
    t = np.arange(128)[None, :]
    same = (s // 64) == (t // 64)
    fwd = (same & (s <= t)).astype(np.float32)
    bwd = (same & (s >= t)).astype(np.float32)
    return np.concatenate([fwd, bwd], 1).astype(ml_dtypes.bfloat16)


def prep_inputs(inp):
    f32 = lambda a: np.ascontiguousarray(np.asarray(a, dtype=np.float32))
    shared = {
        "w_in": f32(inp["w_in"][0]), "w_out": f32(inp["w_out"][0]), "w_xq": f32(inp["w_xq"][0]),
        "w_xkv": f32(inp["w_xkv"][0]), "w_xo": f32(inp["w_xo"][0]), "w_up": f32(inp["w_up"][0]),
        "w_down": f32(inp["w_down"][0]),
        "gains": f32(np.stack([inp["pre_mix_g"][0], inp["post_mix_g"][0], inp["pre_x_g"][0], inp["mem_norm_g"][0],
                               inp["post_x_g"][0], inp["pre_ffn_g"][0], inp["post_ffn_g"][0]])),
        "qkg": f32(np.concatenate([inp["q_norm_g"][0], inp["k_norm_g"][0]])[None, :]),
        "hgo": f32(np.asarray(inp["hg_out_norm_g"][0])[None, :]),
        "lbT": f32(np.asarray(inp["hg_lb"]).reshape(2, 2, 4, 128).transpose(3, 0, 1, 2).reshape(128, 16)),
        "convT": f32(np.concatenate([np.asarray(inp["conv_w"][0]), np.asarray(inp["conv_b"][0])[None, :]], 0)
                     .reshape(4, 44, 128).transpose(2, 1, 0).reshape(128, 176)),
        "ident": np.eye(128, dtype=np.float32).astype(ml_dtypes.bfloat16),
        "rope": rope_tables(),
        "mask": masks(),
    }
    x = np.asarray(inp["x"], dtype=np.float32)
    mem = np.asarray(inp["mem"], dtype=np.float32)
    maps = []
    for i in range(8):
        d = dict(shared)
        d["x"] = np.ascontiguousarray(x[i])
        d["mem"] = np.ascontiguousarray(mem[i])
        maps.append(d)
    return maps


_NC_CACHE = {}


def kernel(**inputs):
    if "nc" not in _NC_CACHE:
        _NC_CACHE["nc"] = build()
    nc = _NC_CACHE["nc"]
    maps = prep_inputs(inputs)
    res = run_bass_kernel_spmd(nc, maps, core_ids=list(range(8)))
    return np.stack([np.asarray(r["out"], dtype=np.float32) for r in res.results], 0)
```

```python
import numpy as np
import ml_dtypes
import concourse.bass as bass
import concourse.mybir as mybir
from concourse.bass_utils import run_bass_kernel_spmd

F32 = mybir.dt.float32
BF16 = mybir.dt.bfloat16
AF = mybir.ActivationFunctionType
ALU = mybir.AluOpType
AX = mybir.AxisListType

N_TOK = 2048
D = 1024
NB = 16
EPS = 1e-6
D_FF = 2816
N_IN = 3328


class Res:
    __slots__ = ("name", "w", "r", "big")

    def __init__(self, name):
        self.name = name
        self.w = None
        self.r = {}
        self.big = False


class EngQ:
    def __init__(self, name):
        self.name = name
        self.ops = []
        self.epoch = 0
        self.cnt = 0
        self.total = 0
        self.waited = {}
        self.pending = False


class Prog:
    ENGS = ("pe", "act", "dve", "pool", "sp")
    LIMIT = 900
    NEPOCH = {"pe": 14, "act": 10, "dve": 14, "pool": 4, "sp": 1}

    def __init__(self, nc, n_dma_sems=32):
        self.nc = nc
        self.q = {e: EngQ(e) for e in self.ENGS}
        self.sems = {}
        self.n_dma_sems = n_dma_sems
        self.dma_cnt = [0] * n_dma_sems
        self.dma_pool = {"pool": list(range(0, 16)), "sp": list(range(16, 26)), "act": list(range(26, 32))}
        self.dma_rr = {"pool": 0, "sp": 0, "act": 0}
        self._ctx = []
        self.res = {}

    def R(self, *key):
        r = self.res.get(key)
        if r is None:
            r = Res(key)
            self.res[key] = r
        return r

    def open(self):
        nc = self.nc
        for e in self.ENGS:
            for ep in range(self.NEPOCH[e]):
                c = nc.semaphore("s_%s%d" % (e, ep))
                self.sems[(e, ep)] = c.__enter__()
                self._ctx.append(c)
        for i in range(self.n_dma_sems):
            c = nc.semaphore("s_dma%d" % i)
            self.sems[("dma", i)] = c.__enter__()
            self._ctx.append(c)

    def close(self):
        for c in reversed(self._ctx):
            c.__exit__(None, None, None)

    def _need(self, eng, tok, same_ok):
        if tok is None:
            return
        key, val = tok
        q = self.q[eng]
        if key[0] == "dma":
            if q.waited.get(key, 0) >= val:
                return
            q.waited[key] = val
        else:
            src, ep = key
            if src == eng and not same_ok:
                return
            if q.waited.get(src, (-1, 0)) >= (ep, val):
                return
            q.waited[src] = (ep, val)
        sem = self.sems[key]
        q.ops.append(lambda h, sem=sem, val=val: h.wait_ge(sem, val))

    def _deps(self, eng, reads, writes):
        for r in reads:
            self._need(eng, r.w, same_ok=(eng != "pe" and not r.big))
        so = (eng != "pe")
        for w in writes:
            self._need(eng, w.w, same_ok=so)
            for tok in w.r.values():
                self._need(eng, tok, same_ok=so)

    def op(self, eng, fn, reads=(), writes=(), inc=True, big=False):
        q = self.q[eng]
        if q.cnt >= self.LIMIT and not q.pending:
            q.epoch += 1
            q.cnt = 0
            assert q.epoch < self.NEPOCH[eng], "out of epochs for " + eng
        self._deps(eng, reads, writes)
        key = (eng, q.epoch)
        val = q.cnt + 1
        tok = (key, val)
        if inc:
            q.cnt = val
            q.total += 1
            sem = self.sems[key]
            q.ops.append(lambda h, fn=fn, sem=sem: fn(h).then_inc(sem, 1))
            q.pending = False
        else:
            q.ops.append(lambda h, fn=fn: fn(h))
            q.pending = True
        for r in reads:
            r.r[eng] = tok
        for w in writes:
            w.w = tok
            w.r = {}
            w.big = big
        return tok

    def dma(self, eng, out, in_, reads=(), writes=(), also_wait=(), **kw):
        pl = self.dma_pool[eng]
        i = pl[self.dma_rr[eng] % len(pl)]
        self.dma_rr[eng] += 1
        key = ("dma", i)
        if self.dma_cnt[i] > 0:
            self._need(eng, (key, 16 * self.dma_cnt[i]), same_ok=True)
        self._deps(eng, reads, list(writes) + list(also_wait))
        self.dma_cnt[i] += 1
        assert self.dma_cnt[i] < 60
        val = 16 * self.dma_cnt[i]
        sem = self.sems[key]
        q = self.q[eng]
        q.ops.append(lambda h, sem=sem, out=out, in_=in_, kw=kw:
                     h.dma_start(out=out, in_=in_, **kw).then_inc(sem, 16))
        tok = (key, val)
        for r in reads:
            r.r[key] = tok
        for w in writes:
            w.w = tok
            w.r = {}
            w.big = False
        return tok

    def wait_all(self, eng, skip=()):
        for r in self.res.values():
            if r.name[0] in skip:
                continue
            self._need(eng, r.w, same_ok=True)
            for tok in list(r.r.values()):
                self._need(eng, tok, same_ok=True)

    def emit(self):
        nc = self.nc
        for e in self.ENGS:
            assert not self.q[e].pending, "engine %s ends with non-inc'd instr" % e
        with nc.Block() as block:
            @block.tensor
            def _(h):
                for f in self.q["pe"].ops:
                    f(h)

            @block.scalar
            def _(h):
                for f in self.q["act"].ops:
                    f(h)

            @block.vector
            def _(h):
                for f in self.q["dve"].ops:
                    f(h)

            @block.gpsimd
            def _(h):
                for f in self.q["pool"].ops:
                    f(h)

            @block.sync
            def _(h):
                for f in self.q["sp"].ops:
                    f(h)


class StopBuild(Exception):
    pass


def build(stop=99, dbg=()):
    st = {}
    try:
        return _build(stop, dbg, st)
    except StopBuild:
        return finish(st["nc"], st["P"])


def _build(stop, dbg, st):
    nc = bass.Bass("TRN2", target_bir_lowering=False)
    P = Prog(nc)
    P.open()
    R = P.R
    st["nc"] = nc
    st["P"] = P

    def chk(level):
        if stop <= level:
            raise StopBuild()

    def din(name, shape, dt=F32):
        return nc.dram_tensor(name, list(shape), dt, kind="ExternalInput").ap()

    x_d = din("x", [N_TOK, D])
    mem_d = din("mem", [256, D])
    w_in_d = din("w_in", [D, N_IN])
    w_out_d = din("w_out", [D, D])
    w_xq_d = din("w_xq", [D, D])
    w_xkv_d = din("w_xkv", [D, 2 * D])
    w_xo_d = din("w_xo", [D, D])
    w_up_d = din("w_up", [D, 2 * D_FF])
    w_down_d = din("w_down", [D_FF, D])
    gains_d = din("gains", [7, D])
    qkg_d = din("qkg", [1, 128])
    hgo_d = din("hgo", [1, 128])
    lbT_d = din("lbT", [128, 16])
    convT_d = din("convT", [128, 44 * 4])
    ident_d = din("ident", [128, 128], BF16)
    rope_d = din("rope", [128, 2 * 16 * 32])
    mask_d = din("mask", [128, 2 * 128], BF16)
    out_d = nc.dram_tensor("out", [N_TOK, D], F32, kind="ExternalOutput").ap()
    dbg_d = {}
    for name, shape in dbg:
        dbg_d[name] = nc.dram_tensor("dbg_" + name, list(shape), F32, kind="ExternalOutput").ap()

    def sb(name, shape, dt):
        return nc.alloc_sbuf_tensor("sb_" + name, list(shape), dt)

    hT = sb("hT", [128, 8, N_TOK], BF16)
    mixT = sb("mixT", [128, 8, N_TOK], BF16)
    arena1 = sb("arena1", [128, 16384], F32)
    arena2 = sb("arena2", [128, 15872], F32)
    ident = sb("ident", [128, 128], BF16)
    ones_bf = sb("ones_bf", [128, 128], BF16)
    gpre = sb("gpre", [128, D], F32)
    gpost = sb("gpost", [128, D], F32)
    stat = sb("stat", [128, 64], F32)
    stat2 = sb("stat2", [128, 64], F32)
    junk = sb("junk", [128, D], BF16)
    hb = sb("hb", [128, 2, D], BF16)
    lbT = sb("lbT", [128, 16], F32)
    lbv = sb("lbv", [128, 40], F32)
    convT = sb("convT", [128, 44 * 4], F32)
    epsc = sb("epsc", [128, 1], F32)
    psf = nc.alloc_psum_tensor("psf", [128, 3, 2, 512], F32)
    psb = nc.alloc_psum_tensor("psb", [128, 2, 1024], BF16)

    X = arena1[:].rearrange("p (b d) -> p b d", d=D)

    def a1_bf(off_bytes, shape):
        n = int(np.prod(shape))
        v = arena1[:, off_bytes // 4: off_bytes // 4 + n // 2].bitcast(BF16)
        return v, off_bytes + n * 2

    def a2_bf(off_bytes, n):
        v = arena2[:, off_bytes // 4: off_bytes // 4 + n // 2].bitcast(BF16)
        return v, off_bytes + n * 2

    def a2_f(off_bytes, n):
        v = arena2[:, off_bytes // 4: off_bytes // 4 + n]
        return v, off_bytes + n * 4

    def a1_f(off_bytes, n):
        v = arena1[:, off_bytes // 4: off_bytes // 4 + n]
        return v, off_bytes + n * 4

    def MM(out, lhsT, rhs, start, stop, reads, writes, inc=True, tp=None):
        if tp is None:
            return P.op("pe", lambda h: h.matmul(out, lhsT=lhsT, rhs=rhs, start=start, stop=stop),
                        reads=reads, writes=writes, inc=inc)
        return P.op("pe", lambda h: h.matmul(out, lhsT=lhsT, rhs=rhs, start=start, stop=stop, tile_position=tp),
                    reads=reads, writes=writes, inc=inc)

    def TR(out, in_, reads, writes, inc=True):
        return P.op("pe", lambda h: h.transpose(out, in_, ident[:]), reads=reads, writes=writes, inc=inc)

    BIG_N = 1024

    def isbig(ap):
        n = 1
        for d_ in ap.shape[1:]:
            n *= d_
        return n >= BIG_N

    def ACT(out, in_, func, reads, writes, scale=None, bias=None, accum=None):
        kw = {}
        if scale is not None:
            kw["scale"] = scale
        if bias is not None:
            kw["bias"] = bias
        if accum is not None:
            kw["accum_out"] = accum
        return P.op("act", lambda h: h.activation(out=out, in_=in_, func=func, **kw), reads=reads, writes=writes,
                    big=(accum is None and isbig(out)))

    def TT(eng, out, in0, in1, op, reads, writes):
        return P.op(eng, lambda h: h.tensor_tensor(out=out, in0=in0, in1=in1, op=op), reads=reads, writes=writes, big=isbig(out))

    def TS(eng, out, in0, s1, s2, op0, op1, reads, writes):
        if s2 is None:
            return P.op(eng, lambda h: h.tensor_scalar(out=out, in0=in0, scalar1=s1, scalar2=None, op0=op0),
                        reads=reads, writes=writes, big=isbig(out))
        return P.op(eng, lambda h: h.tensor_scalar(out=out, in0=in0, scalar1=s1, scalar2=s2, op0=op0, op1=op1),
                    reads=reads, writes=writes, big=isbig(out))

    def STT(eng, out, in0, scalar, in1, op0, op1, reads, writes):
        return P.op(eng, lambda h: h.scalar_tensor_tensor(out=out, in0=in0, scalar=scalar, in1=in1, op0=op0, op1=op1),
                    reads=reads, writes=writes, big=isbig(out))

    def CP(eng, out, in_, reads, writes):
        if eng == "act":
            return ACT(out, in_, AF.Copy, reads, writes)
        return P.op(eng, lambda h: h.tensor_copy(out=out, in_=in_), reads=reads, writes=writes, big=isbig(out))

    def RECIP(out, in_, reads, writes):
        return P.op("dve", lambda h: h.reciprocal(out=out, in_=in_), reads=reads, writes=writes)

    def MEMSET(eng, ap, val, writes):
        return P.op(eng, lambda h: h.memset(ap, val), writes=writes)

    def barrier(skip=()):
        for e in Prog.ENGS:
            P.wait_all(e, skip)

    def load_w(dst, src_rows_cols, reads_w, also_wait=()):
        src = src_rows_cols.rearrange("(c p) n -> p c n", p=128)
        nch = src.shape[1]
        for c in range(nch):
            P.dma("pool", dst[:, c, :], src[:, c, :], writes=reads_w, also_wait=also_wait)

    def gain_load(dst, row, res):
        P.dma("sp", dst[:], gains_d[row:row + 1, :].partition_broadcast(128), writes=[res])

    def dump(name, ap_sb, reads, view=None):
        if name in dbg_d:
            P.dma("sp", dbg_d[name] if view is None else view(dbg_d[name]), ap_sb, reads=reads)

    P.dma("sp", ident[:], ident_d, writes=[R("ident")])
    P.dma("sp", lbT[:], lbT_d, writes=[R("lbT")])
    P.dma("sp", convT[:], convT_d, writes=[R("convT")])
    MEMSET("pool", ones_bf[:], 1.0, [R("ones")])
    MEMSET("pool", epsc[:], EPS, [R("epsc")])
    MEMSET("pool", stat[:], 0.0, [R("stat")])
    MEMSET("pool", stat2[:], 0.0, [R("stat2")])

    def prenorm_T(grow, nblk=NB, src=None, dstT=None, tag="h", srcR=None):
        src = X if src is None else src
        dstT = hT if dstT is None else dstT
        srcR = (lambda b: R("X", b)) if srcR is None else srcR
        gain_load(gpre, grow, R("gpre"))
        for b in range(nblk):
            ACT(junk[:], src[:, b, :], AF.Square, [srcR(b)], [R("junk"), R("stat")], accum=stat[:, b:b + 1])
        TS("dve", stat2[:, 0:nblk], stat[:, 0:nblk], 1.0 / D, EPS, ALU.mult, ALU.add, [R("stat")], [R("stat2")])
        ACT(stat2[:, 0:nblk], stat2[:, 0:nblk], AF.Ln, [R("stat2")], [R("stat2")])
        ACT(stat2[:, 0:nblk], stat2[:, 0:nblk], AF.Exp, [R("stat2")], [R("stat2")], scale=-0.5)
        for b in range(nblk):
            k = b % 2
            STT("dve", hb[:, k, :], src[:, b, :], stat2[:, b:b + 1], gpre[:], ALU.mult, ALU.mult,
                [srcR(b), R("stat2"), R("gpre")], [R("hb", k)])
            for c in range(8):
                TR(psb[:, k, c * 128:(c + 1) * 128], hb[:, k, c * 128:(c + 1) * 128],
                   [R("hb", k), R("ident")], [R("psb", k)], inc=(c == 7))
            CP("act", dstT[:, :, b * 128:(b + 1) * 128], psb[:, k, :].rearrange("p (c t) -> p c t", t=128),
               [], [R("psb", k), R(tag, b)])

    o2 = 0
    Watt, o2 = a2_bf(o2, 8 * 768); Watt = Watt.rearrange("p (c n) -> p c n", n=768)
    rope_raw, o2 = a2_f(o2, 1024)
    qkgB, o2 = a2_f(o2, 128)
    o2_attw = o2
    load_w(Watt, w_in_d[:, 0:768], [R("Watt")])
    P.dma("sp", rope_raw, rope_d, writes=[R("rope_raw")])
    P.dma("sp", qkgB, qkg_d.partition_broadcast(128), writes=[R("qkgB")])
    for b in range(NB):
        P.dma("sp" if b % 2 else "act", X[:, b, :], x_d[b * 128:(b + 1) * 128, :], writes=[R("X", b)])
    prenorm_T(0)
    if "hT" in dbg_d:
        tmpf = arena2[:, 13000:13000 + 2048]
        for c in range(8):
            CP("dve", tmpf, hT[:, c, :], [R("h", b) for b in range(NB)], [R("dbgtmp")])
            P.dma("sp", dbg_d["hT"][c * 128:(c + 1) * 128, :], tmpf, reads=[R("dbgtmp")])
    if stop <= 1:
        return finish(nc, P)
    barrier()
    if stop <= 1.1:
        return finish(nc, P)


    o1 = 0
    ropeT = []
    for i in range(8):
        v, o1 = a1_f(o1, 512)
        ropeT.append(v.rearrange("p (b i) -> p b i", i=32))
    QT, o1 = a1_bf(o1, [4 * N_TOK]); QT = QT.rearrange("p (j t) -> p j t", t=N_TOK)
    KTd, o1 = a1_bf(o1, [2 * N_TOK]); KTd = KTd.rearrange("p (g t) -> p g t", t=N_TOK)
    VA, o1 = a1_bf(o1, [NB * 2 * 192]); VA = VA.rearrange("p (b g d) -> p b g d", g=2, d=192)
    o2 = o2_attw
    qkv = []; sqb = []; tq = []; qn = []; qr = []
    for i in range(2):
        v, o2 = a2_f(o2, 768); qkv.append(v)
        v, o2 = a2_f(o2, 640); sqb.append(v)
        v, o2 = a2_f(o2, 4 * 320); tq.append(v.rearrange("p (a n) -> p a n", n=320))
        v, o2 = a2_f(o2, 640); qn.append(v)
        v, o2 = a2_bf(o2, 768); qr.append(v)
    PT = []
    for i in range(3):
        v, o2 = a2_bf(o2, 1024); PT.append(v.rearrange("p (u n) -> p u n", n=512))
    rc = []
    for i in range(2):
        v, o2 = a2_f(o2, 512); rc.append(v)
    assert o1 <= 65536 and o2 <= 63488, (o1, o2)

    if stop <= 1.16:
        return finish(nc, P)
    MEMSET("pool", VA, 1.0, [R("VA", b) for b in range(NB)])
    if stop <= 1.17:
        return finish(nc, P)
    cosv = rope_raw[:, 0:512].rearrange("p (b i) -> p b i", i=32)
    sinv = rope_raw[:, 512:1024].rearrange("p (b i) -> p b i", i=32)
    for qk in range(2):
        gv = qkgB[:, qk * 64:(qk + 1) * 64].rearrange("p (i two) -> p i two", two=2)
        ge = gv[:, :, 0].unsqueeze(1).broadcast_to([128, NB, 32])
        go = gv[:, :, 1].unsqueeze(1).broadcast_to([128, NB, 32])
        for ti, (tab, gg) in enumerate(((cosv, ge), (sinv, go), (sinv, ge), (cosv, go))):
            TT("dve", ropeT[qk * 4 + ti], tab, gg, ALU.mult, [R("rope_raw"), R("qkgB")], [R("ropeT")])

    def att_proj(b):
        k = b % 2
        tokb = slice(b * 128, (b + 1) * 128)
        for c in range(8):
            MM(psf[:, k, 0, :], hT[:, c, tokb], Watt[:, c, 0:512], c == 0, c == 7,
               [R("h", b), R("Watt")], [R("psf", k, 0)], inc=False)
        for c in range(8):
            MM(psf[:, k, 1, 0:256], hT[:, c, tokb], Watt[:, c, 512:768], c == 0, c == 7,
               [R("h", b), R("Watt")], [R("psf", k, 1)], inc=(c == 7))
        ACT(qkv[k][:, 0:512], psf[:, k, 0, :], AF.Copy, [], [R("psf", k, 0), R("qkv", k)])
        ACT(qkv[k][:, 512:768], psf[:, k, 1, 0:256], AF.Copy, [], [R("psf", k, 1), R("qkv", k)])
        ACT(sqb[k], qkv[k][:, 0:640], AF.Square, [R("qkv", k)], [R("sqb", k)])
        ss = stat[:, 16 + 10 * k: 26 + 10 * k]
        rs = stat2[:, 16 + 10 * k: 26 + 10 * k]
        P.op("dve", lambda h: h.tensor_reduce(out=ss, in_=sqb[k].rearrange("p (h d) -> p h d", d=64),
                                              axis=AX.X, op=ALU.add),
             reads=[R("sqb", k)], writes=[R("ss", k)])
        TS("dve", rs, ss, 1.0 / 64, EPS, ALU.mult, ALU.add, [R("ss", k)], [R("rs", k)])
        ACT(rs, rs, AF.Ln, [R("rs", k)], [R("rs", k)])
        ACT(rs, rs, AF.Exp, [R("rs", k)], [R("rs", k)], scale=-0.5)
        for qk, eng, nh, c0 in ((0, "dve", 8, 0), (1, "dve", 2, 512)):
            src = qkv[k][:, c0:c0 + nh * 64].rearrange("p (h i two) -> p h i two", i=32, two=2)
            xe = src[:, :, :, 0]
            xo = src[:, :, :, 1]
            tabs = [ropeT[qk * 4 + ti][:, b, :].unsqueeze(1).broadcast_to([128, nh, 32]) for ti in range(4)]
            tt = [tq[k][:, a, 0:nh * 32].rearrange("p (h i) -> p h i", i=32) for a in range(4)]
            rd = [R("qkv", k), R("ropeT")]
            wr = [R("tq", k, qk)]
            TT(eng, tt[0], xe, tabs[0], ALU.mult, rd, wr)
            TT(eng, tt[1], xo, tabs[1], ALU.mult, rd, wr)
            TT(eng, tt[2], xe, tabs[2], ALU.mult, rd, wr)
            TT(eng, tt[3], xo, tabs[3], ALU.mult, rd, wr)
            dst = qn[k][:, qk * 512: qk * 512 + nh * 64].rearrange("p (h i two) -> p h i two", i=32, two=2)
            TT(eng, dst[:, :, :, 0], tt[0], tt[1], ALU.subtract, wr, [R("qn", k, qk)])
            TT(eng, dst[:, :, :, 1], tt[2], tt[3], ALU.add, wr, [R("qn", k, qk)])
            rsb = rs[:, qk * 8: qk * 8 + nh]
            if qk == 0:
                TT(eng, qr[k][:, 0:512].rearrange("p (h d) -> p h d", d=64),
                   qn[k][:, 0:512].rearrange("p (h d) -> p h d", d=64),
                   rsb.unsqueeze(2).broadcast_to([128, 8, 64]), ALU.mult,
                   [R("qn", k, 0), R("rs", k)], [R("qr", k, 0)])
            else:
                for dup in range(2):
                    TT(eng, qr[k][:, 512:768].rearrange("p (g u d) -> p g u d", u=2, d=64)[:, :, dup, :],
                       qn[k][:, 512:640].rearrange("p (g d) -> p g d", d=64),
                       rsb.unsqueeze(2).broadcast_to([128, 2, 64]), ALU.mult,
                       [R("qn", k, 1), R("rs", k)], [R("qr", k, 1)])
        CP("dve", VA[:, b, :, 64:128], qkv[k][:, 640:768].rearrange("p (g d) -> p g d", d=64),
           [R("qkv", k)], [R("VA", b)])
    def att_proj2(b):
        k = b % 2
        tokb = slice(b * 128, (b + 1) * 128)
        for j in range(6):
            TR(psb[:, k, j * 128:(j + 1) * 128], qr[k][:, j * 128:(j + 1) * 128],
               [R("qr", k, 0), R("qr", k, 1), R("ident")], [R("psb", k)], inc=(j == 5))
        CP("act", QT[:, :, tokb], psb[:, k, 0:512].rearrange("p (j t) -> p j t", t=128), [], [R("psb", k), R("QT", b)])
        CP("dve", KTd[:, :, tokb], psb[:, k, 512:768].rearrange("p (g t) -> p g t", t=128), [], [R("psb", k), R("KT", b)])

    if stop <= 1.2:
        return finish(nc, P)
    for b in range(NB + 1):
        if b < NB:
            att_proj(b)
        if b >= 1:
            att_proj2(b - 1)
    if "QT" in dbg_d:
        tmpf = arena2[:, 14000:14000 + 1024]
        for c in range(4):
            for hf in range(2):
                CP("dve", tmpf, QT[:, c, hf * 1024:(hf + 1) * 1024], [R("QT", b) for b in range(NB)], [R("dbgtmp")])
                P.dma("sp", dbg_d["QT"][c * 128:(c + 1) * 128, hf * 1024:(hf + 1) * 1024], tmpf, reads=[R("dbgtmp")])
    if stop <= 1.5:
        return finish(nc, P)

    Wh, _ = a2_bf(0, 8 * 640); Wh = Wh.rearrange("p (c n) -> p c n", n=640)
    for gi, c0 in enumerate((768, 1280, 1792, 2304, 2816)):
        src = w_in_d[:, c0: c0 + 128].rearrange("(c p) n -> p c n", p=128)
        P.dma("pool", Wh[:, :, gi * 128:(gi + 1) * 128], src, writes=[R("Wh")], also_wait=[R("Watt")])
    PREFETCHED_WH0 = True
    pt_i = [0]
    psbf = [psb[:, 0, :].bitcast(F32), psb[:, 1, :].bitcast(F32)]
    unit_i = [0]

    def st_mm_unit(m, qt, kb):
        g = m // 2
        qs = slice(qt * 512, (qt + 1) * 512)
        qreads = [R("QT", qt * 4 + i) for i in range(4)]
        k = kb % 2
        for par in range(2):
            rows = slice(par * 64, par * 64 + 64)
            MM(psf[:, k, par, :], KTd[rows, g, kb * 128:(kb + 1) * 128], QT[rows, m, qs], True, True,
               qreads + [R("KT", kb)], [R("psf", k, par)], inc=(par == 1))

    UNITS = [(m_, qt_) for m_ in range(4) for qt_ in range(4)]

    def att_pair_tile(m, qt):
        g = m // 2
        j = m
        qs = slice(qt * 512, (qt + 1) * 512)
        uidx = UNITS.index((m, qt))
        ui = unit_i[0] % 2
        unit_i[0] += 1
        if ui == 0:
            accs = [(psf[:, 2, 0, :], R("psf", 2, 0)), (psf[:, 2, 1, :], R("psf", 2, 1))]
        else:
            accs = [(psbf[0], R("psb", 0)), (psbf[1], R("psb", 1))]
        qreads = [R("QT", qt * 4 + i) for i in range(4)]

        def st_mm(kb):
            k = kb % 2
            for par in range(2):
                rows = slice(par * 64, par * 64 + 64)
                MM(psf[:, k, par, :], KTd[rows, g, kb * 128:(kb + 1) * 128], QT[rows, j, qs], True, True,
                   qreads + [R("KT", kb)], [R("psf", k, par)], inc=(par == 1))

        if uidx == 0:
            st_mm(0)
        for kb in range(16):
            if kb + 1 < 16:
                st_mm(kb + 1)
            elif uidx + 1 < len(UNITS):
                st_mm_unit(UNITS[uidx + 1][0], UNITS[uidx + 1][1], 0)
            pi = pt_i[0] % 3
            pt_i[0] += 1
            k = kb % 2
            ACT(PT[pi], psf[:, k, :, :], AF.Exp, [], [R("psf", k, 0), R("psf", k, 1), R("PT", pi)], scale=0.125)
            for par in range(2):
                vsl = slice(64, 192) if par == 0 else slice(0, 128)
                MM(accs[par][0], VA[:, kb, g, vsl], PT[pi][:, par, :], kb == 0, kb == 15,
                   [R("VA", kb), R("PT", pi)], [accs[par][1]], inc=(par == 1))
        rcb = rc[ui]
        for par in range(2):
            rows = slice(par * 64, par * 64 + 64)
            orow = slice((1 - par) * 64, (1 - par) * 64 + 64)
            acc, accR = accs[par]
            P.op("dve", lambda hh, acc=acc, rows=rows, orow=orow: hh.reciprocal(out=rcb[rows, :], in_=acc[orow, :]),
                 reads=[], writes=[accR, R("rc", ui, par)])
            TT("dve", mixT[rows, j, qs], acc[rows, :], rcb[rows, :], ALU.mult, [R("rc", ui, par)], [accR, R("mixT", j, qt)])

    for m in range(4):
        for qt in range(4):
            att_pair_tile(m, qt)

    if "att" in dbg_d:
        tmpf = arena2[:, 0:2048]
        for c in range(4):
            CP("dve", tmpf, mixT[:, c, :], [R("mixT", c, qt) for qt in range(4)], [R("dbgtmp")])
            P.dma("sp", dbg_d["att"][c * 128:(c + 1) * 128, :], tmpf, reads=[R("dbgtmp")])
    if stop <= 2:
        return finish(nc, P)
    barrier(skip=("Wh",))

    bank_i = [0]

    def bank():
        i = bank_i[0] % 6
        bank_i[0] += 1
        return psf[:, i // 2, i % 2, :], R("psf", i // 2, i % 2)

    o1 = 0
    fb = {}
    for nm in ("qs", "SG", "E", "LF", "B", "KK", "rmask", "gate"):
        fb[nm], o1 = a1_f(o1, 2048)
    Tbig = arena1[:, 2 * 2048: 4 * 2048].rearrange("p (c v) -> p c v", v=128)
    o2 = 0
    Wh, o2 = a2_bf(o2, 8 * 640); Wh = Wh.rearrange("p (c n) -> p c n", n=640)
    Of, o2 = a2_f(o2, 2048); Of3 = Of.rearrange("p (b v) -> p b v", v=128)
    qtl, o2 = a2_bf(o2, 2048)
    ktl, o2 = a2_bf(o2, 2048)
    ktok, o2 = a2_bf(o2, 2048); ktok = ktok.rearrange("p (b v) -> p b v", v=128)
    vtok, o2 = a2_bf(o2, 2048); vtok = vtok.rearrange("p (b v) -> p b v", v=128)
    Sbf, o2 = a2_bf(o2, 32 * 128); Sbf = Sbf.rearrange("p (c v) -> p c v", v=128)
    AT, o2 = a2_bf(o2, 2048); AT = AT.rearrange("p (b v) -> p b v", v=128)
    recb, o2 = a2_bf(o2, 2048); recb = recb.rearrange("p (b v) -> p b v", v=128)
    hgoB, o2 = a2_f(o2, 128)
    maskS, o2 = a2_bf(o2, 256); maskS = maskS.rearrange("p (d t) -> p d t", t=128)
    gtmp, o2 = a2_f(o2, 256)
    assert o1 <= 65536 and o2 <= 63488, (o1, o2)
    gate3 = fb["gate"].rearrange("p (b v) -> p b v", v=128)

    P.dma("sp", hgoB, hgo_d.partition_broadcast(128), writes=[R("hgoB")])
    P.dma("sp", maskS, mask_d.rearrange("p (d t) -> p d t", t=128), writes=[R("maskS")])
    MEMSET("pool", fb["rmask"], 1.0, [R("rmask")])
    MEMSET("pool", fb["rmask"].rearrange("p (c t) -> p c t", t=64)[:, :, 0:1], 0.0, [R("rmask")])
    lv = lbT[:].rearrange("p (d l h) -> p d l h", l=2, h=4)
    TT("dve", lbv[:, 0:8].rearrange("p (d h) -> p d h", h=4), lv[:, :, 0, :], lv[:, :, 1, :], ALU.subtract,
       [R("lbT")], [R("lbv")])
    ACT(lbv[:, 0:8], lbv[:, 0:8], AF.Exp, [R("lbv")], [R("lbv")], scale=-1.0)
    TS("dve", lbv[:, 0:8], lbv[:, 0:8], 1.0, None, ALU.add, None, [R("lbv")], [R("lbv")])
    RECIP(lbv[:, 0:8], lbv[:, 0:8], [R("lbv")], [R("lbv")])
    TS("dve", lbv[:, 8:16], lbv[:, 0:8], -1.0, 1.0, ALU.mult, ALU.add, [R("lbv")], [R("lbv")])
    TS("dve", lbv[:, 16:24], lbv[:, 0:8], -0.5, 0.5, ALU.mult, ALU.add, [R("lbv")], [R("lbv")])
    TS("dve", lbv[:, 24:32], lbv[:, 0:8], 0.5, 0.5, ALU.mult, ALU.add, [R("lbv")], [R("lbv")])
    TS("dve", lbv[:, 32:40], lbv[:, 0:8], 0.5, -0.5, ALU.mult, ALU.add, [R("lbv")], [R("lbv")])

    ALLH = [R("h", b) for b in range(NB)]
    chk(2.1)

    def load_Wh(hh, extra=()):
        for gi, c0 in enumerate((768, 1280, 1792, 2304, 2816)):
            src = w_in_d[:, c0 + hh * 128: c0 + (hh + 1) * 128].rearrange("(c p) n -> p c n", p=128)
            P.dma("pool", Wh[:, :, gi * 128:(gi + 1) * 128], src, writes=[R("Wh")] + list(extra))

    def hgrn_head(hh):
        for bp in range(8):
            bk, bkR = bank()
            for u in range(2):
                b = 2 * bp + u
                for c in range(8):
                    MM(bk[:, u * 256:(u + 1) * 256], hT[:, c, b * 128:(b + 1) * 128], Wh[:, c, 384:640], c == 0, c == 7,
                       [R("h", b), R("Wh")], [bkR], inc=(u == 1 and c == 7))
            bv = bk.rearrange("p (u n) -> p u n", n=256)
            ACT(vtok[:, 2 * bp:2 * bp + 2, :], bv[:, :, 0:128], AF.Copy, [], [bkR, R("vtok")])
            g3 = gtmp.rearrange("p (u n) -> p u n", n=128)
            ACT(gate3[:, 2 * bp:2 * bp + 2, :], bv[:, :, 128:256], AF.Silu, [], [bkR, R("gate")])

        chk(2.2)

        def fm_proj(gi, dst, dstname):
            kept = []
            for qt in range(4):
                bk, bkR = bank()
                for c in range(8):
                    MM(bk, Wh[:, c, gi * 128:(gi + 1) * 128], hT[:, c, qt * 512:(qt + 1) * 512], c == 0, c == 7,
                       ALLH[qt * 4:qt * 4 + 4] + [R("Wh")], [bkR], inc=(c == 7))
                ACT(dst[:, qt * 512:(qt + 1) * 512], bk, AF.Tanh, [], [bkR, R(dstname, qt // 2)], scale=0.5)
                kept.append((bk, bkR))
            return kept

        for qt in range(4):
            bk, bkR = bank()
            for c in range(8):
                MM(bk, Wh[:, c, 0:128], hT[:, c, qt * 512:(qt + 1) * 512], c == 0, c == 7,
                   ALLH[qt * 4:qt * 4 + 4] + [R("Wh")], [bkR], inc=(c == 7))
            sl = slice(qt * 512, (qt + 1) * 512)
            ACT(fb["qs"][:, sl], bk, AF.Silu, [], [bkR, R("qs", qt // 2)])

        chk(2.3)
        for dr in range(2):
            li = dr * 4 + hh
            ha_ap = lbv[:, 16 + li: 17 + li]
            hb_ap = lbv[:, 24 + li: 25 + li]
            nha_ap = lbv[:, 32 + li: 33 + li]
            E, SG, LF, B, KK = fb["E"], fb["SG"], fb["LF"], fb["B"], fb["KK"]
            EB, ENB = SG, LF
            H2 = (slice(0, 1024), slice(1024, 2048))
            if dr == 0:
                fm_proj(1, E, "E")
                TH, THn, CUM, CUMn = E, "E", B, "B"
            else:
                TH, THn, CUM, CUMn = B, "B", E, "E"
            for hf in range(2):
                sl = H2[hf]
                ACT(LF[:, sl], TH[:, sl], AF.Ln, [R(THn, hf), R("lbv")], [R("LF", hf)], scale=ha_ap, bias=hb_ap)
            for hf in range(2):
                sl = H2[hf]
                TS("dve", KK[:, sl], TH[:, sl], nha_ap, ha_ap, ALU.mult, ALU.add, [R(THn, hf), R("lbv")], [R("KK", hf)])
                P.op("dve", lambda h, sl=sl, CUM=CUM: h.tensor_tensor_scan(out=CUM[:, sl], data0=fb["rmask"][:, sl],
                                                                           data1=LF[:, sl], initial=0.0, op0=ALU.mult, op1=ALU.add),
                     reads=[R("rmask"), R("LF", hf)], writes=[R(CUMn, hf)])
                if dr == 1:
                    TT("dve", LF[:, sl], LF[:, sl], CUM[:, sl], ALU.subtract, [R("LF", hf), R(CUMn, hf)], [R("LF", hf)])
                    C3 = CUM[:, sl].rearrange("p (c t) -> p c t", t=64)
                    TT("dve", B[:, sl].rearrange("p (c t) -> p c t", t=64), LF[:, sl].rearrange("p (c t) -> p c t", t=64),
                       C3[:, :, 63:64].broadcast_to([128, 16, 64]), ALU.add, [R("LF", hf), R(CUMn, hf)], [R("B", hf)])
            Bs, Bn = B, "B"
            for hf in range(2):
                sl = H2[hf]
                ACT(EB[:, sl], Bs[:, sl], AF.Exp, [R(Bn, hf)], [R("SG", hf)])
                ACT(ENB[:, sl], Bs[:, sl], AF.Exp, [R(Bn, hf)], [R("LF", hf)], scale=-1.0)
            for hf in range(2):
                sl = H2[hf]
                TT("dve", qtl[:, sl], fb["qs"][:, sl], EB[:, sl], ALU.mult, [R("qs", hf), R("SG", hf)], [R("qtl", hf)])
                TT("dve", ktl[:, sl], KK[:, sl], ENB[:, sl], ALU.mult, [R("KK", hf), R("LF", hf)], [R("ktl", hf)])
            chk(2.4)
            for half in range(2):
                for i in range(8):
                    b = half * 8 + i
                    TR(psb[:, half, i * 128:(i + 1) * 128], ktl[:, b * 128:(b + 1) * 128], [R("ktl", half), R("ident")],
                       [R("psb", half)], inc=(i == 7))
                CP("act" if half else "dve", ktok[:, half * 8:(half + 1) * 8, :],
                   psb[:, half, :].rearrange("p (b v) -> p b v", v=128), [], [R("psb", half), R("ktok")])
            for bg in range(4):
                bk, bkR = bank()
                for u in range(4):
                    b = 4 * bg + u
                    MM(bk[:, u * 128:(u + 1) * 128], ktl[:, b * 128:(b + 1) * 128], qtl[:, b * 128:(b + 1) * 128], True, True,
                       [R("ktl", bg // 2), R("qtl", bg // 2)], [bkR], inc=(u == 3))
                TT("dve", AT[:, 4 * bg:4 * bg + 4, :], bk.rearrange("p (u t) -> p u t", t=128),
                   maskS[:, dr, :].unsqueeze(1).broadcast_to([128, 4, 128]), ALU.mult, [R("maskS")], [bkR, R("AT")])
            chk(2.5)
            MEMSET("pool", Sbf[:, 0 if dr == 0 else 31, :], 0.0, [R("Sbf")])
            DEAD = [R(nm, hf) for nm in ("E", "LF") for hf in range(2)]
            SGR = [R("SG", 0), R("SG", 1)]
            n = 0
            cprev = None
            for bgi in range(4):
                bg = bgi if dr == 0 else 3 - bgi
                ub = [bank() for _ in range(2)]
                for u in range(4):
                    b = 4 * bg + u
                    for j in range(2):
                        MM(ub[j][0][:, u * 128:(u + 1) * 128], ktok[64 * j:64 * j + 64, b, :], vtok[64 * j:64 * j + 64, b, :],
                           True, True, [R("ktok"), R("vtok")], [ub[j][1]], inc=(u == 3 and j == 1), tp=(64 * j, 0))
                order = [(u, j) for u in range(4) for j in range(2)]
                if dr == 1:
                    order = order[::-1]
                for (u, j) in order:
                    c = 2 * (4 * bg + u) + j
                    Uc = ub[j][0][:, u * 128:(u + 1) * 128]
                    if n == 0:
                        CP("dve", Tbig[:, c, :], Uc, [], [ub[j][1], R("Tb", c)] + DEAD)
                    else:
                        dprev = EB[:, 64 * cprev + 63: 64 * cprev + 64] if dr == 0 else EB[:, 64 * cprev: 64 * cprev + 1]
                        STT("dve", Tbig[:, c, :], Tbig[:, cprev, :], dprev, Uc, ALU.mult, ALU.add,
                            [R("Tb", cprev)] + SGR, [ub[j][1], R("Tb", c)])
                    cprev = c
                    n += 1
            EB3 = EB.rearrange("p (c t) -> p c t", t=64)
            if dr == 0:
                TT("dve", Sbf[:, 1:32, :], Tbig[:, 0:31, :], EB3[:, 0:31, 63:64].broadcast_to([128, 31, 128]), ALU.mult,
                   [R("Tb", cprev)] + SGR + DEAD, [R("Sbf")])
            else:
                TT("dve", Sbf[:, 0:31, :], Tbig[:, 1:32, :], EB3[:, 1:32, 0:1].broadcast_to([128, 31, 128]), ALU.mult,
                   [R("Tb", cprev)] + SGR + DEAD, [R("Sbf")])
            if dr == 0:
                fm_proj(2, B, "B")
                if hh < 3:
                    load_Wh(hh + 1)
                else:
                    load_w(hT, w_xkv_d, [R("Wkv")], also_wait=ALLH)
            chk(2.6)
            for bg in range(4):
                bk, bkR = bank()
                for u in range(4):
                    b = 4 * bg + u
                    us = slice(u * 128, (u + 1) * 128)
                    MM(bk[:, us], AT[:, b, :], vtok[:, b, :], True, False, [R("AT"), R("vtok")], [bkR], inc=False)
                    for j in range(2):
                        c = 2 * b + j
                        MM(bk[64 * j:64 * j + 64, us], qtl[:, b * 128 + 64 * j: b * 128 + 64 * j + 64], Sbf[:, c, :],
                           False, j == 1, [R("qtl", bg // 2), R("Sbf")], [bkR], inc=(u == 3 and j == 1), tp=(0, 64 * j))
                o3 = Of3[:, 4 * bg:4 * bg + 4, :]
                if dr == 0:
                    ACT(o3, bk.rearrange("p (u v) -> p u v", v=128), AF.Copy, [], [bkR, R("Of")])
                else:
                    TT("dve", o3, bk.rearrange("p (u v) -> p u v", v=128), o3, ALU.add, [], [bkR, R("Of")])
        chk(2.7)
        ACT(fb["E"], Of, AF.Square, [R("Of")], [R("E", 0), R("E", 1)])
        ss = stat[:, 40:56]
        rs = stat2[:, 40:56]
        P.op("dve", lambda h: h.tensor_reduce(out=ss, in_=fb["E"].rearrange("p (b v) -> p b v", v=128), axis=AX.X, op=ALU.add),
             reads=[R("E", 0), R("E", 1)], writes=[R("ssh")])
        TS("dve", rs, ss, 1.0 / 128, EPS, ALU.mult, ALU.add, [R("ssh")], [R("rsh")])
        ACT(rs, rs, AF.Ln, [R("rsh")], [R("rsh")])
        ACT(rs, rs, AF.Exp, [R("rsh")], [R("rsh")], scale=-0.5)
        SG3 = fb["SG"].rearrange("p (b v) -> p b v", v=128)
        TT("dve", SG3, Of3, rs.unsqueeze(2).broadcast_to([128, NB, 128]), ALU.mult, [R("Of"), R("rsh")], [R("SG", 0), R("SG", 1)])
        TT("dve", SG3, SG3, hgoB.unsqueeze(1).broadcast_to([128, NB, 128]), ALU.mult, [R("SG", 0), R("SG", 1), R("hgoB")],
           [R("SG", 0), R("SG", 1)])
        TT("dve", recb, SG3, gate3, ALU.mult, [R("SG", 0), R("SG", 1), R("gate")], [R("recb")])
        for half in range(2):
            for i in range(8):
                b = half * 8 + i
                TR(psb[:, half, i * 128:(i + 1) * 128], recb[:, b, :], [R("recb"), R("ident")], [R("psb", half)], inc=(i == 7))
            CP("act" if half else "dve", mixT[:, 4 + hh, half * 1024:(half + 1) * 1024], psb[:, half, :], [],
               [R("psb", half), R("mixT", 4 + hh, 2 * half), R("mixT", 4 + hh, 2 * half + 1)])

    if not PREFETCHED_WH0:
        load_Wh(0)
    for hh in range(4):
        hgrn_head(hh)
        chk(2.8 + 0.01 * hh)

    if "rec" in dbg_d:
        tmpf = fb["B"]
        for c in range(4):
            CP("dve", tmpf, mixT[:, 4 + c, :], [R("mixT", 4 + c, qt) for qt in range(4)], [R("dbgtmp")])
            P.dma("sp", dbg_d["rec"][c * 128:(c + 1) * 128, :], tmpf, reads=[R("dbgtmp")])
    if stop <= 3:
        return finish(nc, P)
    barrier(skip=("Wkv",))

    junk3 = junk[:].rearrange("p (u n) -> p u n", n=512)
    gpost3 = gpost[:].rearrange("p (u n) -> p u n", n=512)

    def proj_norm_residual(b, lhs_aps, lhs_reads, w_fn, w_reads, resid_ap, resid_reads, out_ap, out_writes):
        k = b % 2
        n = len(lhs_aps)
        for half in range(2):
            for ci in range(n):
                MM(psf[:, k, half, :], lhs_aps[ci], w_fn(ci, half), ci == 0, ci == n - 1, lhs_reads + w_reads[ci],
                   [R("psf", k, half)], inc=(half == 1 and ci == n - 1))
        ss = stat[:, 16 + b:17 + b]
        rs = stat2[:, 16 + b:17 + b]
        PB = [R("psf", k, 0), R("psf", k, 1)]
        ACT(junk3, psf[:, k, :, :], AF.Square, [], PB + [R("junk"), R("pss", b)], accum=ss)
        ACT(rs, ss, AF.Ln, [R("pss", b), R("epsc")], [R("prs", b)], scale=1.0 / D, bias=epsc[:])
        ACT(rs, rs, AF.Exp, [R("prs", b)], [R("prs", b)], scale=-0.5)
        STT("dve", psf[:, k, :, :], psf[:, k, :, :], rs, gpost3, ALU.mult, ALU.mult, [R("prs", b), R("gpost")], PB)
        TT("dve", out_ap.rearrange("p (u n) -> p u n", n=512), resid_ap.rearrange("p (u n) -> p u n", n=512), psf[:, k, :, :],
           ALU.add, resid_reads, PB + out_writes)

    def prenorm_block(b, src_ap, src_reads):
        k = b % 2
        ss = stat[:, 32 + b:33 + b]
        rs = stat2[:, 32 + b:33 + b]
        ACT(junk[:], src_ap, AF.Square, src_reads, [R("junk"), R("fss", b)], accum=ss)
        ACT(rs, ss, AF.Ln, [R("fss", b), R("epsc")], [R("frs", b)], scale=1.0 / D, bias=epsc[:])
        ACT(rs, rs, AF.Exp, [R("frs", b)], [R("frs", b)], scale=-0.5)
        STT("dve", hb[:, k, :], src_ap, rs, gpre[:], ALU.mult, ALU.mult, src_reads + [R("frs", b), R("gpre")], [R("hb", k)])

    def prenorm_block_tr(b):
        k = b % 2
        for c in range(8):
            TR(psb[:, k, c * 128:(c + 1) * 128], hb[:, k, c * 128:(c + 1) * 128], [R("hb", k), R("ident")], [R("psb", k)],
               inc=(c == 7))
        CP("act", hT[:, :, b * 128:(b + 1) * 128], psb[:, k, :].rearrange("p (c t) -> p c t", t=128), [], [R("psb", k), R("h", b)])

    o2 = 0
    Wo, o2 = a2_bf(o2, 8 * 1024); Wo = Wo.rearrange("p (c n) -> p c n", n=1024)
    load_w(Wo, w_out_d, [R("Wo")])
    gain_load(gpost, 1, R("gpost"))
    Wkv = hT
    Wq, _ = a2_bf(16384, 8 * 1024); Wq = Wq.rearrange("p (c n) -> p c n", n=1024)
    Wo2, _ = a2_bf(32768, 8 * 1024); Wo2 = Wo2.rearrange("p (c n) -> p c n", n=1024)
    load_w(Wq, w_xq_d, [R("Wq")])
    load_w(Wo2, w_xo_d, [R("Wo2")])
    for b in range(14):
        P.dma("sp" if b % 2 else "act", X[:, b, :], x_d[b * 128:(b + 1) * 128, :], writes=[R("X", b)])
    o2 = 49152
    memT, o2 = a2_bf(o2, 8 * 256); memT = memT.rearrange("p (c m) -> p c m", m=256)
    KxT, o2 = a2_bf(o2, 8 * 256); KxT = KxT.rearrange("p (j m) -> p j m", m=256)
    Vx, o2 = a2_bf(o2, 2 * 1024); Vx = Vx.rearrange("p (m n) -> p m n", n=1024)
    assert o2 <= 63488, o2
    for mb in range(2):
        P.dma("sp", X[:, 14 + mb, :], mem_d[mb * 128:(mb + 1) * 128, :], writes=[R("X", 14 + mb)])
    prenorm_T(3, nblk=2, src=X[:, 14:16, :], dstT=memT, tag="memT", srcR=lambda b: R("X", 14 + b))
    MT = [R("memT", 0), R("memT", 1)]
    for j in range(8):
        bk, bkR = bank()
        for c in range(8):
            MM(bk[:, 0:256], Wkv[:, c, j * 128:(j + 1) * 128], memT[:, c, :], c == 0, c == 7, MT + [R("Wkv")], [bkR], inc=(c == 7))
        CP("act" if j % 2 else "dve", KxT[:, j, :], bk[:, 0:256], [], [bkR, R("KxT")])
    for mb in range(2):
        for half in range(2):
            bk, bkR = bank()
            for c in range(8):
                MM(bk, memT[:, c, mb * 128:(mb + 1) * 128], Wkv[:, c, 1024 + half * 512: 1024 + (half + 1) * 512], c == 0, c == 7,
                   MT + [R("Wkv")], [bkR], inc=(c == 7))
            CP("act" if half else "dve", Vx[:, mb, half * 512:(half + 1) * 512], bk, [], [bkR, R("Vx")])
    for b in range(14, 16):
        P.dma("sp" if b % 2 else "act", X[:, b, :], x_d[b * 128:(b + 1) * 128, :], writes=[R("X", b)])
    gain_load(gpre, 2, R("gpre"))
    P._deps("act", [], [R("Wkv")])
    for b in range(NB + 2):
        if b < NB:
            tokb = slice(b * 128, (b + 1) * 128)
            proj_norm_residual(b, [mixT[:, c, tokb] for c in range(8)], [R("mixT", c, b // 4) for c in range(8)],
                               lambda ci, half: Wo[:, ci, half * 512:(half + 1) * 512], [[R("Wo")]] * 8,
                               X[:, b, :], [R("X", b)], X[:, b, :], [R("X", b)])
        if 1 <= b <= NB:
            prenorm_block(b - 1, X[:, b - 1, :], [R("X", b - 1)])
        if b >= 2:
            prenorm_block_tr(b - 2)
    if "x1" in dbg_d:
        for b in range(NB):
            P.dma("sp", dbg_d["x1"][b * 128:(b + 1) * 128, :], X[:, b, :], reads=[R("X", b)])
    if stop <= 4:
        return finish(nc, P)
    barrier(skip=("Wkv", "Wq", "Wo2"))

    o2 = 0
    Qx, o2 = a2_bf(o2, 8 * 512); Qx = Qx.rearrange("p (j n) -> p j n", n=512)
    PTx = []
    for i in range(2):
        v, o2 = a2_bf(o2, 1024); PTx.append(v.rearrange("p (u n) -> p u n", n=512))
    rcx2 = []
    for i in range(2):
        v, o2 = a2_f(o2, 512); rcx2.append(v)
    assert o2 <= 16384, o2
    gain_load(gpost, 4, R("gpost"))
    for qt in range(4):
        qs_ = slice(qt * 512, (qt + 1) * 512)
        for j in range(8):
            bk, bkR = bank()
            for c in range(8):
                MM(bk, Wq[:, c, j * 128:(j + 1) * 128], hT[:, c, qs_], c == 0, c == 7, ALLH[qt * 4:qt * 4 + 4] + [R("Wq")],
                   [bkR], inc=(c == 7))
            CP("act" if j % 2 else "dve", Qx[:, j, :], bk, [], [bkR, R("Qx", j)])
        for hx in range(4):
            k = hx % 2
            for mb in range(2):
                for half in range(2):
                    MM(psf[:, k, mb, :], KxT[:, 2 * hx + half, mb * 128:(mb + 1) * 128], Qx[:, 2 * hx + half, :], half == 0, half == 1,
                       [R("KxT"), R("Qx", 2 * hx + half)], [R("psf", k, mb)], inc=(mb == 1 and half == 1))
            ACT(PTx[k], psf[:, k, :, :], AF.Exp, [], [R("psf", k, 0), R("psf", k, 1), R("PTx", k)], scale=1.0 / 16)
            dbk, dbkR = psbf[k], R("psb", k)
            for mb in range(2):
                MM(dbk, ones_bf[:], PTx[k][:, mb, :], mb == 0, mb == 1, [R("ones"), R("PTx", k)], [dbkR], inc=(mb == 1))
            ACT(rcx2[k], dbk, AF.Ln, [], [dbkR, R("rcx", k)])
            ACT(rcx2[k], rcx2[k], AF.Exp, [R("rcx", k)], [R("rcx", k)], scale=-1.0)
            for dh in range(2):
                bk, bkR = psf[:, 2, dh, :], R("psf", 2, dh)
                for mb in range(2):
                    MM(bk, Vx[:, mb, hx * 256 + dh * 128: hx * 256 + (dh + 1) * 128], PTx[k][:, mb, :], mb == 0, mb == 1,
                       [R("Vx"), R("PTx", k)], [bkR], inc=(mb == 1))
                TT("dve", mixT[:, 2 * hx + dh, qs_], bk, rcx2[k], ALU.mult, [R("rcx", k)], [bkR, R("mixT", 2 * hx + dh, qt)])
    gain_load(gpre, 5, R("gpre"))
    for b in range(NB + 2):
        if b < NB:
            tokb = slice(b * 128, (b + 1) * 128)
            proj_norm_residual(b, [mixT[:, c, tokb] for c in range(8)], [R("mixT", c, b // 4) for c in range(8)],
                               lambda ci, half: Wo2[:, ci, half * 512:(half + 1) * 512], [[R("Wo2")]] * 8,
                               X[:, b, :], [R("X", b)], X[:, b, :], [R("X", b)])
        if 1 <= b <= NB:
            prenorm_block(b - 1, X[:, b - 1, :], [R("X", b - 1)])
        if b >= 2:
            prenorm_block_tr(b - 2)
    if "x2" in dbg_d:
        for b in range(NB):
            P.dma("sp", dbg_d["x2"][b * 128:(b + 1) * 128, :], X[:, b, :], reads=[R("X", b)])
    if stop <= 5:
        return finish(nc, P)
    barrier()

    gain_load(gpost, 6, R("gpost"))
    for b in range(NB):
        P.dma("sp" if b % 2 else "act", out_d[b * 128:(b + 1) * 128, :], X[:, b, :], reads=[R("X", b)], writes=[R("outd", b)])
    barrier()
    aT16, _ = a1_bf(0, [16 * N_TOK]); aT16 = aT16.rearrange("p (j t) -> p j t", t=N_TOK)
    aT = [aT16[:, j, :] for j in range(16)] + [mixT[:, j, :] for j in range(6)]
    Wup = [mixT[:, 6 + i, :].rearrange("p (c n) -> p c n", n=256) for i in range(2)]
    o2 = 0
    ugb = []; uvb = []; agb = []; avb = []; ebb = []
    for i in range(2):
        v, o2 = a2_f(o2, 2050); ugb.append(v)
        v, o2 = a2_f(o2, 2050); uvb.append(v)
        v, o2 = a2_f(o2, 1024); agb.append(v)
        v, o2 = a2_f(o2, 1024); avb.append(v)
        v, o2 = a2_f(o2, 1024); ebb.append(v)
    assert o2 <= 63488, o2
    for i in range(2):
        for ub_, nm in ((ugb[i], "ug"), (uvb[i], "uv")):
            MEMSET("pool", ub_[:, 0:1], 0.0, [R(nm, i)])
            MEMSET("pool", ub_[:, 2049:2050], 0.0, [R(nm, i)])

    def ffn_mm(j):
        s_ = j % 2
        wu = Wup[s_]
        P.dma("pool", wu[:, :, 0:128], w_up_d[:, j * 128:(j + 1) * 128].rearrange("(c p) n -> p c n", p=128),
              writes=[R("Wup", s_)])
        P.dma("pool", wu[:, :, 128:256],
              w_up_d[:, D_FF + j * 128: D_FF + (j + 1) * 128].rearrange("(c p) n -> p c n", p=128), writes=[R("Wup", s_)])
        for qt in range(4):
            qs_ = slice(qt * 512, (qt + 1) * 512)
            bg_, bgR = bank()
            bv_, bvR = bank()
            for c in range(8):
                MM(bg_, wu[:, c, 0:128], hT[:, c, qs_], c == 0, c == 7, ALLH[qt * 4:qt * 4 + 4] + [R("Wup", s_)], [bgR], inc=False)
            for c in range(8):
                MM(bv_, wu[:, c, 128:256], hT[:, c, qs_], c == 0, c == 7, ALLH[qt * 4:qt * 4 + 4] + [R("Wup", s_)], [bvR],
                   inc=(c == 7))
            ACT(ugb[s_][:, 1 + qt * 512: 1 + (qt + 1) * 512], bg_, AF.Copy, [], [bgR, R("ug", s_)])
            ACT(uvb[s_][:, 1 + qt * 512: 1 + (qt + 1) * 512], bv_, AF.Copy, [], [bvR, R("uv", s_)])

    def ffn_ew(j):
        s_ = j % 2
        ug, uv, ag, av, eb_ = ugb[s_], uvb[s_], agb[s_], avb[s_], ebb[s_]
        for hf in range(2):
            base = 1 + hf * 1024
            ts_ = slice(hf * 1024, (hf + 1) * 1024)
            cw = lambda cj, i: convT[:, cj * 4 + i: cj * 4 + i + 1]
            CT = [R("convT")]
            for (u_, uR, cj, dst, dR) in ((ug, R("ug", s_), j, ag, R("ag", s_)), (uv, R("uv", s_), 22 + j, av, R("av", s_))):
                ACT(dst, u_[:, base:base + 1024], AF.Identity, [uR] + CT, [dR], scale=cw(cj, 1), bias=cw(cj, 3))
                STT("dve", dst, u_[:, base - 1:base + 1023], cw(cj, 0), dst, ALU.mult, ALU.add, [uR] + CT, [dR])
                STT("dve", dst, u_[:, base + 1:base + 1025], cw(cj, 2), dst, ALU.mult, ALU.add, [uR] + CT, [dR])
            ACT(eb_, ag, AF.Silu, [R("ag", s_)], [R("eb", s_)])
            TT("dve", aT[j][:, ts_], eb_, av, ALU.mult, [R("eb", s_), R("av", s_)], [R("aT", j)])

    for j in range(23):
        if j < 22:
            ffn_mm(j)
        if j >= 1:
            ffn_ew(j - 1)
    barrier()
    Wd, o2 = a2_bf(0, 22 * 1024); Wd = Wd.rearrange("p (j n) -> p j n", n=1024)
    wstg = []
    for i in range(3):
        v, o2 = a2_f(o2, 1024); wstg.append(v)
    assert o2 <= 63488, o2
    nst = 0
    for j in range(22):
        if j % 2 == 0:
            P.dma("pool", Wd[:, j, :], w_down_d[j * 128:(j + 1) * 128, :], writes=[R("Wd", j)])
        else:
            si = nst % 3
            nst += 1
            P.dma("sp", wstg[si], w_down_d[j * 128:(j + 1) * 128, :], writes=[R("wstg", si)])
            CP("act", Wd[:, j, :], wstg[si], [R("wstg", si)], [R("Wd", j)])
    hTf = hT[:].rearrange("p c t -> p (c t)").bitcast(F32)
    xb_ = [hTf[:, 0:1024], hTf[:, 1024:2048]]
    ob_ = [hTf[:, 2048:3072], hTf[:, 3072:4096]]
    for b in range(NB):
        tokb = slice(b * 128, (b + 1) * 128)
        k = b % 2
        P.dma("sp", xb_[k], out_d[b * 128:(b + 1) * 128, :], reads=[R("outd", b)] + ALLH, writes=[R("xb", k)])
        proj_norm_residual(b, [aT[j][:, tokb] for j in range(22)], [R("aT", j) for j in range(22)],
                           lambda ci, half: Wd[:, ci, half * 512:(half + 1) * 512], [[R("Wd", j)] for j in range(22)],
                           xb_[k], [R("xb", k)], ob_[k], [R("ob", k)])
        P.dma("sp", out_d[b * 128:(b + 1) * 128, :], ob_[k], reads=[R("ob", k)], writes=[R("outd", b)])

    return finish(nc, P)


def finish(nc, P):
    print("COUNTS", {e: P.q[e].total for e in P.ENGS}, "dma", sum(P.dma_cnt), max(P.dma_cnt), flush=True)
    P.wait_all("sp")
    P.emit()
    P.close()
    return nc


def rope_tables():
    rows = N_TOK // 64
    r, c = np.meshgrid(np.arange(rows), np.arange(64), indexing="ij")
    inv = np.power(np.float32(10000.0), -np.arange(16, dtype=np.float32) / np.float32(16)).astype(np.float32)
    ang = np.concatenate([r.reshape(-1, 1).astype(np.float32) * inv, c.reshape(-1, 1).astype(np.float32) * inv], -1)
    cos = np.cos(ang).astype(np.float32)
    sin = np.sin(ang).astype(np.float32)
    f = lambda a: a.reshape(16, 128, 32).transpose(1, 0, 2).reshape(128, 512)
    return np.ascontiguousarray(np.concatenate([f(cos), f(sin)], 1))


def masks():
    s = np.arange(128)[:, None]
    t = np.arange(128)[None, :]
    same = (s // 64) == (t // 64)
    fwd = (same & (s <= t)).astype(np.float32)
    bwd = (same & (s >= t)).astype(np.float32)
    return np.concatenate([fwd, bwd], 1).astype(ml_dtypes.bfloat16)


def prep_inputs(inp):
    f32 = lambda a: np.ascontiguousarray(np.asarray(a, dtype=np.float32))
    shared = {
        "w_in": f32(inp["w_in"][0]), "w_out": f32(inp["w_out"][0]), "w_xq": f32(inp["w_xq"][0]),
        "w_xkv": f32(inp["w_xkv"][0]), "w_xo": f32(inp["w_xo"][0]), "w_up": f32(inp["w_up"][0]),
        "w_down": f32(inp["w_down"][0]),
        "gains": f32(np.stack([inp["pre_mix_g"][0], inp["post_mix_g"][0], inp["pre_x_g"][0], inp["mem_norm_g"][0],
                               inp["post_x_g"][0], inp["pre_ffn_g"][0], inp["post_ffn_g"][0]])),
        "qkg": f32(np.concatenate([inp["q_norm_g"][0], inp["k_norm_g"][0]])[None, :]),
        "hgo": f32(np.asarray(inp["hg_out_norm_g"][0])[None, :]),
        "lbT": f32(np.asarray(inp["hg_lb"]).reshape(2, 2, 4, 128).transpose(3, 0, 1, 2).reshape(128, 16)),
        "convT": f32(np.concatenate([np.asarray(inp["conv_w"][0]), np.asarray(inp["conv_b"][0])[None, :]], 0)
                     .reshape(4, 44, 128).transpose(2, 1, 0).reshape(128, 176)),
        "ident": np.eye(128, dtype=np.float32).astype(ml_dtypes.bfloat16),
        "rope": rope_tables(),
        "mask": masks(),
    }
    x = np.asarray(inp["x"], dtype=np.float32)
    mem = np.asarray(inp["mem"], dtype=np.float32)
    maps = []
    for i in range(8):
        d = dict(shared)
        d["x"] = np.ascontiguousarray(x[i])
        d["mem"] = np.ascontiguousarray(mem[i])
        maps.append(d)
    return maps


_NC_CACHE = {}


def kernel(**inputs):
    if "nc" not in _NC_CACHE:
        _NC_CACHE["nc"] = build()
    nc = _NC_CACHE["nc"]
    maps = prep_inputs(inputs)
    res = run_bass_kernel_spmd(nc, maps, core_ids=list(range(8)))
    return np.stack([np.asarray(r["out"], dtype=np.float32) for r in res.results], 0)
```

```python
import numpy as np
import ml_dtypes
import concourse.bass as bass
import concourse.mybir as mybir
from concourse.bass_utils import run_bass_kernel_spmd

F32 = mybir.dt.float32
BF16 = mybir.dt.bfloat16
AF = mybir.ActivationFunctionType
ALU = mybir.AluOpType
AX = mybir.AxisListType

N_TOK = 2048
D = 1024
NB = 16
EPS = 1e-6
D_FF = 2816
N_IN = 3328


class Res:
    __slots__ = ("name", "w", "r", "big")

    def __init__(self, name):
        self.name = name
        self.w = None
        self.r = {}
        self.big = False


class EngQ:
    def __init__(self, name):
        self.name = name
        self.ops = []
        self.epoch = 0
        self.cnt = 0
        self.total = 0
        self.waited = {}
        self.pending = False


class Prog:
    ENGS = ("pe", "act", "dve", "pool", "sp")
    LIMIT = 900
    NEPOCH = {"pe": 14, "act": 10, "dve": 14, "pool": 4, "sp": 1}

    def __init__(self, nc, n_dma_sems=32):
        self.nc = nc
        self.q = {e: EngQ(e) for e in self.ENGS}
        self.sems = {}
        self.n_dma_sems = n_dma_sems
        self.dma_cnt = [0] * n_dma_sems
        self.dma_pool = {"pool": list(range(0, 16)), "sp": list(range(16, 26)), "act": list(range(26, 32))}
        self.dma_rr = {"pool": 0, "sp": 0, "act": 0}
        self._ctx = []
        self.res = {}

    def R(self, *key):
        r = self.res.get(key)
        if r is None:
            r = Res(key)
            self.res[key] = r
        return r

    def open(self):
        nc = self.nc
        for e in self.ENGS:
            for ep in range(self.NEPOCH[e]):
                c = nc.semaphore("s_%s%d" % (e, ep))
                self.sems[(e, ep)] = c.__enter__()
                self._ctx.append(c)
        for i in range(self.n_dma_sems):
            c = nc.semaphore("s_dma%d" % i)
            self.sems[("dma", i)] = c.__enter__()
            self._ctx.append(c)

    def close(self):
        for c in reversed(self._ctx):
            c.__exit__(None, None, None)

    def _need(self, eng, tok, same_ok):
        if tok is None:
            return
        key, val = tok
        q = self.q[eng]
        if key[0] == "dma":
            if q.waited.get(key, 0) >= val:
                return
            q.waited[key] = val
        else:
            src, ep = key
            if src == eng and not same_ok:
                return
            if q.waited.get(src, (-1, 0)) >= (ep, val):
                return
            q.waited[src] = (ep, val)
        sem = self.sems[key]
        q.ops.append(lambda h, sem=sem, val=val: h.wait_ge(sem, val))

    def _deps(self, eng, reads, writes):
        for r in reads:
            self._need(eng, r.w, same_ok=(eng != "pe" and not r.big))
        so = (eng != "pe")
        for w in writes:
            self._need(eng, w.w, same_ok=so)
            for tok in w.r.values():
                self._need(eng, tok, same_ok=so)

    def op(self, eng, fn, reads=(), writes=(), inc=True, big=False):
        q = self.q[eng]
        if q.cnt >= self.LIMIT and not q.pending:
            q.epoch += 1
            q.cnt = 0
            assert q.epoch < self.NEPOCH[eng], "out of epochs for " + eng
        self._deps(eng, reads, writes)
        key = (eng, q.epoch)
        val = q.cnt + 1
        tok = (key, val)
        if inc:
            q.cnt = val
            q.total += 1
            sem = self.sems[key]
            q.ops.append(lambda h, fn=fn, sem=sem: fn(h).then_inc(sem, 1))
            q.pending = False
        else:
            q.ops.append(lambda h, fn=fn: fn(h))
            q.pending = True
        for r in reads:
            r.r[eng] = tok
        for w in writes:
            w.w = tok
            w.r = {}
            w.big = big
        return tok

    def dma(self, eng, out, in_, reads=(), writes=(), also_wait=(), **kw):
        pl = self.dma_pool[eng]
        i = pl[self.dma_rr[eng] % len(pl)]
        self.dma_rr[eng] += 1
        key = ("dma", i)
        if self.dma_cnt[i] > 0:
            self._need(eng, (key, 16 * self.dma_cnt[i]), same_ok=True)
        self._deps(eng, reads, list(writes) + list(also_wait))
        self.dma_cnt[i] += 1
        assert self.dma_cnt[i] < 60
        val = 16 * self.dma_cnt[i]
        sem = self.sems[key]
        q = self.q[eng]
        q.ops.append(lambda h, sem=sem, out=out, in_=in_, kw=kw:
                     h.dma_start(out=out, in_=in_, **kw).then_inc(sem, 16))
        tok = (key, val)
        for r in reads:
            r.r[key] = tok
        for w in writes:
            w.w = tok
            w.r = {}
            w.big = False
        return tok

    def wait_all(self, eng, skip=()):
        for r in self.res.values():
            if r.name[0] in skip:
                continue
            self._need(eng, r.w, same_ok=True)
            for tok in list(r.r.values()):
                self._need(eng, tok, same_ok=True)

    def emit(self):
        nc = self.nc
        for e in self.ENGS:
            assert not self.q[e].pending, "engine %s ends with non-inc'd instr" % e
        with nc.Block() as block:
            @block.tensor
            def _(h):
                for f in self.q["pe"].ops:
                    f(h)

            @block.scalar
            def _(h):
                for f in self.q["act"].ops:
                    f(h)

            @block.vector
            def _(h):
                for f in self.q["dve"].ops:
                    f(h)

            @block.gpsimd
            def _(h):
                for f in self.q["pool"].ops:
                    f(h)

            @block.sync
            def _(h):
                for f in self.q["sp"].ops:
                    f(h)


class StopBuild(Exception):
    pass


def build(stop=99, dbg=()):
    st = {}
    try:
        return _build(stop, dbg, st)
    except StopBuild:
        return finish(st["nc"], st["P"])


def _build(stop, dbg, st):
    nc = bass.Bass("TRN2", target_bir_lowering=False)
    P = Prog(nc)
    P.open()
    R = P.R
    st["nc"] = nc
    st["P"] = P

    def chk(level):
        if stop <= level:
            raise StopBuild()

    def din(name, shape, dt=F32):
        return nc.dram_tensor(name, list(shape), dt, kind="ExternalInput").ap()

    x_d = din("x", [N_TOK, D])
    mem_d = din("mem", [256, D])
    w_in_d = din("w_in", [D, N_IN])
    w_out_d = din("w_out", [D, D])
    w_xq_d = din("w_xq", [D, D])
    w_xkv_d = din("w_xkv", [D, 2 * D])
    w_xo_d = din("w_xo", [D, D])
    w_up_d = din("w_up", [D, 2 * D_FF])
    w_down_d = din("w_down", [D_FF, D])
    gains_d = din("gains", [7, D])
    qkg_d = din("qkg", [1, 128])
    hgo_d = din("hgo", [1, 128])
    lbT_d = din("lbT", [128, 16])
    convT_d = din("convT", [128, 44 * 4])
    ident_d = din("ident", [128, 128], BF16)
    rope_d = din("rope", [128, 2 * 16 * 32])
    mask_d = din("mask", [128, 2 * 128], BF16)
    out_d = nc.dram_tensor("out", [N_TOK, D], F32, kind="ExternalOutput").ap()
    dbg_d = {}
    for name, shape in dbg:
        dbg_d[name] = nc.dram_tensor("dbg_" + name, list(shape), F32, kind="ExternalOutput").ap()

    def sb(name, shape, dt):
        return nc.alloc_sbuf_tensor("sb_" + name, list(shape), dt)

    hT = sb("hT", [128, 8, N_TOK], BF16)
    mixT = sb("mixT", [128, 8, N_TOK], BF16)
    arena1 = sb("arena1", [128, 16384], F32)
    arena2 = sb("arena2", [128, 15872], F32)
    ident = sb("ident", [128, 128], BF16)
    ones_bf = sb("ones_bf", [128, 128], BF16)
    gpre = sb("gpre", [128, D], F32)
    gpost = sb("gpost", [128, D], F32)
    stat = sb("stat", [128, 64], F32)
    stat2 = sb("stat2", [128, 64], F32)
    junk = sb("junk", [128, D], BF16)
    hb = sb("hb", [128, 2, D], BF16)
    lbT = sb("lbT", [128, 16], F32)
    lbv = sb("lbv", [128, 40], F32)
    convT = sb("convT", [128, 44 * 4], F32)
    epsc = sb("epsc", [128, 1], F32)
    psf = nc.alloc_psum_tensor("psf", [128, 3, 2, 512], F32)
    psb = nc.alloc_psum_tensor("psb", [128, 2, 1024], BF16)

    X = arena1[:].rearrange("p (b d) -> p b d", d=D)

    def a1_bf(off_bytes, shape):
        n = int(np.prod(shape))
        v = arena1[:, off_bytes // 4: off_bytes // 4 + n // 2].bitcast(BF16)
        return v, off_bytes + n * 2

    def a2_bf(off_bytes, n):
        v = arena2[:, off_bytes // 4: off_bytes // 4 + n // 2].bitcast(BF16)
        return v, off_bytes + n * 2

    def a2_f(off_bytes, n):
        v = arena2[:, off_bytes // 4: off_bytes // 4 + n]
        return v, off_bytes + n * 4

    def a1_f(off_bytes, n):
        v = arena1[:, off_bytes // 4: off_bytes // 4 + n]
        return v, off_bytes + n * 4

    def MM(out, lhsT, rhs, start, stop, reads, writes, inc=True, tp=None):
        if tp is None:
            return P.op("pe", lambda h: h.matmul(out, lhsT=lhsT, rhs=rhs, start=start, stop=stop),
                        reads=reads, writes=writes, inc=inc)
        return P.op("pe", lambda h: h.matmul(out, lhsT=lhsT, rhs=rhs, start=start, stop=stop, tile_position=tp),
                    reads=reads, writes=writes, inc=inc)

    def TR(out, in_, reads, writes, inc=True):
        return P.op("pe", lambda h: h.transpose(out, in_, ident[:]), reads=reads, writes=writes, inc=inc)

    BIG_N = 1024

    def isbig(ap):
        n = 1
        for d_ in ap.shape[1:]:
            n *= d_
        return n >= BIG_N

    def ACT(out, in_, func, reads, writes, scale=None, bias=None, accum=None):
        kw = {}
        if scale is not None:
            kw["scale"] = scale
        if bias is not None:
            kw["bias"] = bias
        if accum is not None:
            kw["accum_out"] = accum
        return P.op("act", lambda h: h.activation(out=out, in_=in_, func=func, **kw), reads=reads, writes=writes,
                    big=(accum is None and isbig(out)))

    def TT(eng, out, in0, in1, op, reads, writes):
        return P.op(eng, lambda h: h.tensor_tensor(out=out, in0=in0, in1=in1, op=op), reads=reads, writes=writes, big=isbig(out))

    def TS(eng, out, in0, s1, s2, op0, op1, reads, writes):
        if s2 is None:
            return P.op(eng, lambda h: h.tensor_scalar(out=out, in0=in0, scalar1=s1, scalar2=None, op0=op0),
                        reads=reads, writes=writes, big=isbig(out))
        return P.op(eng, lambda h: h.tensor_scalar(out=out, in0=in0, scalar1=s1, scalar2=s2, op0=op0, op1=op1),
                    reads=reads, writes=writes, big=isbig(out))

    def STT(eng, out, in0, scalar, in1, op0, op1, reads, writes):
        return P.op(eng, lambda h: h.scalar_tensor_tensor(out=out, in0=in0, scalar=scalar, in1=in1, op0=op0, op1=op1),
                    reads=reads, writes=writes, big=isbig(out))

    def CP(eng, out, in_, reads, writes):
        if eng == "act":
            return ACT(out, in_, AF.Copy, reads, writes)
        return P.op(eng, lambda h: h.tensor_copy(out=out, in_=in_), reads=reads, writes=writes, big=isbig(out))

    def RECIP(out, in_, reads, writes):
        return P.op("dve", lambda h: h.reciprocal(out=out, in_=in_), reads=reads, writes=writes)

    def MEMSET(eng, ap, val, writes):
        return P.op(eng, lambda h: h.memset(ap, val), writes=writes)

    def barrier(skip=()):
        for e in Prog.ENGS:
            P.wait_all(e, skip)

    def load_w(dst, src_rows_cols, reads_w, also_wait=()):
        src = src_rows_cols.rearrange("(c p) n -> p c n", p=128)
        nch = src.shape[1]
        for c in range(nch):
            P.dma("pool", dst[:, c, :], src[:, c, :], writes=reads_w, also_wait=also_wait)

    def gain_load(dst, row, res):
        P.dma("sp", dst[:], gains_d[row:row + 1, :].partition_broadcast(128), writes=[res])

    def dump(name, ap_sb, reads, view=None):
        if name in dbg_d:
            P.dma("sp", dbg_d[name] if view is None else view(dbg_d[name]), ap_sb, reads=reads)

    P.dma("sp", ident[:], ident_d, writes=[R("ident")])
    P.dma("sp", lbT[:], lbT_d, writes=[R("lbT")])
    P.dma("sp", convT[:], convT_d, writes=[R("convT")])
    MEMSET("pool", ones_bf[:], 1.0, [R("ones")])
    MEMSET("pool", epsc[:], EPS, [R("epsc")])
    MEMSET("pool", stat[:], 0.0, [R("stat")])
    MEMSET("pool", stat2[:], 0.0, [R("stat2")])

    def prenorm_T(grow, nblk=NB, src=None, dstT=None, tag="h", srcR=None):
        src = X if src is None else src
        dstT = hT if dstT is None else dstT
        srcR = (lambda b: R("X", b)) if srcR is None else srcR
        gain_load(gpre, grow, R("gpre"))
        for b in range(nblk):
            ACT(junk[:], src[:, b, :], AF.Square, [srcR(b)], [R("junk"), R("stat")], accum=stat[:, b:b + 1])
        TS("dve", stat2[:, 0:nblk], stat[:, 0:nblk], 1.0 / D, EPS, ALU.mult, ALU.add, [R("stat")], [R("stat2")])
        ACT(stat2[:, 0:nblk], stat2[:, 0:nblk], AF.Ln, [R("stat2")], [R("stat2")])
        ACT(stat2[:, 0:nblk], stat2[:, 0:nblk], AF.Exp, [R("stat2")], [R("stat2")], scale=-0.5)
        for b in range(nblk):
            k = b % 2
            STT("dve", hb[:, k, :], src[:, b, :], stat2[:, b:b + 1], gpre[:], ALU.mult, ALU.mult,
                [srcR(b), R("stat2"), R("gpre")], [R("hb", k)])
            for c in range(8):
                TR(psb[:, k, c * 128:(c + 1) * 128], hb[:, k, c * 128:(c + 1) * 128],
                   [R("hb", k), R("ident")], [R("psb", k)], inc=(c == 7))
            CP("act", dstT[:, :, b * 128:(b + 1) * 128], psb[:, k, :].rearrange("p (c t) -> p c t", t=128),
               [], [R("psb", k), R(tag, b)])

    o2 = 0
    Watt, o2 = a2_bf(o2, 8 * 768); Watt = Watt.rearrange("p (c n) -> p c n", n=768)
    rope_raw, o2 = a2_f(o2, 1024)
    qkgB, o2 = a2_f(o2, 128)
    o2_attw = o2
    load_w(Watt, w_in_d[:, 0:768], [R("Watt")])
    P.dma("sp", rope_raw, rope_d, writes=[R("rope_raw")])
    P.dma("sp", qkgB, qkg_d.partition_broadcast(128), writes=[R("qkgB")])
    for b in range(NB):
        P.dma("sp" if b % 2 else "act", X[:, b, :], x_d[b * 128:(b + 1) * 128, :], writes=[R("X", b)])
    prenorm_T(0)
    if "hT" in dbg_d:
        tmpf = arena2[:, 13000:13000 + 2048]
        for c in range(8):
            CP("dve", tmpf, hT[:, c, :], [R("h", b) for b in range(NB)], [R("dbgtmp")])
            P.dma("sp", dbg_d["hT"][c * 128:(c + 1) * 128, :], tmpf, reads=[R("dbgtmp")])
    if stop <= 1:
        return finish(nc, P)
    barrier()
    if stop <= 1.1:
        return finish(nc, P)


    o1 = 0
    ropeT = []
    for i in range(8):
        v, o1 = a1_f(o1, 512)
        ropeT.append(v.rearrange("p (b i) -> p b i", i=32))
    QT, o1 = a1_bf(o1, [4 * N_TOK]); QT = QT.rearrange("p (j t) -> p j t", t=N_TOK)
    KTd, o1 = a1_bf(o1, [2 * N_TOK]); KTd = KTd.rearrange("p (g t) -> p g t", t=N_TOK)
    VA, o1 = a1_bf(o1, [NB * 2 * 192]); VA = VA.rearrange("p (b g d) -> p b g d", g=2, d=192)
    o2 = o2_attw
    qkv = []; sqb = []; tq = []; qn = []; qr = []
    for i in range(2):
        v, o2 = a2_f(o2, 768); qkv.append(v)
        v, o2 = a2_f(o2, 640); sqb.append(v)
        v, o2 = a2_f(o2, 4 * 320); tq.append(v.rearrange("p (a n) -> p a n", n=320))
        v, o2 = a2_f(o2, 640); qn.append(v)
        v, o2 = a2_bf(o2, 768); qr.append(v)
    PT = []
    for i in range(3):
        v, o2 = a2_bf(o2, 1024); PT.append(v.rearrange("p (u n) -> p u n", n=512))
    rc = []
    for i in range(2):
        v, o2 = a2_f(o2, 512); rc.append(v)
    assert o1 <= 65536 and o2 <= 63488, (o1, o2)

    if stop <= 1.16:
        return finish(nc, P)
    MEMSET("pool", VA, 1.0, [R("VA", b) for b in range(NB)])
    if stop <= 1.17:
        return finish(nc, P)
    cosv = rope_raw[:, 0:512].rearrange("p (b i) -> p b i", i=32)
    sinv = rope_raw[:, 512:1024].rearrange("p (b i) -> p b i", i=32)
    for qk in range(2):
        gv = qkgB[:, qk * 64:(qk + 1) * 64].rearrange("p (i two) -> p i two", two=2)
        ge = gv[:, :, 0].unsqueeze(1).broadcast_to([128, NB, 32])
        go = gv[:, :, 1].unsqueeze(1).broadcast_to([128, NB, 32])
        for ti, (tab, gg) in enumerate(((cosv, ge), (sinv, go), (sinv, ge), (cosv, go))):
            TT("dve", ropeT[qk * 4 + ti], tab, gg, ALU.mult, [R("rope_raw"), R("qkgB")], [R("ropeT")])

    def att_proj(b):
        k = b % 2
        tokb = slice(b * 128, (b + 1) * 128)
        for c in range(8):
            MM(psf[:, k, 0, :], hT[:, c, tokb], Watt[:, c, 0:512], c == 0, c == 7,
               [R("h", b), R("Watt")], [R("psf", k, 0)], inc=False)
        for c in range(8):
            MM(psf[:, k, 1, 0:256], hT[:, c, tokb], Watt[:, c, 512:768], c == 0, c == 7,
               [R("h", b), R("Watt")], [R("psf", k, 1)], inc=(c == 7))
        ACT(qkv[k][:, 0:512], psf[:, k, 0, :], AF.Copy, [], [R("psf", k, 0), R("qkv", k)])
        ACT(qkv[k][:, 512:768], psf[:, k, 1, 0:256], AF.Copy, [], [R("psf", k, 1), R("qkv", k)])
        ACT(sqb[k], qkv[k][:, 0:640], AF.Square, [R("qkv", k)], [R("sqb", k)])
        ss = stat[:, 16 + 10 * k: 26 + 10 * k]
        rs = stat2[:, 16 + 10 * k: 26 + 10 * k]
        P.op("dve", lambda h: h.tensor_reduce(out=ss, in_=sqb[k].rearrange("p (h d) -> p h d", d=64),
                                              axis=AX.X, op=ALU.add),
             reads=[R("sqb", k)], writes=[R("ss", k)])
        TS("dve", rs, ss, 1.0 / 64, EPS, ALU.mult, ALU.add, [R("ss", k)], [R("rs", k)])
        ACT(rs, rs, AF.Ln, [R("rs", k)], [R("rs", k)])
        ACT(rs, rs, AF.Exp, [R("rs", k)], [R("rs", k)], scale=-0.5)
        for qk, eng, nh, c0 in ((0, "dve", 8, 0), (1, "dve", 2, 512)):
            src = qkv[k][:, c0:c0 + nh * 64].rearrange("p (h i two) -> p h i two", i=32, two=2)
            xe = src[:, :, :, 0]
            xo = src[:, :, :, 1]
            tabs = [ropeT[qk * 4 + ti][:, b, :].unsqueeze(1).broadcast_to([128, nh, 32]) for ti in range(4)]
            tt = [tq[k][:, a, 0:nh * 32].rearrange("p (h i) -> p h i", i=32) for a in range(4)]
            rd = [R("qkv", k), R("ropeT")]
            wr = [R("tq", k, qk)]
            TT(eng, tt[0], xe, tabs[0], ALU.mult, rd, wr)
            TT(eng, tt[1], xo, tabs[1], ALU.mult, rd, wr)
            TT(eng, tt[2], xe, tabs[2], ALU.mult, rd, wr)
            TT(eng, tt[3], xo, tabs[3], ALU.mult, rd, wr)
            dst = qn[k][:, qk * 512: qk * 512 + nh * 64].rearrange("p (h i two) -> p h i two", i=32, two=2)
            TT(eng, dst[:, :, :, 0], tt[0], tt[1], ALU.subtract, wr, [R("qn", k, qk)])
            TT(eng, dst[:, :, :, 1], tt[2], tt[3], ALU.add, wr, [R("qn", k, qk)])
            rsb = rs[:, qk * 8: qk * 8 + nh]
            if qk == 0:
                TT(eng, qr[k][:, 0:512].rearrange("p (h d) -> p h d", d=64),
                   qn[k][:, 0:512].rearrange("p (h d) -> p h d", d=64),
                   rsb.unsqueeze(2).broadcast_to([128, 8, 64]), ALU.mult,
                   [R("qn", k, 0), R("rs", k)], [R("qr", k, 0)])
            else:
                for dup in range(2):
                    TT(eng, qr[k][:, 512:768].rearrange("p (g u d) -> p g u d", u=2, d=64)[:, :, dup, :],
                       qn[k][:, 512:640].rearrange("p (g d) -> p g d", d=64),
                       rsb.unsqueeze(2).broadcast_to([128, 2, 64]), ALU.mult,
                       [R("qn", k, 1), R("rs", k)], [R("qr", k, 1)])
        CP("dve", VA[:, b, :, 64:128], qkv[k][:, 640:768].rearrange("p (g d) -> p g d", d=64),
           [R("qkv", k)], [R("VA", b)])
    def att_proj2(b):
        k = b % 2
        tokb = slice(b * 128, (b + 1) * 128)
        for j in range(6):
            TR(psb[:, k, j * 128:(j + 1) * 128], qr[k][:, j * 128:(j + 1) * 128],
               [R("qr", k, 0), R("qr", k, 1), R("ident")], [R("psb", k)], inc=(j == 5))
        CP("act", QT[:, :, tokb], psb[:, k, 0:512].rearrange("p (j t) -> p j t", t=128), [], [R("psb", k), R("QT", b)])
        CP("dve", KTd[:, :, tokb], psb[:, k, 512:768].rearrange("p (g t) -> p g t", t=128), [], [R("psb", k), R("KT", b)])

    if stop <= 1.2:
        return finish(nc, P)
    for b in range(NB + 1):
        if b < NB:
            att_proj(b)
        if b >= 1:
            att_proj2(b - 1)
    if "QT" in dbg_d:
        tmpf = arena2[:, 14000:14000 + 1024]
        for c in range(4):
            for hf in range(2):
                CP("dve", tmpf, QT[:, c, hf * 1024:(hf + 1) * 1024], [R("QT", b) for b in range(NB)], [R("dbgtmp")])
                P.dma("sp", dbg_d["QT"][c * 128:(c + 1) * 128, hf * 1024:(hf + 1) * 1024], tmpf, reads=[R("dbgtmp")])
    if stop <= 1.5:
        return finish(nc, P)

    Wh, _ = a2_bf(0, 8 * 640); Wh = Wh.rearrange("p (c n) -> p c n", n=640)
    for gi, c0 in enumerate((768, 1280, 1792, 2304, 2816)):
        src = w_in_d[:, c0: c0 + 128].rearrange("(c p) n -> p c n", p=128)
        P.dma("pool", Wh[:, :, gi * 128:(gi + 1) * 128], src, writes=[R("Wh")], also_wait=[R("Watt")])
    PREFETCHED_WH0 = True
    pt_i = [0]
    psbf = [psb[:, 0, :].bitcast(F32), psb[:, 1, :].bitcast(F32)]
    unit_i = [0]

    def st_mm_unit(m, qt, kb):
        g = m // 2
        qs = slice(qt * 512, (qt + 1) * 512)
        qreads = [R("QT", qt * 4 + i) for i in range(4)]
        k = kb % 2
        for par in range(2):
            rows = slice(par * 64, par * 64 + 64)
            MM(psf[:, k, par, :], KTd[rows, g, kb * 128:(kb + 1) * 128], QT[rows, m, qs], True, True,
               qreads + [R("KT", kb)], [R("psf", k, par)], inc=(par == 1))

    UNITS = [(m_, qt_) for m_ in range(4) for qt_ in range(4)]

    def att_pair_tile(m, qt):
        g = m // 2
        j = m
        qs = slice(qt * 512, (qt + 1) * 512)
        uidx = UNITS.index((m, qt))
        ui = unit_i[0] % 2
        unit_i[0] += 1
        if ui == 0:
            accs = [(psf[:, 2, 0, :], R("psf", 2, 0)), (psf[:, 2, 1, :], R("psf", 2, 1))]
        else:
            accs = [(psbf[0], R("psb", 0)), (psbf[1], R("psb", 1))]
        qreads = [R("QT", qt * 4 + i) for i in range(4)]

        def st_mm(kb):
            k = kb % 2
            for par in range(2):
                rows = slice(par * 64, par * 64 + 64)
                MM(psf[:, k, par, :], KTd[rows, g, kb * 128:(kb + 1) * 128], QT[rows, j, qs], True, True,
                   qreads + [R("KT", kb)], [R("psf", k, par)], inc=(par == 1))

        if uidx == 0:
            st_mm(0)
        for kb in range(16):
            if kb + 1 < 16:
                st_mm(kb + 1)
            elif uidx + 1 < len(UNITS):
                st_mm_unit(UNITS[uidx + 1][0], UNITS[uidx + 1][1], 0)
            pi = pt_i[0] % 3
            pt_i[0] += 1
            k = kb % 2
            ACT(PT[pi], psf[:, k, :, :], AF.Exp, [], [R("psf", k, 0), R("psf", k, 1), R("PT", pi)], scale=0.125)
            for par in range(2):
                vsl = slice(64, 192) if par == 0 else slice(0, 128)
                MM(accs[par][0], VA[:, kb, g, vsl], PT[pi][:, par, :], kb == 0, kb == 15,
                   [R("VA", kb), R("PT", pi)], [accs[par][1]], inc=(par == 1))
        rcb = rc[ui]
        for par in range(2):
            rows = slice(par * 64, par * 64 + 64)
            orow = slice((1 - par) * 64, (1 - par) * 64 + 64)
            acc, accR = accs[par]
            P.op("dve", lambda hh, acc=acc, rows=rows, orow=orow: hh.reciprocal(out=rcb[rows, :], in_=acc[orow, :]),
                 reads=[], writes=[accR, R("rc", ui, par)])
            TT("dve", mixT[rows, j, qs], acc[rows, :], rcb[rows, :], ALU.mult, [R("rc", ui, par)], [accR, R("mixT", j, qt)])

    for m in range(4):
        for qt in range(4):
            att_pair_tile(m, qt)

    if "att" in dbg_d:
        tmpf = arena2[:, 0:2048]
        for c in range(4):
            CP("dve", tmpf, mixT[:, c, :], [R("mixT", c, qt) for qt in range(4)], [R("dbgtmp")])
            P.dma("sp", dbg_d["att"][c * 128:(c + 1) * 128, :], tmpf, reads=[R("dbgtmp")])
    if stop <= 2:
        return finish(nc, P)
    barrier(skip=("Wh",))

    bank_i = [0]

    def bank():
        i = bank_i[0] % 6
        bank_i[0] += 1
        return psf[:, i // 2, i % 2, :], R("psf", i // 2, i % 2)

    o1 = 0
    fb = {}
    for nm in ("qs", "SG", "E", "LF", "B", "KK", "rmask", "gate"):
        fb[nm], o1 = a1_f(o1, 2048)
    Tbig = arena1[:, 2 * 2048: 4 * 2048].rearrange("p (c v) -> p c v", v=128)
    o2 = 0
    Wh, o2 = a2_bf(o2, 8 * 640); Wh = Wh.rearrange("p (c n) -> p c n", n=640)
    Of, o2 = a2_f(o2, 2048); Of3 = Of.rearrange("p (b v) -> p b v", v=128)
    qtl, o2 = a2_bf(o2, 2048)
    ktl, o2 = a2_bf(o2, 2048)
    ktok, o2 = a2_bf(o2, 2048); ktok = ktok.rearrange("p (b v) -> p b v", v=128)
    vtok, o2 = a2_bf(o2, 2048); vtok = vtok.rearrange("p (b v) -> p b v", v=128)
    Sbf, o2 = a2_bf(o2, 32 * 128); Sbf = Sbf.rearrange("p (c v) -> p c v", v=128)
    AT, o2 = a2_bf(o2, 2048); AT = AT.rearrange("p (b v) -> p b v", v=128)
    recb, o2 = a2_bf(o2, 2048); recb = recb.rearrange("p (b v) -> p b v", v=128)
    hgoB, o2 = a2_f(o2, 128)
    maskS, o2 = a2_bf(o2, 256); maskS = maskS.rearrange("p (d t) -> p d t", t=128)
    gtmp, o2 = a2_f(o2, 256)
    assert o1 <= 65536 and o2 <= 63488, (o1, o2)
    gate3 = fb["gate"].rearrange("p (b v) -> p b v", v=128)

    P.dma("sp", hgoB, hgo_d.partition_broadcast(128), writes=[R("hgoB")])
    P.dma("sp", maskS, mask_d.rearrange("p (d t) -> p d t", t=128), writes=[R("maskS")])
    MEMSET("pool", fb["rmask"], 1.0, [R("rmask")])
    MEMSET("pool", fb["rmask"].rearrange("p (c t) -> p c t", t=64)[:, :, 0:1], 0.0, [R("rmask")])
    lv = lbT[:].rearrange("p (d l h) -> p d l h", l=2, h=4)
    TT("dve", lbv[:, 0:8].rearrange("p (d h) -> p d h", h=4), lv[:, :, 0, :], lv[:, :, 1, :], ALU.subtract,
       [R("lbT")], [R("lbv")])
    ACT(lbv[:, 0:8], lbv[:, 0:8], AF.Exp, [R("lbv")], [R("lbv")], scale=-1.0)
    TS("dve", lbv[:, 0:8], lbv[:, 0:8], 1.0, None, ALU.add, None, [R("lbv")], [R("lbv")])
    RECIP(lbv[:, 0:8], lbv[:, 0:8], [R("lbv")], [R("lbv")])
    TS("dve", lbv[:, 8:16], lbv[:, 0:8], -1.0, 1.0, ALU.mult, ALU.add, [R("lbv")], [R("lbv")])
    TS("dve", lbv[:, 16:24], lbv[:, 0:8], -0.5, 0.5, ALU.mult, ALU.add, [R("lbv")], [R("lbv")])
    TS("dve", lbv[:, 24:32], lbv[:, 0:8], 0.5, 0.5, ALU.mult, ALU.add, [R("lbv")], [R("lbv")])
    TS("dve", lbv[:, 32:40], lbv[:, 0:8], 0.5, -0.5, ALU.mult, ALU.add, [R("lbv")], [R("lbv")])

    ALLH = [R("h", b) for b in range(NB)]
    chk(2.1)

    def load_Wh(hh, extra=()):
        for gi, c0 in enumerate((768, 1280, 1792, 2304, 2816)):
            src = w_in_d[:, c0 + hh * 128: c0 + (hh + 1) * 128].rearrange("(c p) n -> p c n", p=128)
            P.dma("pool", Wh[:, :, gi * 128:(gi + 1) * 128], src, writes=[R("Wh")] + list(extra))

    def hgrn_head(hh):
        for bp in range(8):
            bk, bkR = bank()
            for u in range(2):
                b = 2 * bp + u
                for c in range(8):
                    MM(bk[:, u * 256:(u + 1) * 256], hT[:, c, b * 128:(b + 1) * 128], Wh[:, c, 384:640], c == 0, c == 7,
                       [R("h", b), R("Wh")], [bkR], inc=(u == 1 and c == 7))
            bv = bk.rearrange("p (u n) -> p u n", n=256)
            ACT(vtok[:, 2 * bp:2 * bp + 2, :], bv[:, :, 0:128], AF.Copy, [], [bkR, R("vtok")])
            g3 = gtmp.rearrange("p (u n) -> p u n", n=128)
            ACT(gate3[:, 2 * bp:2 * bp + 2, :], bv[:, :, 128:256], AF.Silu, [], [bkR, R("gate")])

        chk(2.2)

        def fm_proj(gi, dst, dstname):
            kept = []
            for qt in range(4):
                bk, bkR = bank()
                for c in range(8):
                    MM(bk, Wh[:, c, gi * 128:(gi + 1) * 128], hT[:, c, qt * 512:(qt + 1) * 512], c == 0, c == 7,
                       ALLH[qt * 4:qt * 4 + 4] + [R("Wh")], [bkR], inc=(c == 7))
                ACT(dst[:, qt * 512:(qt + 1) * 512], bk, AF.Tanh, [], [bkR, R(dstname, qt // 2)], scale=0.5)
                kept.append((bk, bkR))
            return kept

        for qt in range(4):
            bk, bkR = bank()
            for c in range(8):
                MM(bk, Wh[:, c, 0:128], hT[:, c, qt * 512:(qt + 1) * 512], c == 0, c == 7,
                   ALLH[qt * 4:qt * 4 + 4] + [R("Wh")], [bkR], inc=(c == 7))
            sl = slice(qt * 512, (qt + 1) * 512)
            ACT(fb["qs"][:, sl], bk, AF.Silu, [], [bkR, R("qs", qt // 2)])

        chk(2.3)
        for dr in range(2):
            li = dr * 4 + hh
            ha_ap = lbv[:, 16 + li: 17 + li]
            hb_ap = lbv[:, 24 + li: 25 + li]
            nha_ap = lbv[:, 32 + li: 33 + li]
            E, SG, LF, B, KK = fb["E"], fb["SG"], fb["LF"], fb["B"], fb["KK"]
            EB, ENB = SG, LF
            H2 = (slice(0, 1024), slice(1024, 2048))
            if dr == 0:
                fm_proj(1, E, "E")
                TH, THn, CUM, CUMn = E, "E", B, "B"
            else:
                TH, THn, CUM, CUMn = B, "B", E, "E"
            for hf in range(2):
                sl = H2[hf]
                ACT(LF[:, sl], TH[:, sl], AF.Ln, [R(THn, hf), R("lbv")], [R("LF", hf)], scale=ha_ap, bias=hb_ap)
            for hf in range(2):
                sl = H2[hf]
                TS("dve", KK[:, sl], TH[:, sl], nha_ap, ha_ap, ALU.mult, ALU.add, [R(THn, hf), R("lbv")], [R("KK", hf)])
                P.op("dve", lambda h, sl=sl, CUM=CUM: h.tensor_tensor_scan(out=CUM[:, sl], data0=fb["rmask"][:, sl],
                                                                           data1=LF[:, sl], initial=0.0, op0=ALU.mult, op1=ALU.add),
                     reads=[R("rmask"), R("LF", hf)], writes=[R(CUMn, hf)])
                if dr == 1:
                    TT("dve", LF[:, sl], LF[:, sl], CUM[:, sl], ALU.subtract, [R("LF", hf), R(CUMn, hf)], [R("LF", hf)])
                    C3 = CUM[:, sl].rearrange("p (c t) -> p c t", t=64)
                    TT("dve", B[:, sl].rearrange("p (c t) -> p c t", t=64), LF[:, sl].rearrange("p (c t) -> p c t", t=64),
                       C3[:, :, 63:64].broadcast_to([128, 16, 64]), ALU.add, [R("LF", hf), R(CUMn, hf)], [R("B", hf)])
            Bs, Bn = B, "B"
            for hf in range(2):
                sl = H2[hf]
                ACT(EB[:, sl], Bs[:, sl], AF.Exp, [R(Bn, hf)], [R("SG", hf)])
                ACT(ENB[:, sl], Bs[:, sl], AF.Exp, [R(Bn, hf)], [R("LF", hf)], scale=-1.0)
            for hf in range(2):
                sl = H2[hf]
                TT("dve", qtl[:, sl], fb["qs"][:, sl], EB[:, sl], ALU.mult, [R("qs", hf), R("SG", hf)], [R("qtl", hf)])
                TT("dve", ktl[:, sl], KK[:, sl], ENB[:, sl], ALU.mult, [R("KK", hf), R("LF", hf)], [R("ktl", hf)])
            chk(2.4)
            for half in range(2):
                for i in range(8):
                    b = half * 8 + i
                    TR(psb[:, half, i * 128:(i + 1) * 128], ktl[:, b * 128:(b + 1) * 128], [R("ktl", half), R("ident")],
                       [R("psb", half)], inc=(i == 7))
                CP("act" if half else "dve", ktok[:, half * 8:(half + 1) * 8, :],
                   psb[:, half, :].rearrange("p (b v) -> p b v", v=128), [], [R("psb", half), R("ktok")])
            for bg in range(4):
                bk, bkR = bank()
                for u in range(4):
                    b = 4 * bg + u
                    MM(bk[:, u * 128:(u + 1) * 128], ktl[:, b * 128:(b + 1) * 128], qtl[:, b * 128:(b + 1) * 128], True, True,
                       [R("ktl", bg // 2), R("qtl", bg // 2)], [bkR], inc=(u == 3))
                TT("dve", AT[:, 4 * bg:4 * bg + 4, :], bk.rearrange("p (u t) -> p u t", t=128),
                   maskS[:, dr, :].unsqueeze(1).broadcast_to([128, 4, 128]), ALU.mult, [R("maskS")], [bkR, R("AT")])
            chk(2.5)
            MEMSET("pool", Sbf[:, 0 if dr == 0 else 31, :], 0.0, [R("Sbf")])
            DEAD = [R(nm, hf) for nm in ("E", "LF") for hf in range(2)]
            SGR = [R("SG", 0), R("SG", 1)]
            n = 0
            cprev = None
            for bgi in range(4):
                bg = bgi if dr == 0 else 3 - bgi
                ub = [bank() for _ in range(2)]
                for u in range(4):
                    b = 4 * bg + u
                    for j in range(2):
                        MM(ub[j][0][:, u * 128:(u + 1) * 128], ktok[64 * j:64 * j + 64, b, :], vtok[64 * j:64 * j + 64, b, :],
                           True, True, [R("ktok"), R("vtok")], [ub[j][1]], inc=(u == 3 and j == 1), tp=(64 * j, 0))
                order = [(u, j) for u in range(4) for j in range(2)]
                if dr == 1:
                    order = order[::-1]
                for (u, j) in order:
                    c = 2 * (4 * bg + u) + j
                    Uc = ub[j][0][:, u * 128:(u + 1) * 128]
                    if n == 0:
                        CP("dve", Tbig[:, c, :], Uc, [], [ub[j][1], R("Tb", c)] + DEAD)
                    else:
                        dprev = EB[:, 64 * cprev + 63: 64 * cprev + 64] if dr == 0 else EB[:, 64 * cprev: 64 * cprev + 1]
                        STT("dve", Tbig[:, c, :], Tbig[:, cprev, :], dprev, Uc, ALU.mult, ALU.add,
                            [R("Tb", cprev)] + SGR, [ub[j][1], R("Tb", c)])
                    cprev = c
                    n += 1
            EB3 = EB.rearrange("p (c t) -> p c t", t=64)
            if dr == 0:
                TT("dve", Sbf[:, 1:32, :], Tbig[:, 0:31, :], EB3[:, 0:31, 63:64].broadcast_to([128, 31, 128]), ALU.mult,
                   [R("Tb", cprev)] + SGR + DEAD, [R("Sbf")])
            else:
                TT("dve", Sbf[:, 0:31, :], Tbig[:, 1:32, :], EB3[:, 1:32, 0:1].broadcast_to([128, 31, 128]), ALU.mult,
                   [R("Tb", cprev)] + SGR + DEAD, [R("Sbf")])
            if dr == 0:
                fm_proj(2, B, "B")
                if hh < 3:
                    load_Wh(hh + 1)
                else:
                    load_w(hT, w_xkv_d, [R("Wkv")], also_wait=ALLH)
            chk(2.6)
            for bg in range(4):
                bk, bkR = bank()
                for u in range(4):
                    b = 4 * bg + u
                    us = slice(u * 128, (u + 1) * 128)
                    MM(bk[:, us], AT[:, b, :], vtok[:, b, :], True, False, [R("AT"), R("vtok")], [bkR], inc=False)
                    for j in range(2):
                        c = 2 * b + j
                        MM(bk[64 * j:64 * j + 64, us], qtl[:, b * 128 + 64 * j: b * 128 + 64 * j + 64], Sbf[:, c, :],
                           False, j == 1, [R("qtl", bg // 2), R("Sbf")], [bkR], inc=(u == 3 and j == 1), tp=(0, 64 * j))
                o3 = Of3[:, 4 * bg:4 * bg + 4, :]
                if dr == 0:
                    ACT(o3, bk.rearrange("p (u v) -> p u v", v=128), AF.Copy, [], [bkR, R("Of")])
                else:
                    TT("dve", o3, bk.rearrange("p (u v) -> p u v", v=128), o3, ALU.add, [], [bkR, R("Of")])
        chk(2.7)
        ACT(fb["E"], Of, AF.Square, [R("Of")], [R("E", 0), R("E", 1)])
        ss = stat[:, 40:56]
        rs = stat2[:, 40:56]
        P.op("dve", lambda h: h.tensor_reduce(out=ss, in_=fb["E"].rearrange("p (b v) -> p b v", v=128), axis=AX.X, op=ALU.add),
             reads=[R("E", 0), R("E", 1)], writes=[R("ssh")])
        TS("dve", rs, ss, 1.0 / 128, EPS, ALU.mult, ALU.add, [R("ssh")], [R("rsh")])
        ACT(rs, rs, AF.Ln, [R("rsh")], [R("rsh")])
        ACT(rs, rs, AF.Exp, [R("rsh")], [R("rsh")], scale=-0.5)
        SG3 = fb["SG"].rearrange("p (b v) -> p b v", v=128)
        TT("dve", SG3, Of3, rs.unsqueeze(2).broadcast_to([128, NB, 128]), ALU.mult, [R("Of"), R("rsh")], [R("SG", 0), R("SG", 1)])
        TT("dve", SG3, SG3, hgoB.unsqueeze(1).broadcast_to([128, NB, 128]), ALU.mult, [R("SG", 0), R("SG", 1), R("hgoB")],
           [R("SG", 0), R("SG", 1)])
        TT("dve", recb, SG3, gate3, ALU.mult, [R("SG", 0), R("SG", 1), R("gate")], [R("recb")])
        for half in range(2):
            for i in range(8):
                b = half * 8 + i
                TR(psb[:, half, i * 128:(i + 1) * 128], recb[:, b, :], [R("recb"), R("ident")], [R("psb", half)], inc=(i == 7))
            CP("act" if half else "dve", mixT[:, 4 + hh, half * 1024:(half + 1) * 1024], psb[:, half, :], [],
               [R("psb", half), R("mixT", 4 + hh, 2 * half), R("mixT", 4 + hh, 2 * half + 1)])

    if not PREFETCHED_WH0:
        load_Wh(0)
    for hh in range(4):
        hgrn_head(hh)
        chk(2.8 + 0.01 * hh)

    if "rec" in dbg_d:
        tmpf = fb["B"]
        for c in range(4):
            CP("dve", tmpf, mixT[:, 4 + c, :], [R("mixT", 4 + c, qt) for qt in range(4)], [R("dbgtmp")])
            P.dma("sp", dbg_d["rec"][c * 128:(c + 1) * 128, :], tmpf, reads=[R("dbgtmp")])
    if stop <= 3:
        return finish(nc, P)
    barrier(skip=("Wkv",))

    junk3 = junk[:].rearrange("p (u n) -> p u n", n=512)
    gpost3 = gpost[:].rearrange("p (u n) -> p u n", n=512)

    def proj_norm_residual(b, lhs_aps, lhs_reads, w_fn, w_reads, resid_ap, resid_reads, out_ap, out_writes):
        k = b % 3
        n = len(lhs_aps)
        for half in range(2):
            for ci in range(n):
                MM(psf[:, k, half, :], lhs_aps[ci], w_fn(ci, half), ci == 0, ci == n - 1, lhs_reads + w_reads[ci],
                   [R("psf", k, half)], inc=(half == 1 and ci == n - 1))
        ss = stat[:, 16 + b:17 + b]
        rs = stat2[:, 16 + b:17 + b]
        PB = [R("psf", k, 0), R("psf", k, 1)]
        ACT(junk3, psf[:, k, :, :], AF.Square, [], PB + [R("junk"), R("pss", b)], accum=ss)
        ACT(rs, ss, AF.Ln, [R("pss", b), R("epsc")], [R("prs", b)], scale=1.0 / D, bias=epsc[:])
        ACT(rs, rs, AF.Exp, [R("prs", b)], [R("prs", b)], scale=-0.5)
        STT("dve", psf[:, k, :, :], psf[:, k, :, :], rs, gpost3, ALU.mult, ALU.mult, [R("prs", b), R("gpost")], PB)
        TT("dve", out_ap.rearrange("p (u n) -> p u n", n=512), resid_ap.rearrange("p (u n) -> p u n", n=512), psf[:, k, :, :],
           ALU.add, resid_reads, PB + out_writes)

    def prenorm_block(b, src_ap, src_reads):
        k = b % 2
        ss = stat[:, 32 + b:33 + b]
        rs = stat2[:, 32 + b:33 + b]
        ACT(junk[:], src_ap, AF.Square, src_reads, [R("junk"), R("fss", b)], accum=ss)
        ACT(rs, ss, AF.Ln, [R("fss", b), R("epsc")], [R("frs", b)], scale=1.0 / D, bias=epsc[:])
        ACT(rs, rs, AF.Exp, [R("frs", b)], [R("frs", b)], scale=-0.5)
        STT("dve", hb[:, k, :], src_ap, rs, gpre[:], ALU.mult, ALU.mult, src_reads + [R("frs", b), R("gpre")], [R("hb", k)])

    def prenorm_block_tr(b):
        k = b % 2
        for c in range(8):
            TR(psb[:, k, c * 128:(c + 1) * 128], hb[:, k, c * 128:(c + 1) * 128], [R("hb", k), R("ident")], [R("psb", k)],
               inc=(c == 7))
        CP("act", hT[:, :, b * 128:(b + 1) * 128], psb[:, k, :].rearrange("p (c t) -> p c t", t=128), [], [R("psb", k), R("h", b)])

    o2 = 0
    Wo, o2 = a2_bf(o2, 8 * 1024); Wo = Wo.rearrange("p (c n) -> p c n", n=1024)
    load_w(Wo, w_out_d, [R("Wo")])
    gain_load(gpost, 1, R("gpost"))
    Wkv = hT
    Wq, _ = a2_bf(16384, 8 * 1024); Wq = Wq.rearrange("p (c n) -> p c n", n=1024)
    Wo2, _ = a2_bf(32768, 8 * 1024); Wo2 = Wo2.rearrange("p (c n) -> p c n", n=1024)
    load_w(Wq, w_xq_d, [R("Wq")])
    load_w(Wo2, w_xo_d, [R("Wo2")])
    for b in range(14):
        P.dma("sp" if b % 2 else "act", X[:, b, :], x_d[b * 128:(b + 1) * 128, :], writes=[R("X", b)])
    o2 = 49152
    memT, o2 = a2_bf(o2, 8 * 256); memT = memT.rearrange("p (c m) -> p c m", m=256)
    KxT, o2 = a2_bf(o2, 8 * 256); KxT = KxT.rearrange("p (j m) -> p j m", m=256)
    Vx, o2 = a2_bf(o2, 2 * 1024); Vx = Vx.rearrange("p (m n) -> p m n", n=1024)
    assert o2 <= 63488, o2
    for mb in range(2):
        P.dma("sp", X[:, 14 + mb, :], mem_d[mb * 128:(mb + 1) * 128, :], writes=[R("X", 14 + mb)])
    prenorm_T(3, nblk=2, src=X[:, 14:16, :], dstT=memT, tag="memT", srcR=lambda b: R("X", 14 + b))
    MT = [R("memT", 0), R("memT", 1)]
    for j in range(8):
        bk, bkR = bank()
        for c in range(8):
            MM(bk[:, 0:256], Wkv[:, c, j * 128:(j + 1) * 128], memT[:, c, :], c == 0, c == 7, MT + [R("Wkv")], [bkR], inc=(c == 7))
        CP("act" if j % 2 else "dve", KxT[:, j, :], bk[:, 0:256], [], [bkR, R("KxT")])
    for mb in range(2):
        for half in range(2):
            bk, bkR = bank()
            for c in range(8):
                MM(bk, memT[:, c, mb * 128:(mb + 1) * 128], Wkv[:, c, 1024 + half * 512: 1024 + (half + 1) * 512], c == 0, c == 7,
                   MT + [R("Wkv")], [bkR], inc=(c == 7))
            CP("act" if half else "dve", Vx[:, mb, half * 512:(half + 1) * 512], bk, [], [bkR, R("Vx")])
    for b in range(14, 16):
        P.dma("sp" if b % 2 else "act", X[:, b, :], x_d[b * 128:(b + 1) * 128, :], writes=[R("X", b)])
    gain_load(gpre, 2, R("gpre"))
    P._deps("act", [], [R("Wkv")])
    for b in range(NB + 2):
        if b < NB:
            tokb = slice(b * 128, (b + 1) * 128)
            proj_norm_residual(b, [mixT[:, c, tokb] for c in range(8)], [R("mixT", c, b // 4) for c in range(8)],
                               lambda ci, half: Wo[:, ci, half * 512:(half + 1) * 512], [[R("Wo")]] * 8,
                               X[:, b, :], [R("X", b)], X[:, b, :], [R("X", b)])
        if 1 <= b <= NB:
            prenorm_block(b - 1, X[:, b - 1, :], [R("X", b - 1)])
        if b >= 2:
            prenorm_block_tr(b - 2)
    if "x1" in dbg_d:
        for b in range(NB):
            P.dma("sp", dbg_d["x1"][b * 128:(b + 1) * 128, :], X[:, b, :], reads=[R("X", b)])
    if stop <= 4:
        return finish(nc, P)
    barrier(skip=("Wkv", "Wq", "Wo2"))

    o2 = 0
    Qx, o2 = a2_bf(o2, 8 * 512); Qx = Qx.rearrange("p (j n) -> p j n", n=512)
    PTx = []
    for i in range(2):
        v, o2 = a2_bf(o2, 1024); PTx.append(v.rearrange("p (u n) -> p u n", n=512))
    rcx2 = []
    for i in range(2):
        v, o2 = a2_f(o2, 512); rcx2.append(v)
    assert o2 <= 16384, o2
    gain_load(gpost, 4, R("gpost"))
    for qt in range(4):
        qs_ = slice(qt * 512, (qt + 1) * 512)
        for j in range(8):
            bk, bkR = bank()
            for c in range(8):
                MM(bk, Wq[:, c, j * 128:(j + 1) * 128], hT[:, c, qs_], c == 0, c == 7, ALLH[qt * 4:qt * 4 + 4] + [R("Wq")],
                   [bkR], inc=(c == 7))
            CP("act" if j % 2 else "dve", Qx[:, j, :], bk, [], [bkR, R("Qx", j)])
        for hx in range(4):
            k = hx % 2
            for mb in range(2):
                for half in range(2):
                    MM(psf[:, k, mb, :], KxT[:, 2 * hx + half, mb * 128:(mb + 1) * 128], Qx[:, 2 * hx + half, :], half == 0, half == 1,
                       [R("KxT"), R("Qx", 2 * hx + half)], [R("psf", k, mb)], inc=(mb == 1 and half == 1))
            ACT(PTx[k], psf[:, k, :, :], AF.Exp, [], [R("psf", k, 0), R("psf", k, 1), R("PTx", k)], scale=1.0 / 16)
            dbk, dbkR = psbf[k], R("psb", k)
            for mb in range(2):
                MM(dbk, ones_bf[:], PTx[k][:, mb, :], mb == 0, mb == 1, [R("ones"), R("PTx", k)], [dbkR], inc=(mb == 1))
            ACT(rcx2[k], dbk, AF.Ln, [], [dbkR, R("rcx", k)])
            ACT(rcx2[k], rcx2[k], AF.Exp, [R("rcx", k)], [R("rcx", k)], scale=-1.0)
            for dh in range(2):
                bk, bkR = psf[:, 2, dh, :], R("psf", 2, dh)
                for mb in range(2):
                    MM(bk, Vx[:, mb, hx * 256 + dh * 128: hx * 256 + (dh + 1) * 128], PTx[k][:, mb, :], mb == 0, mb == 1,
                       [R("Vx"), R("PTx", k)], [bkR], inc=(mb == 1))
                TT("dve", mixT[:, 2 * hx + dh, qs_], bk, rcx2[k], ALU.mult, [R("rcx", k)], [bkR, R("mixT", 2 * hx + dh, qt)])
    gain_load(gpre, 5, R("gpre"))
    for b in range(NB + 2):
        if b < NB:
            tokb = slice(b * 128, (b + 1) * 128)
            proj_norm_residual(b, [mixT[:, c, tokb] for c in range(8)], [R("mixT", c, b // 4) for c in range(8)],
                               lambda ci, half: Wo2[:, ci, half * 512:(half + 1) * 512], [[R("Wo2")]] * 8,
                               X[:, b, :], [R("X", b)], X[:, b, :], [R("X", b)])
        if 1 <= b <= NB:
            prenorm_block(b - 1, X[:, b - 1, :], [R("X", b - 1)])
        if b >= 2:
            prenorm_block_tr(b - 2)
    if "x2" in dbg_d:
        for b in range(NB):
            P.dma("sp", dbg_d["x2"][b * 128:(b + 1) * 128, :], X[:, b, :], reads=[R("X", b)])
    if stop <= 5:
        return finish(nc, P)
    barrier()

    gain_load(gpost, 6, R("gpost"))
    for b in range(NB):
        P.dma("sp" if b % 2 else "act", out_d[b * 128:(b + 1) * 128, :], X[:, b, :], reads=[R("X", b)], writes=[R("outd", b)])
    barrier()
    aT16, _ = a1_bf(0, [16 * N_TOK]); aT16 = aT16.rearrange("p (j t) -> p j t", t=N_TOK)
    aT = [aT16[:, j, :] for j in range(16)] + [mixT[:, j, :] for j in range(6)]
    Wup = [mixT[:, 6 + i, :].rearrange("p (c n) -> p c n", n=256) for i in range(2)]
    o2 = 0
    ugb = []; uvb = []; agb = []; avb = []; ebb = []
    for i in range(2):
        v, o2 = a2_f(o2, 2050); ugb.append(v)
        v, o2 = a2_f(o2, 2050); uvb.append(v)
        v, o2 = a2_f(o2, 1024); agb.append(v)
        v, o2 = a2_f(o2, 1024); avb.append(v)
        v, o2 = a2_f(o2, 1024); ebb.append(v)
    assert o2 <= 63488, o2
    for i in range(2):
        for ub_, nm in ((ugb[i], "ug"), (uvb[i], "uv")):
            MEMSET("pool", ub_[:, 0:1], 0.0, [R(nm, i)])
            MEMSET("pool", ub_[:, 2049:2050], 0.0, [R(nm, i)])

    bank8_i = [0]

    def bank8():
        i = bank8_i[0] % 8
        bank8_i[0] += 1
        if i < 6:
            return psf[:, i // 2, i % 2, :], R("psf", i // 2, i % 2)
        return psbf[i - 6], R("psb", i - 6)

    def ffn_mm(j):
        s_ = j % 2
        wu = Wup[s_]
        P.dma("pool", wu[:, :, 0:128], w_up_d[:, j * 128:(j + 1) * 128].rearrange("(c p) n -> p c n", p=128),
              writes=[R("Wup", s_)])
        P.dma("pool", wu[:, :, 128:256],
              w_up_d[:, D_FF + j * 128: D_FF + (j + 1) * 128].rearrange("(c p) n -> p c n", p=128), writes=[R("Wup", s_)])
        for qt in range(4):
            qs_ = slice(qt * 512, (qt + 1) * 512)
            bg_, bgR = bank8()
            bv_, bvR = bank8()
            for c in range(8):
                MM(bg_, wu[:, c, 0:128], hT[:, c, qs_], c == 0, c == 7, ALLH[qt * 4:qt * 4 + 4] + [R("Wup", s_)], [bgR], inc=False)
            for c in range(8):
                MM(bv_, wu[:, c, 128:256], hT[:, c, qs_], c == 0, c == 7, ALLH[qt * 4:qt * 4 + 4] + [R("Wup", s_)], [bvR],
                   inc=(c == 7))
            ACT(ugb[s_][:, 1 + qt * 512: 1 + (qt + 1) * 512], bg_, AF.Copy, [], [bgR, R("ug", s_)])
            ACT(uvb[s_][:, 1 + qt * 512: 1 + (qt + 1) * 512], bv_, AF.Copy, [], [bvR, R("uv", s_)])

    def ffn_ew(j):
        s_ = j % 2
        ug, uv, ag, av, eb_ = ugb[s_], uvb[s_], agb[s_], avb[s_], ebb[s_]
        for hf in range(2):
            base = 1 + hf * 1024
            ts_ = slice(hf * 1024, (hf + 1) * 1024)
            cw = lambda cj, i: convT[:, cj * 4 + i: cj * 4 + i + 1]
            CT = [R("convT")]
            for (u_, uR, cj, dst, dR) in ((ug, R("ug", s_), j, ag, R("ag", s_)), (uv, R("uv", s_), 22 + j, av, R("av", s_))):
                ACT(dst, u_[:, base:base + 1024], AF.Identity, [uR] + CT, [dR], scale=cw(cj, 1), bias=cw(cj, 3))
                STT("dve", dst, u_[:, base - 1:base + 1023], cw(cj, 0), dst, ALU.mult, ALU.add, [uR] + CT, [dR])
                STT("dve", dst, u_[:, base + 1:base + 1025], cw(cj, 2), dst, ALU.mult, ALU.add, [uR] + CT, [dR])
            ACT(eb_, ag, AF.Silu, [R("ag", s_)], [R("eb", s_)])
            TT("dve", aT[j][:, ts_], eb_, av, ALU.mult, [R("eb", s_), R("av", s_)], [R("aT", j)])

    for j in range(23):
        if j < 22:
            ffn_mm(j)
        if j >= 1:
            ffn_ew(j - 1)
    barrier()
    Wd, o2 = a2_bf(0, 22 * 1024); Wd = Wd.rearrange("p (j n) -> p j n", n=1024)
    wstg = []
    for i in range(3):
        v, o2 = a2_f(o2, 1024); wstg.append(v)
    assert o2 <= 63488, o2
    nst = 0
    for j in range(22):
        if j % 2 == 0:
            P.dma("pool", Wd[:, j, :], w_down_d[j * 128:(j + 1) * 128, :], writes=[R("Wd", j)])
        else:
            si = nst % 3
            nst += 1
            P.dma("sp", wstg[si], w_down_d[j * 128:(j + 1) * 128, :], writes=[R("wstg", si)])
            CP("act", Wd[:, j, :], wstg[si], [R("wstg", si)], [R("Wd", j)])
    hTf = hT[:].rearrange("p c t -> p (c t)").bitcast(F32)
    xb_ = [hTf[:, 0:1024], hTf[:, 1024:2048]]
    ob_ = [hTf[:, 2048:3072], hTf[:, 3072:4096]]
    for b in range(NB):
        tokb = slice(b * 128, (b + 1) * 128)
        k = b % 2
        P.dma("sp", xb_[k], out_d[b * 128:(b + 1) * 128, :], reads=[R("outd", b)] + ALLH, writes=[R("xb", k)])
        proj_norm_residual(b, [aT[j][:, tokb] for j in range(22)], [R("aT", j) for j in range(22)],
                           lambda ci, half: Wd[:, ci, half * 512:(half + 1) * 512], [[R("Wd", j)] for j in range(22)],
                           xb_[k], [R("xb", k)], ob_[k], [R("ob", k)])
        P.dma("sp", out_d[b * 128:(b + 1) * 128, :], ob_[k], reads=[R("ob", k)], writes=[R("outd", b)])

    return finish(nc, P)


def finish(nc, P):
    print("COUNTS", {e: P.q[e].total for e in P.ENGS}, "dma", sum(P.dma_cnt), max(P.dma_cnt), flush=True)
    P.wait_all("sp")
    P.emit()
    P.close()
    return nc


def rope_tables():
    rows = N_TOK // 64
    r, c = np.meshgrid(np.arange(rows), np.arange(64), indexing="ij")
    inv = np.power(np.float32(10000.0), -np.arange(16, dtype=np.float32) / np.float32(16)).astype(np.float32)
    ang = np.concatenate([r.reshape(-1, 1).astype(np.float32) * inv, c.reshape(-1, 1).astype(np.float32) * inv], -1)
    cos = np.cos(ang).astype(np.float32)
    sin = np.sin(ang).astype(np.float32)
    f = lambda a: a.reshape(16, 128, 32).transpose(1, 0, 2).reshape(128, 512)
    return np.ascontiguousarray(np.concatenate([f(cos), f(sin)], 1))


def masks():
    s = np.arange(128)[:, None]
    t = np.arange(128)[None, :]
    same = (s // 64) == (t // 64)
    fwd = (same & (s <= t)).astype(np.float32)
    bwd = (same & (s >= t)).astype(np.float32)
    return np.concatenate([fwd, bwd], 1).astype(ml_dtypes.bfloat16)


def prep_inputs(inp):
    f32 = lambda a: np.ascontiguousarray(np.asarray(a, dtype=np.float32))
    shared = {
        "w_in": f32(inp["w_in"][0]), "w_out": f32(inp["w_out"][0]), "w_xq": f32(inp["w_xq"][0]),
        "w_xkv": f32(inp["w_xkv"][0]), "w_xo": f32(inp["w_xo"][0]), "w_up": f32(inp["w_up"][0]),
        "w_down": f32(inp["w_down"][0]),
        "gains": f32(np.stack([inp["pre_mix_g"][0], inp["post_mix_g"][0], inp["pre_x_g"][0], inp["mem_norm_g"][0],
                               inp["post_x_g"][0], inp["pre_ffn_g"][0], inp["post_ffn_g"][0]])),
        "qkg": f32(np.concatenate([inp["q_norm_g"][0], inp["k_norm_g"][0]])[None, :]),
        "hgo": f32(np.asarray(inp["hg_out_norm_g"][0])[None, :]),
        "lbT": f32(np.asarray(inp["hg_lb"]).reshape(2, 2, 4, 128).transpose(3, 0, 1, 2).reshape(128, 16)),
        "convT": f32(np.concatenate([np.asarray(inp["conv_w"][0]), np.asarray(inp["conv_b"][0])[None, :]], 0)
                     .reshape(4, 44, 128).transpose(2, 1, 0).reshape(128, 176)),
        "ident": np.eye(128, dtype=np.float32).astype(ml_dtypes.bfloat16),
        "rope": rope_tables(),
        "mask": masks(),
    }
    x = np.asarray(inp["x"], dtype=np.float32)
    mem = np.asarray(inp["mem"], dtype=np.float32)
    maps = []
    for i in range(8):
        d = dict(shared)
        d["x"] = np.ascontiguousarray(x[i])
        d["mem"] = np.ascontiguousarray(mem[i])
        maps.append(d)
    return maps


_NC_CACHE = {}


def kernel(**inputs):
    if "nc" not in _NC_CACHE:
        _NC_CACHE["nc"] = build()
    nc = _NC_CACHE["nc"]
    maps = prep_inputs(inputs)
    res = run_bass_kernel_spmd(nc, maps, core_ids=list(range(8)))
    return np.stack([np.asarray(r["out"], dtype=np.float32) for r in res.results], 0)
```

```python
import numpy as np
import ml_dtypes
import concourse.bass as bass
import concourse.mybir as mybir
from concourse.bass_utils import run_bass_kernel_spmd

F32 = mybir.dt.float32
BF16 = mybir.dt.bfloat16
AF = mybir.ActivationFunctionType
ALU = mybir.AluOpType
AX = mybir.AxisListType

N_TOK = 2048
D = 1024
NB = 16
EPS = 1e-6
D_FF = 2816
N_IN = 3328


class Res:
    __slots__ = ("name", "w", "r", "big")

    def __init__(self, name):
        self.name = name
        self.w = None
        self.r = {}
        self.big = False


class EngQ:
    def __init__(self, name):
        self.name = name
        self.ops = []
        self.epoch = 0
        self.cnt = 0
        self.total = 0
        self.waited = {}
        self.pending = False


class Prog:
    ENGS = ("pe", "act", "dve", "pool", "sp")
    LIMIT = 900
    NEPOCH = {"pe": 14, "act": 10, "dve": 14, "pool": 4, "sp": 1}

    def __init__(self, nc, n_dma_sems=32):
        self.nc = nc
        self.q = {e: EngQ(e) for e in self.ENGS}
        self.sems = {}
        self.n_dma_sems = n_dma_sems
        self.dma_cnt = [0] * n_dma_sems
        self.dma_pool = {"pool": list(range(0, 16)), "sp": list(range(16, 26)), "act": list(range(26, 32))}
        self.dma_rr = {"pool": 0, "sp": 0, "act": 0}
        self._ctx = []
        self.res = {}

    def R(self, *key):
        r = self.res.get(key)
        if r is None:
            r = Res(key)
            self.res[key] = r
        return r

    def open(self):
        nc = self.nc
        for e in self.ENGS:
            for ep in range(self.NEPOCH[e]):
                c = nc.semaphore("s_%s%d" % (e, ep))
                self.sems[(e, ep)] = c.__enter__()
                self._ctx.append(c)
        for i in range(self.n_dma_sems):
            c = nc.semaphore("s_dma%d" % i)
            self.sems[("dma", i)] = c.__enter__()
            self._ctx.append(c)

    def close(self):
        for c in reversed(self._ctx):
            c.__exit__(None, None, None)

    def _need(self, eng, tok, same_ok):
        if tok is None:
            return
        key, val = tok
        q = self.q[eng]
        if key[0] == "dma":
            if q.waited.get(key, 0) >= val:
                return
            q.waited[key] = val
        else:
            src, ep = key
            if src == eng and not same_ok:
                return
            if q.waited.get(src, (-1, 0)) >= (ep, val):
                return
            q.waited[src] = (ep, val)
        sem = self.sems[key]
        q.ops.append(lambda h, sem=sem, val=val: h.wait_ge(sem, val))

    def _deps(self, eng, reads, writes):
        for r in reads:
            self._need(eng, r.w, same_ok=(eng != "pe" and not r.big))
        so = (eng != "pe")
        for w in writes:
            self._need(eng, w.w, same_ok=so)
            for tok in w.r.values():
                self._need(eng, tok, same_ok=so)

    def op(self, eng, fn, reads=(), writes=(), inc=True, big=False):
        q = self.q[eng]
        if q.cnt >= self.LIMIT and not q.pending:
            q.epoch += 1
            q.cnt = 0
            assert q.epoch < self.NEPOCH[eng], "out of epochs for " + eng
        self._deps(eng, reads, writes)
        key = (eng, q.epoch)
        val = q.cnt + 1
        tok = (key, val)
        if inc:
            q.cnt = val
            q.total += 1
            sem = self.sems[key]
            q.ops.append(lambda h, fn=fn, sem=sem: fn(h).then_inc(sem, 1))
            q.pending = False
        else:
            q.ops.append(lambda h, fn=fn: fn(h))
            q.pending = True
        for r in reads:
            r.r[eng] = tok
        for w in writes:
            w.w = tok
            w.r = {}
            w.big = big
        return tok

    def dma(self, eng, out, in_, reads=(), writes=(), also_wait=(), **kw):
        pl = self.dma_pool[eng]
        i = pl[self.dma_rr[eng] % len(pl)]
        self.dma_rr[eng] += 1
        key = ("dma", i)
        if self.dma_cnt[i] > 0:
            self._need(eng, (key, 16 * self.dma_cnt[i]), same_ok=True)
        self._deps(eng, reads, list(writes) + list(also_wait))
        self.dma_cnt[i] += 1
        assert self.dma_cnt[i] < 60
        val = 16 * self.dma_cnt[i]
        sem = self.sems[key]
        q = self.q[eng]
        q.ops.append(lambda h, sem=sem, out=out, in_=in_, kw=kw:
                     h.dma_start(out=out, in_=in_, **kw).then_inc(sem, 16))
        tok = (key, val)
        for r in reads:
            r.r[key] = tok
        for w in writes:
            w.w = tok
            w.r = {}
            w.big = False
        return tok

    def wait_all(self, eng, skip=()):
        for r in self.res.values():
            if r.name[0] in skip:
                continue
            self._need(eng, r.w, same_ok=True)
            for tok in list(r.r.values()):
                self._need(eng, tok, same_ok=True)

    def emit(self):
        nc = self.nc
        for e in self.ENGS:
            assert not self.q[e].pending, "engine %s ends with non-inc'd instr" % e
        with nc.Block() as block:
            @block.tensor
            def _(h):
                for f in self.q["pe"].ops:
                    f(h)

            @block.scalar
            def _(h):
                for f in self.q["act"].ops:
                    f(h)

            @block.vector
            def _(h):
                for f in self.q["dve"].ops:
                    f(h)

            @block.gpsimd
            def _(h):
                for f in self.q["pool"].ops:
                    f(h)

            @block.sync
            def _(h):
                for f in self.q["sp"].ops:
                    f(h)


class StopBuild(Exception):
    pass


def build(stop=99, dbg=()):
    st = {}
    try:
        return _build(stop, dbg, st)
    except StopBuild:
        return finish(st["nc"], st["P"])


def _build(stop, dbg, st):
    nc = bass.Bass("TRN2", target_bir_lowering=False)
    P = Prog(nc)
    P.open()
    R = P.R
    st["nc"] = nc
    st["P"] = P

    def chk(level):
        if stop <= level:
            raise StopBuild()

    def din(name, shape, dt=F32):
        return nc.dram_tensor(name, list(shape), dt, kind="ExternalInput").ap()

    x_d = din("x", [N_TOK, D])
    mem_d = din("mem", [256, D])
    w_in_d = din("w_in", [D, N_IN])
    w_out_d = din("w_out", [D, D])
    w_xq_d = din("w_xq", [D, D])
    w_xkv_d = din("w_xkv", [D, 2 * D])
    w_xo_d = din("w_xo", [D, D])
    w_up_d = din("w_up", [D, 2 * D_FF])
    w_down_d = din("w_down", [D_FF, D])
    gains_d = din("gains", [7, D])
    qkg_d = din("qkg", [1, 128])
    hgo_d = din("hgo", [1, 128])
    lbT_d = din("lbT", [128, 16])
    convT_d = din("convT", [128, 44 * 4])
    ident_d = din("ident", [128, 128], BF16)
    rope_d = din("rope", [128, 2 * 16 * 32])
    mask_d = din("mask", [128, 2 * 128], BF16)
    out_d = nc.dram_tensor("out", [N_TOK, D], F32, kind="ExternalOutput").ap()
    dbg_d = {}
    for name, shape in dbg:
        dbg_d[name] = nc.dram_tensor("dbg_" + name, list(shape), F32, kind="ExternalOutput").ap()

    def sb(name, shape, dt):
        return nc.alloc_sbuf_tensor("sb_" + name, list(shape), dt)

    hT = sb("hT", [128, 8, N_TOK], BF16)
    mixT = sb("mixT", [128, 8, N_TOK], BF16)
    arena1 = sb("arena1", [128, 16384], F32)
    arena2 = sb("arena2", [128, 15872], F32)
    ident = sb("ident", [128, 128], BF16)
    ones_bf = sb("ones_bf", [128, 128], BF16)
    gpre = sb("gpre", [128, D], F32)
    gpost = sb("gpost", [128, D], F32)
    stat = sb("stat", [128, 64], F32)
    stat2 = sb("stat2", [128, 64], F32)
    junk = sb("junk", [128, D], BF16)
    hb = sb("hb", [128, 2, D], BF16)
    lbT = sb("lbT", [128, 16], F32)
    lbv = sb("lbv", [128, 40], F32)
    convT = sb("convT", [128, 44 * 4], F32)
    epsc = sb("epsc", [128, 1], F32)
    psf = nc.alloc_psum_tensor("psf", [128, 3, 2, 512], F32)
    psb = nc.alloc_psum_tensor("psb", [128, 2, 1024], BF16)

    X = arena1[:].rearrange("p (b d) -> p b d", d=D)

    def a1_bf(off_bytes, shape):
        n = int(np.prod(shape))
        v = arena1[:, off_bytes // 4: off_bytes // 4 + n // 2].bitcast(BF16)
        return v, off_bytes + n * 2

    def a2_bf(off_bytes, n):
        v = arena2[:, off_bytes // 4: off_bytes // 4 + n // 2].bitcast(BF16)
        return v, off_bytes + n * 2

    def a2_f(off_bytes, n):
        v = arena2[:, off_bytes // 4: off_bytes // 4 + n]
        return v, off_bytes + n * 4

    def a1_f(off_bytes, n):
        v = arena1[:, off_bytes // 4: off_bytes // 4 + n]
        return v, off_bytes + n * 4

    def MM(out, lhsT, rhs, start, stop, reads, writes, inc=True, tp=None):
        if tp is None:
            return P.op("pe", lambda h: h.matmul(out, lhsT=lhsT, rhs=rhs, start=start, stop=stop),
                        reads=reads, writes=writes, inc=inc)
        return P.op("pe", lambda h: h.matmul(out, lhsT=lhsT, rhs=rhs, start=start, stop=stop, tile_position=tp),
                    reads=reads, writes=writes, inc=inc)

    def TR(out, in_, reads, writes, inc=True):
        return P.op("pe", lambda h: h.transpose(out, in_, ident[:]), reads=reads, writes=writes, inc=inc)

    BIG_N = 128

    def isbig(ap):
        n = 1
        for d_ in ap.shape[1:]:
            n *= d_
        return n >= BIG_N

    def ACT(out, in_, func, reads, writes, scale=None, bias=None, accum=None):
        kw = {}
        if scale is not None:
            kw["scale"] = scale
        if bias is not None:
            kw["bias"] = bias
        if accum is not None:
            kw["accum_out"] = accum
        return P.op("act", lambda h: h.activation(out=out, in_=in_, func=func, **kw), reads=reads, writes=writes,
                    big=(accum is None and isbig(out)))

    def TT(eng, out, in0, in1, op, reads, writes):
        return P.op(eng, lambda h: h.tensor_tensor(out=out, in0=in0, in1=in1, op=op), reads=reads, writes=writes, big=isbig(out))

    def TS(eng, out, in0, s1, s2, op0, op1, reads, writes):
        if s2 is None:
            return P.op(eng, lambda h: h.tensor_scalar(out=out, in0=in0, scalar1=s1, scalar2=None, op0=op0),
                        reads=reads, writes=writes, big=isbig(out))
        return P.op(eng, lambda h: h.tensor_scalar(out=out, in0=in0, scalar1=s1, scalar2=s2, op0=op0, op1=op1),
                    reads=reads, writes=writes, big=isbig(out))

    def STT(eng, out, in0, scalar, in1, op0, op1, reads, writes):
        return P.op(eng, lambda h: h.scalar_tensor_tensor(out=out, in0=in0, scalar=scalar, in1=in1, op0=op0, op1=op1),
                    reads=reads, writes=writes, big=isbig(out))

    def CP(eng, out, in_, reads, writes):
        if eng == "act":
            return ACT(out, in_, AF.Copy, reads, writes)
        return P.op(eng, lambda h: h.tensor_copy(out=out, in_=in_), reads=reads, writes=writes, big=isbig(out))

    def RECIP(out, in_, reads, writes):
        return P.op("dve", lambda h: h.reciprocal(out=out, in_=in_), reads=reads, writes=writes)

    def MEMSET(eng, ap, val, writes):
        return P.op(eng, lambda h: h.memset(ap, val), writes=writes)

    def barrier(skip=()):
        for e in Prog.ENGS:
            P.wait_all(e, skip)

    def load_w(dst, src_rows_cols, reads_w, also_wait=()):
        src = src_rows_cols.rearrange("(c p) n -> p c n", p=128)
        nch = src.shape[1]
        for c in range(nch):
            P.dma("pool", dst[:, c, :], src[:, c, :], writes=reads_w, also_wait=also_wait)

    def gain_load(dst, row, res):
        P.dma("sp", dst[:], gains_d[row:row + 1, :].partition_broadcast(128), writes=[res])

    def dump(name, ap_sb, reads, view=None):
        if name in dbg_d:
            P.dma("sp", dbg_d[name] if view is None else view(dbg_d[name]), ap_sb, reads=reads)

    P.dma("sp", ident[:], ident_d, writes=[R("ident")])
    P.dma("sp", lbT[:], lbT_d, writes=[R("lbT")])
    P.dma("sp", convT[:], convT_d, writes=[R("convT")])
    MEMSET("pool", ones_bf[:], 1.0, [R("ones")])
    MEMSET("pool", epsc[:], EPS, [R("epsc")])
    MEMSET("pool", stat[:], 0.0, [R("stat")])
    MEMSET("pool", stat2[:], 0.0, [R("stat2")])

    def prenorm_T(grow, nblk=NB, src=None, dstT=None, tag="h", srcR=None):
        src = X if src is None else src
        dstT = hT if dstT is None else dstT
        srcR = (lambda b: R("X", b)) if srcR is None else srcR
        gain_load(gpre, grow, R("gpre"))
        for b in range(nblk):
            ACT(junk[:], src[:, b, :], AF.Square, [srcR(b)], [R("junk"), R("stat")], accum=stat[:, b:b + 1])
        TS("dve", stat2[:, 0:nblk], stat[:, 0:nblk], 1.0 / D, EPS, ALU.mult, ALU.add, [R("stat")], [R("stat2")])
        ACT(stat2[:, 0:nblk], stat2[:, 0:nblk], AF.Ln, [R("stat2")], [R("stat2")])
        ACT(stat2[:, 0:nblk], stat2[:, 0:nblk], AF.Exp, [R("stat2")], [R("stat2")], scale=-0.5)
        for b in range(nblk):
            k = b % 2
            STT("dve", hb[:, k, :], src[:, b, :], stat2[:, b:b + 1], gpre[:], ALU.mult, ALU.mult,
                [srcR(b), R("stat2"), R("gpre")], [R("hb", k)])
            for c in range(8):
                TR(psb[:, k, c * 128:(c + 1) * 128], hb[:, k, c * 128:(c + 1) * 128],
                   [R("hb", k), R("ident")], [R("psb", k)], inc=(c == 7))
            CP("act", dstT[:, :, b * 128:(b + 1) * 128], psb[:, k, :].rearrange("p (c t) -> p c t", t=128),
               [], [R("psb", k), R(tag, b)])

    o2 = 0
    Watt, o2 = a2_bf(o2, 8 * 768); Watt = Watt.rearrange("p (c n) -> p c n", n=768)
    rope_raw, o2 = a2_f(o2, 1024)
    qkgB, o2 = a2_f(o2, 128)
    o2_attw = o2
    load_w(Watt, w_in_d[:, 0:768], [R("Watt")])
    P.dma("sp", rope_raw, rope_d, writes=[R("rope_raw")])
    P.dma("sp", qkgB, qkg_d.partition_broadcast(128), writes=[R("qkgB")])
    for b in range(NB):
        P.dma("sp" if b % 2 else "act", X[:, b, :], x_d[b * 128:(b + 1) * 128, :], writes=[R("X", b)])
    prenorm_T(0)
    if "hT" in dbg_d:
        tmpf = arena2[:, 13000:13000 + 2048]
        for c in range(8):
            CP("dve", tmpf, hT[:, c, :], [R("h", b) for b in range(NB)], [R("dbgtmp")])
            P.dma("sp", dbg_d["hT"][c * 128:(c + 1) * 128, :], tmpf, reads=[R("dbgtmp")])
    if stop <= 1:
        return finish(nc, P)
    barrier()
    if stop <= 1.1:
        return finish(nc, P)


    o1 = 0
    ropeT = []
    for i in range(8):
        v, o1 = a1_f(o1, 512)
        ropeT.append(v.rearrange("p (b i) -> p b i", i=32))
    QT, o1 = a1_bf(o1, [4 * N_TOK]); QT = QT.rearrange("p (j t) -> p j t", t=N_TOK)
    KTd, o1 = a1_bf(o1, [2 * N_TOK]); KTd = KTd.rearrange("p (g t) -> p g t", t=N_TOK)
    VA, o1 = a1_bf(o1, [NB * 2 * 192]); VA = VA.rearrange("p (b g d) -> p b g d", g=2, d=192)
    o2 = o2_attw
    qkv = []; sqb = []; tq = []; qn = []; qr = []
    for i in range(2):
        v, o2 = a2_f(o2, 768); qkv.append(v)
        v, o2 = a2_f(o2, 640); sqb.append(v)
        v, o2 = a2_f(o2, 4 * 320); tq.append(v.rearrange("p (a n) -> p a n", n=320))
        v, o2 = a2_f(o2, 640); qn.append(v)
        v, o2 = a2_bf(o2, 768); qr.append(v)
    PT = []
    for i in range(3):
        v, o2 = a2_bf(o2, 1024); PT.append(v.rearrange("p (u n) -> p u n", n=512))
    rc = []
    for i in range(2):
        v, o2 = a2_f(o2, 512); rc.append(v)
    assert o1 <= 65536 and o2 <= 63488, (o1, o2)

    if stop <= 1.16:
        return finish(nc, P)
    MEMSET("pool", VA, 1.0, [R("VA", b) for b in range(NB)])
    if stop <= 1.17:
        return finish(nc, P)
    cosv = rope_raw[:, 0:512].rearrange("p (b i) -> p b i", i=32)
    sinv = rope_raw[:, 512:1024].rearrange("p (b i) -> p b i", i=32)
    for qk in range(2):
        gv = qkgB[:, qk * 64:(qk + 1) * 64].rearrange("p (i two) -> p i two", two=2)
        ge = gv[:, :, 0].unsqueeze(1).broadcast_to([128, NB, 32])
        go = gv[:, :, 1].unsqueeze(1).broadcast_to([128, NB, 32])
        for ti, (tab, gg) in enumerate(((cosv, ge), (sinv, go), (sinv, ge), (cosv, go))):
            TT("dve", ropeT[qk * 4 + ti], tab, gg, ALU.mult, [R("rope_raw"), R("qkgB")], [R("ropeT")])

    def att_proj(b):
        k = b % 2
        tokb = slice(b * 128, (b + 1) * 128)
        for c in range(8):
            MM(psf[:, k, 0, :], hT[:, c, tokb], Watt[:, c, 0:512], c == 0, c == 7,
               [R("h", b), R("Watt")], [R("psf", k, 0)], inc=False)
        for c in range(8):
            MM(psf[:, k, 1, 0:256], hT[:, c, tokb], Watt[:, c, 512:768], c == 0, c == 7,
               [R("h", b), R("Watt")], [R("psf", k, 1)], inc=(c == 7))
        ACT(qkv[k][:, 0:512], psf[:, k, 0, :], AF.Copy, [], [R("psf", k, 0), R("qkv", k)])
        ACT(qkv[k][:, 512:768], psf[:, k, 1, 0:256], AF.Copy, [], [R("psf", k, 1), R("qkv", k)])
        ACT(sqb[k], qkv[k][:, 0:640], AF.Square, [R("qkv", k)], [R("sqb", k)])
        ss = stat[:, 16 + 10 * k: 26 + 10 * k]
        rs = stat2[:, 16 + 10 * k: 26 + 10 * k]
        P.op("dve", lambda h: h.tensor_reduce(out=ss, in_=sqb[k].rearrange("p (h d) -> p h d", d=64),
                                              axis=AX.X, op=ALU.add),
             reads=[R("sqb", k)], writes=[R("ss", k)])
        TS("dve", rs, ss, 1.0 / 64, EPS, ALU.mult, ALU.add, [R("ss", k)], [R("rs", k)])
        ACT(rs, rs, AF.Ln, [R("rs", k)], [R("rs", k)])
        ACT(rs, rs, AF.Exp, [R("rs", k)], [R("rs", k)], scale=-0.5)
        for qk, eng, nh, c0 in ((0, "dve", 8, 0), (1, "dve", 2, 512)):
            src = qkv[k][:, c0:c0 + nh * 64].rearrange("p (h i two) -> p h i two", i=32, two=2)
            xe = src[:, :, :, 0]
            xo = src[:, :, :, 1]
            tabs = [ropeT[qk * 4 + ti][:, b, :].unsqueeze(1).broadcast_to([128, nh, 32]) for ti in range(4)]
            tt = [tq[k][:, a, 0:nh * 32].rearrange("p (h i) -> p h i", i=32) for a in range(4)]
            rd = [R("qkv", k), R("ropeT")]
            wr = [R("tq", k, qk)]
            TT(eng, tt[0], xe, tabs[0], ALU.mult, rd, wr)
            TT(eng, tt[1], xo, tabs[1], ALU.mult, rd, wr)
            TT(eng, tt[2], xe, tabs[2], ALU.mult, rd, wr)
            TT(eng, tt[3], xo, tabs[3], ALU.mult, rd, wr)
            dst = qn[k][:, qk * 512: qk * 512 + nh * 64].rearrange("p (h i two) -> p h i two", i=32, two=2)
            TT(eng, dst[:, :, :, 0], tt[0], tt[1], ALU.subtract, wr, [R("qn", k, qk)])
            TT(eng, dst[:, :, :, 1], tt[2], tt[3], ALU.add, wr, [R("qn", k, qk)])
            rsb = rs[:, qk * 8: qk * 8 + nh]
            if qk == 0:
                TT(eng, qr[k][:, 0:512].rearrange("p (h d) -> p h d", d=64),
                   qn[k][:, 0:512].rearrange("p (h d) -> p h d", d=64),
                   rsb.unsqueeze(2).broadcast_to([128, 8, 64]), ALU.mult,
                   [R("qn", k, 0), R("rs", k)], [R("qr", k, 0)])
            else:
                for dup in range(2):
                    TT(eng, qr[k][:, 512:768].rearrange("p (g u d) -> p g u d", u=2, d=64)[:, :, dup, :],
                       qn[k][:, 512:640].rearrange("p (g d) -> p g d", d=64),
                       rsb.unsqueeze(2).broadcast_to([128, 2, 64]), ALU.mult,
                       [R("qn", k, 1), R("rs", k)], [R("qr", k, 1)])
        CP("dve", VA[:, b, :, 64:128], qkv[k][:, 640:768].rearrange("p (g d) -> p g d", d=64),
           [R("qkv", k)], [R("VA", b)])
    def att_proj2(b):
        k = b % 2
        tokb = slice(b * 128, (b + 1) * 128)
        for j in range(6):
            TR(psb[:, k, j * 128:(j + 1) * 128], qr[k][:, j * 128:(j + 1) * 128],
               [R("qr", k, 0), R("qr", k, 1), R("ident")], [R("psb", k)], inc=(j == 5))
        CP("act", QT[:, :, tokb], psb[:, k, 0:512].rearrange("p (j t) -> p j t", t=128), [], [R("psb", k), R("QT", b)])
        CP("dve", KTd[:, :, tokb], psb[:, k, 512:768].rearrange("p (g t) -> p g t", t=128), [], [R("psb", k), R("KT", b)])

    if stop <= 1.2:
        return finish(nc, P)
    for b in range(NB + 1):
        if b < NB:
            att_proj(b)
        if b >= 1:
            att_proj2(b - 1)
    if "QT" in dbg_d:
        tmpf = arena2[:, 14000:14000 + 1024]
        for c in range(4):
            for hf in range(2):
                CP("dve", tmpf, QT[:, c, hf * 1024:(hf + 1) * 1024], [R("QT", b) for b in range(NB)], [R("dbgtmp")])
                P.dma("sp", dbg_d["QT"][c * 128:(c + 1) * 128, hf * 1024:(hf + 1) * 1024], tmpf, reads=[R("dbgtmp")])
    if stop <= 1.5:
        return finish(nc, P)

    Wh, _ = a2_bf(0, 8 * 640); Wh = Wh.rearrange("p (c n) -> p c n", n=640)
    for gi, c0 in enumerate((768, 1280, 1792, 2304, 2816)):
        src = w_in_d[:, c0: c0 + 128].rearrange("(c p) n -> p c n", p=128)
        P.dma("pool", Wh[:, :, gi * 128:(gi + 1) * 128], src, writes=[R("Wh")], also_wait=[R("Watt")])
    PREFETCHED_WH0 = True
    pt_i = [0]
    psbf = [psb[:, 0, :].bitcast(F32), psb[:, 1, :].bitcast(F32)]
    unit_i = [0]

    def st_mm_unit(m, qt, kb):
        g = m // 2
        qs = slice(qt * 512, (qt + 1) * 512)
        qreads = [R("QT", qt * 4 + i) for i in range(4)]
        k = kb % 2
        for par in range(2):
            rows = slice(par * 64, par * 64 + 64)
            MM(psf[:, k, par, :], KTd[rows, g, kb * 128:(kb + 1) * 128], QT[rows, m, qs], True, True,
               qreads + [R("KT", kb)], [R("psf", k, par)], inc=(par == 1))

    UNITS = [(m_, qt_) for m_ in range(4) for qt_ in range(4)]

    def att_pair_tile(m, qt):
        g = m // 2
        j = m
        qs = slice(qt * 512, (qt + 1) * 512)
        uidx = UNITS.index((m, qt))
        ui = unit_i[0] % 2
        unit_i[0] += 1
        if ui == 0:
            accs = [(psf[:, 2, 0, :], R("psf", 2, 0)), (psf[:, 2, 1, :], R("psf", 2, 1))]
        else:
            accs = [(psbf[0], R("psb", 0)), (psbf[1], R("psb", 1))]
        qreads = [R("QT", qt * 4 + i) for i in range(4)]

        def st_mm(kb):
            k = kb % 2
            for par in range(2):
                rows = slice(par * 64, par * 64 + 64)
                MM(psf[:, k, par, :], KTd[rows, g, kb * 128:(kb + 1) * 128], QT[rows, j, qs], True, True,
                   qreads + [R("KT", kb)], [R("psf", k, par)], inc=(par == 1))

        if uidx == 0:
            st_mm(0)
        for kb in range(16):
            if kb + 1 < 16:
                st_mm(kb + 1)
            elif uidx + 1 < len(UNITS):
                st_mm_unit(UNITS[uidx + 1][0], UNITS[uidx + 1][1], 0)
            pi = pt_i[0] % 3
            pt_i[0] += 1
            k = kb % 2
            ACT(PT[pi], psf[:, k, :, :], AF.Exp, [], [R("psf", k, 0), R("psf", k, 1), R("PT", pi)], scale=0.125)
            for par in range(2):
                vsl = slice(64, 192) if par == 0 else slice(0, 128)
                MM(accs[par][0], VA[:, kb, g, vsl], PT[pi][:, par, :], kb == 0, kb == 15,
                   [R("VA", kb), R("PT", pi)], [accs[par][1]], inc=(par == 1))
        rcb = rc[ui]
        for par in range(2):
            rows = slice(par * 64, par * 64 + 64)
            orow = slice((1 - par) * 64, (1 - par) * 64 + 64)
            acc, accR = accs[par]
            P.op("dve", lambda hh, acc=acc, rows=rows, orow=orow: hh.reciprocal(out=rcb[rows, :], in_=acc[orow, :]),
                 reads=[], writes=[accR, R("rc", ui, par)])
            TT("dve", mixT[rows, j, qs], acc[rows, :], rcb[rows, :], ALU.mult, [R("rc", ui, par)], [accR, R("mixT", j, qt)])

    for m in range(4):
        for qt in range(4):
            att_pair_tile(m, qt)

    if "att" in dbg_d:
        tmpf = arena2[:, 0:2048]
        for c in range(4):
            CP("dve", tmpf, mixT[:, c, :], [R("mixT", c, qt) for qt in range(4)], [R("dbgtmp")])
            P.dma("sp", dbg_d["att"][c * 128:(c + 1) * 128, :], tmpf, reads=[R("dbgtmp")])
    if stop <= 2:
        return finish(nc, P)
    barrier(skip=("Wh",))

    bank_i = [0]

    def bank():
        i = bank_i[0] % 6
        bank_i[0] += 1
        return psf[:, i // 2, i % 2, :], R("psf", i // 2, i % 2)

    o1 = 0
    fb = {}
    for nm in ("qs", "SG", "E", "LF", "B", "KK", "rmask", "gate"):
        fb[nm], o1 = a1_f(o1, 2048)
    Tbig = arena1[:, 2 * 2048: 4 * 2048].rearrange("p (c v) -> p c v", v=128)
    o2 = 0
    Wh, o2 = a2_bf(o2, 8 * 640); Wh = Wh.rearrange("p (c n) -> p c n", n=640)
    Of, o2 = a2_f(o2, 2048); Of3 = Of.rearrange("p (b v) -> p b v", v=128)
    qtl, o2 = a2_bf(o2, 2048)
    ktl, o2 = a2_bf(o2, 2048)
    ktok, o2 = a2_bf(o2, 2048); ktok = ktok.rearrange("p (b v) -> p b v", v=128)
    vtok, o2 = a2_bf(o2, 2048); vtok = vtok.rearrange("p (b v) -> p b v", v=128)
    Sbf, o2 = a2_bf(o2, 32 * 128); Sbf = Sbf.rearrange("p (c v) -> p c v", v=128)
    AT, o2 = a2_bf(o2, 2048); AT = AT.rearrange("p (b v) -> p b v", v=128)
    recb, o2 = a2_bf(o2, 2048); recb = recb.rearrange("p (b v) -> p b v", v=128)
    hgoB, o2 = a2_f(o2, 128)
    maskS, o2 = a2_bf(o2, 256); maskS = maskS.rearrange("p (d t) -> p d t", t=128)
    gtmp, o2 = a2_f(o2, 256)
    assert o1 <= 65536 and o2 <= 63488, (o1, o2)
    gate3 = fb["gate"].rearrange("p (b v) -> p b v", v=128)

    P.dma("sp", hgoB, hgo_d.partition_broadcast(128), writes=[R("hgoB")])
    P.dma("sp", maskS, mask_d.rearrange("p (d t) -> p d t", t=128), writes=[R("maskS")])
    MEMSET("pool", fb["rmask"], 1.0, [R("rmask")])
    MEMSET("pool", fb["rmask"].rearrange("p (c t) -> p c t", t=64)[:, :, 0:1], 0.0, [R("rmask")])
    lv = lbT[:].rearrange("p (d l h) -> p d l h", l=2, h=4)
    TT("dve", lbv[:, 0:8].rearrange("p (d h) -> p d h", h=4), lv[:, :, 0, :], lv[:, :, 1, :], ALU.subtract,
       [R("lbT")], [R("lbv")])
    ACT(lbv[:, 0:8], lbv[:, 0:8], AF.Exp, [R("lbv")], [R("lbv")], scale=-1.0)
    TS("dve", lbv[:, 0:8], lbv[:, 0:8], 1.0, None, ALU.add, None, [R("lbv")], [R("lbv")])
    RECIP(lbv[:, 0:8], lbv[:, 0:8], [R("lbv")], [R("lbv")])
    TS("dve", lbv[:, 8:16], lbv[:, 0:8], -1.0, 1.0, ALU.mult, ALU.add, [R("lbv")], [R("lbv")])
    TS("dve", lbv[:, 16:24], lbv[:, 0:8], -0.5, 0.5, ALU.mult, ALU.add, [R("lbv")], [R("lbv")])
    TS("dve", lbv[:, 24:32], lbv[:, 0:8], 0.5, 0.5, ALU.mult, ALU.add, [R("lbv")], [R("lbv")])
    TS("dve", lbv[:, 32:40], lbv[:, 0:8], 0.5, -0.5, ALU.mult, ALU.add, [R("lbv")], [R("lbv")])

    ALLH = [R("h", b) for b in range(NB)]
    chk(2.1)

    def load_Wh(hh, extra=()):
        for gi, c0 in enumerate((768, 1280, 1792, 2304, 2816)):
            src = w_in_d[:, c0 + hh * 128: c0 + (hh + 1) * 128].rearrange("(c p) n -> p c n", p=128)
            P.dma("pool", Wh[:, :, gi * 128:(gi + 1) * 128], src, writes=[R("Wh")] + list(extra))

    def hgrn_head(hh):
        for bp in range(8):
            bk, bkR = bank()
            for u in range(2):
                b = 2 * bp + u
                for c in range(8):
                    MM(bk[:, u * 256:(u + 1) * 256], hT[:, c, b * 128:(b + 1) * 128], Wh[:, c, 384:640], c == 0, c == 7,
                       [R("h", b), R("Wh")], [bkR], inc=(u == 1 and c == 7))
            bv = bk.rearrange("p (u n) -> p u n", n=256)
            ACT(vtok[:, 2 * bp:2 * bp + 2, :], bv[:, :, 0:128], AF.Copy, [], [bkR, R("vtok")])
            g3 = gtmp.rearrange("p (u n) -> p u n", n=128)
            ACT(gate3[:, 2 * bp:2 * bp + 2, :], bv[:, :, 128:256], AF.Silu, [], [bkR, R("gate")])

        chk(2.2)

        def fm_proj(gi, dst, dstname):
            kept = []
            for qt in range(4):
                bk, bkR = bank()
                for c in range(8):
                    MM(bk, Wh[:, c, gi * 128:(gi + 1) * 128], hT[:, c, qt * 512:(qt + 1) * 512], c == 0, c == 7,
                       ALLH[qt * 4:qt * 4 + 4] + [R("Wh")], [bkR], inc=(c == 7))
                ACT(dst[:, qt * 512:(qt + 1) * 512], bk, AF.Tanh, [], [bkR, R(dstname, qt // 2)], scale=0.5)
                kept.append((bk, bkR))
            return kept

        for qt in range(4):
            bk, bkR = bank()
            for c in range(8):
                MM(bk, Wh[:, c, 0:128], hT[:, c, qt * 512:(qt + 1) * 512], c == 0, c == 7,
                   ALLH[qt * 4:qt * 4 + 4] + [R("Wh")], [bkR], inc=(c == 7))
            sl = slice(qt * 512, (qt + 1) * 512)
            ACT(fb["qs"][:, sl], bk, AF.Silu, [], [bkR, R("qs", qt // 2)])

        chk(2.3)
        for dr in range(2):
            li = dr * 4 + hh
            ha_ap = lbv[:, 16 + li: 17 + li]
            hb_ap = lbv[:, 24 + li: 25 + li]
            nha_ap = lbv[:, 32 + li: 33 + li]
            E, SG, LF, B, KK = fb["E"], fb["SG"], fb["LF"], fb["B"], fb["KK"]
            EB, ENB = SG, LF
            H2 = (slice(0, 1024), slice(1024, 2048))
            if dr == 0:
                fm_proj(1, E, "E")
                TH, THn, CUM, CUMn = E, "E", B, "B"
            else:
                TH, THn, CUM, CUMn = B, "B", E, "E"
            for hf in range(2):
                sl = H2[hf]
                ACT(LF[:, sl], TH[:, sl], AF.Ln, [R(THn, hf), R("lbv")], [R("LF", hf)], scale=ha_ap, bias=hb_ap)
            for hf in range(2):
                sl = H2[hf]
                TS("dve", KK[:, sl], TH[:, sl], nha_ap, ha_ap, ALU.mult, ALU.add, [R(THn, hf), R("lbv")], [R("KK", hf)])
                P.op("dve", lambda h, sl=sl, CUM=CUM: h.tensor_tensor_scan(out=CUM[:, sl], data0=fb["rmask"][:, sl],
                                                                           data1=LF[:, sl], initial=0.0, op0=ALU.mult, op1=ALU.add),
                     reads=[R("rmask"), R("LF", hf)], writes=[R(CUMn, hf)])
                if dr == 1:
                    TT("dve", LF[:, sl], LF[:, sl], CUM[:, sl], ALU.subtract, [R("LF", hf), R(CUMn, hf)], [R("LF", hf)])
                    C3 = CUM[:, sl].rearrange("p (c t) -> p c t", t=64)
                    TT("dve", B[:, sl].rearrange("p (c t) -> p c t", t=64), LF[:, sl].rearrange("p (c t) -> p c t", t=64),
                       C3[:, :, 63:64].broadcast_to([128, 16, 64]), ALU.add, [R("LF", hf), R(CUMn, hf)], [R("B", hf)])
            Bs, Bn = B, "B"
            for hf in range(2):
                sl = H2[hf]
                ACT(EB[:, sl], Bs[:, sl], AF.Exp, [R(Bn, hf)], [R("SG", hf)])
                ACT(ENB[:, sl], Bs[:, sl], AF.Exp, [R(Bn, hf)], [R("LF", hf)], scale=-1.0)
            for hf in range(2):
                sl = H2[hf]
                TT("dve", qtl[:, sl], fb["qs"][:, sl], EB[:, sl], ALU.mult, [R("qs", hf), R("SG", hf)], [R("qtl", hf)])
                TT("dve", ktl[:, sl], KK[:, sl], ENB[:, sl], ALU.mult, [R("KK", hf), R("LF", hf)], [R("ktl", hf)])
            chk(2.4)
            for half in range(2):
                for i in range(8):
                    b = half * 8 + i
                    TR(psb[:, half, i * 128:(i + 1) * 128], ktl[:, b * 128:(b + 1) * 128], [R("ktl", half), R("ident")],
                       [R("psb", half)], inc=(i == 7))
                CP("act" if half else "dve", ktok[:, half * 8:(half + 1) * 8, :],
                   psb[:, half, :].rearrange("p (b v) -> p b v", v=128), [], [R("psb", half), R("ktok")])
            for bg in range(4):
                bk, bkR = bank()
                for u in range(4):
                    b = 4 * bg + u
                    MM(bk[:, u * 128:(u + 1) * 128], ktl[:, b * 128:(b + 1) * 128], qtl[:, b * 128:(b + 1) * 128], True, True,
                       [R("ktl", bg // 2), R("qtl", bg // 2)], [bkR], inc=(u == 3))
                TT("dve", AT[:, 4 * bg:4 * bg + 4, :], bk.rearrange("p (u t) -> p u t", t=128),
                   maskS[:, dr, :].unsqueeze(1).broadcast_to([128, 4, 128]), ALU.mult, [R("maskS")], [bkR, R("AT")])
            chk(2.5)
            MEMSET("pool", Sbf[:, 0 if dr == 0 else 31, :], 0.0, [R("Sbf")])
            DEAD = [R(nm, hf) for nm in ("E", "LF") for hf in range(2)]
            SGR = [R("SG", 0), R("SG", 1)]
            n = 0
            cprev = None
            for bgi in range(4):
                bg = bgi if dr == 0 else 3 - bgi
                ub = [bank() for _ in range(2)]
                for u in range(4):
                    b = 4 * bg + u
                    for j in range(2):
                        MM(ub[j][0][:, u * 128:(u + 1) * 128], ktok[64 * j:64 * j + 64, b, :], vtok[64 * j:64 * j + 64, b, :],
                           True, True, [R("ktok"), R("vtok")], [ub[j][1]], inc=(u == 3 and j == 1), tp=(64 * j, 0))
                order = [(u, j) for u in range(4) for j in range(2)]
                if dr == 1:
                    order = order[::-1]
                for (u, j) in order:
                    c = 2 * (4 * bg + u) + j
                    Uc = ub[j][0][:, u * 128:(u + 1) * 128]
                    if n == 0:
                        CP("dve", Tbig[:, c, :], Uc, [], [ub[j][1], R("Tb", c)] + DEAD)
                    else:
                        dprev = EB[:, 64 * cprev + 63: 64 * cprev + 64] if dr == 0 else EB[:, 64 * cprev: 64 * cprev + 1]
                        STT("dve", Tbig[:, c, :], Tbig[:, cprev, :], dprev, Uc, ALU.mult, ALU.add,
                            [R("Tb", cprev)] + SGR, [ub[j][1], R("Tb", c)])
                    cprev = c
                    n += 1
            EB3 = EB.rearrange("p (c t) -> p c t", t=64)
            if dr == 0:
                TT("dve", Sbf[:, 1:32, :], Tbig[:, 0:31, :], EB3[:, 0:31, 63:64].broadcast_to([128, 31, 128]), ALU.mult,
                   [R("Tb", cprev)] + SGR + DEAD, [R("Sbf")])
            else:
                TT("dve", Sbf[:, 0:31, :], Tbig[:, 1:32, :], EB3[:, 1:32, 0:1].broadcast_to([128, 31, 128]), ALU.mult,
                   [R("Tb", cprev)] + SGR + DEAD, [R("Sbf")])
            if dr == 0:
                fm_proj(2, B, "B")
                if hh < 3:
                    load_Wh(hh + 1)
                else:
                    load_w(hT, w_xkv_d, [R("Wkv")], also_wait=ALLH)
            chk(2.6)
            for bg in range(4):
                bk, bkR = bank()
                for u in range(4):
                    b = 4 * bg + u
                    us = slice(u * 128, (u + 1) * 128)
                    MM(bk[:, us], AT[:, b, :], vtok[:, b, :], True, False, [R("AT"), R("vtok")], [bkR], inc=False)
                    for j in range(2):
                        c = 2 * b + j
                        MM(bk[64 * j:64 * j + 64, us], qtl[:, b * 128 + 64 * j: b * 128 + 64 * j + 64], Sbf[:, c, :],
                           False, j == 1, [R("qtl", bg // 2), R("Sbf")], [bkR], inc=(u == 3 and j == 1), tp=(0, 64 * j))
                o3 = Of3[:, 4 * bg:4 * bg + 4, :]
                if dr == 0:
                    ACT(o3, bk.rearrange("p (u v) -> p u v", v=128), AF.Copy, [], [bkR, R("Of")])
                else:
                    TT("dve", o3, bk.rearrange("p (u v) -> p u v", v=128), o3, ALU.add, [], [bkR, R("Of")])
        chk(2.7)
        ACT(fb["E"], Of, AF.Square, [R("Of")], [R("E", 0), R("E", 1)])
        ss = stat[:, 40:56]
        rs = stat2[:, 40:56]
        P.op("dve", lambda h: h.tensor_reduce(out=ss, in_=fb["E"].rearrange("p (b v) -> p b v", v=128), axis=AX.X, op=ALU.add),
             reads=[R("E", 0), R("E", 1)], writes=[R("ssh")])
        TS("dve", rs, ss, 1.0 / 128, EPS, ALU.mult, ALU.add, [R("ssh")], [R("rsh")])
        ACT(rs, rs, AF.Ln, [R("rsh")], [R("rsh")])
        ACT(rs, rs, AF.Exp, [R("rsh")], [R("rsh")], scale=-0.5)
        SG3 = fb["SG"].rearrange("p (b v) -> p b v", v=128)
        TT("dve", SG3, Of3, rs.unsqueeze(2).broadcast_to([128, NB, 128]), ALU.mult, [R("Of"), R("rsh")], [R("SG", 0), R("SG", 1)])
        TT("dve", SG3, SG3, hgoB.unsqueeze(1).broadcast_to([128, NB, 128]), ALU.mult, [R("SG", 0), R("SG", 1), R("hgoB")],
           [R("SG", 0), R("SG", 1)])
        TT("dve", recb, SG3, gate3, ALU.mult, [R("SG", 0), R("SG", 1), R("gate")], [R("recb")])
        for half in range(2):
            for i in range(8):
                b = half * 8 + i
                TR(psb[:, half, i * 128:(i + 1) * 128], recb[:, b, :], [R("recb"), R("ident")], [R("psb", half)], inc=(i == 7))
            CP("act" if half else "dve", mixT[:, 4 + hh, half * 1024:(half + 1) * 1024], psb[:, half, :], [],
               [R("psb", half), R("mixT", 4 + hh, 2 * half), R("mixT", 4 + hh, 2 * half + 1)])

    if not PREFETCHED_WH0:
        load_Wh(0)
    for hh in range(4):
        hgrn_head(hh)
        chk(2.8 + 0.01 * hh)

    if "rec" in dbg_d:
        tmpf = fb["B"]
        for c in range(4):
            CP("dve", tmpf, mixT[:, 4 + c, :], [R("mixT", 4 + c, qt) for qt in range(4)], [R("dbgtmp")])
            P.dma("sp", dbg_d["rec"][c * 128:(c + 1) * 128, :], tmpf, reads=[R("dbgtmp")])
    if stop <= 3:
        return finish(nc, P)
    barrier(skip=("Wkv",))

    junk3 = junk[:].rearrange("p (u n) -> p u n", n=512)
    gpost3 = gpost[:].rearrange("p (u n) -> p u n", n=512)

    def proj_norm_residual(b, lhs_aps, lhs_reads, w_fn, w_reads, resid_ap, resid_reads, out_ap, out_writes):
        k = b % 3
        n = len(lhs_aps)
        for half in range(2):
            for ci in range(n):
                MM(psf[:, k, half, :], lhs_aps[ci], w_fn(ci, half), ci == 0, ci == n - 1, lhs_reads + w_reads[ci],
                   [R("psf", k, half)], inc=(half == 1 and ci == n - 1))
        ss = stat[:, 16 + b:17 + b]
        rs = stat2[:, 16 + b:17 + b]
        PB = [R("psf", k, 0), R("psf", k, 1)]
        ACT(junk3, psf[:, k, :, :], AF.Square, [], PB + [R("junk"), R("pss", b)], accum=ss)
        ACT(rs, ss, AF.Ln, [R("pss", b), R("epsc")], [R("prs", b)], scale=1.0 / D, bias=epsc[:])
        ACT(rs, rs, AF.Exp, [R("prs", b)], [R("prs", b)], scale=-0.5)
        STT("dve", psf[:, k, :, :], psf[:, k, :, :], rs, gpost3, ALU.mult, ALU.mult, [R("prs", b), R("gpost")], PB)
        TT("dve", out_ap.rearrange("p (u n) -> p u n", n=512), resid_ap.rearrange("p (u n) -> p u n", n=512), psf[:, k, :, :],
           ALU.add, resid_reads, PB + out_writes)

    def prenorm_block(b, src_ap, src_reads):
        k = b % 2
        ss = stat[:, 32 + b:33 + b]
        rs = stat2[:, 32 + b:33 + b]
        ACT(junk[:], src_ap, AF.Square, src_reads, [R("junk"), R("fss", b)], accum=ss)
        ACT(rs, ss, AF.Ln, [R("fss", b), R("epsc")], [R("frs", b)], scale=1.0 / D, bias=epsc[:])
        ACT(rs, rs, AF.Exp, [R("frs", b)], [R("frs", b)], scale=-0.5)
        STT("dve", hb[:, k, :], src_ap, rs, gpre[:], ALU.mult, ALU.mult, src_reads + [R("frs", b), R("gpre")], [R("hb", k)])

    def prenorm_block_tr(b):
        k = b % 2
        for c in range(8):
            TR(psb[:, k, c * 128:(c + 1) * 128], hb[:, k, c * 128:(c + 1) * 128], [R("hb", k), R("ident")], [R("psb", k)],
               inc=(c == 7))
        CP("act", hT[:, :, b * 128:(b + 1) * 128], psb[:, k, :].rearrange("p (c t) -> p c t", t=128), [], [R("psb", k), R("h", b)])

    o2 = 0
    Wo, o2 = a2_bf(o2, 8 * 1024); Wo = Wo.rearrange("p (c n) -> p c n", n=1024)
    load_w(Wo, w_out_d, [R("Wo")])
    gain_load(gpost, 1, R("gpost"))
    Wkv = hT
    Wq, _ = a2_bf(16384, 8 * 1024); Wq = Wq.rearrange("p (c n) -> p c n", n=1024)
    Wo2, _ = a2_bf(32768, 8 * 1024); Wo2 = Wo2.rearrange("p (c n) -> p c n", n=1024)
    load_w(Wq, w_xq_d, [R("Wq")])
    load_w(Wo2, w_xo_d, [R("Wo2")])
    for b in range(14):
        P.dma("sp" if b % 2 else "act", X[:, b, :], x_d[b * 128:(b + 1) * 128, :], writes=[R("X", b)])
    o2 = 49152
    memT, o2 = a2_bf(o2, 8 * 256); memT = memT.rearrange("p (c m) -> p c m", m=256)
    KxT, o2 = a2_bf(o2, 8 * 256); KxT = KxT.rearrange("p (j m) -> p j m", m=256)
    Vx, o2 = a2_bf(o2, 2 * 1024); Vx = Vx.rearrange("p (m n) -> p m n", n=1024)
    assert o2 <= 63488, o2
    for mb in range(2):
        P.dma("sp", X[:, 14 + mb, :], mem_d[mb * 128:(mb + 1) * 128, :], writes=[R("X", 14 + mb)])
    prenorm_T(3, nblk=2, src=X[:, 14:16, :], dstT=memT, tag="memT", srcR=lambda b: R("X", 14 + b))
    MT = [R("memT", 0), R("memT", 1)]
    for j in range(8):
        bk, bkR = bank()
        for c in range(8):
            MM(bk[:, 0:256], Wkv[:, c, j * 128:(j + 1) * 128], memT[:, c, :], c == 0, c == 7, MT + [R("Wkv")], [bkR], inc=(c == 7))
        CP("act" if j % 2 else "dve", KxT[:, j, :], bk[:, 0:256], [], [bkR, R("KxT")])
    for mb in range(2):
        for half in range(2):
            bk, bkR = bank()
            for c in range(8):
                MM(bk, memT[:, c, mb * 128:(mb + 1) * 128], Wkv[:, c, 1024 + half * 512: 1024 + (half + 1) * 512], c == 0, c == 7,
                   MT + [R("Wkv")], [bkR], inc=(c == 7))
            CP("act" if half else "dve", Vx[:, mb, half * 512:(half + 1) * 512], bk, [], [bkR, R("Vx")])
    for b in range(14, 16):
        P.dma("sp" if b % 2 else "act", X[:, b, :], x_d[b * 128:(b + 1) * 128, :], writes=[R("X", b)])
    gain_load(gpre, 2, R("gpre"))
    P._deps("act", [], [R("Wkv")])
    for b in range(NB + 2):
        if b < NB:
            tokb = slice(b * 128, (b + 1) * 128)
            proj_norm_residual(b, [mixT[:, c, tokb] for c in range(8)], [R("mixT", c, b // 4) for c in range(8)],
                               lambda ci, half: Wo[:, ci, half * 512:(half + 1) * 512], [[R("Wo")]] * 8,
                               X[:, b, :], [R("X", b)], X[:, b, :], [R("X", b)])
        if 1 <= b <= NB:
            prenorm_block(b - 1, X[:, b - 1, :], [R("X", b - 1)])
        if b >= 2:
            prenorm_block_tr(b - 2)
    if "x1" in dbg_d:
        for b in range(NB):
            P.dma("sp", dbg_d["x1"][b * 128:(b + 1) * 128, :], X[:, b, :], reads=[R("X", b)])
    if stop <= 4:
        return finish(nc, P)
    barrier(skip=("Wkv", "Wq", "Wo2"))

    o2 = 0
    Qx, o2 = a2_bf(o2, 8 * 512); Qx = Qx.rearrange("p (j n) -> p j n", n=512)
    PTx = []
    for i in range(2):
        v, o2 = a2_bf(o2, 1024); PTx.append(v.rearrange("p (u n) -> p u n", n=512))
    rcx2 = []
    for i in range(2):
        v, o2 = a2_f(o2, 512); rcx2.append(v)
    assert o2 <= 16384, o2
    gain_load(gpost, 4, R("gpost"))
    for qt in range(4):
        qs_ = slice(qt * 512, (qt + 1) * 512)
        for j in range(8):
            bk, bkR = bank()
            for c in range(8):
                MM(bk, Wq[:, c, j * 128:(j + 1) * 128], hT[:, c, qs_], c == 0, c == 7, ALLH[qt * 4:qt * 4 + 4] + [R("Wq")],
                   [bkR], inc=(c == 7))
            CP("act" if j % 2 else "dve", Qx[:, j, :], bk, [], [bkR, R("Qx", j)])
        for hx in range(4):
            k = hx % 2
            for mb in range(2):
                for half in range(2):
                    MM(psf[:, k, mb, :], KxT[:, 2 * hx + half, mb * 128:(mb + 1) * 128], Qx[:, 2 * hx + half, :], half == 0, half == 1,
                       [R("KxT"), R("Qx", 2 * hx + half)], [R("psf", k, mb)], inc=(mb == 1 and half == 1))
            ACT(PTx[k], psf[:, k, :, :], AF.Exp, [], [R("psf", k, 0), R("psf", k, 1), R("PTx", k)], scale=1.0 / 16)
            dbk, dbkR = psbf[k], R("psb", k)
            for mb in range(2):
                MM(dbk, ones_bf[:], PTx[k][:, mb, :], mb == 0, mb == 1, [R("ones"), R("PTx", k)], [dbkR], inc=(mb == 1))
            ACT(rcx2[k], dbk, AF.Ln, [], [dbkR, R("rcx", k)])
            ACT(rcx2[k], rcx2[k], AF.Exp, [R("rcx", k)], [R("rcx", k)], scale=-1.0)
            for dh in range(2):
                bk, bkR = psf[:, 2, dh, :], R("psf", 2, dh)
                for mb in range(2):
                    MM(bk, Vx[:, mb, hx * 256 + dh * 128: hx * 256 + (dh + 1) * 128], PTx[k][:, mb, :], mb == 0, mb == 1,
                       [R("Vx"), R("PTx", k)], [bkR], inc=(mb == 1))
                TT("dve", mixT[:, 2 * hx + dh, qs_], bk, rcx2[k], ALU.mult, [R("rcx", k)], [bkR, R("mixT", 2 * hx + dh, qt)])
    gain_load(gpre, 5, R("gpre"))
    for b in range(NB + 2):
        if b < NB:
            tokb = slice(b * 128, (b + 1) * 128)
            proj_norm_residual(b, [mixT[:, c, tokb] for c in range(8)], [R("mixT", c, b // 4) for c in range(8)],
                               lambda ci, half: Wo2[:, ci, half * 512:(half + 1) * 512], [[R("Wo2")]] * 8,
                               X[:, b, :], [R("X", b)], X[:, b, :], [R("X", b)])
        if 1 <= b <= NB:
            prenorm_block(b - 1, X[:, b - 1, :], [R("X", b - 1)])
        if b >= 2:
            prenorm_block_tr(b - 2)
    if "x2" in dbg_d:
        for b in range(NB):
            P.dma("sp", dbg_d["x2"][b * 128:(b + 1) * 128, :], X[:, b, :], reads=[R("X", b)])
    if stop <= 5:
        return finish(nc, P)
    barrier()

    gain_load(gpost, 6, R("gpost"))
    for b in range(NB):
        P.dma("sp" if b % 2 else "act", out_d[b * 128:(b + 1) * 128, :], X[:, b, :], reads=[R("X", b)], writes=[R("outd", b)])
    barrier()
    aT16, _ = a1_bf(0, [16 * N_TOK]); aT16 = aT16.rearrange("p (j t) -> p j t", t=N_TOK)
    aT = [aT16[:, j, :] for j in range(16)] + [mixT[:, j, :] for j in range(6)]
    Wup = [mixT[:, 6 + i, :].rearrange("p (c n) -> p c n", n=256) for i in range(2)]
    o2 = 0
    ugb = []; uvb = []; agb = []; avb = []; ebb = []
    for i in range(2):
        v, o2 = a2_f(o2, 2050); ugb.append(v)
        v, o2 = a2_f(o2, 2050); uvb.append(v)
        v, o2 = a2_f(o2, 1024); agb.append(v)
        v, o2 = a2_f(o2, 1024); avb.append(v)
        v, o2 = a2_f(o2, 1024); ebb.append(v)
    assert o2 <= 63488, o2
    for i in range(2):
        for ub_, nm in ((ugb[i], "ug"), (uvb[i], "uv")):
            MEMSET("pool", ub_[:, 0:1], 0.0, [R(nm, i)])
            MEMSET("pool", ub_[:, 2049:2050], 0.0, [R(nm, i)])

    bank8_i = [0]

    def bank8():
        i = bank8_i[0] % 8
        bank8_i[0] += 1
        if i < 6:
            return psf[:, i // 2, i % 2, :], R("psf", i // 2, i % 2)
        return psbf[i - 6], R("psb", i - 6)

    def ffn_mm(j):
        s_ = j % 2
        wu = Wup[s_]
        P.dma("pool", wu[:, :, 0:128], w_up_d[:, j * 128:(j + 1) * 128].rearrange("(c p) n -> p c n", p=128),
              writes=[R("Wup", s_)])
        P.dma("pool", wu[:, :, 128:256],
              w_up_d[:, D_FF + j * 128: D_FF + (j + 1) * 128].rearrange("(c p) n -> p c n", p=128), writes=[R("Wup", s_)])
        for qt in range(4):
            qs_ = slice(qt * 512, (qt + 1) * 512)
            bg_, bgR = bank8()
            bv_, bvR = bank8()
            for c in range(8):
                MM(bg_, wu[:, c, 0:128], hT[:, c, qs_], c == 0, c == 7, ALLH[qt * 4:qt * 4 + 4] + [R("Wup", s_)], [bgR], inc=False)
            for c in range(8):
                MM(bv_, wu[:, c, 128:256], hT[:, c, qs_], c == 0, c == 7, ALLH[qt * 4:qt * 4 + 4] + [R("Wup", s_)], [bvR],
                   inc=(c == 7))
            ACT(ugb[s_][:, 1 + qt * 512: 1 + (qt + 1) * 512], bg_, AF.Copy, [], [bgR, R("ug", s_)])
            ACT(uvb[s_][:, 1 + qt * 512: 1 + (qt + 1) * 512], bv_, AF.Copy, [], [bvR, R("uv", s_)])

    def ffn_ew(j):
        s_ = j % 2
        ug, uv, ag, av, eb_ = ugb[s_], uvb[s_], agb[s_], avb[s_], ebb[s_]
        for hf in range(2):
            base = 1 + hf * 1024
            ts_ = slice(hf * 1024, (hf + 1) * 1024)
            cw = lambda cj, i: convT[:, cj * 4 + i: cj * 4 + i + 1]
            CT = [R("convT")]
            for (u_, uR, cj, dst, dR) in ((ug, R("ug", s_), j, ag, R("ag", s_)), (uv, R("uv", s_), 22 + j, av, R("av", s_))):
                ACT(dst, u_[:, base:base + 1024], AF.Identity, [uR] + CT, [dR], scale=cw(cj, 1), bias=cw(cj, 3))
                STT("dve", dst, u_[:, base - 1:base + 1023], cw(cj, 0), dst, ALU.mult, ALU.add, [uR] + CT, [dR])
                STT("dve", dst, u_[:, base + 1:base + 1025], cw(cj, 2), dst, ALU.mult, ALU.add, [uR] + CT, [dR])
            ACT(eb_, ag, AF.Silu, [R("ag", s_)], [R("eb", s_)])
            TT("dve", aT[j][:, ts_], eb_, av, ALU.mult, [R("eb", s_), R("av", s_)], [R("aT", j)])

    for j in range(23):
        if j < 22:
            ffn_mm(j)
        if j >= 1:
            ffn_ew(j - 1)
    barrier()
    Wd, o2 = a2_bf(0, 22 * 1024); Wd = Wd.rearrange("p (j n) -> p j n", n=1024)
    wstg = []
    for i in range(3):
        v, o2 = a2_f(o2, 1024); wstg.append(v)
    assert o2 <= 63488, o2
    nst = 0
    for j in range(22):
        if j % 2 == 0:
            P.dma("pool", Wd[:, j, :], w_down_d[j * 128:(j + 1) * 128, :], writes=[R("Wd", j)])
        else:
            si = nst % 3
            nst += 1
            P.dma("sp", wstg[si], w_down_d[j * 128:(j + 1) * 128, :], writes=[R("wstg", si)])
            CP("act", Wd[:, j, :], wstg[si], [R("wstg", si)], [R("Wd", j)])
    hTf = hT[:].rearrange("p c t -> p (c t)").bitcast(F32)
    xb_ = [hTf[:, 0:1024], hTf[:, 1024:2048]]
    ob_ = [hTf[:, 2048:3072], hTf[:, 3072:4096]]
    for b in range(NB):
        tokb = slice(b * 128, (b + 1) * 128)
        k = b % 2
        P.dma("sp", xb_[k], out_d[b * 128:(b + 1) * 128, :], reads=[R("outd", b)] + ALLH, writes=[R("xb", k)])
        proj_norm_residual(b, [aT[j][:, tokb] for j in range(22)], [R("aT", j) for j in range(22)],
                           lambda ci, half: Wd[:, ci, half * 512:(half + 1) * 512], [[R("Wd", j)] for j in range(22)],
                           xb_[k], [R("xb", k)], ob_[k], [R("ob", k)])
        P.dma("sp", out_d[b * 128:(b + 1) * 128, :], ob_[k], reads=[R("ob", k)], writes=[R("outd", b)])

    return finish(nc, P)


def finish(nc, P):
    print("COUNTS", {e: P.q[e].total for e in P.ENGS}, "dma", sum(P.dma_cnt), max(P.dma_cnt), flush=True)
    P.wait_all("sp")
    P.emit()
    P.close()
    return nc


def rope_tables():
    rows = N_TOK // 64
    r, c = np.meshgrid(np.arange(rows), np.arange(64), indexing="ij")
    inv = np.power(np.float32(10000.0), -np.arange(16, dtype=np.float32) / np.float32(16)).astype(np.float32)
    ang = np.concatenate([r.reshape(-1, 1).astype(np.float32) * inv, c.reshape(-1, 1).astype(np.float32) * inv], -1)
    cos = np.cos(ang).astype(np.float32)
    sin = np.sin(ang).astype(np.float32)
    f = lambda a: a.reshape(16, 128, 32).transpose(1, 0, 2).reshape(128, 512)
    return np.ascontiguousarray(np.concatenate([f(cos), f(sin)], 1))


def masks():
    s = np.arange(128)[:, None]
    t = np.arange(128)[None, :]
    same = (s // 64) == (t // 64)
    fwd = (same & (s <= t)).astype(np.float32)
    bwd = (same & (s >= t)).astype(np.float32)
    return np.concatenate([fwd, bwd], 1).astype(ml_dtypes.bfloat16)


def prep_inputs(inp):
    f32 = lambda a: np.ascontiguousarray(np.asarray(a, dtype=np.float32))
    shared = {
        "w_in": f32(inp["w_in"][0]), "w_out": f32(inp["w_out"][0]), "w_xq": f32(inp["w_xq"][0]),
        "w_xkv": f32(inp["w_xkv"][0]), "w_xo": f32(inp["w_xo"][0]), "w_up": f32(inp["w_up"][0]),
        "w_down": f32(inp["w_down"][0]),
        "gains": f32(np.stack([inp["pre_mix_g"][0], inp["post_mix_g"][0], inp["pre_x_g"][0], inp["mem_norm_g"][0],
                               inp["post_x_g"][0], inp["pre_ffn_g"][0], inp["post_ffn_g"][0]])),
        "qkg": f32(np.concatenate([inp["q_norm_g"][0], inp["k_norm_g"][0]])[None, :]),
        "hgo": f32(np.asarray(inp["hg_out_norm_g"][0])[None, :]),
        "lbT": f32(np.asarray(inp["hg_lb"]).reshape(2, 2, 4, 128).transpose(3, 0, 1, 2).reshape(128, 16)),
        "convT": f32(np.concatenate([np.asarray(inp["conv_w"][0]), np.asarray(inp["conv_b"][0])[None, :]], 0)
                     .reshape(4, 44, 128).transpose(2, 1, 0).reshape(128, 176)),
        "ident": np.eye(128, dtype=np.float32).astype(ml_dtypes.bfloat16),
        "rope": rope_tables(),
        "mask": masks(),
    }
    x = np.asarray(inp["x"], dtype=np.float32)
    mem = np.asarray(inp["mem"], dtype=np.float32)
    maps = []
    for i in range(8):
        d = dict(shared)
        d["x"] = np.ascontiguousarray(x[i])
        d["mem"] = np.ascontiguousarray(mem[i])
        maps.append(d)
    return maps


_NC_CACHE = {}


def kernel(**inputs):
    if "nc" not in _NC_CACHE:
        _NC_CACHE["nc"] = build()
    nc = _NC_CACHE["nc"]
    maps = prep_inputs(inputs)
    res = run_bass_kernel_spmd(nc, maps, core_ids=list(range(8)))
    return np.stack([np.asarray(r["out"], dtype=np.float32) for r in res.results], 0)
```

```python
import numpy as np
import ml_dtypes
import concourse.bass as bass
import concourse.mybir as mybir
from concourse.bass_utils import run_bass_kernel_spmd

F32 = mybir.dt.float32
BF16 = mybir.dt.bfloat16
AF = mybir.ActivationFunctionType
ALU = mybir.AluOpType
AX = mybir.AxisListType

N_TOK = 2048
D = 1024
NB = 16
EPS = 1e-6
D_FF = 2816
N_IN = 3328


class Res:
    __slots__ = ("name", "w", "r", "big")

    def __init__(self, name):
        self.name = name
        self.w = None
        self.r = {}
        self.big = False


class EngQ:
    def __init__(self, name):
        self.name = name
        self.ops = []
        self.epoch = 0
        self.cnt = 0
        self.total = 0
        self.waited = {}
        self.pending = False


class Prog:
    ENGS = ("pe", "act", "dve", "pool", "sp")
    LIMIT = 900
    NEPOCH = {"pe": 14, "act": 10, "dve": 14, "pool": 4, "sp": 1}

    def __init__(self, nc, n_dma_sems=32):
        self.nc = nc
        self.q = {e: EngQ(e) for e in self.ENGS}
        self.sems = {}
        self.n_dma_sems = n_dma_sems
        self.dma_cnt = [0] * n_dma_sems
        self.dma_pool = {"pool": list(range(0, 16)), "sp": list(range(16, 26)), "act": list(range(26, 32))}
        self.dma_rr = {"pool": 0, "sp": 0, "act": 0}
        self._ctx = []
        self.res = {}

    def R(self, *key):
        r = self.res.get(key)
        if r is None:
            r = Res(key)
            self.res[key] = r
        return r

    def open(self):
        nc = self.nc
        for e in self.ENGS:
            for ep in range(self.NEPOCH[e]):
                c = nc.semaphore("s_%s%d" % (e, ep))
                self.sems[(e, ep)] = c.__enter__()
                self._ctx.append(c)
        for i in range(self.n_dma_sems):
            c = nc.semaphore("s_dma%d" % i)
            self.sems[("dma", i)] = c.__enter__()
            self._ctx.append(c)

    def close(self):
        for c in reversed(self._ctx):
            c.__exit__(None, None, None)

    def _need(self, eng, tok, same_ok):
        if tok is None:
            return
        key, val = tok
        q = self.q[eng]
        if key[0] == "dma":
            if q.waited.get(key, 0) >= val:
                return
            q.waited[key] = val
        else:
            src, ep = key
            if src == eng and not same_ok:
                return
            if q.waited.get(src, (-1, 0)) >= (ep, val):
                return
            q.waited[src] = (ep, val)
        sem = self.sems[key]
        q.ops.append(lambda h, sem=sem, val=val: h.wait_ge(sem, val))

    def _deps(self, eng, reads, writes):
        for r in reads:
            self._need(eng, r.w, same_ok=(eng != "pe" and not r.big))
        so = (eng != "pe")
        for w in writes:
            self._need(eng, w.w, same_ok=so)
            for tok in w.r.values():
                self._need(eng, tok, same_ok=so)

    def op(self, eng, fn, reads=(), writes=(), inc=True, big=False):
        q = self.q[eng]
        if q.cnt >= self.LIMIT and not q.pending:
            q.epoch += 1
            q.cnt = 0
            assert q.epoch < self.NEPOCH[eng], "out of epochs for " + eng
        self._deps(eng, reads, writes)
        key = (eng, q.epoch)
        val = q.cnt + 1
        tok = (key, val)
        if inc:
            q.cnt = val
            q.total += 1
            sem = self.sems[key]
            q.ops.append(lambda h, fn=fn, sem=sem: fn(h).then_inc(sem, 1))
            q.pending = False
        else:
            q.ops.append(lambda h, fn=fn: fn(h))
            q.pending = True
        for r in reads:
            r.r[eng] = tok
        for w in writes:
            w.w = tok
            w.r = {}
            w.big = big
        return tok

    def dma(self, eng, out, in_, reads=(), writes=(), also_wait=(), **kw):
        pl = self.dma_pool[eng]
        i = pl[self.dma_rr[eng] % len(pl)]
        self.dma_rr[eng] += 1
        key = ("dma", i)
        if self.dma_cnt[i] > 0:
            self._need(eng, (key, 16 * self.dma_cnt[i]), same_ok=True)
        self._deps(eng, reads, list(writes) + list(also_wait))
        self.dma_cnt[i] += 1
        assert self.dma_cnt[i] < 60
        val = 16 * self.dma_cnt[i]
        sem = self.sems[key]
        q = self.q[eng]
        q.ops.append(lambda h, sem=sem, out=out, in_=in_, kw=kw:
                     h.dma_start(out=out, in_=in_, **kw).then_inc(sem, 16))
        tok = (key, val)
        for r in reads:
            r.r[key] = tok
        for w in writes:
            w.w = tok
            w.r = {}
            w.big = False
        return tok

    def wait_all(self, eng, skip=()):
        for r in self.res.values():
            if r.name[0] in skip:
                continue
            self._need(eng, r.w, same_ok=True)
            for tok in list(r.r.values()):
                self._need(eng, tok, same_ok=True)

    def emit(self):
        nc = self.nc
        for e in self.ENGS:
            assert not self.q[e].pending, "engine %s ends with non-inc'd instr" % e
        with nc.Block() as block:
            @block.tensor
            def _(h):
                for f in self.q["pe"].ops:
                    f(h)

            @block.scalar
            def _(h):
                for f in self.q["act"].ops:
                    f(h)

            @block.vector
            def _(h):
                for f in self.q["dve"].ops:
                    f(h)

            @block.gpsimd
            def _(h):
                for f in self.q["pool"].ops:
                    f(h)

            @block.sync
            def _(h):
                for f in self.q["sp"].ops:
                    f(h)


class StopBuild(Exception):
    pass


def build(stop=99, dbg=()):
    st = {}
    try:
        return _build(stop, dbg, st)
    except StopBuild:
        return finish(st["nc"], st["P"])


def _build(stop, dbg, st):
    nc = bass.Bass("TRN2", target_bir_lowering=False)
    P = Prog(nc)
    P.open()
    R = P.R
    st["nc"] = nc
    st["P"] = P

    def chk(level):
        if stop <= level:
            raise StopBuild()

    def din(name, shape, dt=F32):
        return nc.dram_tensor(name, list(shape), dt, kind="ExternalInput").ap()

    x_d = din("x", [N_TOK, D])
    mem_d = din("mem", [256, D])
    w_in_d = din("w_in", [D, N_IN])
    w_out_d = din("w_out", [D, D])
    w_xq_d = din("w_xq", [D, D])
    w_xkv_d = din("w_xkv", [D, 2 * D])
    w_xo_d = din("w_xo", [D, D])
    w_up_d = din("w_up", [D, 2 * D_FF])
    w_down_d = din("w_down", [D_FF, D])
    gains_d = din("gains", [7, D])
    qkg_d = din("qkg", [1, 128])
    hgo_d = din("hgo", [1, 128])
    lbT_d = din("lbT", [128, 16])
    convT_d = din("convT", [128, 44 * 4])
    ident_d = din("ident", [128, 128], BF16)
    rope_d = din("rope", [128, 2 * 16 * 32])
    mask_d = din("mask", [128, 2 * 128], BF16)
    out_d = nc.dram_tensor("out", [N_TOK, D], F32, kind="ExternalOutput").ap()
    dbg_d = {}
    for name, shape in dbg:
        dbg_d[name] = nc.dram_tensor("dbg_" + name, list(shape), F32, kind="ExternalOutput").ap()

    def sb(name, shape, dt):
        return nc.alloc_sbuf_tensor("sb_" + name, list(shape), dt)

    hT = sb("hT", [128, 8, N_TOK], BF16)
    mixT = sb("mixT", [128, 8, N_TOK], BF16)
    arena1 = sb("arena1", [128, 16384], F32)
    arena2 = sb("arena2", [128, 15872], F32)
    ident = sb("ident", [128, 128], BF16)
    ones_bf = sb("ones_bf", [128, 128], BF16)
    gpre = sb("gpre", [128, D], F32)
    gpost = sb("gpost", [128, D], F32)
    stat = sb("stat", [128, 64], F32)
    stat2 = sb("stat2", [128, 64], F32)
    junk = sb("junk", [128, D], BF16)
    hb = sb("hb", [128, 2, D], BF16)
    lbT = sb("lbT", [128, 16], F32)
    lbv = sb("lbv", [128, 40], F32)
    convT = sb("convT", [128, 44 * 4], F32)
    epsc = sb("epsc", [128, 1], F32)
    psf = nc.alloc_psum_tensor("psf", [128, 3, 2, 512], F32)
    psb = nc.alloc_psum_tensor("psb", [128, 2, 1024], BF16)

    X = arena1[:].rearrange("p (b d) -> p b d", d=D)

    def a1_bf(off_bytes, shape):
        n = int(np.prod(shape))
        v = arena1[:, off_bytes // 4: off_bytes // 4 + n // 2].bitcast(BF16)
        return v, off_bytes + n * 2

    def a2_bf(off_bytes, n):
        v = arena2[:, off_bytes // 4: off_bytes // 4 + n // 2].bitcast(BF16)
        return v, off_bytes + n * 2

    def a2_f(off_bytes, n):
        v = arena2[:, off_bytes // 4: off_bytes // 4 + n]
        return v, off_bytes + n * 4

    def a1_f(off_bytes, n):
        v = arena1[:, off_bytes // 4: off_bytes // 4 + n]
        return v, off_bytes + n * 4

    def MM(out, lhsT, rhs, start, stop, reads, writes, inc=True, tp=None):
        if tp is None:
            return P.op("pe", lambda h: h.matmul(out, lhsT=lhsT, rhs=rhs, start=start, stop=stop),
                        reads=reads, writes=writes, inc=inc)
        return P.op("pe", lambda h: h.matmul(out, lhsT=lhsT, rhs=rhs, start=start, stop=stop, tile_position=tp),
                    reads=reads, writes=writes, inc=inc)

    def TR(out, in_, reads, writes, inc=True):
        return P.op("pe", lambda h: h.transpose(out, in_, ident[:]), reads=reads, writes=writes, inc=inc)

    BIG_N = 128

    def isbig(ap):
        n = 1
        for d_ in ap.shape[1:]:
            n *= d_
        return n >= BIG_N

    def ACT(out, in_, func, reads, writes, scale=None, bias=None, accum=None):
        kw = {}
        if scale is not None:
            kw["scale"] = scale
        if bias is not None:
            kw["bias"] = bias
        if accum is not None:
            kw["accum_out"] = accum
        return P.op("act", lambda h: h.activation(out=out, in_=in_, func=func, **kw), reads=reads, writes=writes,
                    big=(accum is None and isbig(out)))

    def TT(eng, out, in0, in1, op, reads, writes):
        return P.op(eng, lambda h: h.tensor_tensor(out=out, in0=in0, in1=in1, op=op), reads=reads, writes=writes, big=isbig(out))

    def TS(eng, out, in0, s1, s2, op0, op1, reads, writes):
        if s2 is None:
            return P.op(eng, lambda h: h.tensor_scalar(out=out, in0=in0, scalar1=s1, scalar2=None, op0=op0),
                        reads=reads, writes=writes, big=isbig(out))
        return P.op(eng, lambda h: h.tensor_scalar(out=out, in0=in0, scalar1=s1, scalar2=s2, op0=op0, op1=op1),
                    reads=reads, writes=writes, big=isbig(out))

    def STT(eng, out, in0, scalar, in1, op0, op1, reads, writes):
        return P.op(eng, lambda h: h.scalar_tensor_tensor(out=out, in0=in0, scalar=scalar, in1=in1, op0=op0, op1=op1),
                    reads=reads, writes=writes, big=isbig(out))

    def CP(eng, out, in_, reads, writes):
        if eng == "act":
            return ACT(out, in_, AF.Copy, reads, writes)
        return P.op(eng, lambda h: h.tensor_copy(out=out, in_=in_), reads=reads, writes=writes, big=isbig(out))

    def RECIP(out, in_, reads, writes):
        return P.op("dve", lambda h: h.reciprocal(out=out, in_=in_), reads=reads, writes=writes)

    def MEMSET(eng, ap, val, writes):
        return P.op(eng, lambda h: h.memset(ap, val), writes=writes)

    def barrier(skip=()):
        for e in Prog.ENGS:
            P.wait_all(e, skip)

    def load_w(dst, src_rows_cols, reads_w, also_wait=()):
        src = src_rows_cols.rearrange("(c p) n -> p c n", p=128)
        nch = src.shape[1]
        for c in range(nch):
            P.dma("pool", dst[:, c, :], src[:, c, :], writes=reads_w, also_wait=also_wait)

    def gain_load(dst, row, res):
        P.dma("sp", dst[:], gains_d[row:row + 1, :].partition_broadcast(128), writes=[res])

    def dump(name, ap_sb, reads, view=None):
        if name in dbg_d:
            P.dma("sp", dbg_d[name] if view is None else view(dbg_d[name]), ap_sb, reads=reads)

    P.dma("sp", ident[:], ident_d, writes=[R("ident")])
    P.dma("sp", lbT[:], lbT_d, writes=[R("lbT")])
    P.dma("sp", convT[:], convT_d, writes=[R("convT")])
    MEMSET("pool", ones_bf[:], 1.0, [R("ones")])
    MEMSET("pool", epsc[:], EPS, [R("epsc")])
    MEMSET("pool", stat[:], 0.0, [R("stat")])
    MEMSET("pool", stat2[:], 0.0, [R("stat2")])

    def prenorm_T(grow, nblk=NB, src=None, dstT=None, tag="h", srcR=None):
        src = X if src is None else src
        dstT = hT if dstT is None else dstT
        srcR = (lambda b: R("X", b)) if srcR is None else srcR
        gain_load(gpre, grow, R("gpre"))
        for b in range(nblk):
            ACT(junk[:], src[:, b, :], AF.Square, [srcR(b)], [R("junk"), R("stat")], accum=stat[:, b:b + 1])
        TS("dve", stat2[:, 0:nblk], stat[:, 0:nblk], 1.0 / D, EPS, ALU.mult, ALU.add, [R("stat")], [R("stat2")])
        ACT(stat2[:, 0:nblk], stat2[:, 0:nblk], AF.Ln, [R("stat2")], [R("stat2")])
        ACT(stat2[:, 0:nblk], stat2[:, 0:nblk], AF.Exp, [R("stat2")], [R("stat2")], scale=-0.5)
        for b in range(nblk):
            k = b % 2
            STT("dve", hb[:, k, :], src[:, b, :], stat2[:, b:b + 1], gpre[:], ALU.mult, ALU.mult,
                [srcR(b), R("stat2"), R("gpre")], [R("hb", k)])
            for c in range(8):
                TR(psb[:, k, c * 128:(c + 1) * 128], hb[:, k, c * 128:(c + 1) * 128],
                   [R("hb", k), R("ident")], [R("psb", k)], inc=(c == 7))
            CP("act", dstT[:, :, b * 128:(b + 1) * 128], psb[:, k, :].rearrange("p (c t) -> p c t", t=128),
               [], [R("psb", k), R(tag, b)])

    o2 = 0
    Watt, o2 = a2_bf(o2, 8 * 768); Watt = Watt.rearrange("p (c n) -> p c n", n=768)
    rope_raw, o2 = a2_f(o2, 1024)
    qkgB, o2 = a2_f(o2, 128)
    o2_attw = o2
    load_w(Watt, w_in_d[:, 0:768], [R("Watt")])
    P.dma("sp", rope_raw, rope_d, writes=[R("rope_raw")])
    P.dma("sp", qkgB, qkg_d.partition_broadcast(128), writes=[R("qkgB")])
    for b in range(NB):
        P.dma("sp" if b % 2 else "act", X[:, b, :], x_d[b * 128:(b + 1) * 128, :], writes=[R("X", b)])
    prenorm_T(0)
    if "hT" in dbg_d:
        tmpf = arena2[:, 13000:13000 + 2048]
        for c in range(8):
            CP("dve", tmpf, hT[:, c, :], [R("h", b) for b in range(NB)], [R("dbgtmp")])
            P.dma("sp", dbg_d["hT"][c * 128:(c + 1) * 128, :], tmpf, reads=[R("dbgtmp")])
    if stop <= 1:
        return finish(nc, P)
    barrier()
    if stop <= 1.1:
        return finish(nc, P)


    o1 = 0
    ropeT = []
    for i in range(8):
        v, o1 = a1_f(o1, 512)
        ropeT.append(v.rearrange("p (b i) -> p b i", i=32))
    QT, o1 = a1_bf(o1, [4 * N_TOK]); QT = QT.rearrange("p (j t) -> p j t", t=N_TOK)
    KTd, o1 = a1_bf(o1, [2 * N_TOK]); KTd = KTd.rearrange("p (g t) -> p g t", t=N_TOK)
    VA, o1 = a1_bf(o1, [NB * 2 * 192]); VA = VA.rearrange("p (b g d) -> p b g d", g=2, d=192)
    o2 = o2_attw
    qkv = []; sqb = []; tq = []; qn = []; qr = []
    for i in range(2):
        v, o2 = a2_f(o2, 768); qkv.append(v)
        v, o2 = a2_f(o2, 640); sqb.append(v)
        v, o2 = a2_f(o2, 4 * 320); tq.append(v.rearrange("p (a n) -> p a n", n=320))
        v, o2 = a2_f(o2, 640); qn.append(v)
        v, o2 = a2_bf(o2, 768); qr.append(v)
    PT = []
    for i in range(3):
        v, o2 = a2_bf(o2, 1024); PT.append(v.rearrange("p (u n) -> p u n", n=512))
    rc = []
    for i in range(2):
        v, o2 = a2_f(o2, 512); rc.append(v)
    assert o1 <= 65536 and o2 <= 63488, (o1, o2)

    if stop <= 1.16:
        return finish(nc, P)
    MEMSET("pool", VA, 1.0, [R("VA", b) for b in range(NB)])
    if stop <= 1.17:
        return finish(nc, P)
    cosv = rope_raw[:, 0:512].rearrange("p (b i) -> p b i", i=32)
    sinv = rope_raw[:, 512:1024].rearrange("p (b i) -> p b i", i=32)
    for qk in range(2):
        gv = qkgB[:, qk * 64:(qk + 1) * 64].rearrange("p (i two) -> p i two", two=2)
        ge = gv[:, :, 0].unsqueeze(1).broadcast_to([128, NB, 32])
        go = gv[:, :, 1].unsqueeze(1).broadcast_to([128, NB, 32])
        for ti, (tab, gg) in enumerate(((cosv, ge), (sinv, go), (sinv, ge), (cosv, go))):
            TT("dve", ropeT[qk * 4 + ti], tab, gg, ALU.mult, [R("rope_raw"), R("qkgB")], [R("ropeT")])

    def att_proj(b):
        k = b % 2
        tokb = slice(b * 128, (b + 1) * 128)
        for c in range(8):
            MM(psf[:, k, 0, :], hT[:, c, tokb], Watt[:, c, 0:512], c == 0, c == 7,
               [R("h", b), R("Watt")], [R("psf", k, 0)], inc=False)
        for c in range(8):
            MM(psf[:, k, 1, 0:256], hT[:, c, tokb], Watt[:, c, 512:768], c == 0, c == 7,
               [R("h", b), R("Watt")], [R("psf", k, 1)], inc=(c == 7))
        ACT(qkv[k][:, 0:512], psf[:, k, 0, :], AF.Copy, [], [R("psf", k, 0), R("qkv", k)])
        ACT(qkv[k][:, 512:768], psf[:, k, 1, 0:256], AF.Copy, [], [R("psf", k, 1), R("qkv", k)])
        ACT(sqb[k], qkv[k][:, 0:640], AF.Square, [R("qkv", k)], [R("sqb", k)])
        ss = stat[:, 16 + 10 * k: 26 + 10 * k]
        rs = stat2[:, 16 + 10 * k: 26 + 10 * k]
        P.op("dve", lambda h: h.tensor_reduce(out=ss, in_=sqb[k].rearrange("p (h d) -> p h d", d=64),
                                              axis=AX.X, op=ALU.add),
             reads=[R("sqb", k)], writes=[R("ss", k)])
        TS("dve", rs, ss, 1.0 / 64, EPS, ALU.mult, ALU.add, [R("ss", k)], [R("rs", k)])
        ACT(rs, rs, AF.Ln, [R("rs", k)], [R("rs", k)])
        ACT(rs, rs, AF.Exp, [R("rs", k)], [R("rs", k)], scale=-0.5)
        for qk, eng, nh, c0 in ((0, "dve", 8, 0), (1, "dve", 2, 512)):
            src = qkv[k][:, c0:c0 + nh * 64].rearrange("p (h i two) -> p h i two", i=32, two=2)
            xe = src[:, :, :, 0]
            xo = src[:, :, :, 1]
            tabs = [ropeT[qk * 4 + ti][:, b, :].unsqueeze(1).broadcast_to([128, nh, 32]) for ti in range(4)]
            tt = [tq[k][:, a, 0:nh * 32].rearrange("p (h i) -> p h i", i=32) for a in range(4)]
            rd = [R("qkv", k), R("ropeT")]
            wr = [R("tq", k, qk)]
            TT(eng, tt[0], xe, tabs[0], ALU.mult, rd, wr)
            TT(eng, tt[1], xo, tabs[1], ALU.mult, rd, wr)
            TT(eng, tt[2], xe, tabs[2], ALU.mult, rd, wr)
            TT(eng, tt[3], xo, tabs[3], ALU.mult, rd, wr)
            dst = qn[k][:, qk * 512: qk * 512 + nh * 64].rearrange("p (h i two) -> p h i two", i=32, two=2)
            TT(eng, dst[:, :, :, 0], tt[0], tt[1], ALU.subtract, wr, [R("qn", k, qk)])
            TT(eng, dst[:, :, :, 1], tt[2], tt[3], ALU.add, wr, [R("qn", k, qk)])
            rsb = rs[:, qk * 8: qk * 8 + nh]
            if qk == 0:
                TT(eng, qr[k][:, 0:512].rearrange("p (h d) -> p h d", d=64),
                   qn[k][:, 0:512].rearrange("p (h d) -> p h d", d=64),
                   rsb.unsqueeze(2).broadcast_to([128, 8, 64]), ALU.mult,
                   [R("qn", k, 0), R("rs", k)], [R("qr", k, 0)])
            else:
                for dup in range(2):
                    TT(eng, qr[k][:, 512:768].rearrange("p (g u d) -> p g u d", u=2, d=64)[:, :, dup, :],
                       qn[k][:, 512:640].rearrange("p (g d) -> p g d", d=64),
                       rsb.unsqueeze(2).broadcast_to([128, 2, 64]), ALU.mult,
                       [R("qn", k, 1), R("rs", k)], [R("qr", k, 1)])
        CP("dve", VA[:, b, :, 64:128], qkv[k][:, 640:768].rearrange("p (g d) -> p g d", d=64),
           [R("qkv", k)], [R("VA", b)])
    def att_proj2(b):
        k = b % 2
        tokb = slice(b * 128, (b + 1) * 128)
        for j in range(6):
            TR(psb[:, k, j * 128:(j + 1) * 128], qr[k][:, j * 128:(j + 1) * 128],
               [R("qr", k, 0), R("qr", k, 1), R("ident")], [R("psb", k)], inc=(j == 5))
        CP("act", QT[:, :, tokb], psb[:, k, 0:512].rearrange("p (j t) -> p j t", t=128), [], [R("psb", k), R("QT", b)])
        CP("dve", KTd[:, :, tokb], psb[:, k, 512:768].rearrange("p (g t) -> p g t", t=128), [], [R("psb", k), R("KT", b)])

    if stop <= 1.2:
        return finish(nc, P)
    for b in range(NB + 1):
        if b < NB:
            att_proj(b)
        if b >= 1:
            att_proj2(b - 1)
    if "QT" in dbg_d:
        tmpf = arena2[:, 14000:14000 + 1024]
        for c in range(4):
            for hf in range(2):
                CP("dve", tmpf, QT[:, c, hf * 1024:(hf + 1) * 1024], [R("QT", b) for b in range(NB)], [R("dbgtmp")])
                P.dma("sp", dbg_d["QT"][c * 128:(c + 1) * 128, hf * 1024:(hf + 1) * 1024], tmpf, reads=[R("dbgtmp")])
    if stop <= 1.5:
        return finish(nc, P)

    Wh, _ = a2_bf(0, 8 * 640); Wh = Wh.rearrange("p (c n) -> p c n", n=640)
    for gi, c0 in enumerate((768, 1280, 1792, 2304, 2816)):
        src = w_in_d[:, c0: c0 + 128].rearrange("(c p) n -> p c n", p=128)
        P.dma("pool", Wh[:, :, gi * 128:(gi + 1) * 128], src, writes=[R("Wh")], also_wait=[R("Watt")])
    PREFETCHED_WH0 = True
    pt_i = [0]
    psbf = [psb[:, 0, :].bitcast(F32), psb[:, 1, :].bitcast(F32)]
    unit_i = [0]

    def st_mm_unit(m, qt, kb):
        g = m // 2
        qs = slice(qt * 512, (qt + 1) * 512)
        qreads = [R("QT", qt * 4 + i) for i in range(4)]
        k = kb % 2
        for par in range(2):
            rows = slice(par * 64, par * 64 + 64)
            MM(psf[:, k, par, :], KTd[rows, g, kb * 128:(kb + 1) * 128], QT[rows, m, qs], True, True,
               qreads + [R("KT", kb)], [R("psf", k, par)], inc=(par == 1))

    UNITS = [(m_, qt_) for m_ in range(4) for qt_ in range(4)]

    def att_pair_tile(m, qt):
        g = m // 2
        j = m
        qs = slice(qt * 512, (qt + 1) * 512)
        uidx = UNITS.index((m, qt))
        ui = unit_i[0] % 2
        unit_i[0] += 1
        if ui == 0:
            accs = [(psf[:, 2, 0, :], R("psf", 2, 0)), (psf[:, 2, 1, :], R("psf", 2, 1))]
        else:
            accs = [(psbf[0], R("psb", 0)), (psbf[1], R("psb", 1))]
        qreads = [R("QT", qt * 4 + i) for i in range(4)]

        def st_mm(kb):
            k = kb % 2
            for par in range(2):
                rows = slice(par * 64, par * 64 + 64)
                MM(psf[:, k, par, :], KTd[rows, g, kb * 128:(kb + 1) * 128], QT[rows, j, qs], True, True,
                   qreads + [R("KT", kb)], [R("psf", k, par)], inc=(par == 1))

        if uidx == 0:
            st_mm(0)
        for kb in range(16):
            if kb + 1 < 16:
                st_mm(kb + 1)
            elif uidx + 1 < len(UNITS):
                st_mm_unit(UNITS[uidx + 1][0], UNITS[uidx + 1][1], 0)
            pi = pt_i[0] % 3
            pt_i[0] += 1
            k = kb % 2
            ACT(PT[pi], psf[:, k, :, :], AF.Exp, [], [R("psf", k, 0), R("psf", k, 1), R("PT", pi)], scale=0.125)
            for par in range(2):
                vsl = slice(64, 192) if par == 0 else slice(0, 128)
                MM(accs[par][0], VA[:, kb, g, vsl], PT[pi][:, par, :], kb == 0, kb == 15,
                   [R("VA", kb), R("PT", pi)], [accs[par][1]], inc=(par == 1))
        rcb = rc[ui]
        for par in range(2):
            rows = slice(par * 64, par * 64 + 64)
            orow = slice((1 - par) * 64, (1 - par) * 64 + 64)
            acc, accR = accs[par]
            P.op("dve", lambda hh, acc=acc, rows=rows, orow=orow: hh.reciprocal(out=rcb[rows, :], in_=acc[orow, :]),
                 reads=[], writes=[accR, R("rc", ui, par)])
            TT("dve", mixT[rows, j, qs], acc[rows, :], rcb[rows, :], ALU.mult, [R("rc", ui, par)], [accR, R("mixT", j, qt)])

    for m in range(4):
        for qt in range(4):
            att_pair_tile(m, qt)

    if "att" in dbg_d:
        tmpf = arena2[:, 0:2048]
        for c in range(4):
            CP("dve", tmpf, mixT[:, c, :], [R("mixT", c, qt) for qt in range(4)], [R("dbgtmp")])
            P.dma("sp", dbg_d["att"][c * 128:(c + 1) * 128, :], tmpf, reads=[R("dbgtmp")])
    if stop <= 2:
        return finish(nc, P)
    barrier(skip=("Wh",))

    bank_i = [0]

    def bank():
        i = bank_i[0] % 6
        bank_i[0] += 1
        return psf[:, i // 2, i % 2, :], R("psf", i // 2, i % 2)

    o1 = 0
    fb = {}
    for nm in ("qs", "SG", "E", "LF", "B", "KK", "rmask", "gate"):
        fb[nm], o1 = a1_f(o1, 2048)
    Tbig = arena1[:, 2 * 2048: 4 * 2048].rearrange("p (c v) -> p c v", v=128)
    o2 = 0
    Wh, o2 = a2_bf(o2, 8 * 640); Wh = Wh.rearrange("p (c n) -> p c n", n=640)
    Of, o2 = a2_f(o2, 2048); Of3 = Of.rearrange("p (b v) -> p b v", v=128)
    qtl, o2 = a2_bf(o2, 2048)
    ktl, o2 = a2_bf(o2, 2048)
    ktok, o2 = a2_bf(o2, 2048); ktok = ktok.rearrange("p (b v) -> p b v", v=128)
    vtok, o2 = a2_bf(o2, 2048); vtok = vtok.rearrange("p (b v) -> p b v", v=128)
    Sbf, o2 = a2_bf(o2, 32 * 128); Sbf = Sbf.rearrange("p (c v) -> p c v", v=128)
    AT, o2 = a2_bf(o2, 2048); AT = AT.rearrange("p (b v) -> p b v", v=128)
    recb, o2 = a2_bf(o2, 2048); recb = recb.rearrange("p (b v) -> p b v", v=128)
    hgoB, o2 = a2_f(o2, 128)
    maskS, o2 = a2_bf(o2, 256); maskS = maskS.rearrange("p (d t) -> p d t", t=128)
    gtmp, o2 = a2_f(o2, 256)
    assert o1 <= 65536 and o2 <= 63488, (o1, o2)
    gate3 = fb["gate"].rearrange("p (b v) -> p b v", v=128)

    P.dma("sp", hgoB, hgo_d.partition_broadcast(128), writes=[R("hgoB")])
    P.dma("sp", maskS, mask_d.rearrange("p (d t) -> p d t", t=128), writes=[R("maskS")])
    MEMSET("pool", fb["rmask"], 1.0, [R("rmask")])
    MEMSET("pool", fb["rmask"].rearrange("p (c t) -> p c t", t=64)[:, :, 0:1], 0.0, [R("rmask")])
    lv = lbT[:].rearrange("p (d l h) -> p d l h", l=2, h=4)
    TT("dve", lbv[:, 0:8].rearrange("p (d h) -> p d h", h=4), lv[:, :, 0, :], lv[:, :, 1, :], ALU.subtract,
       [R("lbT")], [R("lbv")])
    ACT(lbv[:, 0:8], lbv[:, 0:8], AF.Exp, [R("lbv")], [R("lbv")], scale=-1.0)
    TS("dve", lbv[:, 0:8], lbv[:, 0:8], 1.0, None, ALU.add, None, [R("lbv")], [R("lbv")])
    RECIP(lbv[:, 0:8], lbv[:, 0:8], [R("lbv")], [R("lbv")])
    TS("dve", lbv[:, 8:16], lbv[:, 0:8], -1.0, 1.0, ALU.mult, ALU.add, [R("lbv")], [R("lbv")])
    TS("dve", lbv[:, 16:24], lbv[:, 0:8], -0.5, 0.5, ALU.mult, ALU.add, [R("lbv")], [R("lbv")])
    TS("dve", lbv[:, 24:32], lbv[:, 0:8], 0.5, 0.5, ALU.mult, ALU.add, [R("lbv")], [R("lbv")])
    TS("dve", lbv[:, 32:40], lbv[:, 0:8], 0.5, -0.5, ALU.mult, ALU.add, [R("lbv")], [R("lbv")])

    ALLH = [R("h", b) for b in range(NB)]
    chk(2.1)

    def load_Wh(hh, extra=()):
        for gi, c0 in enumerate((768, 1280, 1792, 2304, 2816)):
            src = w_in_d[:, c0 + hh * 128: c0 + (hh + 1) * 128].rearrange("(c p) n -> p c n", p=128)
            P.dma("pool", Wh[:, :, gi * 128:(gi + 1) * 128], src, writes=[R("Wh")] + list(extra))

    def hgrn_head(hh):
        for bp in range(8):
            bk, bkR = bank()
            for u in range(2):
                b = 2 * bp + u
                for c in range(8):
                    MM(bk[:, u * 256:(u + 1) * 256], hT[:, c, b * 128:(b + 1) * 128], Wh[:, c, 384:640], c == 0, c == 7,
                       [R("h", b), R("Wh")], [bkR], inc=(u == 1 and c == 7))
            bv = bk.rearrange("p (u n) -> p u n", n=256)
            ACT(vtok[:, 2 * bp:2 * bp + 2, :], bv[:, :, 0:128], AF.Copy, [], [bkR, R("vtok")])
            g3 = gtmp.rearrange("p (u n) -> p u n", n=128)
            ACT(gate3[:, 2 * bp:2 * bp + 2, :], bv[:, :, 128:256], AF.Silu, [], [bkR, R("gate")])

        chk(2.2)

        def fm_proj(gi, dst, dstname):
            kept = []
            for qt in range(4):
                bk, bkR = bank()
                for c in range(8):
                    MM(bk, Wh[:, c, gi * 128:(gi + 1) * 128], hT[:, c, qt * 512:(qt + 1) * 512], c == 0, c == 7,
                       ALLH[qt * 4:qt * 4 + 4] + [R("Wh")], [bkR], inc=(c == 7))
                ACT(dst[:, qt * 512:(qt + 1) * 512], bk, AF.Tanh, [], [bkR, R(dstname, qt // 2)], scale=0.5)
                kept.append((bk, bkR))
            return kept

        for qt in range(4):
            bk, bkR = bank()
            for c in range(8):
                MM(bk, Wh[:, c, 0:128], hT[:, c, qt * 512:(qt + 1) * 512], c == 0, c == 7,
                   ALLH[qt * 4:qt * 4 + 4] + [R("Wh")], [bkR], inc=(c == 7))
            sl = slice(qt * 512, (qt + 1) * 512)
            ACT(fb["qs"][:, sl], bk, AF.Silu, [], [bkR, R("qs", qt // 2)])

        chk(2.3)
        for dr in range(2):
            li = dr * 4 + hh
            ha_ap = lbv[:, 16 + li: 17 + li]
            hb_ap = lbv[:, 24 + li: 25 + li]
            nha_ap = lbv[:, 32 + li: 33 + li]
            E, SG, LF, B, KK = fb["E"], fb["SG"], fb["LF"], fb["B"], fb["KK"]
            EB, ENB = SG, LF
            H2 = (slice(0, 1024), slice(1024, 2048))
            if dr == 0:
                fm_proj(1, E, "E")
                TH, THn, CUM, CUMn = E, "E", B, "B"
            else:
                TH, THn, CUM, CUMn = B, "B", E, "E"
            for hf in range(2):
                sl = H2[hf]
                ACT(LF[:, sl], TH[:, sl], AF.Ln, [R(THn, hf), R("lbv")], [R("LF", hf)], scale=ha_ap, bias=hb_ap)
            for hf in range(2):
                sl = H2[hf]
                TS("dve", KK[:, sl], TH[:, sl], nha_ap, ha_ap, ALU.mult, ALU.add, [R(THn, hf), R("lbv")], [R("KK", hf)])
                P.op("dve", lambda h, sl=sl, CUM=CUM: h.tensor_tensor_scan(out=CUM[:, sl], data0=fb["rmask"][:, sl],
                                                                           data1=LF[:, sl], initial=0.0, op0=ALU.mult, op1=ALU.add),
                     reads=[R("rmask"), R("LF", hf)], writes=[R(CUMn, hf)])
                if dr == 1:
                    TT("dve", LF[:, sl], LF[:, sl], CUM[:, sl], ALU.subtract, [R("LF", hf), R(CUMn, hf)], [R("LF", hf)])
                    C3 = CUM[:, sl].rearrange("p (c t) -> p c t", t=64)
                    TT("dve", B[:, sl].rearrange("p (c t) -> p c t", t=64), LF[:, sl].rearrange("p (c t) -> p c t", t=64),
                       C3[:, :, 63:64].broadcast_to([128, 16, 64]), ALU.add, [R("LF", hf), R(CUMn, hf)], [R("B", hf)])
            Bs, Bn = B, "B"
            for hf in range(2):
                sl = H2[hf]
                ACT(EB[:, sl], Bs[:, sl], AF.Exp, [R(Bn, hf)], [R("SG", hf)])
                ACT(ENB[:, sl], Bs[:, sl], AF.Exp, [R(Bn, hf)], [R("LF", hf)], scale=-1.0)
            for hf in range(2):
                sl = H2[hf]
                TT("dve", qtl[:, sl], fb["qs"][:, sl], EB[:, sl], ALU.mult, [R("qs", hf), R("SG", hf)], [R("qtl", hf)])
                TT("dve", ktl[:, sl], KK[:, sl], ENB[:, sl], ALU.mult, [R("KK", hf), R("LF", hf)], [R("ktl", hf)])
            chk(2.4)
            for half in range(2):
                for i in range(8):
                    b = half * 8 + i
                    TR(psb[:, half, i * 128:(i + 1) * 128], ktl[:, b * 128:(b + 1) * 128], [R("ktl", half), R("ident")],
                       [R("psb", half)], inc=(i == 7))
                CP("act" if half else "dve", ktok[:, half * 8:(half + 1) * 8, :],
                   psb[:, half, :].rearrange("p (b v) -> p b v", v=128), [], [R("psb", half), R("ktok")])
            for bg in range(4):
                bk, bkR = bank()
                for u in range(4):
                    b = 4 * bg + u
                    MM(bk[:, u * 128:(u + 1) * 128], ktl[:, b * 128:(b + 1) * 128], qtl[:, b * 128:(b + 1) * 128], True, True,
                       [R("ktl", bg // 2), R("qtl", bg // 2)], [bkR], inc=(u == 3))
                TT("dve", AT[:, 4 * bg:4 * bg + 4, :], bk.rearrange("p (u t) -> p u t", t=128),
                   maskS[:, dr, :].unsqueeze(1).broadcast_to([128, 4, 128]), ALU.mult, [R("maskS")], [bkR, R("AT")])
            chk(2.5)
            MEMSET("pool", Sbf[:, 0 if dr == 0 else 31, :], 0.0, [R("Sbf")])
            DEAD = [R(nm, hf) for nm in ("E", "LF") for hf in range(2)]
            SGR = [R("SG", 0), R("SG", 1)]
            n = 0
            cprev = None
            for bgi in range(4):
                bg = bgi if dr == 0 else 3 - bgi
                ub = [bank() for _ in range(2)]
                for u in range(4):
                    b = 4 * bg + u
                    for j in range(2):
                        MM(ub[j][0][:, u * 128:(u + 1) * 128], ktok[64 * j:64 * j + 64, b, :], vtok[64 * j:64 * j + 64, b, :],
                           True, True, [R("ktok"), R("vtok")], [ub[j][1]], inc=(u == 3 and j == 1), tp=(64 * j, 0))
                order = [(u, j) for u in range(4) for j in range(2)]
                if dr == 1:
                    order = order[::-1]
                for (u, j) in order:
                    c = 2 * (4 * bg + u) + j
                    Uc = ub[j][0][:, u * 128:(u + 1) * 128]
                    if n == 0:
                        CP("dve", Tbig[:, c, :], Uc, [], [ub[j][1], R("Tb", c)] + DEAD)
                    else:
                        dprev = EB[:, 64 * cprev + 63: 64 * cprev + 64] if dr == 0 else EB[:, 64 * cprev: 64 * cprev + 1]
                        STT("dve", Tbig[:, c, :], Tbig[:, cprev, :], dprev, Uc, ALU.mult, ALU.add,
                            [R("Tb", cprev)] + SGR, [ub[j][1], R("Tb", c)])
                    cprev = c
                    n += 1
            EB3 = EB.rearrange("p (c t) -> p c t", t=64)
            if dr == 0:
                TT("dve", Sbf[:, 1:32, :], Tbig[:, 0:31, :], EB3[:, 0:31, 63:64].broadcast_to([128, 31, 128]), ALU.mult,
                   [R("Tb", cprev)] + SGR + DEAD, [R("Sbf")])
            else:
                TT("dve", Sbf[:, 0:31, :], Tbig[:, 1:32, :], EB3[:, 1:32, 0:1].broadcast_to([128, 31, 128]), ALU.mult,
                   [R("Tb", cprev)] + SGR + DEAD, [R("Sbf")])
            if dr == 0:
                fm_proj(2, B, "B")
                if hh < 3:
                    load_Wh(hh + 1)
                else:
                    load_w(hT, w_xkv_d, [R("Wkv")], also_wait=ALLH)
            chk(2.6)
            for bg in range(4):
                bk, bkR = bank()
                for u in range(4):
                    b = 4 * bg + u
                    us = slice(u * 128, (u + 1) * 128)
                    MM(bk[:, us], AT[:, b, :], vtok[:, b, :], True, False, [R("AT"), R("vtok")], [bkR], inc=False)
                    for j in range(2):
                        c = 2 * b + j
                        MM(bk[64 * j:64 * j + 64, us], qtl[:, b * 128 + 64 * j: b * 128 + 64 * j + 64], Sbf[:, c, :],
                           False, j == 1, [R("qtl", bg // 2), R("Sbf")], [bkR], inc=(u == 3 and j == 1), tp=(0, 64 * j))
                o3 = Of3[:, 4 * bg:4 * bg + 4, :]
                if dr == 0:
                    ACT(o3, bk.rearrange("p (u v) -> p u v", v=128), AF.Copy, [], [bkR, R("Of")])
                else:
                    TT("dve", o3, bk.rearrange("p (u v) -> p u v", v=128), o3, ALU.add, [], [bkR, R("Of")])
        chk(2.7)
        ACT(fb["E"], Of, AF.Square, [R("Of")], [R("E", 0), R("E", 1)])
        ss = stat[:, 40:56]
        rs = stat2[:, 40:56]
        P.op("dve", lambda h: h.tensor_reduce(out=ss, in_=fb["E"].rearrange("p (b v) -> p b v", v=128), axis=AX.X, op=ALU.add),
             reads=[R("E", 0), R("E", 1)], writes=[R("ssh")])
        TS("dve", rs, ss, 1.0 / 128, EPS, ALU.mult, ALU.add, [R("ssh")], [R("rsh")])
        ACT(rs, rs, AF.Ln, [R("rsh")], [R("rsh")])
        ACT(rs, rs, AF.Exp, [R("rsh")], [R("rsh")], scale=-0.5)
        SG3 = fb["SG"].rearrange("p (b v) -> p b v", v=128)
        TT("dve", SG3, Of3, rs.unsqueeze(2).broadcast_to([128, NB, 128]), ALU.mult, [R("Of"), R("rsh")], [R("SG", 0), R("SG", 1)])
        TT("dve", SG3, SG3, hgoB.unsqueeze(1).broadcast_to([128, NB, 128]), ALU.mult, [R("SG", 0), R("SG", 1), R("hgoB")],
           [R("SG", 0), R("SG", 1)])
        TT("dve", recb, SG3, gate3, ALU.mult, [R("SG", 0), R("SG", 1), R("gate")], [R("recb")])
        for half in range(2):
            for i in range(8):
                b = half * 8 + i
                TR(psb[:, half, i * 128:(i + 1) * 128], recb[:, b, :], [R("recb"), R("ident")], [R("psb", half)], inc=(i == 7))
            CP("act" if half else "dve", mixT[:, 4 + hh, half * 1024:(half + 1) * 1024], psb[:, half, :], [],
               [R("psb", half), R("mixT", 4 + hh, 2 * half), R("mixT", 4 + hh, 2 * half + 1)])

    if not PREFETCHED_WH0:
        load_Wh(0)
    for hh in range(4):
        hgrn_head(hh)
        chk(2.8 + 0.01 * hh)

    if "rec" in dbg_d:
        tmpf = fb["B"]
        for c in range(4):
            CP("dve", tmpf, mixT[:, 4 + c, :], [R("mixT", 4 + c, qt) for qt in range(4)], [R("dbgtmp")])
            P.dma("sp", dbg_d["rec"][c * 128:(c + 1) * 128, :], tmpf, reads=[R("dbgtmp")])
    if stop <= 3:
        return finish(nc, P)
    barrier(skip=("Wkv",))

    junk3 = junk[:].rearrange("p (u n) -> p u n", n=512)
    gpost3 = gpost[:].rearrange("p (u n) -> p u n", n=512)

    def proj_norm_residual(b, lhs_aps, lhs_reads, w_fn, w_reads, resid_ap, resid_reads, out_ap, out_writes):
        k = b % 3
        n = len(lhs_aps)
        for half in range(2):
            for ci in range(n):
                MM(psf[:, k, half, :], lhs_aps[ci], w_fn(ci, half), ci == 0, ci == n - 1, lhs_reads + w_reads[ci],
                   [R("psf", k, half)], inc=(half == 1 and ci == n - 1))
        ss = stat[:, 16 + b:17 + b]
        rs = stat2[:, 16 + b:17 + b]
        PB = [R("psf", k, 0), R("psf", k, 1)]
        ACT(junk3, psf[:, k, :, :], AF.Square, [], PB + [R("junk"), R("pss", b)], accum=ss)
        ACT(rs, ss, AF.Ln, [R("pss", b), R("epsc")], [R("prs", b)], scale=1.0 / D, bias=epsc[:])
        ACT(rs, rs, AF.Exp, [R("prs", b)], [R("prs", b)], scale=-0.5)
        STT("dve", psf[:, k, :, :], psf[:, k, :, :], rs, gpost3, ALU.mult, ALU.mult, [R("prs", b), R("gpost")], PB)
        TT("dve", out_ap.rearrange("p (u n) -> p u n", n=512), resid_ap.rearrange("p (u n) -> p u n", n=512), psf[:, k, :, :],
           ALU.add, resid_reads, PB + out_writes)

    def prenorm_block(b, src_ap, src_reads):
        k = b % 2
        ss = stat[:, 32 + b:33 + b]
        rs = stat2[:, 32 + b:33 + b]
        ACT(junk[:], src_ap, AF.Square, src_reads, [R("junk"), R("fss", b)], accum=ss)
        ACT(rs, ss, AF.Ln, [R("fss", b), R("epsc")], [R("frs", b)], scale=1.0 / D, bias=epsc[:])
        ACT(rs, rs, AF.Exp, [R("frs", b)], [R("frs", b)], scale=-0.5)
        STT("dve", hb[:, k, :], src_ap, rs, gpre[:], ALU.mult, ALU.mult, src_reads + [R("frs", b), R("gpre")], [R("hb", k)])

    def prenorm_block_tr(b):
        k = b % 2
        for c in range(8):
            TR(psb[:, k, c * 128:(c + 1) * 128], hb[:, k, c * 128:(c + 1) * 128], [R("hb", k), R("ident")], [R("psb", k)],
               inc=(c == 7))
        CP("act", hT[:, :, b * 128:(b + 1) * 128], psb[:, k, :].rearrange("p (c t) -> p c t", t=128), [], [R("psb", k), R("h", b)])

    o2 = 0
    Wo, o2 = a2_bf(o2, 8 * 1024); Wo = Wo.rearrange("p (c n) -> p c n", n=1024)
    load_w(Wo, w_out_d, [R("Wo")])
    gain_load(gpost, 1, R("gpost"))
    Wkv = hT
    Wq, _ = a2_bf(16384, 8 * 1024); Wq = Wq.rearrange("p (c n) -> p c n", n=1024)
    Wo2, _ = a2_bf(32768, 8 * 1024); Wo2 = Wo2.rearrange("p (c n) -> p c n", n=1024)
    load_w(Wq, w_xq_d, [R("Wq")])
    load_w(Wo2, w_xo_d, [R("Wo2")])
    for b in range(14):
        P.dma("sp" if b % 2 else "act", X[:, b, :], x_d[b * 128:(b + 1) * 128, :], writes=[R("X", b)])
    o2 = 49152
    memT, o2 = a2_bf(o2, 8 * 256); memT = memT.rearrange("p (c m) -> p c m", m=256)
    KxT, o2 = a2_bf(o2, 8 * 256); KxT = KxT.rearrange("p (j m) -> p j m", m=256)
    Vx, o2 = a2_bf(o2, 2 * 1024); Vx = Vx.rearrange("p (m n) -> p m n", n=1024)
    assert o2 <= 63488, o2
    for mb in range(2):
        P.dma("sp", X[:, 14 + mb, :], mem_d[mb * 128:(mb + 1) * 128, :], writes=[R("X", 14 + mb)])
    prenorm_T(3, nblk=2, src=X[:, 14:16, :], dstT=memT, tag="memT", srcR=lambda b: R("X", 14 + b))
    MT = [R("memT", 0), R("memT", 1)]
    for j in range(8):
        bk, bkR = bank()
        for c in range(8):
            MM(bk[:, 0:256], Wkv[:, c, j * 128:(j + 1) * 128], memT[:, c, :], c == 0, c == 7, MT + [R("Wkv")], [bkR], inc=(c == 7))
        CP("act" if j % 2 else "dve", KxT[:, j, :], bk[:, 0:256], [], [bkR, R("KxT")])
    for mb in range(2):
        for half in range(2):
            bk, bkR = bank()
            for c in range(8):
                MM(bk, memT[:, c, mb * 128:(mb + 1) * 128], Wkv[:, c, 1024 + half * 512: 1024 + (half + 1) * 512], c == 0, c == 7,
                   MT + [R("Wkv")], [bkR], inc=(c == 7))
            CP("act" if half else "dve", Vx[:, mb, half * 512:(half + 1) * 512], bk, [], [bkR, R("Vx")])
    for b in range(14, 16):
        P.dma("sp" if b % 2 else "act", X[:, b, :], x_d[b * 128:(b + 1) * 128, :], writes=[R("X", b)])
    gain_load(gpre, 2, R("gpre"))
    P._deps("act", [], [R("Wkv")])
    for b in range(NB + 2):
        if b < NB:
            tokb = slice(b * 128, (b + 1) * 128)
            proj_norm_residual(b, [mixT[:, c, tokb] for c in range(8)], [R("mixT", c, b // 4) for c in range(8)],
                               lambda ci, half: Wo[:, ci, half * 512:(half + 1) * 512], [[R("Wo")]] * 8,
                               X[:, b, :], [R("X", b)], X[:, b, :], [R("X", b)])
        if 1 <= b <= NB:
            prenorm_block(b - 1, X[:, b - 1, :], [R("X", b - 1)])
        if b >= 2:
            prenorm_block_tr(b - 2)
    if "x1" in dbg_d:
        for b in range(NB):
            P.dma("sp", dbg_d["x1"][b * 128:(b + 1) * 128, :], X[:, b, :], reads=[R("X", b)])
    if stop <= 4:
        return finish(nc, P)
    barrier(skip=("Wkv", "Wq", "Wo2"))

    o2 = 0
    Qx, o2 = a2_bf(o2, 8 * 512); Qx = Qx.rearrange("p (j n) -> p j n", n=512)
    PTx = []
    for i in range(2):
        v, o2 = a2_bf(o2, 1024); PTx.append(v.rearrange("p (u n) -> p u n", n=512))
    rcx2 = []
    for i in range(2):
        v, o2 = a2_f(o2, 512); rcx2.append(v)
    assert o2 <= 16384, o2
    gain_load(gpost, 4, R("gpost"))
    for qt in range(4):
        qs_ = slice(qt * 512, (qt + 1) * 512)
        for j in range(8):
            bk, bkR = bank()
            for c in range(8):
                MM(bk, Wq[:, c, j * 128:(j + 1) * 128], hT[:, c, qs_], c == 0, c == 7, ALLH[qt * 4:qt * 4 + 4] + [R("Wq")],
                   [bkR], inc=(c == 7))
            CP("act" if j % 2 else "dve", Qx[:, j, :], bk, [], [bkR, R("Qx", j)])
        for hx in range(4):
            k = hx % 2
            for mb in range(2):
                for half in range(2):
                    MM(psf[:, k, mb, :], KxT[:, 2 * hx + half, mb * 128:(mb + 1) * 128], Qx[:, 2 * hx + half, :], half == 0, half == 1,
                       [R("KxT"), R("Qx", 2 * hx + half)], [R("psf", k, mb)], inc=(mb == 1 and half == 1))
            ACT(PTx[k], psf[:, k, :, :], AF.Exp, [], [R("psf", k, 0), R("psf", k, 1), R("PTx", k)], scale=1.0 / 16)
            dbk, dbkR = psbf[k], R("psb", k)
            for mb in range(2):
                MM(dbk, ones_bf[:], PTx[k][:, mb, :], mb == 0, mb == 1, [R("ones"), R("PTx", k)], [dbkR], inc=(mb == 1))
            ACT(rcx2[k], dbk, AF.Ln, [], [dbkR, R("rcx", k)])
            ACT(rcx2[k], rcx2[k], AF.Exp, [R("rcx", k)], [R("rcx", k)], scale=-1.0)
            for dh in range(2):
                bk, bkR = psf[:, 2, dh, :], R("psf", 2, dh)
                for mb in range(2):
                    MM(bk, Vx[:, mb, hx * 256 + dh * 128: hx * 256 + (dh + 1) * 128], PTx[k][:, mb, :], mb == 0, mb == 1,
                       [R("Vx"), R("PTx", k)], [bkR], inc=(mb == 1))
                TT("dve", mixT[:, 2 * hx + dh, qs_], bk, rcx2[k], ALU.mult, [R("rcx", k)], [bkR, R("mixT", 2 * hx + dh, qt)])
    gain_load(gpre, 5, R("gpre"))
    for b in range(NB + 2):
        if b < NB:
            tokb = slice(b * 128, (b + 1) * 128)
            proj_norm_residual(b, [mixT[:, c, tokb] for c in range(8)], [R("mixT", c, b // 4) for c in range(8)],
                               lambda ci, half: Wo2[:, ci, half * 512:(half + 1) * 512], [[R("Wo2")]] * 8,
                               X[:, b, :], [R("X", b)], X[:, b, :], [R("X", b)])
        if 1 <= b <= NB:
            prenorm_block(b - 1, X[:, b - 1, :], [R("X", b - 1)])
        if b >= 2:
            prenorm_block_tr(b - 2)
    if "x2" in dbg_d:
        for b in range(NB):
            P.dma("sp", dbg_d["x2"][b * 128:(b + 1) * 128, :], X[:, b, :], reads=[R("X", b)])
    if stop <= 5:
        return finish(nc, P)
    barrier()

    gain_load(gpost, 6, R("gpost"))
    for b in range(NB):
        P.dma("sp" if b % 2 else "act", out_d[b * 128:(b + 1) * 128, :], X[:, b, :], reads=[R("X", b)], writes=[R("outd", b)])
    aT16, _ = a1_bf(0, [16 * N_TOK]); aT16 = aT16.rearrange("p (j t) -> p j t", t=N_TOK)
    aT = [aT16[:, j, :] for j in range(16)] + [mixT[:, j, :] for j in range(6)]
    Wup = [mixT[:, 6 + i, :].rearrange("p (c n) -> p c n", n=256) for i in range(2)]
    o2 = 0
    ugb = []; uvb = []; agb = []; avb = []; ebb = []
    for i in range(2):
        v, o2 = a2_f(o2, 2050); ugb.append(v)
        v, o2 = a2_f(o2, 2050); uvb.append(v)
        v, o2 = a2_f(o2, 1024); agb.append(v)
        v, o2 = a2_f(o2, 1024); avb.append(v)
        v, o2 = a2_f(o2, 1024); ebb.append(v)
    assert o2 <= 63488, o2
    for i in range(2):
        for ub_, nm in ((ugb[i], "ug"), (uvb[i], "uv")):
            MEMSET("pool", ub_[:, 0:1], 0.0, [R(nm, i)])
            MEMSET("pool", ub_[:, 2049:2050], 0.0, [R(nm, i)])

    bank8_i = [0]

    def bank8():
        i = bank8_i[0] % 8
        bank8_i[0] += 1
        if i < 6:
            return psf[:, i // 2, i % 2, :], R("psf", i // 2, i % 2)
        return psbf[i - 6], R("psb", i - 6)

    def ffn_mm(j):
        s_ = j % 2
        wu = Wup[s_]
        P.dma("pool", wu[:, :, 0:128], w_up_d[:, j * 128:(j + 1) * 128].rearrange("(c p) n -> p c n", p=128),
              writes=[R("Wup", s_)])
        P.dma("pool", wu[:, :, 128:256],
              w_up_d[:, D_FF + j * 128: D_FF + (j + 1) * 128].rearrange("(c p) n -> p c n", p=128), writes=[R("Wup", s_)])
        for qt in range(4):
            qs_ = slice(qt * 512, (qt + 1) * 512)
            bg_, bgR = bank8()
            bv_, bvR = bank8()
            for c in range(8):
                MM(bg_, wu[:, c, 0:128], hT[:, c, qs_], c == 0, c == 7, ALLH[qt * 4:qt * 4 + 4] + [R("Wup", s_)], [bgR], inc=False)
            for c in range(8):
                MM(bv_, wu[:, c, 128:256], hT[:, c, qs_], c == 0, c == 7, ALLH[qt * 4:qt * 4 + 4] + [R("Wup", s_)], [bvR],
                   inc=(c == 7))
            ACT(ugb[s_][:, 1 + qt * 512: 1 + (qt + 1) * 512], bg_, AF.Copy, [], [bgR, R("ug", s_)])
            ACT(uvb[s_][:, 1 + qt * 512: 1 + (qt + 1) * 512], bv_, AF.Copy, [], [bvR, R("uv", s_)])

    def ffn_ew(j):
        s_ = j % 2
        ug, uv, ag, av, eb_ = ugb[s_], uvb[s_], agb[s_], avb[s_], ebb[s_]
        for hf in range(2):
            base = 1 + hf * 1024
            ts_ = slice(hf * 1024, (hf + 1) * 1024)
            cw = lambda cj, i: convT[:, cj * 4 + i: cj * 4 + i + 1]
            CT = [R("convT")]
            for (u_, uR, cj, dst, dR) in ((ug, R("ug", s_), j, ag, R("ag", s_)), (uv, R("uv", s_), 22 + j, av, R("av", s_))):
                ACT(dst, u_[:, base:base + 1024], AF.Identity, [uR] + CT, [dR], scale=cw(cj, 1), bias=cw(cj, 3))
                STT("dve", dst, u_[:, base - 1:base + 1023], cw(cj, 0), dst, ALU.mult, ALU.add, [uR] + CT, [dR])
                STT("dve", dst, u_[:, base + 1:base + 1025], cw(cj, 2), dst, ALU.mult, ALU.add, [uR] + CT, [dR])
            ACT(eb_, ag, AF.Silu, [R("ag", s_)], [R("eb", s_)])
            TT("dve", aT[j][:, ts_], eb_, av, ALU.mult, [R("eb", s_), R("av", s_)],
               [R("aT", j)] + ([R("X", j)] if j < 16 else []))

    for j in range(23):
        if j < 22:
            ffn_mm(j)
        if j >= 1:
            ffn_ew(j - 1)
    barrier()
    Wd, o2 = a2_bf(0, 22 * 1024); Wd = Wd.rearrange("p (j n) -> p j n", n=1024)
    wstg = []
    for i in range(3):
        v, o2 = a2_f(o2, 1024); wstg.append(v)
    assert o2 <= 63488, o2
    nst = 0
    for j in range(22):
        if j % 2 == 0:
            P.dma("pool", Wd[:, j, :], w_down_d[j * 128:(j + 1) * 128, :], writes=[R("Wd", j)])
        else:
            si = nst % 3
            nst += 1
            P.dma("sp", wstg[si], w_down_d[j * 128:(j + 1) * 128, :], writes=[R("wstg", si)])
            CP("act", Wd[:, j, :], wstg[si], [R("wstg", si)], [R("Wd", j)])
    hTf = hT[:].rearrange("p c t -> p (c t)").bitcast(F32)
    xb_ = [hTf[:, 0:1024], hTf[:, 1024:2048]]
    ob_ = [hTf[:, 2048:3072], hTf[:, 3072:4096]]
    for b in range(NB):
        tokb = slice(b * 128, (b + 1) * 128)
        k = b % 2
        P.dma("sp", xb_[k], out_d[b * 128:(b + 1) * 128, :], reads=[R("outd", b)] + ALLH, writes=[R("xb", k)])
        proj_norm_residual(b, [aT[j][:, tokb] for j in range(22)], [R("aT", j) for j in range(22)],
                           lambda ci, half: Wd[:, ci, half * 512:(half + 1) * 512], [[R("Wd", j)] for j in range(22)],
                           xb_[k], [R("xb", k)], ob_[k], [R("ob", k)])
        P.dma("sp", out_d[b * 128:(b + 1) * 128, :], ob_[k], reads=[R("ob", k)], writes=[R("outd", b)])

    return finish(nc, P)


def finish(nc, P):
    print("COUNTS", {e: P.q[e].total for e in P.ENGS}, "dma", sum(P.dma_cnt), max(P.dma_cnt), flush=True)
    P.wait_all("sp")
    P.emit()
    P.close()
    return nc


def rope_tables():
    rows = N_TOK // 64
    r, c = np.meshgrid(np.arange(rows), np.arange(64), indexing="ij")
    inv = np.power(np.float32(10000.0), -np.arange(16, dtype=np.float32) / np.float32(16)).astype(np.float32)
    ang = np.concatenate([r.reshape(-1, 1).astype(np.float32) * inv, c.reshape(-1, 1).astype(np.float32) * inv], -1)
    cos = np.cos(ang).astype(np.float32)
    sin = np.sin(ang).astype(np.float32)
    f = lambda a: a.reshape(16, 128, 32).transpose(1, 0, 2).reshape(128, 512)
    return np.ascontiguousarray(np.concatenate([f(cos), f(sin)], 1))


def masks():
    s = np.arange(128)[:, None]
    t = np.arange(128)[None, :]
    same = (s // 64) == (t // 64)
    fwd = (same & (s <= t)).astype(np.float32)
    bwd = (same & (s >= t)).astype(np.float32)
    return np.concatenate([fwd, bwd], 1).astype(ml_dtypes.bfloat16)


def prep_inputs(inp):
    f32 = lambda a: np.ascontiguousarray(np.asarray(a, dtype=np.float32))
    shared = {
        "w_in": f32(inp["w_in"][0]), "w_out": f32(inp["w_out"][0]), "w_xq": f32(inp["w_xq"][0]),
        "w_xkv": f32(inp["w_xkv"][0]), "w_xo": f32(inp["w_xo"][0]), "w_up": f32(inp["w_up"][0]),
        "w_down": f32(inp["w_down"][0]),
        "gains": f32(np.stack([inp["pre_mix_g"][0], inp["post_mix_g"][0], inp["pre_x_g"][0], inp["mem_norm_g"][0],
                               inp["post_x_g"][0], inp["pre_ffn_g"][0], inp["post_ffn_g"][0]])),
        "qkg": f32(np.concatenate([inp["q_norm_g"][0], inp["k_norm_g"][0]])[None, :]),
        "hgo": f32(np.asarray(inp["hg_out_norm_g"][0])[None, :]),
        "lbT": f32(np.asarray(inp["hg_lb"]).reshape(2, 2, 4, 128).transpose(3, 0, 1, 2).reshape(128, 16)),
        "convT": f32(np.concatenate([np.asarray(inp["conv_w"][0]), np.asarray(inp["conv_b"][0])[None, :]], 0)
                     .reshape(4, 44, 128).transpose(2, 1, 0).reshape(128, 176)),
        "ident": np.eye(128, dtype=np.float32).astype(ml_dtypes.bfloat16),
        "rope": rope_tables(),
        "mask": masks(),
    }
    x = np.asarray(inp["x"], dtype=np.float32)
    mem = np.asarray(inp["mem"], dtype=np.float32)
    maps = []
    for i in range(8):
        d = dict(shared)
        d["x"] = np.ascontiguousarray(x[i])
        d["mem"] = np.ascontiguousarray(mem[i])
        maps.append(d)
    return maps


_NC_CACHE = {}


def kernel(**inputs):
    if "nc" not in _NC_CACHE:
        _NC_CACHE["nc"] = build()
    nc = _NC_CACHE["nc"]
    maps = prep_inputs(inputs)
    res = run_bass_kernel_spmd(nc, maps, core_ids=list(range(8)))
    return np.stack([np.asarray(r["out"], dtype=np.float32) for r in res.results], 0)
```

```python
import numpy as np
import ml_dtypes
import concourse.bass as bass
import concourse.mybir as mybir
from concourse.bass_utils import run_bass_kernel_spmd

F32 = mybir.dt.float32
BF16 = mybir.dt.bfloat16
AF = mybir.ActivationFunctionType
ALU = mybir.AluOpType
AX = mybir.AxisListType

N_TOK = 2048
D = 1024
NB = 16
EPS = 1e-6
D_FF = 2816
N_IN = 3328


class Res:
    __slots__ = ("name", "w", "r", "big")

    def __init__(self, name):
        self.name = name
        self.w = None
        self.r = {}
        self.big = False


class EngQ:
    def __init__(self, name):
        self.name = name
        self.ops = []
        self.epoch = 0
        self.cnt = 0
        self.total = 0
        self.waited = {}
        self.pending = False


class Prog:
    ENGS = ("pe", "act", "dve", "pool", "sp")
    LIMIT = 900
    NEPOCH = {"pe": 14, "act": 10, "dve": 14, "pool": 4, "sp": 1}

    def __init__(self, nc, n_dma_sems=32):
        self.nc = nc
        self.q = {e: EngQ(e) for e in self.ENGS}
        self.sems = {}
        self.n_dma_sems = n_dma_sems
        self.dma_cnt = [0] * n_dma_sems
        self.dma_pool = {"pool": list(range(0, 16)), "sp": list(range(16, 26)), "act": list(range(26, 32))}
        self.dma_rr = {"pool": 0, "sp": 0, "act": 0}
        self._ctx = []
        self.res = {}

    def R(self, *key):
        r = self.res.get(key)
        if r is None:
            r = Res(key)
            self.res[key] = r
        return r

    def open(self):
        nc = self.nc
        for e in self.ENGS:
            for ep in range(self.NEPOCH[e]):
                c = nc.semaphore("s_%s%d" % (e, ep))
                self.sems[(e, ep)] = c.__enter__()
                self._ctx.append(c)
        for i in range(self.n_dma_sems):
            c = nc.semaphore("s_dma%d" % i)
            self.sems[("dma", i)] = c.__enter__()
            self._ctx.append(c)

    def close(self):
        for c in reversed(self._ctx):
            c.__exit__(None, None, None)

    def _need(self, eng, tok, same_ok):
        if tok is None:
            return
        key, val = tok
        q = self.q[eng]
        if key[0] == "dma":
            if q.waited.get(key, 0) >= val:
                return
            q.waited[key] = val
        else:
            src, ep = key
            if src == eng and not same_ok:
                return
            if q.waited.get(src, (-1, 0)) >= (ep, val):
                return
            q.waited[src] = (ep, val)
        sem = self.sems[key]
        q.ops.append(lambda h, sem=sem, val=val: h.wait_ge(sem, val))

    def _deps(self, eng, reads, writes):
        for r in reads:
            self._need(eng, r.w, same_ok=(eng != "pe" and not r.big))
        so = (eng != "pe")
        for w in writes:
            self._need(eng, w.w, same_ok=so)
            for tok in w.r.values():
                self._need(eng, tok, same_ok=so)

    def op(self, eng, fn, reads=(), writes=(), inc=True, big=False):
        q = self.q[eng]
        if q.cnt >= self.LIMIT and not q.pending:
            q.epoch += 1
            q.cnt = 0
            assert q.epoch < self.NEPOCH[eng], "out of epochs for " + eng
        self._deps(eng, reads, writes)
        key = (eng, q.epoch)
        val = q.cnt + 1
        tok = (key, val)
        if inc:
            q.cnt = val
            q.total += 1
            sem = self.sems[key]
            q.ops.append(lambda h, fn=fn, sem=sem: fn(h).then_inc(sem, 1))
            q.pending = False
        else:
            q.ops.append(lambda h, fn=fn: fn(h))
            q.pending = True
        for r in reads:
            r.r[eng] = tok
        for w in writes:
            w.w = tok
            w.r = {}
            w.big = big
        return tok

    def dma(self, eng, out, in_, reads=(), writes=(), also_wait=(), **kw):
        pl = self.dma_pool[eng]
        i = pl[self.dma_rr[eng] % len(pl)]
        self.dma_rr[eng] += 1
        key = ("dma", i)
        if self.dma_cnt[i] > 0:
            self._need(eng, (key, 16 * self.dma_cnt[i]), same_ok=True)
        self._deps(eng, reads, list(writes) + list(also_wait))
        self.dma_cnt[i] += 1
        assert self.dma_cnt[i] < 60
        val = 16 * self.dma_cnt[i]
        sem = self.sems[key]
        q = self.q[eng]
        q.ops.append(lambda h, sem=sem, out=out, in_=in_, kw=kw:
                     h.dma_start(out=out, in_=in_, **kw).then_inc(sem, 16))
        tok = (key, val)
        for r in reads:
            r.r[key] = tok
        for w in writes:
            w.w = tok
            w.r = {}
            w.big = False
        return tok

    def wait_all(self, eng, skip=()):
        for r in self.res.values():
            if r.name[0] in skip:
                continue
            self._need(eng, r.w, same_ok=True)
            for tok in list(r.r.values()):
                self._need(eng, tok, same_ok=True)

    def emit(self):
        nc = self.nc
        for e in self.ENGS:
            assert not self.q[e].pending, "engine %s ends with non-inc'd instr" % e
        with nc.Block() as block:
            @block.tensor
            def _(h):
                for f in self.q["pe"].ops:
                    f(h)

            @block.scalar
            def _(h):
                for f in self.q["act"].ops:
                    f(h)

            @block.vector
            def _(h):
                for f in self.q["dve"].ops:
                    f(h)

            @block.gpsimd
            def _(h):
                for f in self.q["pool"].ops:
                    f(h)

            @block.sync
            def _(h):
                for f in self.q["sp"].ops:
                    f(h)


class StopBuild(Exception):
    pass


def build(stop=99, dbg=()):
    st = {}
    try:
        return _build(stop, dbg, st)
    except StopBuild:
        return finish(st["nc"], st["P"])


def _build(stop, dbg, st):
    nc = bass.Bass("TRN2", target_bir_lowering=False)
    P = Prog(nc)
    P.open()
    R = P.R
    st["nc"] = nc
    st["P"] = P

    def chk(level):
        if stop <= level:
            raise StopBuild()

    def din(name, shape, dt=F32):
        return nc.dram_tensor(name, list(shape), dt, kind="ExternalInput").ap()

    x_d = din("x", [N_TOK, D])
    mem_d = din("mem", [256, D])
    w_in_d = din("w_in", [D, N_IN])
    w_out_d = din("w_out", [D, D])
    w_xq_d = din("w_xq", [D, D])
    w_xkv_d = din("w_xkv", [D, 2 * D])
    w_xo_d = din("w_xo", [D, D])
    w_up_d = din("w_up", [D, 2 * D_FF])
    w_down_d = din("w_down", [D_FF, D])
    gains_d = din("gains", [7, D])
    qkg_d = din("qkg", [1, 128])
    hgo_d = din("hgo", [1, 128])
    lbT_d = din("lbT", [128, 16])
    convT_d = din("convT", [128, 44 * 4])
    ident_d = din("ident", [128, 128], BF16)
    rope_d = din("rope", [128, 2 * 16 * 32])
    mask_d = din("mask", [128, 2 * 128], BF16)
    out_d = nc.dram_tensor("out", [N_TOK, D], F32, kind="ExternalOutput").ap()
    dbg_d = {}
    for name, shape in dbg:
        dbg_d[name] = nc.dram_tensor("dbg_" + name, list(shape), F32, kind="ExternalOutput").ap()

    def sb(name, shape, dt):
        return nc.alloc_sbuf_tensor("sb_" + name, list(shape), dt)

    hT = sb("hT", [128, 8, N_TOK], BF16)
    mixT = sb("mixT", [128, 8, N_TOK], BF16)
    arena1 = sb("arena1", [128, 16384], F32)
    arena2 = sb("arena2", [128, 15872], F32)
    ident = sb("ident", [128, 128], BF16)
    ones_bf = sb("ones_bf", [128, 128], BF16)
    gpre = sb("gpre", [128, D], F32)
    gpost = sb("gpost", [128, D], F32)
    stat = sb("stat", [128, 64], F32)
    stat2 = sb("stat2", [128, 64], F32)
    junk = sb("junk", [128, D], BF16)
    hb = sb("hb", [128, 2, D], BF16)
    lbT = sb("lbT", [128, 16], F32)
    lbv = sb("lbv", [128, 40], F32)
    convT = sb("convT", [128, 44 * 4], F32)
    epsc = sb("epsc", [128, 1], F32)
    psf = nc.alloc_psum_tensor("psf", [128, 3, 2, 512], F32)
    psb = nc.alloc_psum_tensor("psb", [128, 2, 1024], BF16)

    X = arena1[:].rearrange("p (b d) -> p b d", d=D)

    def a1_bf(off_bytes, shape):
        n = int(np.prod(shape))
        v = arena1[:, off_bytes // 4: off_bytes // 4 + n // 2].bitcast(BF16)
        return v, off_bytes + n * 2

    def a2_bf(off_bytes, n):
        v = arena2[:, off_bytes // 4: off_bytes // 4 + n // 2].bitcast(BF16)
        return v, off_bytes + n * 2

    def a2_f(off_bytes, n):
        v = arena2[:, off_bytes // 4: off_bytes // 4 + n]
        return v, off_bytes + n * 4

    def a1_f(off_bytes, n):
        v = arena1[:, off_bytes // 4: off_bytes // 4 + n]
        return v, off_bytes + n * 4

    def MM(out, lhsT, rhs, start, stop, reads, writes, inc=True, tp=None):
        if tp is None:
            return P.op("pe", lambda h: h.matmul(out, lhsT=lhsT, rhs=rhs, start=start, stop=stop),
                        reads=reads, writes=writes, inc=inc)
        return P.op("pe", lambda h: h.matmul(out, lhsT=lhsT, rhs=rhs, start=start, stop=stop, tile_position=tp),
                    reads=reads, writes=writes, inc=inc)

    def TR(out, in_, reads, writes, inc=True):
        return P.op("pe", lambda h: h.transpose(out, in_, ident[:]), reads=reads, writes=writes, inc=inc)

    BIG_N = 128

    def isbig(ap):
        n = 1
        for d_ in ap.shape[1:]:
            n *= d_
        return n >= BIG_N

    def ACT(out, in_, func, reads, writes, scale=None, bias=None, accum=None):
        kw = {}
        if scale is not None:
            kw["scale"] = scale
        if bias is not None:
            kw["bias"] = bias
        if accum is not None:
            kw["accum_out"] = accum
        return P.op("act", lambda h: h.activation(out=out, in_=in_, func=func, **kw), reads=reads, writes=writes,
                    big=(accum is None and isbig(out)))

    def TT(eng, out, in0, in1, op, reads, writes):
        return P.op(eng, lambda h: h.tensor_tensor(out=out, in0=in0, in1=in1, op=op), reads=reads, writes=writes, big=isbig(out))

    def TS(eng, out, in0, s1, s2, op0, op1, reads, writes):
        if s2 is None:
            return P.op(eng, lambda h: h.tensor_scalar(out=out, in0=in0, scalar1=s1, scalar2=None, op0=op0),
                        reads=reads, writes=writes, big=isbig(out))
        return P.op(eng, lambda h: h.tensor_scalar(out=out, in0=in0, scalar1=s1, scalar2=s2, op0=op0, op1=op1),
                    reads=reads, writes=writes, big=isbig(out))

    def STT(eng, out, in0, scalar, in1, op0, op1, reads, writes):
        return P.op(eng, lambda h: h.scalar_tensor_tensor(out=out, in0=in0, scalar=scalar, in1=in1, op0=op0, op1=op1),
                    reads=reads, writes=writes, big=isbig(out))

    def CP(eng, out, in_, reads, writes):
        if eng == "act":
            return ACT(out, in_, AF.Copy, reads, writes)
        return P.op(eng, lambda h: h.tensor_copy(out=out, in_=in_), reads=reads, writes=writes, big=isbig(out))

    def RECIP(out, in_, reads, writes):
        return P.op("dve", lambda h: h.reciprocal(out=out, in_=in_), reads=reads, writes=writes)

    def MEMSET(eng, ap, val, writes):
        return P.op(eng, lambda h: h.memset(ap, val), writes=writes)

    def barrier(skip=()):
        for e in Prog.ENGS:
            P.wait_all(e, skip)

    def load_w(dst, src_rows_cols, reads_w, also_wait=()):
        src = src_rows_cols.rearrange("(c p) n -> p c n", p=128)
        nch = src.shape[1]
        for c in range(nch):
            P.dma("pool", dst[:, c, :], src[:, c, :], writes=reads_w, also_wait=also_wait)

    def gain_load(dst, row, res):
        P.dma("sp", dst[:], gains_d[row:row + 1, :].partition_broadcast(128), writes=[res])

    def dump(name, ap_sb, reads, view=None):
        if name in dbg_d:
            P.dma("sp", dbg_d[name] if view is None else view(dbg_d[name]), ap_sb, reads=reads)

    P.dma("sp", ident[:], ident_d, writes=[R("ident")])
    P.dma("sp", lbT[:], lbT_d, writes=[R("lbT")])
    P.dma("sp", convT[:], convT_d, writes=[R("convT")])
    MEMSET("pool", ones_bf[:], 1.0, [R("ones")])
    MEMSET("pool", epsc[:], EPS, [R("epsc")])
    MEMSET("pool", stat[:], 0.0, [R("stat")])
    MEMSET("pool", stat2[:], 0.0, [R("stat2")])

    def prenorm_T(grow, nblk=NB, src=None, dstT=None, tag="h", srcR=None):
        src = X if src is None else src
        dstT = hT if dstT is None else dstT
        srcR = (lambda b: R("X", b)) if srcR is None else srcR
        gain_load(gpre, grow, R("gpre"))
        for b in range(nblk):
            ACT(junk[:], src[:, b, :], AF.Square, [srcR(b)], [R("junk"), R("stat")], accum=stat[:, b:b + 1])
        TS("dve", stat2[:, 0:nblk], stat[:, 0:nblk], 1.0 / D, EPS, ALU.mult, ALU.add, [R("stat")], [R("stat2")])
        ACT(stat2[:, 0:nblk], stat2[:, 0:nblk], AF.Ln, [R("stat2")], [R("stat2")])
        ACT(stat2[:, 0:nblk], stat2[:, 0:nblk], AF.Exp, [R("stat2")], [R("stat2")], scale=-0.5)
        for b in range(nblk):
            k = b % 2
            STT("dve", hb[:, k, :], src[:, b, :], stat2[:, b:b + 1], gpre[:], ALU.mult, ALU.mult,
                [srcR(b), R("stat2"), R("gpre")], [R("hb", k)])
            for c in range(8):
                TR(psb[:, k, c * 128:(c + 1) * 128], hb[:, k, c * 128:(c + 1) * 128],
                   [R("hb", k), R("ident")], [R("psb", k)], inc=(c == 7))
            CP("act", dstT[:, :, b * 128:(b + 1) * 128], psb[:, k, :].rearrange("p (c t) -> p c t", t=128),
               [], [R("psb", k), R(tag, b)])

    o2 = 0
    Watt, o2 = a2_bf(o2, 8 * 768); Watt = Watt.rearrange("p (c n) -> p c n", n=768)
    rope_raw, o2 = a2_f(o2, 1024)
    qkgB, o2 = a2_f(o2, 128)
    o2_attw = o2
    load_w(Watt, w_in_d[:, 0:768], [R("Watt")])
    P.dma("sp", rope_raw, rope_d, writes=[R("rope_raw")])
    P.dma("sp", qkgB, qkg_d.partition_broadcast(128), writes=[R("qkgB")])
    for b in range(NB):
        P.dma("sp" if b % 2 else "act", X[:, b, :], x_d[b * 128:(b + 1) * 128, :], writes=[R("X", b)])
    prenorm_T(0)
    if "hT" in dbg_d:
        tmpf = arena2[:, 13000:13000 + 2048]
        for c in range(8):
            CP("dve", tmpf, hT[:, c, :], [R("h", b) for b in range(NB)], [R("dbgtmp")])
            P.dma("sp", dbg_d["hT"][c * 128:(c + 1) * 128, :], tmpf, reads=[R("dbgtmp")])
    if stop <= 1:
        return finish(nc, P)
    barrier()
    if stop <= 1.1:
        return finish(nc, P)


    o1 = 0
    ropeT = []
    for i in range(8):
        v, o1 = a1_f(o1, 512)
        ropeT.append(v.rearrange("p (b i) -> p b i", i=32))
    QT, o1 = a1_bf(o1, [4 * N_TOK]); QT = QT.rearrange("p (j t) -> p j t", t=N_TOK)
    KTd, o1 = a1_bf(o1, [2 * N_TOK]); KTd = KTd.rearrange("p (g t) -> p g t", t=N_TOK)
    VA, o1 = a1_bf(o1, [NB * 2 * 192]); VA = VA.rearrange("p (b g d) -> p b g d", g=2, d=192)
    o2 = o2_attw
    qkv = []; sqb = []; tq = []; qn = []; qr = []
    for i in range(2):
        v, o2 = a2_f(o2, 768); qkv.append(v)
        v, o2 = a2_f(o2, 640); sqb.append(v)
        v, o2 = a2_f(o2, 4 * 320); tq.append(v.rearrange("p (a n) -> p a n", n=320))
        v, o2 = a2_f(o2, 640); qn.append(v)
        v, o2 = a2_bf(o2, 768); qr.append(v)
    PT = []
    for i in range(3):
        v, o2 = a2_bf(o2, 1024); PT.append(v.rearrange("p (u n) -> p u n", n=512))
    rc = []
    for i in range(2):
        v, o2 = a2_f(o2, 512); rc.append(v)
    assert o1 <= 65536 and o2 <= 63488, (o1, o2)

    if stop <= 1.16:
        return finish(nc, P)
    MEMSET("pool", VA, 1.0, [R("VA", b) for b in range(NB)])
    if stop <= 1.17:
        return finish(nc, P)
    cosv = rope_raw[:, 0:512].rearrange("p (b i) -> p b i", i=32)
    sinv = rope_raw[:, 512:1024].rearrange("p (b i) -> p b i", i=32)
    for qk in range(2):
        gv = qkgB[:, qk * 64:(qk + 1) * 64].rearrange("p (i two) -> p i two", two=2)
        ge = gv[:, :, 0].unsqueeze(1).broadcast_to([128, NB, 32])
        go = gv[:, :, 1].unsqueeze(1).broadcast_to([128, NB, 32])
        for ti, (tab, gg) in enumerate(((cosv, ge), (sinv, go), (sinv, ge), (cosv, go))):
            TT("dve", ropeT[qk * 4 + ti], tab, gg, ALU.mult, [R("rope_raw"), R("qkgB")], [R("ropeT")])

    def att_proj(b):
        k = b % 2
        tokb = slice(b * 128, (b + 1) * 128)
        for c in range(8):
            MM(psf[:, k, 0, :], hT[:, c, tokb], Watt[:, c, 0:512], c == 0, c == 7,
               [R("h", b), R("Watt")], [R("psf", k, 0)], inc=False)
        for c in range(8):
            MM(psf[:, k, 1, 0:256], hT[:, c, tokb], Watt[:, c, 512:768], c == 0, c == 7,
               [R("h", b), R("Watt")], [R("psf", k, 1)], inc=(c == 7))
        ACT(qkv[k][:, 0:512], psf[:, k, 0, :], AF.Copy, [], [R("psf", k, 0), R("qkv", k)])
        ACT(qkv[k][:, 512:768], psf[:, k, 1, 0:256], AF.Copy, [], [R("psf", k, 1), R("qkv", k)])
        ACT(sqb[k], qkv[k][:, 0:640], AF.Square, [R("qkv", k)], [R("sqb", k)])
        ss = stat[:, 16 + 10 * k: 26 + 10 * k]
        rs = stat2[:, 16 + 10 * k: 26 + 10 * k]
        P.op("dve", lambda h: h.tensor_reduce(out=ss, in_=sqb[k].rearrange("p (h d) -> p h d", d=64),
                                              axis=AX.X, op=ALU.add),
             reads=[R("sqb", k)], writes=[R("ss", k)])
        TS("dve", rs, ss, 1.0 / 64, EPS, ALU.mult, ALU.add, [R("ss", k)], [R("rs", k)])
        ACT(rs, rs, AF.Ln, [R("rs", k)], [R("rs", k)])
        ACT(rs, rs, AF.Exp, [R("rs", k)], [R("rs", k)], scale=-0.5)
        for qk, eng, nh, c0 in ((0, "dve", 8, 0), (1, "dve", 2, 512)):
            src = qkv[k][:, c0:c0 + nh * 64].rearrange("p (h i two) -> p h i two", i=32, two=2)
            xe = src[:, :, :, 0]
            xo = src[:, :, :, 1]
            tabs = [ropeT[qk * 4 + ti][:, b, :].unsqueeze(1).broadcast_to([128, nh, 32]) for ti in range(4)]
            tt = [tq[k][:, a, 0:nh * 32].rearrange("p (h i) -> p h i", i=32) for a in range(4)]
            rd = [R("qkv", k), R("ropeT")]
            wr = [R("tq", k, qk)]
            TT(eng, tt[0], xe, tabs[0], ALU.mult, rd, wr)
            TT(eng, tt[1], xo, tabs[1], ALU.mult, rd, wr)
            TT(eng, tt[2], xe, tabs[2], ALU.mult, rd, wr)
            TT(eng, tt[3], xo, tabs[3], ALU.mult, rd, wr)
            dst = qn[k][:, qk * 512: qk * 512 + nh * 64].rearrange("p (h i two) -> p h i two", i=32, two=2)
            TT(eng, dst[:, :, :, 0], tt[0], tt[1], ALU.subtract, wr, [R("qn", k, qk)])
            TT(eng, dst[:, :, :, 1], tt[2], tt[3], ALU.add, wr, [R("qn", k, qk)])
            rsb = rs[:, qk * 8: qk * 8 + nh]
            if qk == 0:
                TT(eng, qr[k][:, 0:512].rearrange("p (h d) -> p h d", d=64),
                   qn[k][:, 0:512].rearrange("p (h d) -> p h d", d=64),
                   rsb.unsqueeze(2).broadcast_to([128, 8, 64]), ALU.mult,
                   [R("qn", k, 0), R("rs", k)], [R("qr", k, 0)])
            else:
                for dup in range(2):
                    TT(eng, qr[k][:, 512:768].rearrange("p (g u d) -> p g u d", u=2, d=64)[:, :, dup, :],
                       qn[k][:, 512:640].rearrange("p (g d) -> p g d", d=64),
                       rsb.unsqueeze(2).broadcast_to([128, 2, 64]), ALU.mult,
                       [R("qn", k, 1), R("rs", k)], [R("qr", k, 1)])
        CP("dve", VA[:, b, :, 64:128], qkv[k][:, 640:768].rearrange("p (g d) -> p g d", d=64),
           [R("qkv", k)], [R("VA", b)])
    def att_proj2(b):
        k = b % 2
        tokb = slice(b * 128, (b + 1) * 128)
        for j in range(6):
            TR(psb[:, k, j * 128:(j + 1) * 128], qr[k][:, j * 128:(j + 1) * 128],
               [R("qr", k, 0), R("qr", k, 1), R("ident")], [R("psb", k)], inc=(j == 5))
        CP("act", QT[:, :, tokb], psb[:, k, 0:512].rearrange("p (j t) -> p j t", t=128), [], [R("psb", k), R("QT", b)])
        CP("dve", KTd[:, :, tokb], psb[:, k, 512:768].rearrange("p (g t) -> p g t", t=128), [], [R("psb", k), R("KT", b)])

    if stop <= 1.2:
        return finish(nc, P)
    for b in range(NB + 1):
        if b < NB:
            att_proj(b)
        if b >= 1:
            att_proj2(b - 1)
    if "QT" in dbg_d:
        tmpf = arena2[:, 14000:14000 + 1024]
        for c in range(4):
            for hf in range(2):
                CP("dve", tmpf, QT[:, c, hf * 1024:(hf + 1) * 1024], [R("QT", b) for b in range(NB)], [R("dbgtmp")])
                P.dma("sp", dbg_d["QT"][c * 128:(c + 1) * 128, hf * 1024:(hf + 1) * 1024], tmpf, reads=[R("dbgtmp")])
    if stop <= 1.5:
        return finish(nc, P)

    Wh, _ = a2_bf(0, 8 * 640); Wh = Wh.rearrange("p (c n) -> p c n", n=640)
    for gi, c0 in enumerate((768, 1280, 1792, 2304, 2816)):
        src = w_in_d[:, c0: c0 + 128].rearrange("(c p) n -> p c n", p=128)
        P.dma("pool", Wh[:, :, gi * 128:(gi + 1) * 128], src, writes=[R("Wh")], also_wait=[R("Watt")])
    PREFETCHED_WH0 = True
    pt_i = [0]
    psbf = [psb[:, 0, :].bitcast(F32), psb[:, 1, :].bitcast(F32)]
    unit_i = [0]

    def st_mm_unit(m, qt, kb):
        g = m // 2
        qs = slice(qt * 512, (qt + 1) * 512)
        qreads = [R("QT", qt * 4 + i) for i in range(4)]
        k = kb % 2
        for par in range(2):
            rows = slice(par * 64, par * 64 + 64)
            MM(psf[:, k, par, :], KTd[rows, g, kb * 128:(kb + 1) * 128], QT[rows, m, qs], True, True,
               qreads + [R("KT", kb)], [R("psf", k, par)], inc=(par == 1))

    UNITS = [(m_, qt_) for m_ in range(4) for qt_ in range(4)]

    def att_pair_tile(m, qt):
        g = m // 2
        j = m
        qs = slice(qt * 512, (qt + 1) * 512)
        uidx = UNITS.index((m, qt))
        ui = unit_i[0] % 2
        unit_i[0] += 1
        if ui == 0:
            accs = [(psf[:, 2, 0, :], R("psf", 2, 0)), (psf[:, 2, 1, :], R("psf", 2, 1))]
        else:
            accs = [(psbf[0], R("psb", 0)), (psbf[1], R("psb", 1))]
        qreads = [R("QT", qt * 4 + i) for i in range(4)]

        def st_mm(kb):
            k = kb % 2
            for par in range(2):
                rows = slice(par * 64, par * 64 + 64)
                MM(psf[:, k, par, :], KTd[rows, g, kb * 128:(kb + 1) * 128], QT[rows, j, qs], True, True,
                   qreads + [R("KT", kb)], [R("psf", k, par)], inc=(par == 1))

        if uidx == 0:
            st_mm(0)
        for kb in range(16):
            if kb + 1 < 16:
                st_mm(kb + 1)
            elif uidx + 1 < len(UNITS):
                st_mm_unit(UNITS[uidx + 1][0], UNITS[uidx + 1][1], 0)
            pi = pt_i[0] % 3
            pt_i[0] += 1
            k = kb % 2
            ACT(PT[pi], psf[:, k, :, :], AF.Exp, [], [R("psf", k, 0), R("psf", k, 1), R("PT", pi)], scale=0.125)
            for par in range(2):
                vsl = slice(64, 192) if par == 0 else slice(0, 128)
                MM(accs[par][0], VA[:, kb, g, vsl], PT[pi][:, par, :], kb == 0, kb == 15,
                   [R("VA", kb), R("PT", pi)], [accs[par][1]], inc=(par == 1))
        rcb = rc[ui]
        for par in range(2):
            rows = slice(par * 64, par * 64 + 64)
            orow = slice((1 - par) * 64, (1 - par) * 64 + 64)
            acc, accR = accs[par]
            P.op("dve", lambda hh, acc=acc, rows=rows, orow=orow: hh.reciprocal(out=rcb[rows, :], in_=acc[orow, :]),
                 reads=[], writes=[accR, R("rc", ui, par)])
            TT("dve", mixT[rows, j, qs], acc[rows, :], rcb[rows, :], ALU.mult, [R("rc", ui, par)], [accR, R("mixT", j, qt)])

    for m in range(4):
        for qt in range(4):
            att_pair_tile(m, qt)

    if "att" in dbg_d:
        tmpf = arena2[:, 0:2048]
        for c in range(4):
            CP("dve", tmpf, mixT[:, c, :], [R("mixT", c, qt) for qt in range(4)], [R("dbgtmp")])
            P.dma("sp", dbg_d["att"][c * 128:(c + 1) * 128, :], tmpf, reads=[R("dbgtmp")])
    if stop <= 2:
        return finish(nc, P)
    barrier(skip=("Wh",))

    bank_i = [0]

    def bank():
        i = bank_i[0] % 6
        bank_i[0] += 1
        return psf[:, i // 2, i % 2, :], R("psf", i // 2, i % 2)

    o1 = 0
    fb = {}
    for nm in ("qs", "SG", "E", "LF", "B", "KK", "rmask", "gate"):
        fb[nm], o1 = a1_f(o1, 2048)
    Tbig = arena1[:, 2 * 2048: 4 * 2048].rearrange("p (c v) -> p c v", v=128)
    o2 = 0
    Wh, o2 = a2_bf(o2, 8 * 640); Wh = Wh.rearrange("p (c n) -> p c n", n=640)
    Of, o2 = a2_f(o2, 2048); Of3 = Of.rearrange("p (b v) -> p b v", v=128)
    qtl, o2 = a2_bf(o2, 2048)
    ktl, o2 = a2_bf(o2, 2048)
    ktok, o2 = a2_bf(o2, 2048); ktok = ktok.rearrange("p (b v) -> p b v", v=128)
    vtok, o2 = a2_bf(o2, 2048); vtok = vtok.rearrange("p (b v) -> p b v", v=128)
    Sbf, o2 = a2_bf(o2, 32 * 128); Sbf = Sbf.rearrange("p (c v) -> p c v", v=128)
    AT, o2 = a2_bf(o2, 2048); AT = AT.rearrange("p (b v) -> p b v", v=128)
    recb, o2 = a2_bf(o2, 2048); recb = recb.rearrange("p (b v) -> p b v", v=128)
    hgoB, o2 = a2_f(o2, 128)
    maskS, o2 = a2_bf(o2, 256); maskS = maskS.rearrange("p (d t) -> p d t", t=128)
    gtmp, o2 = a2_f(o2, 256)
    assert o1 <= 65536 and o2 <= 63488, (o1, o2)
    gate3 = fb["gate"].rearrange("p (b v) -> p b v", v=128)

    P.dma("sp", hgoB, hgo_d.partition_broadcast(128), writes=[R("hgoB")])
    P.dma("sp", maskS, mask_d.rearrange("p (d t) -> p d t", t=128), writes=[R("maskS")])
    MEMSET("pool", fb["rmask"], 1.0, [R("rmask")])
    MEMSET("pool", fb["rmask"].rearrange("p (c t) -> p c t", t=64)[:, :, 0:1], 0.0, [R("rmask")])
    lv = lbT[:].rearrange("p (d l h) -> p d l h", l=2, h=4)
    TT("dve", lbv[:, 0:8].rearrange("p (d h) -> p d h", h=4), lv[:, :, 0, :], lv[:, :, 1, :], ALU.subtract,
       [R("lbT")], [R("lbv")])
    ACT(lbv[:, 0:8], lbv[:, 0:8], AF.Exp, [R("lbv")], [R("lbv")], scale=-1.0)
    TS("dve", lbv[:, 0:8], lbv[:, 0:8], 1.0, None, ALU.add, None, [R("lbv")], [R("lbv")])
    RECIP(lbv[:, 0:8], lbv[:, 0:8], [R("lbv")], [R("lbv")])
    TS("dve", lbv[:, 8:16], lbv[:, 0:8], -1.0, 1.0, ALU.mult, ALU.add, [R("lbv")], [R("lbv")])
    TS("dve", lbv[:, 16:24], lbv[:, 0:8], -0.5, 0.5, ALU.mult, ALU.add, [R("lbv")], [R("lbv")])
    TS("dve", lbv[:, 24:32], lbv[:, 0:8], 0.5, 0.5, ALU.mult, ALU.add, [R("lbv")], [R("lbv")])
    TS("dve", lbv[:, 32:40], lbv[:, 0:8], 0.5, -0.5, ALU.mult, ALU.add, [R("lbv")], [R("lbv")])

    ALLH = [R("h", b) for b in range(NB)]
    chk(2.1)

    def load_Wh(hh, extra=()):
        for gi, c0 in enumerate((768, 1280, 1792, 2304, 2816)):
            src = w_in_d[:, c0 + hh * 128: c0 + (hh + 1) * 128].rearrange("(c p) n -> p c n", p=128)
            P.dma("pool", Wh[:, :, gi * 128:(gi + 1) * 128], src, writes=[R("Wh")] + list(extra))

    def hgrn_head(hh):
        for bp in range(8):
            bk, bkR = bank()
            for u in range(2):
                b = 2 * bp + u
                for c in range(8):
                    MM(bk[:, u * 256:(u + 1) * 256], hT[:, c, b * 128:(b + 1) * 128], Wh[:, c, 384:640], c == 0, c == 7,
                       [R("h", b), R("Wh")], [bkR], inc=(u == 1 and c == 7))
            bv = bk.rearrange("p (u n) -> p u n", n=256)
            ACT(vtok[:, 2 * bp:2 * bp + 2, :], bv[:, :, 0:128], AF.Copy, [], [bkR, R("vtok")])
            g3 = gtmp.rearrange("p (u n) -> p u n", n=128)
            ACT(gate3[:, 2 * bp:2 * bp + 2, :], bv[:, :, 128:256], AF.Silu, [], [bkR, R("gate")])

        chk(2.2)

        def fm_proj(gi, dst, dstname):
            kept = []
            for qt in range(4):
                bk, bkR = bank()
                for c in range(8):
                    MM(bk, Wh[:, c, gi * 128:(gi + 1) * 128], hT[:, c, qt * 512:(qt + 1) * 512], c == 0, c == 7,
                       ALLH[qt * 4:qt * 4 + 4] + [R("Wh")], [bkR], inc=(c == 7))
                ACT(dst[:, qt * 512:(qt + 1) * 512], bk, AF.Tanh, [], [bkR, R(dstname, qt // 2)], scale=0.5)
                kept.append((bk, bkR))
            return kept

        for qt in range(4):
            bk, bkR = bank()
            for c in range(8):
                MM(bk, Wh[:, c, 0:128], hT[:, c, qt * 512:(qt + 1) * 512], c == 0, c == 7,
                   ALLH[qt * 4:qt * 4 + 4] + [R("Wh")], [bkR], inc=(c == 7))
            sl = slice(qt * 512, (qt + 1) * 512)
            ACT(fb["qs"][:, sl], bk, AF.Silu, [], [bkR, R("qs", qt // 2)])

        chk(2.3)
        for dr in range(2):
            li = dr * 4 + hh
            ha_ap = lbv[:, 16 + li: 17 + li]
            hb_ap = lbv[:, 24 + li: 25 + li]
            nha_ap = lbv[:, 32 + li: 33 + li]
            E, SG, LF, B, KK = fb["E"], fb["SG"], fb["LF"], fb["B"], fb["KK"]
            EB, ENB = SG, LF
            H2 = (slice(0, 1024), slice(1024, 2048))
            if dr == 0:
                fm_proj(1, E, "E")
                TH, THn, CUM, CUMn = E, "E", B, "B"
            else:
                TH, THn, CUM, CUMn = B, "B", E, "E"
            for hf in range(2):
                sl = H2[hf]
                ACT(LF[:, sl], TH[:, sl], AF.Ln, [R(THn, hf), R("lbv")], [R("LF", hf)], scale=ha_ap, bias=hb_ap)
            for hf in range(2):
                sl = H2[hf]
                TS("dve", KK[:, sl], TH[:, sl], nha_ap, ha_ap, ALU.mult, ALU.add, [R(THn, hf), R("lbv")], [R("KK", hf)])
                P.op("dve", lambda h, sl=sl, CUM=CUM: h.tensor_tensor_scan(out=CUM[:, sl], data0=fb["rmask"][:, sl],
                                                                           data1=LF[:, sl], initial=0.0, op0=ALU.mult, op1=ALU.add),
                     reads=[R("rmask"), R("LF", hf)], writes=[R(CUMn, hf)])
                if dr == 1:
                    TT("dve", LF[:, sl], LF[:, sl], CUM[:, sl], ALU.subtract, [R("LF", hf), R(CUMn, hf)], [R("LF", hf)])
                    C3 = CUM[:, sl].rearrange("p (c t) -> p c t", t=64)
                    TT("dve", B[:, sl].rearrange("p (c t) -> p c t", t=64), LF[:, sl].rearrange("p (c t) -> p c t", t=64),
                       C3[:, :, 63:64].broadcast_to([128, 16, 64]), ALU.add, [R("LF", hf), R(CUMn, hf)], [R("B", hf)])
            Bs, Bn = B, "B"
            for hf in range(2):
                sl = H2[hf]
                ACT(EB[:, sl], Bs[:, sl], AF.Exp, [R(Bn, hf)], [R("SG", hf)])
                ACT(ENB[:, sl], Bs[:, sl], AF.Exp, [R(Bn, hf)], [R("LF", hf)], scale=-1.0)
            for hf in range(2):
                sl = H2[hf]
                TT("dve", qtl[:, sl], fb["qs"][:, sl], EB[:, sl], ALU.mult, [R("qs", hf), R("SG", hf)], [R("qtl", hf)])
                TT("dve", ktl[:, sl], KK[:, sl], ENB[:, sl], ALU.mult, [R("KK", hf), R("LF", hf)], [R("ktl", hf)])
            chk(2.4)
            for half in range(2):
                for i in range(8):
                    b = half * 8 + i
                    TR(psb[:, half, i * 128:(i + 1) * 128], ktl[:, b * 128:(b + 1) * 128], [R("ktl", half), R("ident")],
                       [R("psb", half)], inc=(i == 7))
                CP("act" if half else "dve", ktok[:, half * 8:(half + 1) * 8, :],
                   psb[:, half, :].rearrange("p (b v) -> p b v", v=128), [], [R("psb", half), R("ktok")])
            for bg in range(4):
                bk, bkR = bank()
                for u in range(4):
                    b = 4 * bg + u
                    MM(bk[:, u * 128:(u + 1) * 128], ktl[:, b * 128:(b + 1) * 128], qtl[:, b * 128:(b + 1) * 128], True, True,
                       [R("ktl", bg // 2), R("qtl", bg // 2)], [bkR], inc=(u == 3))
                TT("dve", AT[:, 4 * bg:4 * bg + 4, :], bk.rearrange("p (u t) -> p u t", t=128),
                   maskS[:, dr, :].unsqueeze(1).broadcast_to([128, 4, 128]), ALU.mult, [R("maskS")], [bkR, R("AT")])
            chk(2.5)
            MEMSET("pool", Sbf[:, 0 if dr == 0 else 31, :], 0.0, [R("Sbf")])
            DEAD = [R(nm, hf) for nm in ("E", "LF") for hf in range(2)]
            SGR = [R("SG", 0), R("SG", 1)]
            n = 0
            cprev = None
            for bgi in range(4):
                bg = bgi if dr == 0 else 3 - bgi
                ub = [bank() for _ in range(2)]
                for u in range(4):
                    b = 4 * bg + u
                    for j in range(2):
                        MM(ub[j][0][:, u * 128:(u + 1) * 128], ktok[64 * j:64 * j + 64, b, :], vtok[64 * j:64 * j + 64, b, :],
                           True, True, [R("ktok"), R("vtok")], [ub[j][1]], inc=(u == 3 and j == 1), tp=(64 * j, 0))
                order = [(u, j) for u in range(4) for j in range(2)]
                if dr == 1:
                    order = order[::-1]
                for (u, j) in order:
                    c = 2 * (4 * bg + u) + j
                    Uc = ub[j][0][:, u * 128:(u + 1) * 128]
                    if n == 0:
                        CP("dve", Tbig[:, c, :], Uc, [], [ub[j][1], R("Tb", c)] + DEAD)
                    else:
                        dprev = EB[:, 64 * cprev + 63: 64 * cprev + 64] if dr == 0 else EB[:, 64 * cprev: 64 * cprev + 1]
                        STT("dve", Tbig[:, c, :], Tbig[:, cprev, :], dprev, Uc, ALU.mult, ALU.add,
                            [R("Tb", cprev)] + SGR, [ub[j][1], R("Tb", c)])
                    cprev = c
                    n += 1
            EB3 = EB.rearrange("p (c t) -> p c t", t=64)
            if dr == 0:
                TT("dve", Sbf[:, 1:32, :], Tbig[:, 0:31, :], EB3[:, 0:31, 63:64].broadcast_to([128, 31, 128]), ALU.mult,
                   [R("Tb", cprev)] + SGR + DEAD, [R("Sbf")])
            else:
                TT("dve", Sbf[:, 0:31, :], Tbig[:, 1:32, :], EB3[:, 1:32, 0:1].broadcast_to([128, 31, 128]), ALU.mult,
                   [R("Tb", cprev)] + SGR + DEAD, [R("Sbf")])
            if dr == 0:
                fm_proj(2, B, "B")
                if hh < 3:
                    load_Wh(hh + 1)
                else:
                    load_w(hT, w_xkv_d, [R("Wkv")], also_wait=ALLH)
            chk(2.6)
            for bg in range(4):
                bk, bkR = bank()
                for u in range(4):
                    b = 4 * bg + u
                    us = slice(u * 128, (u + 1) * 128)
                    MM(bk[:, us], AT[:, b, :], vtok[:, b, :], True, False, [R("AT"), R("vtok")], [bkR], inc=False)
                    for j in range(2):
                        c = 2 * b + j
                        MM(bk[64 * j:64 * j + 64, us], qtl[:, b * 128 + 64 * j: b * 128 + 64 * j + 64], Sbf[:, c, :],
                           False, j == 1, [R("qtl", bg // 2), R("Sbf")], [bkR], inc=(u == 3 and j == 1), tp=(0, 64 * j))
                o3 = Of3[:, 4 * bg:4 * bg + 4, :]
                if dr == 0:
                    ACT(o3, bk.rearrange("p (u v) -> p u v", v=128), AF.Copy, [], [bkR, R("Of")])
                else:
                    TT("dve", o3, bk.rearrange("p (u v) -> p u v", v=128), o3, ALU.add, [], [bkR, R("Of")])
        chk(2.7)
        ACT(fb["E"], Of, AF.Square, [R("Of")], [R("E", 0), R("E", 1)])
        ss = stat[:, 40:56]
        rs = stat2[:, 40:56]
        P.op("dve", lambda h: h.tensor_reduce(out=ss, in_=fb["E"].rearrange("p (b v) -> p b v", v=128), axis=AX.X, op=ALU.add),
             reads=[R("E", 0), R("E", 1)], writes=[R("ssh")])
        TS("dve", rs, ss, 1.0 / 128, EPS, ALU.mult, ALU.add, [R("ssh")], [R("rsh")])
        ACT(rs, rs, AF.Ln, [R("rsh")], [R("rsh")])
        ACT(rs, rs, AF.Exp, [R("rsh")], [R("rsh")], scale=-0.5)
        SG3 = fb["SG"].rearrange("p (b v) -> p b v", v=128)
        TT("dve", SG3, Of3, rs.unsqueeze(2).broadcast_to([128, NB, 128]), ALU.mult, [R("Of"), R("rsh")], [R("SG", 0), R("SG", 1)])
        TT("dve", SG3, SG3, hgoB.unsqueeze(1).broadcast_to([128, NB, 128]), ALU.mult, [R("SG", 0), R("SG", 1), R("hgoB")],
           [R("SG", 0), R("SG", 1)])
        TT("dve", recb, SG3, gate3, ALU.mult, [R("SG", 0), R("SG", 1), R("gate")], [R("recb")])
        for half in range(2):
            for i in range(8):
                b = half * 8 + i
                TR(psb[:, half, i * 128:(i + 1) * 128], recb[:, b, :], [R("recb"), R("ident")], [R("psb", half)], inc=(i == 7))
            CP("act" if half else "dve", mixT[:, 4 + hh, half * 1024:(half + 1) * 1024], psb[:, half, :], [],
               [R("psb", half), R("mixT", 4 + hh, 2 * half), R("mixT", 4 + hh, 2 * half + 1)])

    if not PREFETCHED_WH0:
        load_Wh(0)
    for hh in range(4):
        hgrn_head(hh)
        chk(2.8 + 0.01 * hh)

    if "rec" in dbg_d:
        tmpf = fb["B"]
        for c in range(4):
            CP("dve", tmpf, mixT[:, 4 + c, :], [R("mixT", 4 + c, qt) for qt in range(4)], [R("dbgtmp")])
            P.dma("sp", dbg_d["rec"][c * 128:(c + 1) * 128, :], tmpf, reads=[R("dbgtmp")])
    if stop <= 3:
        return finish(nc, P)
    barrier(skip=("Wkv",))

    junk3 = junk[:].rearrange("p (u n) -> p u n", n=512)
    gpost3 = gpost[:].rearrange("p (u n) -> p u n", n=512)

    def proj_norm_residual(b, lhs_aps, lhs_reads, w_fn, w_reads, resid_ap, resid_reads, out_ap, out_writes):
        k = b % 3
        n = len(lhs_aps)
        for half in range(2):
            for ci in range(n):
                MM(psf[:, k, half, :], lhs_aps[ci], w_fn(ci, half), ci == 0, ci == n - 1, lhs_reads + w_reads[ci],
                   [R("psf", k, half)], inc=(half == 1 and ci == n - 1))
        ss = stat[:, 16 + b:17 + b]
        rs = stat2[:, 16 + b:17 + b]
        PB = [R("psf", k, 0), R("psf", k, 1)]
        ACT(junk3, psf[:, k, :, :], AF.Square, [], PB + [R("junk"), R("pss", b)], accum=ss)
        ACT(rs, ss, AF.Ln, [R("pss", b), R("epsc")], [R("prs", b)], scale=1.0 / D, bias=epsc[:])
        ACT(rs, rs, AF.Exp, [R("prs", b)], [R("prs", b)], scale=-0.5)
        STT("dve", psf[:, k, :, :], psf[:, k, :, :], rs, gpost3, ALU.mult, ALU.mult, [R("prs", b), R("gpost")], PB)
        TT("dve", out_ap.rearrange("p (u n) -> p u n", n=512), resid_ap.rearrange("p (u n) -> p u n", n=512), psf[:, k, :, :],
           ALU.add, resid_reads, PB + out_writes)

    def prenorm_block(b, src_ap, src_reads):
        k = b % 2
        ss = stat[:, 32 + b:33 + b]
        rs = stat2[:, 32 + b:33 + b]
        ACT(junk[:], src_ap, AF.Square, src_reads, [R("junk"), R("fss", b)], accum=ss)
        ACT(rs, ss, AF.Ln, [R("fss", b), R("epsc")], [R("frs", b)], scale=1.0 / D, bias=epsc[:])
        ACT(rs, rs, AF.Exp, [R("frs", b)], [R("frs", b)], scale=-0.5)
        STT("dve", hb[:, k, :], src_ap, rs, gpre[:], ALU.mult, ALU.mult, src_reads + [R("frs", b), R("gpre")], [R("hb", k)])

    def prenorm_block_tr(b):
        k = b % 2
        for c in range(8):
            TR(psb[:, k, c * 128:(c + 1) * 128], hb[:, k, c * 128:(c + 1) * 128], [R("hb", k), R("ident")], [R("psb", k)],
               inc=(c == 7))
        CP("act", hT[:, :, b * 128:(b + 1) * 128], psb[:, k, :].rearrange("p (c t) -> p c t", t=128), [], [R("psb", k), R("h", b)])

    o2 = 0
    Wo, o2 = a2_bf(o2, 8 * 1024); Wo = Wo.rearrange("p (c n) -> p c n", n=1024)
    load_w(Wo, w_out_d, [R("Wo")])
    gain_load(gpost, 1, R("gpost"))
    Wkv = hT
    Wq, _ = a2_bf(16384, 8 * 1024); Wq = Wq.rearrange("p (c n) -> p c n", n=1024)
    Wo2, _ = a2_bf(32768, 8 * 1024); Wo2 = Wo2.rearrange("p (c n) -> p c n", n=1024)
    load_w(Wq, w_xq_d, [R("Wq")])
    load_w(Wo2, w_xo_d, [R("Wo2")])
    for b in range(14):
        P.dma("sp" if b % 2 else "act", X[:, b, :], x_d[b * 128:(b + 1) * 128, :], writes=[R("X", b)])
    o2 = 49152
    memT, o2 = a2_bf(o2, 8 * 256); memT = memT.rearrange("p (c m) -> p c m", m=256)
    KxT, o2 = a2_bf(o2, 8 * 256); KxT = KxT.rearrange("p (j m) -> p j m", m=256)
    Vx, o2 = a2_bf(o2, 2 * 1024); Vx = Vx.rearrange("p (m n) -> p m n", n=1024)
    assert o2 <= 63488, o2
    for mb in range(2):
        P.dma("sp", X[:, 14 + mb, :], mem_d[mb * 128:(mb + 1) * 128, :], writes=[R("X", 14 + mb)])
    prenorm_T(3, nblk=2, src=X[:, 14:16, :], dstT=memT, tag="memT", srcR=lambda b: R("X", 14 + b))
    MT = [R("memT", 0), R("memT", 1)]
    for j in range(8):
        bk, bkR = bank()
        for c in range(8):
            MM(bk[:, 0:256], Wkv[:, c, j * 128:(j + 1) * 128], memT[:, c, :], c == 0, c == 7, MT + [R("Wkv")], [bkR], inc=(c == 7))
        CP("act" if j % 2 else "dve", KxT[:, j, :], bk[:, 0:256], [], [bkR, R("KxT")])
    for mb in range(2):
        for half in range(2):
            bk, bkR = bank()
            for c in range(8):
                MM(bk, memT[:, c, mb * 128:(mb + 1) * 128], Wkv[:, c, 1024 + half * 512: 1024 + (half + 1) * 512], c == 0, c == 7,
                   MT + [R("Wkv")], [bkR], inc=(c == 7))
            CP("act" if half else "dve", Vx[:, mb, half * 512:(half + 1) * 512], bk, [], [bkR, R("Vx")])
    for b in range(14, 16):
        P.dma("sp" if b % 2 else "act", X[:, b, :], x_d[b * 128:(b + 1) * 128, :], writes=[R("X", b)])
    gain_load(gpre, 2, R("gpre"))
    P._deps("act", [], [R("Wkv")])
    for b in range(NB + 2):
        if b < NB:
            tokb = slice(b * 128, (b + 1) * 128)
            proj_norm_residual(b, [mixT[:, c, tokb] for c in range(8)], [R("mixT", c, b // 4) for c in range(8)],
                               lambda ci, half: Wo[:, ci, half * 512:(half + 1) * 512], [[R("Wo")]] * 8,
                               X[:, b, :], [R("X", b)], X[:, b, :], [R("X", b)])
        if 1 <= b <= NB:
            prenorm_block(b - 1, X[:, b - 1, :], [R("X", b - 1)])
        if b >= 2:
            prenorm_block_tr(b - 2)
    if "x1" in dbg_d:
        for b in range(NB):
            P.dma("sp", dbg_d["x1"][b * 128:(b + 1) * 128, :], X[:, b, :], reads=[R("X", b)])
    if stop <= 4:
        return finish(nc, P)
    for e_ in ("act", "dve"):
        P._deps(e_, [], [R("Wo")])

    o2 = 0
    Qx, o2 = a2_bf(o2, 8 * 512); Qx = Qx.rearrange("p (j n) -> p j n", n=512)
    PTx = []
    for i in range(2):
        v, o2 = a2_bf(o2, 1024); PTx.append(v.rearrange("p (u n) -> p u n", n=512))
    rcx2 = []
    for i in range(2):
        v, o2 = a2_f(o2, 512); rcx2.append(v)
    assert o2 <= 16384, o2
    gain_load(gpost, 4, R("gpost"))
    for qt in range(4):
        qs_ = slice(qt * 512, (qt + 1) * 512)
        for j in range(8):
            bk, bkR = bank()
            for c in range(8):
                MM(bk, Wq[:, c, j * 128:(j + 1) * 128], hT[:, c, qs_], c == 0, c == 7, ALLH[qt * 4:qt * 4 + 4] + [R("Wq")],
                   [bkR], inc=(c == 7))
            CP("act" if j % 2 else "dve", Qx[:, j, :], bk, [], [bkR, R("Qx", j)])
        for hx in range(4):
            k = hx % 2
            for mb in range(2):
                for half in range(2):
                    MM(psf[:, k, mb, :], KxT[:, 2 * hx + half, mb * 128:(mb + 1) * 128], Qx[:, 2 * hx + half, :], half == 0, half == 1,
                       [R("KxT"), R("Qx", 2 * hx + half)], [R("psf", k, mb)], inc=(mb == 1 and half == 1))
            ACT(PTx[k], psf[:, k, :, :], AF.Exp, [], [R("psf", k, 0), R("psf", k, 1), R("PTx", k)], scale=1.0 / 16)
            dbk, dbkR = psbf[k], R("psb", k)
            for mb in range(2):
                MM(dbk, ones_bf[:], PTx[k][:, mb, :], mb == 0, mb == 1, [R("ones"), R("PTx", k)], [dbkR], inc=(mb == 1))
            ACT(rcx2[k], dbk, AF.Ln, [], [dbkR, R("rcx", k)])
            ACT(rcx2[k], rcx2[k], AF.Exp, [R("rcx", k)], [R("rcx", k)], scale=-1.0)
            for dh in range(2):
                bk, bkR = psf[:, 2, dh, :], R("psf", 2, dh)
                for mb in range(2):
                    MM(bk, Vx[:, mb, hx * 256 + dh * 128: hx * 256 + (dh + 1) * 128], PTx[k][:, mb, :], mb == 0, mb == 1,
                       [R("Vx"), R("PTx", k)], [bkR], inc=(mb == 1))
                TT("dve", mixT[:, 2 * hx + dh, qs_], bk, rcx2[k], ALU.mult, [R("rcx", k)], [bkR, R("mixT", 2 * hx + dh, qt)])
    gain_load(gpre, 5, R("gpre"))
    for b in range(NB + 2):
        if b < NB:
            tokb = slice(b * 128, (b + 1) * 128)
            proj_norm_residual(b, [mixT[:, c, tokb] for c in range(8)], [R("mixT", c, b // 4) for c in range(8)],
                               lambda ci, half: Wo2[:, ci, half * 512:(half + 1) * 512], [[R("Wo2")]] * 8,
                               X[:, b, :], [R("X", b)], X[:, b, :], [R("X", b)])
        if 1 <= b <= NB:
            prenorm_block(b - 1, X[:, b - 1, :], [R("X", b - 1)])
        if b >= 2:
            prenorm_block_tr(b - 2)
    if "x2" in dbg_d:
        for b in range(NB):
            P.dma("sp", dbg_d["x2"][b * 128:(b + 1) * 128, :], X[:, b, :], reads=[R("X", b)])
    if stop <= 5:
        return finish(nc, P)
    barrier()

    gain_load(gpost, 6, R("gpost"))
    for b in range(NB):
        P.dma("sp" if b % 2 else "act", out_d[b * 128:(b + 1) * 128, :], X[:, b, :], reads=[R("X", b)], writes=[R("outd", b)])
    aT16, _ = a1_bf(0, [16 * N_TOK]); aT16 = aT16.rearrange("p (j t) -> p j t", t=N_TOK)
    aT = [aT16[:, j, :] for j in range(16)] + [mixT[:, j, :] for j in range(6)]
    Wup = [mixT[:, 6 + i, :].rearrange("p (c n) -> p c n", n=256) for i in range(2)]
    o2 = 0
    ugb = []; uvb = []; agb = []; avb = []; ebb = []
    for i in range(2):
        v, o2 = a2_f(o2, 2050); ugb.append(v)
        v, o2 = a2_f(o2, 2050); uvb.append(v)
        v, o2 = a2_f(o2, 1024); agb.append(v)
        v, o2 = a2_f(o2, 1024); avb.append(v)
        v, o2 = a2_f(o2, 1024); ebb.append(v)
    assert o2 <= 63488, o2
    for i in range(2):
        for ub_, nm in ((ugb[i], "ug"), (uvb[i], "uv")):
            MEMSET("pool", ub_[:, 0:1], 0.0, [R(nm, i)])
            MEMSET("pool", ub_[:, 2049:2050], 0.0, [R(nm, i)])

    bank8_i = [0]

    def bank8():
        i = bank8_i[0] % 8
        bank8_i[0] += 1
        if i < 6:
            return psf[:, i // 2, i % 2, :], R("psf", i // 2, i % 2)
        return psbf[i - 6], R("psb", i - 6)

    def ffn_mm(j):
        s_ = j % 2
        wu = Wup[s_]
        P.dma("pool", wu[:, :, 0:128], w_up_d[:, j * 128:(j + 1) * 128].rearrange("(c p) n -> p c n", p=128),
              writes=[R("Wup", s_)])
        P.dma("pool", wu[:, :, 128:256],
              w_up_d[:, D_FF + j * 128: D_FF + (j + 1) * 128].rearrange("(c p) n -> p c n", p=128), writes=[R("Wup", s_)])
        for qt in range(4):
            qs_ = slice(qt * 512, (qt + 1) * 512)
            bg_, bgR = bank8()
            bv_, bvR = bank8()
            for c in range(8):
                MM(bg_, wu[:, c, 0:128], hT[:, c, qs_], c == 0, c == 7, ALLH[qt * 4:qt * 4 + 4] + [R("Wup", s_)], [bgR], inc=False)
            for c in range(8):
                MM(bv_, wu[:, c, 128:256], hT[:, c, qs_], c == 0, c == 7, ALLH[qt * 4:qt * 4 + 4] + [R("Wup", s_)], [bvR],
                   inc=(c == 7))
            ACT(ugb[s_][:, 1 + qt * 512: 1 + (qt + 1) * 512], bg_, AF.Copy, [], [bgR, R("ug", s_)])
            ACT(uvb[s_][:, 1 + qt * 512: 1 + (qt + 1) * 512], bv_, AF.Copy, [], [bvR, R("uv", s_)])

    def ffn_ew(j):
        s_ = j % 2
        ug, uv, ag, av, eb_ = ugb[s_], uvb[s_], agb[s_], avb[s_], ebb[s_]
        for hf in range(2):
            base = 1 + hf * 1024
            ts_ = slice(hf * 1024, (hf + 1) * 1024)
            cw = lambda cj, i: convT[:, cj * 4 + i: cj * 4 + i + 1]
            CT = [R("convT")]
            for (u_, uR, cj, dst, dR) in ((ug, R("ug", s_), j, ag, R("ag", s_)), (uv, R("uv", s_), 22 + j, av, R("av", s_))):
                ACT(dst, u_[:, base:base + 1024], AF.Identity, [uR] + CT, [dR], scale=cw(cj, 1), bias=cw(cj, 3))
                STT("dve", dst, u_[:, base - 1:base + 1023], cw(cj, 0), dst, ALU.mult, ALU.add, [uR] + CT, [dR])
                STT("dve", dst, u_[:, base + 1:base + 1025], cw(cj, 2), dst, ALU.mult, ALU.add, [uR] + CT, [dR])
            ACT(eb_, ag, AF.Silu, [R("ag", s_)], [R("eb", s_)])
            TT("dve", aT[j][:, ts_], eb_, av, ALU.mult, [R("eb", s_), R("av", s_)],
               [R("aT", j)] + ([R("X", j)] if j < 16 else []))

    for j in range(23):
        if j < 22:
            ffn_mm(j)
        if j >= 1:
            ffn_ew(j - 1)
    SETR = [[R(nm, s_) for nm in ("ug", "uv", "ag", "av", "eb")] for s_ in range(2)]

    def wd_alias(j):
        return SETR[0] if j <= 13 else (SETR[0] + SETR[1] if j == 14 else SETR[1])

    Wd, o2 = a2_bf(0, 22 * 1024); Wd = Wd.rearrange("p (j n) -> p j n", n=1024)
    wstg = []
    for i in range(3):
        v, o2 = a2_f(o2, 1024); wstg.append(v)
    assert o2 <= 63488, o2
    nst = 0
    for j in range(22):
        if j % 2 == 0:
            P.dma("pool", Wd[:, j, :], w_down_d[j * 128:(j + 1) * 128, :], writes=[R("Wd", j)], also_wait=wd_alias(j))
        else:
            si = nst % 3
            nst += 1
            P.dma("sp", wstg[si], w_down_d[j * 128:(j + 1) * 128, :], writes=[R("wstg", si)], also_wait=SETR[1])
            P._deps("act", [], wd_alias(j))
            CP("act", Wd[:, j, :], wstg[si], [R("wstg", si)], [R("Wd", j)])
    hTf = hT[:].rearrange("p c t -> p (c t)").bitcast(F32)
    xb_ = [hTf[:, 0:1024], hTf[:, 1024:2048]]
    ob_ = [hTf[:, 2048:3072], hTf[:, 3072:4096]]
    P._deps("dve", [], ALLH)
    for b in range(NB):
        tokb = slice(b * 128, (b + 1) * 128)
        k = b % 2
        P.dma("sp", xb_[k], out_d[b * 128:(b + 1) * 128, :], reads=[R("outd", b)], writes=[R("xb", k)], also_wait=ALLH)
        proj_norm_residual(b, [aT[j][:, tokb] for j in range(22)], [R("aT", j) for j in range(22)],
                           lambda ci, half: Wd[:, ci, half * 512:(half + 1) * 512], [[R("Wd", j)] for j in range(22)],
                           xb_[k], [R("xb", k)], ob_[k], [R("ob", k)])
        P.dma("sp", out_d[b * 128:(b + 1) * 128, :], ob_[k], reads=[R("ob", k)], writes=[R("outd", b)])

    return finish(nc, P)


def finish(nc, P):
    print("COUNTS", {e: P.q[e].total for e in P.ENGS}, "dma", sum(P.dma_cnt), max(P.dma_cnt), flush=True)
    P.wait_all("sp")
    P.emit()
    P.close()
    return nc


def rope_tables():
    rows = N_TOK // 64
    r, c = np.meshgrid(np.arange(rows), np.arange(64), indexing="ij")
    inv = np.power(np.float32(10000.0), -np.arange(16, dtype=np.float32) / np.float32(16)).astype(np.float32)
    ang = np.concatenate([r.reshape(-1, 1).astype(np.float32) * inv, c.reshape(-1, 1).astype(np.float32) * inv], -1)
    cos = np.cos(ang).astype(np.float32)
    sin = np.sin(ang).astype(np.float32)
    f = lambda a: a.reshape(16, 128, 32).transpose(1, 0, 2).reshape(128, 512)
    return np.ascontiguousarray(np.concatenate([f(cos), f(sin)], 1))


def masks():
    s = np.arange(128)[:, None]
    t = np.arange(128)[None, :]
    same = (s // 64) == (t // 64)
    fwd = (same & (s <= t)).astype(np.float32)
    bwd = (same & (s >= t)).astype(np.float32)
    return np.concatenate([fwd, bwd], 1).astype(ml_dtypes.bfloat16)


def prep_inputs(inp):
    f32 = lambda a: np.ascontiguousarray(np.asarray(a, dtype=np.float32))
    shared = {
        "w_in": f32(inp["w_in"][0]), "w_out": f32(inp["w_out"][0]), "w_xq": f32(inp["w_xq"][0]),
        "w_xkv": f32(inp["w_xkv"][0]), "w_xo": f32(inp["w_xo"][0]), "w_up": f32(inp["w_up"][0]),
        "w_down": f32(inp["w_down"][0]),
        "gains": f32(np.stack([inp["pre_mix_g"][0], inp["post_mix_g"][0], inp["pre_x_g"][0], inp["mem_norm_g"][0],
                               inp["post_x_g"][0], inp["pre_ffn_g"][0], inp["post_ffn_g"][0]])),
        "qkg": f32(np.concatenate([inp["q_norm_g"][0], inp["k_norm_g"][0]])[None, :]),
        "hgo": f32(np.asarray(inp["hg_out_norm_g"][0])[None, :]),
        "lbT": f32(np.asarray(inp["hg_lb"]).reshape(2, 2, 4, 128).transpose(3, 0, 1, 2).reshape(128, 16)),
        "convT": f32(np.concatenate([np.asarray(inp["conv_w"][0]), np.asarray(inp["conv_b"][0])[None, :]], 0)
                     .reshape(4, 44, 128).transpose(2, 1, 0).reshape(128, 176)),
        "ident": np.eye(128, dtype=np.float32).astype(ml_dtypes.bfloat16),
        "rope": rope_tables(),
        "mask": masks(),
    }
    x = np.asarray(inp["x"], dtype=np.float32)
    mem = np.asarray(inp["mem"], dtype=np.float32)
    maps = []
    for i in range(8):
        d = dict(shared)
        d["x"] = np.ascontiguousarray(x[i])
        d["mem"] = np.ascontiguousarray(mem[i])
        maps.append(d)
    return maps


_NC_CACHE = {}


def kernel(**inputs):
    if "nc" not in _NC_CACHE:
        _NC_CACHE["nc"] = build()
    nc = _NC_CACHE["nc"]
    maps = prep_inputs(inputs)
    res = run_bass_kernel_spmd(nc, maps, core_ids=list(range(8)))
    return np.stack([np.asarray(r["out"], dtype=np.float32) for r in res.results], 0)
```

```python
import numpy as np
import ml_dtypes
import concourse.bass as bass
import concourse.mybir as mybir
from concourse.bass_utils import run_bass_kernel_spmd

F32 = mybir.dt.float32
BF16 = mybir.dt.bfloat16
AF = mybir.ActivationFunctionType
ALU = mybir.AluOpType
AX = mybir.AxisListType

N_TOK = 2048
D = 1024
NB = 16
EPS = 1e-6
D_FF = 2816
N_IN = 3328


class Res:
    __slots__ = ("name", "w", "r", "big")

    def __init__(self, name):
        self.name = name
        self.w = None
        self.r = {}
        self.big = False


class EngQ:
    def __init__(self, name):
        self.name = name
        self.ops = []
        self.epoch = 0
        self.cnt = 0
        self.total = 0
        self.waited = {}
        self.pending = False


class Prog:
    ENGS = ("pe", "act", "dve", "pool", "sp")
    LIMIT = 900
    NEPOCH = {"pe": 14, "act": 10, "dve": 14, "pool": 4, "sp": 1}

    def __init__(self, nc, n_dma_sems=32):
        self.nc = nc
        self.q = {e: EngQ(e) for e in self.ENGS}
        self.sems = {}
        self.n_dma_sems = n_dma_sems
        self.dma_cnt = [0] * n_dma_sems
        self.dma_pool = {"pool": list(range(0, 16)), "sp": list(range(16, 26)), "act": list(range(26, 32))}
        self.dma_rr = {"pool": 0, "sp": 0, "act": 0}
        self._ctx = []
        self.res = {}

    def R(self, *key):
        r = self.res.get(key)
        if r is None:
            r = Res(key)
            self.res[key] = r
        return r

    def open(self):
        nc = self.nc
        for e in self.ENGS:
            for ep in range(self.NEPOCH[e]):
                c = nc.semaphore("s_%s%d" % (e, ep))
                self.sems[(e, ep)] = c.__enter__()
                self._ctx.append(c)
        for i in range(self.n_dma_sems):
            c = nc.semaphore("s_dma%d" % i)
            self.sems[("dma", i)] = c.__enter__()
            self._ctx.append(c)

    def close(self):
        for c in reversed(self._ctx):
            c.__exit__(None, None, None)

    def _need(self, eng, tok, same_ok):
        if tok is None:
            return
        key, val = tok
        q = self.q[eng]
        if key[0] == "dma":
            if q.waited.get(key, 0) >= val:
                return
            q.waited[key] = val
        else:
            src, ep = key
            if src == eng and not same_ok:
                return
            if q.waited.get(src, (-1, 0)) >= (ep, val):
                return
            q.waited[src] = (ep, val)
        sem = self.sems[key]
        q.ops.append(lambda h, sem=sem, val=val: h.wait_ge(sem, val))

    def _deps(self, eng, reads, writes):
        for r in reads:
            self._need(eng, r.w, same_ok=(eng != "pe" and not r.big))
        so = (eng != "pe")
        for w in writes:
            self._need(eng, w.w, same_ok=so)
            for tok in w.r.values():
                self._need(eng, tok, same_ok=so)

    def op(self, eng, fn, reads=(), writes=(), inc=True, big=False):
        q = self.q[eng]
        if q.cnt >= self.LIMIT and not q.pending:
            q.epoch += 1
            q.cnt = 0
            assert q.epoch < self.NEPOCH[eng], "out of epochs for " + eng
        self._deps(eng, reads, writes)
        key = (eng, q.epoch)
        val = q.cnt + 1
        tok = (key, val)
        if inc:
            q.cnt = val
            q.total += 1
            sem = self.sems[key]
            q.ops.append(lambda h, fn=fn, sem=sem: fn(h).then_inc(sem, 1))
            q.pending = False
        else:
            q.ops.append(lambda h, fn=fn: fn(h))
            q.pending = True
        for r in reads:
            r.r[eng] = tok
        for w in writes:
            w.w = tok
            w.r = {}
            w.big = big
        return tok

    def dma(self, eng, out, in_, reads=(), writes=(), also_wait=(), **kw):
        pl = self.dma_pool[eng]
        i = pl[self.dma_rr[eng] % len(pl)]
        self.dma_rr[eng] += 1
        key = ("dma", i)
        if self.dma_cnt[i] > 0:
            self._need(eng, (key, 16 * self.dma_cnt[i]), same_ok=True)
        self._deps(eng, reads, list(writes) + list(also_wait))
        self.dma_cnt[i] += 1
        assert self.dma_cnt[i] < 60
        val = 16 * self.dma_cnt[i]
        sem = self.sems[key]
        q = self.q[eng]
        q.ops.append(lambda h, sem=sem, out=out, in_=in_, kw=kw:
                     h.dma_start(out=out, in_=in_, **kw).then_inc(sem, 16))
        tok = (key, val)
        for r in reads:
            r.r[key] = tok
        for w in writes:
            w.w = tok
            w.r = {}
            w.big = False
        return tok

    def wait_all(self, eng, skip=()):
        for r in self.res.values():
            if r.name[0] in skip:
                continue
            self._need(eng, r.w, same_ok=True)
            for tok in list(r.r.values()):
                self._need(eng, tok, same_ok=True)

    def emit(self):
        nc = self.nc
        for e in self.ENGS:
            assert not self.q[e].pending, "engine %s ends with non-inc'd instr" % e
        with nc.Block() as block:
            @block.tensor
            def _(h):
                for f in self.q["pe"].ops:
                    f(h)

            @block.scalar
            def _(h):
                for f in self.q["act"].ops:
                    f(h)

            @block.vector
            def _(h):
                for f in self.q["dve"].ops:
                    f(h)

            @block.gpsimd
            def _(h):
                for f in self.q["pool"].ops:
                    f(h)

            @block.sync
            def _(h):
                for f in self.q["sp"].ops:
                    f(h)


class StopBuild(Exception):
    pass


def build(stop=99, dbg=()):
    st = {}
    try:
        return _build(stop, dbg, st)
    except StopBuild:
        return finish(st["nc"], st["P"])


def _build(stop, dbg, st):
    nc = bass.Bass("TRN2", target_bir_lowering=False)
    P = Prog(nc)
    P.open()
    R = P.R
    st["nc"] = nc
    st["P"] = P

    def chk(level):
        if stop <= level:
            raise StopBuild()

    def din(name, shape, dt=F32):
        return nc.dram_tensor(name, list(shape), dt, kind="ExternalInput").ap()

    x_d = din("x", [N_TOK, D])
    mem_d = din("mem", [256, D])
    w_in_d = din("w_in", [D, N_IN])
    w_out_d = din("w_out", [D, D])
    w_xq_d = din("w_xq", [D, D])
    w_xkv_d = din("w_xkv", [D, 2 * D])
    w_xo_d = din("w_xo", [D, D])
    w_up_d = din("w_up", [D, 2 * D_FF])
    w_down_d = din("w_down", [D_FF, D])
    gains_d = din("gains", [7, D])
    qkg_d = din("qkg", [1, 128])
    hgo_d = din("hgo", [1, 128])
    lbT_d = din("lbT", [128, 16])
    convT_d = din("convT", [128, 44 * 4])
    ident_d = din("ident", [128, 128], BF16)
    rope_d = din("rope", [128, 2 * 16 * 32])
    mask_d = din("mask", [128, 2 * 128], BF16)
    out_d = nc.dram_tensor("out", [N_TOK, D], F32, kind="ExternalOutput").ap()
    dbg_d = {}
    for name, shape in dbg:
        dbg_d[name] = nc.dram_tensor("dbg_" + name, list(shape), F32, kind="ExternalOutput").ap()

    def sb(name, shape, dt):
        return nc.alloc_sbuf_tensor("sb_" + name, list(shape), dt)

    hT = sb("hT", [128, 8, N_TOK], BF16)
    mixT = sb("mixT", [128, 8, N_TOK], BF16)
    arena1 = sb("arena1", [128, 16384], F32)
    arena2 = sb("arena2", [128, 15872], F32)
    ident = sb("ident", [128, 128], BF16)
    ones_bf = sb("ones_bf", [128, 128], BF16)
    gpre = sb("gpre", [128, D], F32)
    gpost = sb("gpost", [128, D], F32)
    stat = sb("stat", [128, 64], F32)
    stat2 = sb("stat2", [128, 64], F32)
    junk = sb("junk", [128, D], BF16)
    hb = sb("hb", [128, 2, D], BF16)
    lbT = sb("lbT", [128, 16], F32)
    lbv = sb("lbv", [128, 40], F32)
    convT = sb("convT", [128, 44 * 4], F32)
    epsc = sb("epsc", [128, 1], F32)
    psf = nc.alloc_psum_tensor("psf", [128, 3, 2, 512], F32)
    psb = nc.alloc_psum_tensor("psb", [128, 2, 1024], BF16)

    X = arena1[:].rearrange("p (b d) -> p b d", d=D)

    def a1_bf(off_bytes, shape):
        n = int(np.prod(shape))
        v = arena1[:, off_bytes // 4: off_bytes // 4 + n // 2].bitcast(BF16)
        return v, off_bytes + n * 2

    def a2_bf(off_bytes, n):
        v = arena2[:, off_bytes // 4: off_bytes // 4 + n // 2].bitcast(BF16)
        return v, off_bytes + n * 2

    def a2_f(off_bytes, n):
        v = arena2[:, off_bytes // 4: off_bytes // 4 + n]
        return v, off_bytes + n * 4

    def a1_f(off_bytes, n):
        v = arena1[:, off_bytes // 4: off_bytes // 4 + n]
        return v, off_bytes + n * 4

    def MM(out, lhsT, rhs, start, stop, reads, writes, inc=True, tp=None):
        if tp is None:
            return P.op("pe", lambda h: h.matmul(out, lhsT=lhsT, rhs=rhs, start=start, stop=stop),
                        reads=reads, writes=writes, inc=inc)
        return P.op("pe", lambda h: h.matmul(out, lhsT=lhsT, rhs=rhs, start=start, stop=stop, tile_position=tp),
                    reads=reads, writes=writes, inc=inc)

    def TR(out, in_, reads, writes, inc=True):
        return P.op("pe", lambda h: h.transpose(out, in_, ident[:]), reads=reads, writes=writes, inc=inc)

    BIG_N = 128

    def isbig(ap):
        n = 1
        for d_ in ap.shape[1:]:
            n *= d_
        return n >= BIG_N

    def ACT(out, in_, func, reads, writes, scale=None, bias=None, accum=None):
        kw = {}
        if scale is not None:
            kw["scale"] = scale
        if bias is not None:
            kw["bias"] = bias
        if accum is not None:
            kw["accum_out"] = accum
        return P.op("act", lambda h: h.activation(out=out, in_=in_, func=func, **kw), reads=reads, writes=writes,
                    big=(accum is None and isbig(out)))

    def TT(eng, out, in0, in1, op, reads, writes):
        return P.op(eng, lambda h: h.tensor_tensor(out=out, in0=in0, in1=in1, op=op), reads=reads, writes=writes, big=isbig(out))

    def TS(eng, out, in0, s1, s2, op0, op1, reads, writes):
        if s2 is None:
            return P.op(eng, lambda h: h.tensor_scalar(out=out, in0=in0, scalar1=s1, scalar2=None, op0=op0),
                        reads=reads, writes=writes, big=isbig(out))
        return P.op(eng, lambda h: h.tensor_scalar(out=out, in0=in0, scalar1=s1, scalar2=s2, op0=op0, op1=op1),
                    reads=reads, writes=writes, big=isbig(out))

    def STT(eng, out, in0, scalar, in1, op0, op1, reads, writes):
        return P.op(eng, lambda h: h.scalar_tensor_tensor(out=out, in0=in0, scalar=scalar, in1=in1, op0=op0, op1=op1),
                    reads=reads, writes=writes, big=isbig(out))

    def CP(eng, out, in_, reads, writes):
        if eng == "act":
            return ACT(out, in_, AF.Copy, reads, writes)
        return P.op(eng, lambda h: h.tensor_copy(out=out, in_=in_), reads=reads, writes=writes, big=isbig(out))

    def RECIP(out, in_, reads, writes):
        return P.op("dve", lambda h: h.reciprocal(out=out, in_=in_), reads=reads, writes=writes)

    def MEMSET(eng, ap, val, writes):
        return P.op(eng, lambda h: h.memset(ap, val), writes=writes)

    def barrier(skip=()):
        for e in Prog.ENGS:
            P.wait_all(e, skip)

    def load_w(dst, src_rows_cols, reads_w, also_wait=()):
        src = src_rows_cols.rearrange("(c p) n -> p c n", p=128)
        nch = src.shape[1]
        for c in range(nch):
            P.dma("pool", dst[:, c, :], src[:, c, :], writes=reads_w, also_wait=also_wait)

    def gain_load(dst, row, res):
        P.dma("sp", dst[:], gains_d[row:row + 1, :].partition_broadcast(128), writes=[res])

    def dump(name, ap_sb, reads, view=None):
        if name in dbg_d:
            P.dma("sp", dbg_d[name] if view is None else view(dbg_d[name]), ap_sb, reads=reads)

    P.dma("sp", ident[:], ident_d, writes=[R("ident")])
    P.dma("sp", lbT[:], lbT_d, writes=[R("lbT")])
    P.dma("sp", convT[:], convT_d, writes=[R("convT")])
    MEMSET("pool", ones_bf[:], 1.0, [R("ones")])
    MEMSET("pool", epsc[:], EPS, [R("epsc")])
    MEMSET("pool", stat[:], 0.0, [R("stat")])
    MEMSET("pool", stat2[:], 0.0, [R("stat2")])

    def prenorm_T(grow, nblk=NB, src=None, dstT=None, tag="h", srcR=None):
        src = X if src is None else src
        dstT = hT if dstT is None else dstT
        srcR = (lambda b: R("X", b)) if srcR is None else srcR
        gain_load(gpre, grow, R("gpre"))
        for b in range(nblk):
            ACT(junk[:], src[:, b, :], AF.Square, [srcR(b)], [R("junk"), R("stat")], accum=stat[:, b:b + 1])
        TS("dve", stat2[:, 0:nblk], stat[:, 0:nblk], 1.0 / D, EPS, ALU.mult, ALU.add, [R("stat")], [R("stat2")])
        ACT(stat2[:, 0:nblk], stat2[:, 0:nblk], AF.Ln, [R("stat2")], [R("stat2")])
        ACT(stat2[:, 0:nblk], stat2[:, 0:nblk], AF.Exp, [R("stat2")], [R("stat2")], scale=-0.5)
        for b in range(nblk):
            k = b % 2
            STT("dve", hb[:, k, :], src[:, b, :], stat2[:, b:b + 1], gpre[:], ALU.mult, ALU.mult,
                [srcR(b), R("stat2"), R("gpre")], [R("hb", k)])
            for c in range(8):
                TR(psb[:, k, c * 128:(c + 1) * 128], hb[:, k, c * 128:(c + 1) * 128],
                   [R("hb", k), R("ident")], [R("psb", k)], inc=(c == 7))
            CP("act", dstT[:, :, b * 128:(b + 1) * 128], psb[:, k, :].rearrange("p (c t) -> p c t", t=128),
               [], [R("psb", k), R(tag, b)])

    o2 = 0
    Watt, o2 = a2_bf(o2, 8 * 768); Watt = Watt.rearrange("p (c n) -> p c n", n=768)
    rope_raw, o2 = a2_f(o2, 1024)
    qkgB, o2 = a2_f(o2, 128)
    o2_attw = o2
    load_w(Watt, w_in_d[:, 0:768], [R("Watt")])
    P.dma("sp", rope_raw, rope_d, writes=[R("rope_raw")])
    P.dma("sp", qkgB, qkg_d.partition_broadcast(128), writes=[R("qkgB")])
    for b in range(NB):
        P.dma("sp" if b % 2 else "act", X[:, b, :], x_d[b * 128:(b + 1) * 128, :], writes=[R("X", b)])
    prenorm_T(0)
    if "hT" in dbg_d:
        tmpf = arena2[:, 13000:13000 + 2048]
        for c in range(8):
            CP("dve", tmpf, hT[:, c, :], [R("h", b) for b in range(NB)], [R("dbgtmp")])
            P.dma("sp", dbg_d["hT"][c * 128:(c + 1) * 128, :], tmpf, reads=[R("dbgtmp")])
    if stop <= 1:
        return finish(nc, P)
    barrier()
    if stop <= 1.1:
        return finish(nc, P)


    o1 = 0
    ropeT = []
    for i in range(8):
        v, o1 = a1_f(o1, 512)
        ropeT.append(v.rearrange("p (b i) -> p b i", i=32))
    QT, o1 = a1_bf(o1, [4 * N_TOK]); QT = QT.rearrange("p (j t) -> p j t", t=N_TOK)
    KTd, o1 = a1_bf(o1, [2 * N_TOK]); KTd = KTd.rearrange("p (g t) -> p g t", t=N_TOK)
    VA, o1 = a1_bf(o1, [NB * 2 * 192]); VA = VA.rearrange("p (b g d) -> p b g d", g=2, d=192)
    o2 = o2_attw
    qkv = []; sqb = []; tq = []; qn = []; qr = []
    for i in range(2):
        v, o2 = a2_f(o2, 768); qkv.append(v)
        v, o2 = a2_f(o2, 640); sqb.append(v)
        v, o2 = a2_f(o2, 4 * 320); tq.append(v.rearrange("p (a n) -> p a n", n=320))
        v, o2 = a2_f(o2, 640); qn.append(v)
        v, o2 = a2_bf(o2, 768); qr.append(v)
    PT = []
    for i in range(3):
        v, o2 = a2_bf(o2, 1024); PT.append(v.rearrange("p (u n) -> p u n", n=512))
    rc = []
    for i in range(2):
        v, o2 = a2_f(o2, 512); rc.append(v)
    assert o1 <= 65536 and o2 <= 63488, (o1, o2)

    if stop <= 1.16:
        return finish(nc, P)
    MEMSET("pool", VA, 1.0, [R("VA", b) for b in range(NB)])
    if stop <= 1.17:
        return finish(nc, P)
    cosv = rope_raw[:, 0:512].rearrange("p (b i) -> p b i", i=32)
    sinv = rope_raw[:, 512:1024].rearrange("p (b i) -> p b i", i=32)
    for qk in range(2):
        gv = qkgB[:, qk * 64:(qk + 1) * 64].rearrange("p (i two) -> p i two", two=2)
        ge = gv[:, :, 0].unsqueeze(1).broadcast_to([128, NB, 32])
        go = gv[:, :, 1].unsqueeze(1).broadcast_to([128, NB, 32])
        for ti, (tab, gg) in enumerate(((cosv, ge), (sinv, go), (sinv, ge), (cosv, go))):
            TT("dve", ropeT[qk * 4 + ti], tab, gg, ALU.mult, [R("rope_raw"), R("qkgB")], [R("ropeT")])

    def att_proj(b):
        k = b % 2
        tokb = slice(b * 128, (b + 1) * 128)
        for c in range(8):
            MM(psf[:, k, 0, :], hT[:, c, tokb], Watt[:, c, 0:512], c == 0, c == 7,
               [R("h", b), R("Watt")], [R("psf", k, 0)], inc=False)
        for c in range(8):
            MM(psf[:, k, 1, 0:256], hT[:, c, tokb], Watt[:, c, 512:768], c == 0, c == 7,
               [R("h", b), R("Watt")], [R("psf", k, 1)], inc=(c == 7))
        ACT(qkv[k][:, 0:512], psf[:, k, 0, :], AF.Copy, [], [R("psf", k, 0), R("qkv", k)])
        ACT(qkv[k][:, 512:768], psf[:, k, 1, 0:256], AF.Copy, [], [R("psf", k, 1), R("qkv", k)])
        ACT(sqb[k], qkv[k][:, 0:640], AF.Square, [R("qkv", k)], [R("sqb", k)])
        ss = stat[:, 16 + 10 * k: 26 + 10 * k]
        rs = stat2[:, 16 + 10 * k: 26 + 10 * k]
        P.op("dve", lambda h: h.tensor_reduce(out=ss, in_=sqb[k].rearrange("p (h d) -> p h d", d=64),
                                              axis=AX.X, op=ALU.add),
             reads=[R("sqb", k)], writes=[R("ss", k)])
        TS("dve", rs, ss, 1.0 / 64, EPS, ALU.mult, ALU.add, [R("ss", k)], [R("rs", k)])
        ACT(rs, rs, AF.Ln, [R("rs", k)], [R("rs", k)])
        ACT(rs, rs, AF.Exp, [R("rs", k)], [R("rs", k)], scale=-0.5)
        for qk, eng, nh, c0 in ((0, "dve", 8, 0), (1, "dve", 2, 512)):
            src = qkv[k][:, c0:c0 + nh * 64].rearrange("p (h i two) -> p h i two", i=32, two=2)
            xe = src[:, :, :, 0]
            xo = src[:, :, :, 1]
            tabs = [ropeT[qk * 4 + ti][:, b, :].unsqueeze(1).broadcast_to([128, nh, 32]) for ti in range(4)]
            tt = [tq[k][:, a, 0:nh * 32].rearrange("p (h i) -> p h i", i=32) for a in range(4)]
            rd = [R("qkv", k), R("ropeT")]
            wr = [R("tq", k, qk)]
            TT(eng, tt[0], xe, tabs[0], ALU.mult, rd, wr)
            TT(eng, tt[1], xo, tabs[1], ALU.mult, rd, wr)
            TT(eng, tt[2], xe, tabs[2], ALU.mult, rd, wr)
            TT(eng, tt[3], xo, tabs[3], ALU.mult, rd, wr)
            dst = qn[k][:, qk * 512: qk * 512 + nh * 64].rearrange("p (h i two) -> p h i two", i=32, two=2)
            TT(eng, dst[:, :, :, 0], tt[0], tt[1], ALU.subtract, wr, [R("qn", k, qk)])
            TT(eng, dst[:, :, :, 1], tt[2], tt[3], ALU.add, wr, [R("qn", k, qk)])
            rsb = rs[:, qk * 8: qk * 8 + nh]
            if qk == 0:
                TT(eng, qr[k][:, 0:512].rearrange("p (h d) -> p h d", d=64),
                   qn[k][:, 0:512].rearrange("p (h d) -> p h d", d=64),
                   rsb.unsqueeze(2).broadcast_to([128, 8, 64]), ALU.mult,
                   [R("qn", k, 0), R("rs", k)], [R("qr", k, 0)])
            else:
                for dup in range(2):
                    TT(eng, qr[k][:, 512:768].rearrange("p (g u d) -> p g u d", u=2, d=64)[:, :, dup, :],
                       qn[k][:, 512:640].rearrange("p (g d) -> p g d", d=64),
                       rsb.unsqueeze(2).broadcast_to([128, 2, 64]), ALU.mult,
                       [R("qn", k, 1), R("rs", k)], [R("qr", k, 1)])
        CP("dve", VA[:, b, :, 64:128], qkv[k][:, 640:768].rearrange("p (g d) -> p g d", d=64),
           [R("qkv", k)], [R("VA", b)])
    def att_proj2(b):
        k = b % 2
        tokb = slice(b * 128, (b + 1) * 128)
        for j in range(6):
            TR(psb[:, k, j * 128:(j + 1) * 128], qr[k][:, j * 128:(j + 1) * 128],
               [R("qr", k, 0), R("qr", k, 1), R("ident")], [R("psb", k)], inc=(j == 5))
        CP("act", QT[:, :, tokb], psb[:, k, 0:512].rearrange("p (j t) -> p j t", t=128), [], [R("psb", k), R("QT", b)])
        CP("dve", KTd[:, :, tokb], psb[:, k, 512:768].rearrange("p (g t) -> p g t", t=128), [], [R("psb", k), R("KT", b)])

    if stop <= 1.2:
        return finish(nc, P)
    for b in range(NB + 1):
        if b < NB:
            att_proj(b)
        if b >= 1:
            att_proj2(b - 1)
    if "QT" in dbg_d:
        tmpf = arena2[:, 14000:14000 + 1024]
        for c in range(4):
            for hf in range(2):
                CP("dve", tmpf, QT[:, c, hf * 1024:(hf + 1) * 1024], [R("QT", b) for b in range(NB)], [R("dbgtmp")])
                P.dma("sp", dbg_d["QT"][c * 128:(c + 1) * 128, hf * 1024:(hf + 1) * 1024], tmpf, reads=[R("dbgtmp")])
    if stop <= 1.5:
        return finish(nc, P)

    Wh, _ = a2_bf(0, 8 * 640); Wh = Wh.rearrange("p (c n) -> p c n", n=640)
    for gi, c0 in enumerate((768, 1280, 1792, 2304, 2816)):
        src = w_in_d[:, c0: c0 + 128].rearrange("(c p) n -> p c n", p=128)
        P.dma("pool", Wh[:, :, gi * 128:(gi + 1) * 128], src, writes=[R("Wh")], also_wait=[R("Watt")])
    PREFETCHED_WH0 = True
    pt_i = [0]
    psbf = [psb[:, 0, :].bitcast(F32), psb[:, 1, :].bitcast(F32)]
    unit_i = [0]

    def st_mm_unit(m, qt, kb):
        g = m // 2
        qs = slice(qt * 512, (qt + 1) * 512)
        qreads = [R("QT", qt * 4 + i) for i in range(4)]
        k = kb % 2
        for par in range(2):
            rows = slice(par * 64, par * 64 + 64)
            MM(psf[:, k, par, :], KTd[rows, g, kb * 128:(kb + 1) * 128], QT[rows, m, qs], True, True,
               qreads + [R("KT", kb)], [R("psf", k, par)], inc=(par == 1))

    UNITS = [(m_, qt_) for m_ in range(4) for qt_ in range(4)]

    def att_pair_tile(m, qt):
        g = m // 2
        j = m
        qs = slice(qt * 512, (qt + 1) * 512)
        uidx = UNITS.index((m, qt))
        ui = unit_i[0] % 2
        unit_i[0] += 1
        if ui == 0:
            accs = [(psf[:, 2, 0, :], R("psf", 2, 0)), (psf[:, 2, 1, :], R("psf", 2, 1))]
        else:
            accs = [(psbf[0], R("psb", 0)), (psbf[1], R("psb", 1))]
        qreads = [R("QT", qt * 4 + i) for i in range(4)]

        def st_mm(kb):
            k = kb % 2
            for par in range(2):
                rows = slice(par * 64, par * 64 + 64)
                MM(psf[:, k, par, :], KTd[rows, g, kb * 128:(kb + 1) * 128], QT[rows, j, qs], True, True,
                   qreads + [R("KT", kb)], [R("psf", k, par)], inc=(par == 1))

        if uidx == 0:
            st_mm(0)
        for kb in range(16):
            if kb + 1 < 16:
                st_mm(kb + 1)
            elif uidx + 1 < len(UNITS):
                st_mm_unit(UNITS[uidx + 1][0], UNITS[uidx + 1][1], 0)
            pi = pt_i[0] % 3
            pt_i[0] += 1
            k = kb % 2
            ACT(PT[pi], psf[:, k, :, :], AF.Exp, [], [R("psf", k, 0), R("psf", k, 1), R("PT", pi)], scale=0.125)
            for par in range(2):
                vsl = slice(64, 192) if par == 0 else slice(0, 128)
                MM(accs[par][0], VA[:, kb, g, vsl], PT[pi][:, par, :], kb == 0, kb == 15,
                   [R("VA", kb), R("PT", pi)], [accs[par][1]], inc=(par == 1))
        rcb = rc[ui]
        for par in range(2):
            rows = slice(par * 64, par * 64 + 64)
            orow = slice((1 - par) * 64, (1 - par) * 64 + 64)
            acc, accR = accs[par]
            P.op("dve", lambda hh, acc=acc, rows=rows, orow=orow: hh.reciprocal(out=rcb[rows, :], in_=acc[orow, :]),
                 reads=[], writes=[accR, R("rc", ui, par)])
            TT("dve", mixT[rows, j, qs], acc[rows, :], rcb[rows, :], ALU.mult, [R("rc", ui, par)], [accR, R("mixT", j, qt)])

    for m in range(4):
        for qt in range(4):
            att_pair_tile(m, qt)

    if "att" in dbg_d:
        tmpf = arena2[:, 0:2048]
        for c in range(4):
            CP("dve", tmpf, mixT[:, c, :], [R("mixT", c, qt) for qt in range(4)], [R("dbgtmp")])
            P.dma("sp", dbg_d["att"][c * 128:(c + 1) * 128, :], tmpf, reads=[R("dbgtmp")])
    if stop <= 2:
        return finish(nc, P)
    barrier(skip=("Wh",))

    bank_i = [0]

    def bank():
        i = bank_i[0] % 6
        bank_i[0] += 1
        return psf[:, i // 2, i % 2, :], R("psf", i // 2, i % 2)

    o1 = 0
    fb = {}
    for nm in ("qs", "SG", "E", "LF", "B", "KK", "rmask", "gate"):
        fb[nm], o1 = a1_f(o1, 2048)
    Tbig = arena1[:, 2 * 2048: 4 * 2048].rearrange("p (c v) -> p c v", v=128)
    o2 = 0
    Wh, o2 = a2_bf(o2, 8 * 640); Wh = Wh.rearrange("p (c n) -> p c n", n=640)
    Of, o2 = a2_f(o2, 2048); Of3 = Of.rearrange("p (b v) -> p b v", v=128)
    qtl, o2 = a2_bf(o2, 2048)
    ktl, o2 = a2_bf(o2, 2048)
    ktok, o2 = a2_bf(o2, 2048); ktok = ktok.rearrange("p (b v) -> p b v", v=128)
    vtok, o2 = a2_bf(o2, 2048); vtok = vtok.rearrange("p (b v) -> p b v", v=128)
    Sbf, o2 = a2_bf(o2, 32 * 128); Sbf = Sbf.rearrange("p (c v) -> p c v", v=128)
    AT, o2 = a2_bf(o2, 2048); AT = AT.rearrange("p (b v) -> p b v", v=128)
    recb, o2 = a2_bf(o2, 2048); recb = recb.rearrange("p (b v) -> p b v", v=128)
    hgoB, o2 = a2_f(o2, 128)
    maskS, o2 = a2_bf(o2, 256); maskS = maskS.rearrange("p (d t) -> p d t", t=128)
    gtmp, o2 = a2_f(o2, 256)
    assert o1 <= 65536 and o2 <= 63488, (o1, o2)
    gate3 = fb["gate"].rearrange("p (b v) -> p b v", v=128)

    P.dma("sp", hgoB, hgo_d.partition_broadcast(128), writes=[R("hgoB")])
    P.dma("sp", maskS, mask_d.rearrange("p (d t) -> p d t", t=128), writes=[R("maskS")])
    MEMSET("pool", fb["rmask"], 1.0, [R("rmask")])
    MEMSET("pool", fb["rmask"].rearrange("p (c t) -> p c t", t=64)[:, :, 0:1], 0.0, [R("rmask")])
    lv = lbT[:].rearrange("p (d l h) -> p d l h", l=2, h=4)
    TT("dve", lbv[:, 0:8].rearrange("p (d h) -> p d h", h=4), lv[:, :, 0, :], lv[:, :, 1, :], ALU.subtract,
       [R("lbT")], [R("lbv")])
    ACT(lbv[:, 0:8], lbv[:, 0:8], AF.Exp, [R("lbv")], [R("lbv")], scale=-1.0)
    TS("dve", lbv[:, 0:8], lbv[:, 0:8], 1.0, None, ALU.add, None, [R("lbv")], [R("lbv")])
    RECIP(lbv[:, 0:8], lbv[:, 0:8], [R("lbv")], [R("lbv")])
    TS("dve", lbv[:, 8:16], lbv[:, 0:8], -1.0, 1.0, ALU.mult, ALU.add, [R("lbv")], [R("lbv")])
    TS("dve", lbv[:, 16:24], lbv[:, 0:8], -0.5, 0.5, ALU.mult, ALU.add, [R("lbv")], [R("lbv")])
    TS("dve", lbv[:, 24:32], lbv[:, 0:8], 0.5, 0.5, ALU.mult, ALU.add, [R("lbv")], [R("lbv")])
    TS("dve", lbv[:, 32:40], lbv[:, 0:8], 0.5, -0.5, ALU.mult, ALU.add, [R("lbv")], [R("lbv")])

    ALLH = [R("h", b) for b in range(NB)]
    chk(2.1)

    def load_Wh(hh, extra=()):
        for gi, c0 in enumerate((768, 1280, 1792, 2304, 2816)):
            src = w_in_d[:, c0 + hh * 128: c0 + (hh + 1) * 128].rearrange("(c p) n -> p c n", p=128)
            P.dma("pool", Wh[:, :, gi * 128:(gi + 1) * 128], src, writes=[R("Wh")] + list(extra))

    def fm_proj(gi, dst, dstname):
        for qt in range(4):
            bk, bkR = bank()
            for c in range(8):
                MM(bk, Wh[:, c, gi * 128:(gi + 1) * 128], hT[:, c, qt * 512:(qt + 1) * 512], c == 0, c == 7,
                   ALLH[qt * 4:qt * 4 + 4] + [R("Wh")], [bkR], inc=(c == 7))
            ACT(dst[:, qt * 512:(qt + 1) * 512], bk, AF.Tanh, [], [bkR, R(dstname, qt // 2)], scale=0.5)

    def q_proj():
        for qt in range(4):
            bk, bkR = bank()
            for c in range(8):
                MM(bk, Wh[:, c, 0:128], hT[:, c, qt * 512:(qt + 1) * 512], c == 0, c == 7,
                   ALLH[qt * 4:qt * 4 + 4] + [R("Wh")], [bkR], inc=(c == 7))
            sl = slice(qt * 512, (qt + 1) * 512)
            ACT(fb["qs"][:, sl], bk, AF.Silu, [], [bkR, R("qs", qt // 2)])

    def hgrn_head(hh):
        for bp in range(8):
            bk, bkR = bank()
            for u in range(2):
                b = 2 * bp + u
                for c in range(8):
                    MM(bk[:, u * 256:(u + 1) * 256], hT[:, c, b * 128:(b + 1) * 128], Wh[:, c, 384:640], c == 0, c == 7,
                       [R("h", b), R("Wh")], [bkR], inc=(u == 1 and c == 7))
            bv = bk.rearrange("p (u n) -> p u n", n=256)
            ACT(vtok[:, 2 * bp:2 * bp + 2, :], bv[:, :, 0:128], AF.Copy, [], [bkR, R("vtok")])
            g3 = gtmp.rearrange("p (u n) -> p u n", n=128)
            ACT(gate3[:, 2 * bp:2 * bp + 2, :], bv[:, :, 128:256], AF.Silu, [], [bkR, R("gate")])

        chk(2.2)
        if hh == 0:
            q_proj()

        chk(2.3)
        for dr in range(2):
            li = dr * 4 + hh
            ha_ap = lbv[:, 16 + li: 17 + li]
            hb_ap = lbv[:, 24 + li: 25 + li]
            nha_ap = lbv[:, 32 + li: 33 + li]
            E, SG, LF, B, KK = fb["E"], fb["SG"], fb["LF"], fb["B"], fb["KK"]
            EB, ENB = SG, LF
            H2 = (slice(0, 1024), slice(1024, 2048))
            if dr == 0:
                if hh == 0:
                    fm_proj(1, E, "E")
                TH, THn, CUM, CUMn = E, "E", B, "B"
            else:
                TH, THn, CUM, CUMn = B, "B", E, "E"
            for hf in range(2):
                sl = H2[hf]
                ACT(LF[:, sl], TH[:, sl], AF.Ln, [R(THn, hf), R("lbv")], [R("LF", hf)], scale=ha_ap, bias=hb_ap)
            for hf in range(2):
                sl = H2[hf]
                TS("dve", KK[:, sl], TH[:, sl], nha_ap, ha_ap, ALU.mult, ALU.add, [R(THn, hf), R("lbv")], [R("KK", hf)])
                P.op("dve", lambda h, sl=sl, CUM=CUM: h.tensor_tensor_scan(out=CUM[:, sl], data0=fb["rmask"][:, sl],
                                                                           data1=LF[:, sl], initial=0.0, op0=ALU.mult, op1=ALU.add),
                     reads=[R("rmask"), R("LF", hf)], writes=[R(CUMn, hf)])
                if dr == 1:
                    TT("dve", LF[:, sl], LF[:, sl], CUM[:, sl], ALU.subtract, [R("LF", hf), R(CUMn, hf)], [R("LF", hf)])
                    C3 = CUM[:, sl].rearrange("p (c t) -> p c t", t=64)
                    TT("dve", B[:, sl].rearrange("p (c t) -> p c t", t=64), LF[:, sl].rearrange("p (c t) -> p c t", t=64),
                       C3[:, :, 63:64].broadcast_to([128, 16, 64]), ALU.add, [R("LF", hf), R(CUMn, hf)], [R("B", hf)])
            Bs, Bn = B, "B"
            for hf in range(2):
                sl = H2[hf]
                ACT(EB[:, sl], Bs[:, sl], AF.Exp, [R(Bn, hf)], [R("SG", hf)])
                ACT(ENB[:, sl], Bs[:, sl], AF.Exp, [R(Bn, hf)], [R("LF", hf)], scale=-1.0)
            for hf in range(2):
                sl = H2[hf]
                TT("dve", qtl[:, sl], fb["qs"][:, sl], EB[:, sl], ALU.mult, [R("qs", hf), R("SG", hf)], [R("qtl", hf)])
                TT("dve", ktl[:, sl], KK[:, sl], ENB[:, sl], ALU.mult, [R("KK", hf), R("LF", hf)], [R("ktl", hf)])
            chk(2.4)
            for half in range(2):
                for i in range(8):
                    b = half * 8 + i
                    TR(psb[:, half, i * 128:(i + 1) * 128], ktl[:, b * 128:(b + 1) * 128], [R("ktl", half), R("ident")],
                       [R("psb", half)], inc=(i == 7))
                CP("act" if half else "dve", ktok[:, half * 8:(half + 1) * 8, :],
                   psb[:, half, :].rearrange("p (b v) -> p b v", v=128), [], [R("psb", half), R("ktok")])
            for bg in range(4):
                bk, bkR = bank()
                for u in range(4):
                    b = 4 * bg + u
                    MM(bk[:, u * 128:(u + 1) * 128], ktl[:, b * 128:(b + 1) * 128], qtl[:, b * 128:(b + 1) * 128], True, True,
                       [R("ktl", bg // 2), R("qtl", bg // 2)], [bkR], inc=(u == 3))
                TT("dve", AT[:, 4 * bg:4 * bg + 4, :], bk.rearrange("p (u t) -> p u t", t=128),
                   maskS[:, dr, :].unsqueeze(1).broadcast_to([128, 4, 128]), ALU.mult, [R("maskS")], [bkR, R("AT")])
            chk(2.5)
            MEMSET("pool", Sbf[:, 0 if dr == 0 else 31, :], 0.0, [R("Sbf")])
            DEAD = [R(nm, hf) for nm in ("E", "LF") for hf in range(2)]
            SGR = [R("SG", 0), R("SG", 1)]
            n = 0
            cprev = None
            for bgi in range(4):
                bg = bgi if dr == 0 else 3 - bgi
                ub = [bank() for _ in range(2)]
                for u in range(4):
                    b = 4 * bg + u
                    for j in range(2):
                        MM(ub[j][0][:, u * 128:(u + 1) * 128], ktok[64 * j:64 * j + 64, b, :], vtok[64 * j:64 * j + 64, b, :],
                           True, True, [R("ktok"), R("vtok")], [ub[j][1]], inc=(u == 3 and j == 1), tp=(64 * j, 0))
                order = [(u, j) for u in range(4) for j in range(2)]
                if dr == 1:
                    order = order[::-1]
                for (u, j) in order:
                    c = 2 * (4 * bg + u) + j
                    Uc = ub[j][0][:, u * 128:(u + 1) * 128]
                    if n == 0:
                        CP("dve", Tbig[:, c, :], Uc, [], [ub[j][1], R("Tb", c)] + DEAD)
                    else:
                        dprev = EB[:, 64 * cprev + 63: 64 * cprev + 64] if dr == 0 else EB[:, 64 * cprev: 64 * cprev + 1]
                        STT("dve", Tbig[:, c, :], Tbig[:, cprev, :], dprev, Uc, ALU.mult, ALU.add,
                            [R("Tb", cprev)] + SGR, [ub[j][1], R("Tb", c)])
                    cprev = c
                    n += 1
            EB3 = EB.rearrange("p (c t) -> p c t", t=64)
            if dr == 0:
                TT("dve", Sbf[:, 1:32, :], Tbig[:, 0:31, :], EB3[:, 0:31, 63:64].broadcast_to([128, 31, 128]), ALU.mult,
                   [R("Tb", cprev)] + SGR + DEAD, [R("Sbf")])
            else:
                TT("dve", Sbf[:, 0:31, :], Tbig[:, 1:32, :], EB3[:, 1:32, 0:1].broadcast_to([128, 31, 128]), ALU.mult,
                   [R("Tb", cprev)] + SGR + DEAD, [R("Sbf")])
            if dr == 0:
                fm_proj(2, B, "B")
                if hh < 3:
                    load_Wh(hh + 1)
                else:
                    load_w(hT, w_xkv_d, [R("Wkv")], also_wait=ALLH)
            chk(2.6)
            for bg in range(4):
                bk, bkR = bank()
                for u in range(4):
                    b = 4 * bg + u
                    us = slice(u * 128, (u + 1) * 128)
                    MM(bk[:, us], AT[:, b, :], vtok[:, b, :], True, False, [R("AT"), R("vtok")], [bkR], inc=False)
                    for j in range(2):
                        c = 2 * b + j
                        MM(bk[64 * j:64 * j + 64, us], qtl[:, b * 128 + 64 * j: b * 128 + 64 * j + 64], Sbf[:, c, :],
                           False, j == 1, [R("qtl", bg // 2), R("Sbf")], [bkR], inc=(u == 3 and j == 1), tp=(0, 64 * j))
                o3 = Of3[:, 4 * bg:4 * bg + 4, :]
                if dr == 0:
                    ACT(o3, bk.rearrange("p (u v) -> p u v", v=128), AF.Copy, [], [bkR, R("Of")])
                else:
                    TT("dve", o3, bk.rearrange("p (u v) -> p u v", v=128), o3, ALU.add, [], [bkR, R("Of")])
        chk(2.7)
        ACT(fb["E"], Of, AF.Square, [R("Of")], [R("E", 0), R("E", 1)])
        ss = stat[:, 40:56]
        rs = stat2[:, 40:56]
        P.op("dve", lambda h: h.tensor_reduce(out=ss, in_=fb["E"].rearrange("p (b v) -> p b v", v=128), axis=AX.X, op=ALU.add),
             reads=[R("E", 0), R("E", 1)], writes=[R("ssh")])
        TS("dve", rs, ss, 1.0 / 128, EPS, ALU.mult, ALU.add, [R("ssh")], [R("rsh")])
        ACT(rs, rs, AF.Ln, [R("rsh")], [R("rsh")])
        ACT(rs, rs, AF.Exp, [R("rsh")], [R("rsh")], scale=-0.5)
        if hh < 3:
            q_proj()
            fm_proj(1, fb["E"], "E")
        SG3 = fb["SG"].rearrange("p (b v) -> p b v", v=128)
        TT("dve", SG3, Of3, rs.unsqueeze(2).broadcast_to([128, NB, 128]), ALU.mult, [R("Of"), R("rsh")], [R("SG", 0), R("SG", 1)])
        TT("dve", SG3, SG3, hgoB.unsqueeze(1).broadcast_to([128, NB, 128]), ALU.mult, [R("SG", 0), R("SG", 1), R("hgoB")],
           [R("SG", 0), R("SG", 1)])
        TT("dve", recb, SG3, gate3, ALU.mult, [R("SG", 0), R("SG", 1), R("gate")], [R("recb")])
        for half in range(2):
            for i in range(8):
                b = half * 8 + i
                TR(psb[:, half, i * 128:(i + 1) * 128], recb[:, b, :], [R("recb"), R("ident")], [R("psb", half)], inc=(i == 7))
            CP("act" if half else "dve", mixT[:, 4 + hh, half * 1024:(half + 1) * 1024], psb[:, half, :], [],
               [R("psb", half), R("mixT", 4 + hh, 2 * half), R("mixT", 4 + hh, 2 * half + 1)])

    if not PREFETCHED_WH0:
        load_Wh(0)
    for hh in range(4):
        hgrn_head(hh)
        chk(2.8 + 0.01 * hh)

    if "rec" in dbg_d:
        tmpf = fb["B"]
        for c in range(4):
            CP("dve", tmpf, mixT[:, 4 + c, :], [R("mixT", 4 + c, qt) for qt in range(4)], [R("dbgtmp")])
            P.dma("sp", dbg_d["rec"][c * 128:(c + 1) * 128, :], tmpf, reads=[R("dbgtmp")])
    if stop <= 3:
        return finish(nc, P)
    barrier(skip=("Wkv",))

    junk3 = junk[:].rearrange("p (u n) -> p u n", n=512)
    gpost3 = gpost[:].rearrange("p (u n) -> p u n", n=512)

    def proj_norm_residual(b, lhs_aps, lhs_reads, w_fn, w_reads, resid_ap, resid_reads, out_ap, out_writes):
        k = b % 3
        n = len(lhs_aps)
        for half in range(2):
            for ci in range(n):
                MM(psf[:, k, half, :], lhs_aps[ci], w_fn(ci, half), ci == 0, ci == n - 1, lhs_reads + w_reads[ci],
                   [R("psf", k, half)], inc=(half == 1 and ci == n - 1))
        ss = stat[:, 16 + b:17 + b]
        rs = stat2[:, 16 + b:17 + b]
        PB = [R("psf", k, 0), R("psf", k, 1)]
        ACT(junk3, psf[:, k, :, :], AF.Square, [], PB + [R("junk"), R("pss", b)], accum=ss)
        ACT(rs, ss, AF.Ln, [R("pss", b), R("epsc")], [R("prs", b)], scale=1.0 / D, bias=epsc[:])
        ACT(rs, rs, AF.Exp, [R("prs", b)], [R("prs", b)], scale=-0.5)
        STT("dve", psf[:, k, :, :], psf[:, k, :, :], rs, gpost3, ALU.mult, ALU.mult, [R("prs", b), R("gpost")], PB)
        TT("dve", out_ap.rearrange("p (u n) -> p u n", n=512), resid_ap.rearrange("p (u n) -> p u n", n=512), psf[:, k, :, :],
           ALU.add, resid_reads, PB + out_writes)

    def prenorm_block(b, src_ap, src_reads):
        k = b % 2
        ss = stat[:, 32 + b:33 + b]
        rs = stat2[:, 32 + b:33 + b]
        ACT(junk[:], src_ap, AF.Square, src_reads, [R("junk"), R("fss", b)], accum=ss)
        ACT(rs, ss, AF.Ln, [R("fss", b), R("epsc")], [R("frs", b)], scale=1.0 / D, bias=epsc[:])
        ACT(rs, rs, AF.Exp, [R("frs", b)], [R("frs", b)], scale=-0.5)
        STT("dve", hb[:, k, :], src_ap, rs, gpre[:], ALU.mult, ALU.mult, src_reads + [R("frs", b), R("gpre")], [R("hb", k)])

    def prenorm_block_tr(b):
        k = b % 2
        for c in range(8):
            TR(psb[:, k, c * 128:(c + 1) * 128], hb[:, k, c * 128:(c + 1) * 128], [R("hb", k), R("ident")], [R("psb", k)],
               inc=(c == 7))
        CP("act", hT[:, :, b * 128:(b + 1) * 128], psb[:, k, :].rearrange("p (c t) -> p c t", t=128), [], [R("psb", k), R("h", b)])

    o2 = 0
    Wo, o2 = a2_bf(o2, 8 * 1024); Wo = Wo.rearrange("p (c n) -> p c n", n=1024)
    load_w(Wo, w_out_d, [R("Wo")])
    gain_load(gpost, 1, R("gpost"))
    Wkv = hT
    Wq, _ = a2_bf(16384, 8 * 1024); Wq = Wq.rearrange("p (c n) -> p c n", n=1024)
    Wo2, _ = a2_bf(32768, 8 * 1024); Wo2 = Wo2.rearrange("p (c n) -> p c n", n=1024)
    load_w(Wq, w_xq_d, [R("Wq")])
    load_w(Wo2, w_xo_d, [R("Wo2")])
    for b in range(14):
        P.dma("sp" if b % 2 else "act", X[:, b, :], x_d[b * 128:(b + 1) * 128, :], writes=[R("X", b)])
    o2 = 49152
    memT, o2 = a2_bf(o2, 8 * 256); memT = memT.rearrange("p (c m) -> p c m", m=256)
    KxT, o2 = a2_bf(o2, 8 * 256); KxT = KxT.rearrange("p (j m) -> p j m", m=256)
    Vx, o2 = a2_bf(o2, 2 * 1024); Vx = Vx.rearrange("p (m n) -> p m n", n=1024)
    assert o2 <= 63488, o2
    for mb in range(2):
        P.dma("sp", X[:, 14 + mb, :], mem_d[mb * 128:(mb + 1) * 128, :], writes=[R("X", 14 + mb)])
    prenorm_T(3, nblk=2, src=X[:, 14:16, :], dstT=memT, tag="memT", srcR=lambda b: R("X", 14 + b))
    MT = [R("memT", 0), R("memT", 1)]
    for j in range(8):
        bk, bkR = bank()
        for c in range(8):
            MM(bk[:, 0:256], Wkv[:, c, j * 128:(j + 1) * 128], memT[:, c, :], c == 0, c == 7, MT + [R("Wkv")], [bkR], inc=(c == 7))
        CP("act" if j % 2 else "dve", KxT[:, j, :], bk[:, 0:256], [], [bkR, R("KxT")])
    for mb in range(2):
        for half in range(2):
            bk, bkR = bank()
            for c in range(8):
                MM(bk, memT[:, c, mb * 128:(mb + 1) * 128], Wkv[:, c, 1024 + half * 512: 1024 + (half + 1) * 512], c == 0, c == 7,
                   MT + [R("Wkv")], [bkR], inc=(c == 7))
            CP("act" if half else "dve", Vx[:, mb, half * 512:(half + 1) * 512], bk, [], [bkR, R("Vx")])
    for b in range(14, 16):
        P.dma("sp" if b % 2 else "act", X[:, b, :], x_d[b * 128:(b + 1) * 128, :], writes=[R("X", b)])
    gain_load(gpre, 2, R("gpre"))
    P._deps("act", [], [R("Wkv")])
    for b in range(NB + 2):
        if b < NB:
            tokb = slice(b * 128, (b + 1) * 128)
            proj_norm_residual(b, [mixT[:, c, tokb] for c in range(8)], [R("mixT", c, b // 4) for c in range(8)],
                               lambda ci, half: Wo[:, ci, half * 512:(half + 1) * 512], [[R("Wo")]] * 8,
                               X[:, b, :], [R("X", b)], X[:, b, :], [R("X", b)])
        if 1 <= b <= NB:
            prenorm_block(b - 1, X[:, b - 1, :], [R("X", b - 1)])
        if b >= 2:
            prenorm_block_tr(b - 2)
    if "x1" in dbg_d:
        for b in range(NB):
            P.dma("sp", dbg_d["x1"][b * 128:(b + 1) * 128, :], X[:, b, :], reads=[R("X", b)])
    if stop <= 4:
        return finish(nc, P)
    for e_ in ("act", "dve"):
        P._deps(e_, [], [R("Wo")])

    o2 = 0
    Qx, o2 = a2_bf(o2, 8 * 512); Qx = Qx.rearrange("p (j n) -> p j n", n=512)
    PTx = []
    for i in range(2):
        v, o2 = a2_bf(o2, 1024); PTx.append(v.rearrange("p (u n) -> p u n", n=512))
    rcx2 = []
    for i in range(2):
        v, o2 = a2_f(o2, 512); rcx2.append(v)
    assert o2 <= 16384, o2
    gain_load(gpost, 4, R("gpost"))
    for qt in range(4):
        qs_ = slice(qt * 512, (qt + 1) * 512)
        for j in range(8):
            bk, bkR = bank()
            for c in range(8):
                MM(bk, Wq[:, c, j * 128:(j + 1) * 128], hT[:, c, qs_], c == 0, c == 7, ALLH[qt * 4:qt * 4 + 4] + [R("Wq")],
                   [bkR], inc=(c == 7))
            CP("act" if j % 2 else "dve", Qx[:, j, :], bk, [], [bkR, R("Qx", j)])
        for hx in range(4):
            k = hx % 2
            for mb in range(2):
                for half in range(2):
                    MM(psf[:, k, mb, :], KxT[:, 2 * hx + half, mb * 128:(mb + 1) * 128], Qx[:, 2 * hx + half, :], half == 0, half == 1,
                       [R("KxT"), R("Qx", 2 * hx + half)], [R("psf", k, mb)], inc=(mb == 1 and half == 1))
            ACT(PTx[k], psf[:, k, :, :], AF.Exp, [], [R("psf", k, 0), R("psf", k, 1), R("PTx", k)], scale=1.0 / 16)
            dbk, dbkR = psbf[k], R("psb", k)
            for mb in range(2):
                MM(dbk, ones_bf[:], PTx[k][:, mb, :], mb == 0, mb == 1, [R("ones"), R("PTx", k)], [dbkR], inc=(mb == 1))
            ACT(rcx2[k], dbk, AF.Ln, [], [dbkR, R("rcx", k)])
            ACT(rcx2[k], rcx2[k], AF.Exp, [R("rcx", k)], [R("rcx", k)], scale=-1.0)
            for dh in range(2):
                bk, bkR = psf[:, 2, dh, :], R("psf", 2, dh)
                for mb in range(2):
                    MM(bk, Vx[:, mb, hx * 256 + dh * 128: hx * 256 + (dh + 1) * 128], PTx[k][:, mb, :], mb == 0, mb == 1,
                       [R("Vx"), R("PTx", k)], [bkR], inc=(mb == 1))
                TT("dve", mixT[:, 2 * hx + dh, qs_], bk, rcx2[k], ALU.mult, [R("rcx", k)], [bkR, R("mixT", 2 * hx + dh, qt)])
    gain_load(gpre, 5, R("gpre"))
    for b in range(NB + 2):
        if b < NB:
            tokb = slice(b * 128, (b + 1) * 128)
            proj_norm_residual(b, [mixT[:, c, tokb] for c in range(8)], [R("mixT", c, b // 4) for c in range(8)],
                               lambda ci, half: Wo2[:, ci, half * 512:(half + 1) * 512], [[R("Wo2")]] * 8,
                               X[:, b, :], [R("X", b)], X[:, b, :], [R("X", b)])
        if 1 <= b <= NB:
            prenorm_block(b - 1, X[:, b - 1, :], [R("X", b - 1)])
        if b >= 2:
            prenorm_block_tr(b - 2)
    if "x2" in dbg_d:
        for b in range(NB):
            P.dma("sp", dbg_d["x2"][b * 128:(b + 1) * 128, :], X[:, b, :], reads=[R("X", b)])
    if stop <= 5:
        return finish(nc, P)
    barrier()

    gain_load(gpost, 6, R("gpost"))
    for b in range(NB):
        P.dma("sp" if b % 2 else "act", out_d[b * 128:(b + 1) * 128, :], X[:, b, :], reads=[R("X", b)], writes=[R("outd", b)])
    aT16, _ = a1_bf(0, [16 * N_TOK]); aT16 = aT16.rearrange("p (j t) -> p j t", t=N_TOK)
    aT = [aT16[:, j, :] for j in range(16)] + [mixT[:, j, :] for j in range(6)]
    Wup = [mixT[:, 6 + i, :].rearrange("p (c n) -> p c n", n=256) for i in range(2)]
    o2 = 0
    ugb = []; uvb = []; agb = []; avb = []; ebb = []
    for i in range(2):
        v, o2 = a2_f(o2, 2050); ugb.append(v)
        v, o2 = a2_f(o2, 2050); uvb.append(v)
        v, o2 = a2_f(o2, 1024); agb.append(v)
        v, o2 = a2_f(o2, 1024); avb.append(v)
        v, o2 = a2_f(o2, 1024); ebb.append(v)
    assert o2 <= 63488, o2
    for i in range(2):
        for ub_, nm in ((ugb[i], "ug"), (uvb[i], "uv")):
            MEMSET("pool", ub_[:, 0:1], 0.0, [R(nm, i)])
            MEMSET("pool", ub_[:, 2049:2050], 0.0, [R(nm, i)])

    bank8_i = [0]

    def bank8():
        i = bank8_i[0] % 8
        bank8_i[0] += 1
        if i < 6:
            return psf[:, i // 2, i % 2, :], R("psf", i // 2, i % 2)
        return psbf[i - 6], R("psb", i - 6)

    def ffn_mm(j):
        s_ = j % 2
        wu = Wup[s_]
        P.dma("pool", wu[:, :, 0:128], w_up_d[:, j * 128:(j + 1) * 128].rearrange("(c p) n -> p c n", p=128),
              writes=[R("Wup", s_)])
        P.dma("pool", wu[:, :, 128:256],
              w_up_d[:, D_FF + j * 128: D_FF + (j + 1) * 128].rearrange("(c p) n -> p c n", p=128), writes=[R("Wup", s_)])
        for qt in range(4):
            qs_ = slice(qt * 512, (qt + 1) * 512)
            bg_, bgR = bank8()
            bv_, bvR = bank8()
            for c in range(8):
                MM(bg_, wu[:, c, 0:128], hT[:, c, qs_], c == 0, c == 7, ALLH[qt * 4:qt * 4 + 4] + [R("Wup", s_)], [bgR], inc=False)
            for c in range(8):
                MM(bv_, wu[:, c, 128:256], hT[:, c, qs_], c == 0, c == 7, ALLH[qt * 4:qt * 4 + 4] + [R("Wup", s_)], [bvR],
                   inc=(c == 7))
            ACT(ugb[s_][:, 1 + qt * 512: 1 + (qt + 1) * 512], bg_, AF.Copy, [], [bgR, R("ug", s_)])
            ACT(uvb[s_][:, 1 + qt * 512: 1 + (qt + 1) * 512], bv_, AF.Copy, [], [bvR, R("uv", s_)])

    def ffn_ew(j):
        s_ = j % 2
        ug, uv, ag, av, eb_ = ugb[s_], uvb[s_], agb[s_], avb[s_], ebb[s_]
        for hf in range(2):
            base = 1 + hf * 1024
            ts_ = slice(hf * 1024, (hf + 1) * 1024)
            cw = lambda cj, i: convT[:, cj * 4 + i: cj * 4 + i + 1]
            CT = [R("convT")]
            for (u_, uR, cj, dst, dR) in ((ug, R("ug", s_), j, ag, R("ag", s_)), (uv, R("uv", s_), 22 + j, av, R("av", s_))):
                ACT(dst, u_[:, base:base + 1024], AF.Identity, [uR] + CT, [dR], scale=cw(cj, 1), bias=cw(cj, 3))
                STT("dve", dst, u_[:, base - 1:base + 1023], cw(cj, 0), dst, ALU.mult, ALU.add, [uR] + CT, [dR])
                STT("dve", dst, u_[:, base + 1:base + 1025], cw(cj, 2), dst, ALU.mult, ALU.add, [uR] + CT, [dR])
            ACT(eb_, ag, AF.Silu, [R("ag", s_)], [R("eb", s_)])
            TT("dve", aT[j][:, ts_], eb_, av, ALU.mult, [R("eb", s_), R("av", s_)],
               [R("aT", j)] + ([R("X", j)] if j < 16 else []))

    for j in range(23):
        if j < 22:
            ffn_mm(j)
        if j >= 1:
            ffn_ew(j - 1)
    SETR = [[R(nm, s_) for nm in ("ug", "uv", "ag", "av", "eb")] for s_ in range(2)]

    def wd_alias(j):
        return SETR[0] if j <= 13 else (SETR[0] + SETR[1] if j == 14 else SETR[1])

    Wd, o2 = a2_bf(0, 22 * 1024); Wd = Wd.rearrange("p (j n) -> p j n", n=1024)
    wstg = []
    for i in range(3):
        v, o2 = a2_f(o2, 1024); wstg.append(v)
    assert o2 <= 63488, o2
    nst = 0
    for j in range(22):
        if j % 2 == 0:
            P.dma("pool", Wd[:, j, :], w_down_d[j * 128:(j + 1) * 128, :], writes=[R("Wd", j)], also_wait=wd_alias(j))
        else:
            si = nst % 3
            nst += 1
            P.dma("sp", wstg[si], w_down_d[j * 128:(j + 1) * 128, :], writes=[R("wstg", si)], also_wait=SETR[1])
            P._deps("act", [], wd_alias(j))
            CP("act", Wd[:, j, :], wstg[si], [R("wstg", si)], [R("Wd", j)])
    hTf = hT[:].rearrange("p c t -> p (c t)").bitcast(F32)
    xb_ = [hTf[:, 0:1024], hTf[:, 1024:2048]]
    ob_ = [hTf[:, 2048:3072], hTf[:, 3072:4096]]
    P._deps("dve", [], ALLH)
    for b in range(NB):
        tokb = slice(b * 128, (b + 1) * 128)
        k = b % 2
        P.dma("sp", xb_[k], out_d[b * 128:(b + 1) * 128, :], reads=[R("outd", b)], writes=[R("xb", k)], also_wait=ALLH)
        proj_norm_residual(b, [aT[j][:, tokb] for j in range(22)], [R("aT", j) for j in range(22)],
                           lambda ci, half: Wd[:, ci, half * 512:(half + 1) * 512], [[R("Wd", j)] for j in range(22)],
                           xb_[k], [R("xb", k)], ob_[k], [R("ob", k)])
        P.dma("sp", out_d[b * 128:(b + 1) * 128, :], ob_[k], reads=[R("ob", k)], writes=[R("outd", b)])

    return finish(nc, P)


def finish(nc, P):
    print("COUNTS", {e: P.q[e].total for e in P.ENGS}, "dma", sum(P.dma_cnt), max(P.dma_cnt), flush=True)
    P.wait_all("sp")
    P.emit()
    P.close()
    return nc


def rope_tables():
    rows = N_TOK // 64
    r, c = np.meshgrid(np.arange(rows), np.arange(64), indexing="ij")
    inv = np.power(np.float32(10000.0), -np.arange(16, dtype=np.float32) / np.float32(16)).astype(np.float32)
    ang = np.concatenate([r.reshape(-1, 1).astype(np.float32) * inv, c.reshape(-1, 1).astype(np.float32) * inv], -1)
    cos = np.cos(ang).astype(np.float32)
    sin = np.sin(ang).astype(np.float32)
    f = lambda a: a.reshape(16, 128, 32).transpose(1, 0, 2).reshape(128, 512)
    return np.ascontiguousarray(np.concatenate([f(cos), f(sin)], 1))


def masks():
    s = np.arange(128)[:, None]
    t = np.arange(128)[None, :]
    same = (s // 64) == (t // 64)
    fwd = (same & (s <= t)).astype(np.float32)
    bwd = (same & (s >= t)).astype(np.float32)
    return np.concatenate([fwd, bwd], 1).astype(ml_dtypes.bfloat16)


def prep_inputs(inp):
    f32 = lambda a: np.ascontiguousarray(np.asarray(a, dtype=np.float32))
    shared = {
        "w_in": f32(inp["w_in"][0]), "w_out": f32(inp["w_out"][0]), "w_xq": f32(inp["w_xq"][0]),
        "w_xkv": f32(inp["w_xkv"][0]), "w_xo": f32(inp["w_xo"][0]), "w_up": f32(inp["w_up"][0]),
        "w_down": f32(inp["w_down"][0]),
        "gains": f32(np.stack([inp["pre_mix_g"][0], inp["post_mix_g"][0], inp["pre_x_g"][0], inp["mem_norm_g"][0],
                               inp["post_x_g"][0], inp["pre_ffn_g"][0], inp["post_ffn_g"][0]])),
        "qkg": f32(np.concatenate([inp["q_norm_g"][0], inp["k_norm_g"][0]])[None, :]),
        "hgo": f32(np.asarray(inp["hg_out_norm_g"][0])[None, :]),
        "lbT": f32(np.asarray(inp["hg_lb"]).reshape(2, 2, 4, 128).transpose(3, 0, 1, 2).reshape(128, 16)),
        "convT": f32(np.concatenate([np.asarray(inp["conv_w"][0]), np.asarray(inp["conv_b"][0])[None, :]], 0)
                     .reshape(4, 44, 128).transpose(2, 1, 0).reshape(128, 176)),
        "ident": np.eye(128, dtype=np.float32).astype(ml_dtypes.bfloat16),
        "rope": rope_tables(),
        "mask": masks(),
    }
    x = np.asarray(inp["x"], dtype=np.float32)
    mem = np.asarray(inp["mem"], dtype=np.float32)
    maps = []
    for i in range(8):
        d = dict(shared)
        d["x"] = np.ascontiguousarray(x[i])
        d["mem"] = np.ascontiguousarray(mem[i])
        maps.append(d)
    return maps


_NC_CACHE = {}


def kernel(**inputs):
    if "nc" not in _NC_CACHE:
        _NC_CACHE["nc"] = build()
    nc = _NC_CACHE["nc"]
    maps = prep_inputs(inputs)
    res = run_bass_kernel_spmd(nc, maps, core_ids=list(range(8)))
    return np.stack([np.asarray(r["out"], dtype=np.float32) for r in res.results], 0)
```
